# Optimizing a Trainium2 kernel written in Bass

```python
import math
import jax, jax.numpy as jnp
from jax import lax
import numpy as np


D_MODEL = 2048
BATCH = 2
SEQ = 16384
DEPTH = 1
DEC_BATCH = 4
DEC_SEQ = 8192
PAST_LEN = 128

RW_HEAD = 64
RW_WIDTH = D_MODEL // 2
RW_HEADS = RW_WIDTH // RW_HEAD
DECAY_LORA = 96
ICLR_LORA = 96
GATE_LORA = 256
M_INNER = D_MODEL
M_HEADDIM = 64
M_HEADS = M_INNER // M_HEADDIM
M_STATE = 128
M_GROUPS = 8
CONV_W = 5
CHUNK = 128
N_MEM = 256
X_HEADS = 4
X_HEAD_DIM = D_MODEL // X_HEADS
D_FF = 4 * D_MODEL
ALPHA = (2.0 * DEPTH) ** 0.25
BETA = (8.0 * DEPTH) ** -0.25
LN_EPS = 1e-5
GN_EPS = 64e-5

RW_SHIFT_COLS = 3 * RW_WIDTH + DECAY_LORA + ICLR_LORA + GATE_LORA
M_CONV_CH = M_INNER + 2 * M_GROUPS * M_STATE
IN_COLS = RW_SHIFT_COLS + M_INNER + M_CONV_CH + M_HEADS + 2 * D_MODEL
RW_SPLITS = (RW_WIDTH, 2 * RW_WIDTH, 3 * RW_WIDTH, 3 * RW_WIDTH + DECAY_LORA, 3 * RW_WIDTH + DECAY_LORA + ICLR_LORA)
IN_SPLITS = (RW_SHIFT_COLS, RW_SHIFT_COLS + M_INNER, RW_SHIFT_COLS + M_INNER + M_CONV_CH, RW_SHIFT_COLS + M_INNER + M_CONV_CH + M_HEADS)

kernel_name = 'hybrid_rwkv7_mamba2_memory_encoder'


def layer_norm(x, g, b):
    xf = x.astype(jnp.float32)
    mu = jnp.mean(xf, axis=-1, keepdims=True)
    var = jnp.mean(jnp.square(xf - mu), axis=-1, keepdims=True)
    return ((xf - mu) * lax.rsqrt(var + LN_EPS) * g + b).astype(x.dtype)


def shift_prev(p):
    return jnp.pad(p[:, :-1], ((0, 0), (1, 0), (0, 0)))


def shift_next(p):
    return jnp.pad(p[:, 1:], ((0, 0), (0, 1), (0, 0)))


def bidir(fwd, bwd):
    return jnp.concatenate([fwd, jnp.flip(bwd, axis=1)], axis=0)


def merge_bidir(y, n):
    return y[:n] + jnp.flip(y[n:], axis=1)


def rw_heads(t):
    return t.reshape(t.shape[:-1] + (RW_HEADS, RW_HEAD))


def wkv7_scan(r, w, k, v, a, b):
    def step(S, inp):
        r_t, w_t, k_t, v_t, a_t, b_t = inp
        sa = jnp.einsum('nhij,nhj->nhi', S, a_t)
        S = S * w_t[:, :, None, :] + sa[..., None] * b_t[:, :, None, :] + v_t[..., None] * k_t[:, :, None, :]
        return S, jnp.einsum('nhij,nhj->nhi', S, r_t)
    n, h, d = r.shape[1:]
    _, y = lax.scan(step, jnp.zeros((n, h, d, d), jnp.float32), (r, w, k, v, a, b))
    return y


def rwkv7_branch(p, mu_prev, mu_next, w0, w2, a0, a2, g2, k_k, k_a, r_k, gn_g, gn_b):
    f32 = jnp.float32
    n = p.shape[0]
    T = p.shape[1]
    p = p + mu_prev * (shift_prev(p) - p) + mu_next * (shift_next(p) - p)
    r, k, v, dw, da, dg = jnp.split(p, RW_SPLITS, axis=-1)
    g = jax.nn.sigmoid(dg) @ g2
    hw = jnp.tanh(dw) @ w2
    ha = da @ a2
    logw = -jax.nn.softplus(-(w0[:, None, None, :] + hw[None]).astype(f32)) - 0.5
    decay = rw_heads(jnp.exp(-jnp.exp(logw)))
    a = jax.nn.sigmoid((a0[:, None, None, :] + ha[None]).astype(f32))
    kf = k.astype(f32)
    kk = rw_heads(kf * k_k)
    kk = kk / jnp.maximum(jnp.linalg.norm(kk, axis=-1, keepdims=True), 1e-12)
    k_dir = rw_heads(kf[None] * (1.0 + (a - 1.0) * k_a))
    a = rw_heads(a)
    rh = rw_heads(r.astype(f32))
    vh = rw_heads(v.astype(f32))
    xs = (bidir(rh, rh), bidir(decay[0], decay[1]), bidir(k_dir[0], k_dir[1]), bidir(vh, vh),
          bidir(-kk, -kk), bidir(kk * a[0], kk * a[1]))
    xs = tuple(jnp.moveaxis(t, 1, 0) for t in xs)
    y = merge_bidir(jnp.moveaxis(wkv7_scan(*xs), 0, 1), n)
    mu = jnp.mean(y, axis=-1, keepdims=True)
    var = jnp.mean(jnp.square(y - mu), axis=-1, keepdims=True)
    y = ((y - mu) * lax.rsqrt(var + GN_EPS)).reshape(n, T, RW_WIDTH) * gn_g + gn_b
    bonus = jnp.sum(rh * rw_heads(kf) * r_k, axis=-1, keepdims=True) * vh
    return ((y + bonus.reshape(n, T, RW_WIDTH)) * g).astype(p.dtype)


def depthwise_conv_centred(x, w, b):
    y = lax.conv_general_dilated(x, w[:, None, :], window_strides=(1,),
                                 padding=[(CONV_W // 2, CONV_W // 2)],
                                 dimension_numbers=('NWC', 'WIO', 'NWC'),
                                 feature_group_count=x.shape[-1])
    return y + b


def segsum(x):
    L = x.shape[-1]
    cs = jnp.cumsum(x, axis=-1)
    seg = cs[..., :, None] - cs[..., None, :]
    return jnp.where(jnp.tril(jnp.ones((L, L), bool)), seg, -jnp.inf)


def ssd_chunked(x, dA, Bm, Cm):
    n, T, H, P = x.shape
    c = T // CHUNK
    E = H // M_GROUPS
    x = x.reshape(n, c, CHUNK, M_GROUPS, E, P)
    Bm = Bm.reshape(n, c, CHUNK, M_GROUPS, M_STATE)
    Cm = Cm.reshape(n, c, CHUNK, M_GROUPS, M_STATE)
    dA = dA.reshape(n, c, CHUNK, M_GROUPS, E).transpose(0, 3, 4, 1, 2)
    cs = jnp.cumsum(dA, axis=-1)
    cb = jnp.einsum('nclgd,ncsgd->ngcls', Cm, Bm)
    m = cb[:, :, None] * jnp.exp(segsum(dA))
    y_diag = jnp.einsum('ngecls,ncsgep->nclgep', m, x)
    decay_states = jnp.exp(cs[..., -1:] - cs)
    states = jnp.einsum('nclgd,ngecl,nclgep->ncgepd', Bm, decay_states, x)
    chunk_decay = jnp.exp(cs[..., -1])

    def step(h, inp):
        s_c, dec_c = inp
        return h * dec_c[..., None, None] + s_c, h
    h0 = jnp.zeros((n, M_GROUPS, E, P, M_STATE), jnp.float32)
    _, h_in = lax.scan(step, h0, (jnp.moveaxis(states, 1, 0), jnp.moveaxis(chunk_decay, 3, 0)))
    h_in = jnp.moveaxis(h_in, 0, 1)
    y_off = jnp.einsum('nclgd,ncgepd,ngecl->nclgep', Cm, h_in, jnp.exp(cs))
    return (y_diag + y_off).reshape(n, T, H, P)


def mamba2_branch(z, xbc, dt_raw, conv_w, conv_b, dt_bias, a_log, d_skip, norm_g):
    f32 = jnp.float32
    n, T, _ = z.shape
    xbc = jax.nn.silu(depthwise_conv_centred(xbc, conv_w, conv_b))
    xs, Bm, Cm = jnp.split(xbc, (M_INNER, M_INNER + M_GROUPS * M_STATE), axis=-1)
    xh = xs.reshape(n, T, M_HEADS, M_HEADDIM)
    Bm = Bm.reshape(n, T, M_GROUPS, M_STATE).astype(f32)
    Cm = Cm.reshape(n, T, M_GROUPS, M_STATE).astype(f32)
    dt = jax.nn.softplus((dt_raw[None] + dt_bias[:, None, None, :]).astype(f32))
    A = -jnp.exp(a_log.astype(f32))
    dA = dt * A[:, None, None, :]
    xdt = xh[None].astype(f32) * dt[..., None]
    y = ssd_chunked(bidir(xdt[0], xdt[1]), bidir(dA[0], dA[1]), bidir(Bm, Bm), bidir(Cm, Cm))
    y = merge_bidir(y, n) + d_skip[:, None] * xh.astype(f32)
    y = y.reshape(n, T, M_INNER) * jax.nn.silu(z.astype(f32))
    yg = y.reshape(n, T, M_GROUPS, M_INNER // M_GROUPS)
    yg = yg * lax.rsqrt(jnp.mean(yg * yg, axis=-1, keepdims=True) + LN_EPS)
    return (yg.reshape(n, T, M_INNER) * norm_g).astype(z.dtype)


def memory_cross_attention(x, mem, w_q, w_kv, w_co):
    n, T, _ = x.shape
    nm = mem.shape[1]
    q = (x @ w_q).reshape(n, T, X_HEADS, X_HEAD_DIM)
    k, v = jnp.split(mem @ w_kv, 2, axis=-1)
    k = k.reshape(n, nm, X_HEADS, X_HEAD_DIM)
    v = v.reshape(n, nm, X_HEADS, X_HEAD_DIM)
    s = jnp.einsum('bthd,bmhd->bhtm', q, k).astype(jnp.float32) / math.sqrt(X_HEAD_DIM)
    pr = jax.nn.softmax(s, axis=-1).astype(v.dtype)
    o = jnp.einsum('bhtm,bmhd->bthd', pr, v).reshape(n, T, D_MODEL)
    return o @ w_co


def encoder_layer(x, mem, w_in, rw_mu_prev, rw_mu_next, rw_w0, rw_w2, rw_a0, rw_a2, rw_g2,
                  rw_k_k, rw_k_a, rw_r_k, rw_gn_g, rw_gn_b, m_conv_w, m_conv_b, m_dt_bias,
                  m_a_log, m_d, m_norm_g, w_br, w_bm, w_o, ln1_g, ln1_b, w_q, w_kv, w_co,
                  ln2_g, ln2_b, w_up, w_down, ln3_g, ln3_b):
    proj = x @ w_in
    p_rw, z, xbc, dt_raw, gates = jnp.split(proj, IN_SPLITS, axis=-1)
    u_rw = rwkv7_branch(p_rw, rw_mu_prev, rw_mu_next, rw_w0, rw_w2, rw_a0, rw_a2, rw_g2,
                        rw_k_k, rw_k_a, rw_r_k, rw_gn_g, rw_gn_b) @ w_br
    u_m = mamba2_branch(z, xbc, dt_raw, m_conv_w, m_conv_b, m_dt_bias, m_a_log, m_d, m_norm_g) @ w_bm
    g_rw, g_m = jnp.split(jax.nn.sigmoid(gates), 2, axis=-1)
    mix = (g_rw * u_rw + g_m * u_m) @ w_o
    x = layer_norm(ALPHA * x + mix, ln1_g, ln1_b)
    x = layer_norm(ALPHA * x + memory_cross_attention(x, mem, w_q, w_kv, w_co), ln2_g, ln2_b)
    h = jax.nn.relu(x @ w_up)
    return layer_norm(ALPHA * x + (h * h) @ w_down, ln3_g, ln3_b)


def encoder_trunk(x, mem, params):
    for l in range(DEPTH):
        x = encoder_layer(x, mem, *[p[l] for p in params])
    return x


def setup_inputs(seed: int = 0) -> dict:
    key = jax.random.key(seed)
    ks = iter(jax.random.split(key, 48))
    f32 = jnp.float32
    L = DEPTH

    def nrm(shape, scale):
        return jax.random.normal(next(ks), shape, f32) * scale

    def unif(shape, lo, hi):
        return jax.random.uniform(next(ks), shape, f32, lo, hi)

    chan = jnp.arange(RW_WIDTH, dtype=f32) / (RW_WIDTH - 1)
    w0_ramp = -6.0 + 5.0 * chan ** 0.85 + 0.5
    x_prompt = nrm((BATCH, SEQ, D_MODEL), 1.0)
    x_sample = nrm((DEC_BATCH, DEC_SEQ, D_MODEL), 1.0)
    mem_prompt = nrm((BATCH, N_MEM, D_MODEL), 1.0)
    mem_sample = nrm((DEC_BATCH, N_MEM, D_MODEL), 1.0)
    w_in = nrm((L, D_MODEL, IN_COLS), D_MODEL ** -0.5)
    rw_mu_prev = unif((L, RW_SHIFT_COLS), 0.0, 0.5)
    rw_mu_next = unif((L, RW_SHIFT_COLS), 0.0, 0.5)
    rw_w0 = w0_ramp + nrm((L, 2, RW_WIDTH), 0.1)
    rw_w2 = nrm((L, DECAY_LORA, RW_WIDTH), 0.5 * DECAY_LORA ** -0.5)
    rw_a0 = nrm((L, 2, RW_WIDTH), 0.1)
    rw_a2 = nrm((L, ICLR_LORA, RW_WIDTH), 0.5 * ICLR_LORA ** -0.5)
    rw_g2 = nrm((L, GATE_LORA, RW_WIDTH), GATE_LORA ** -0.5)
    rw_k_k = 0.85 + nrm((L, RW_WIDTH), 0.05)
    rw_k_a = 1.0 + nrm((L, RW_WIDTH), 0.05)
    rw_r_k = nrm((L, RW_HEADS, RW_HEAD), 0.1)
    rw_gn_g = 1.0 + nrm((L, RW_WIDTH), 0.05)
    rw_gn_b = nrm((L, RW_WIDTH), 0.02)
    m_conv_w = nrm((L, CONV_W, M_CONV_CH), CONV_W ** -0.5)
    m_conv_b = nrm((L, M_CONV_CH), 0.02)
    dt0 = jnp.exp(unif((L, 2, M_HEADS), math.log(1e-3), math.log(1e-1)))
    m_dt_bias = dt0 + jnp.log(-jnp.expm1(-dt0))
    m_a_log = jnp.log(unif((L, 2, M_HEADS), 1.0, 16.0))
    m_d = 1.0 + nrm((L, M_HEADS), 0.1)
    m_norm_g = 1.0 + nrm((L, M_INNER), 0.05)
    w_br = nrm((L, RW_WIDTH, D_MODEL), RW_WIDTH ** -0.5)
    w_bm = nrm((L, M_INNER, D_MODEL), M_INNER ** -0.5)
    w_o = nrm((L, D_MODEL, D_MODEL), BETA * D_MODEL ** -0.5)
    ln1_g = 1.0 + nrm((L, D_MODEL), 0.05)
    ln1_b = nrm((L, D_MODEL), 0.02)
    w_q = nrm((L, D_MODEL, D_MODEL), D_MODEL ** -0.5)
    w_kv = jnp.concatenate([nrm((L, D_MODEL, D_MODEL), D_MODEL ** -0.5),
                            nrm((L, D_MODEL, D_MODEL), BETA * D_MODEL ** -0.5)], axis=-1)
    w_co = nrm((L, D_MODEL, D_MODEL), BETA * D_MODEL ** -0.5)
    ln2_g = 1.0 + nrm((L, D_MODEL), 0.05)
    ln2_b = nrm((L, D_MODEL), 0.02)
    w_up = nrm((L, D_MODEL, D_FF), D_MODEL ** -0.5)
    w_down = nrm((L, D_FF, D_MODEL), BETA * D_FF ** -0.5)
    ln3_g = 1.0 + nrm((L, D_MODEL), 0.05)
    ln3_b = nrm((L, D_MODEL), 0.02)
    return {'x_prompt': x_prompt, 'x_sample': x_sample, 'mem_prompt': mem_prompt,
            'mem_sample': mem_sample, 'w_in': w_in, 'rw_mu_prev': rw_mu_prev,
            'rw_mu_next': rw_mu_next, 'rw_w0': rw_w0, 'rw_w2': rw_w2, 'rw_a0': rw_a0,
            'rw_a2': rw_a2, 'rw_g2': rw_g2, 'rw_k_k': rw_k_k, 'rw_k_a': rw_k_a,
            'rw_r_k': rw_r_k, 'rw_gn_g': rw_gn_g, 'rw_gn_b': rw_gn_b, 'm_conv_w': m_conv_w,
            'm_conv_b': m_conv_b, 'm_dt_bias': m_dt_bias, 'm_a_log': m_a_log, 'm_d': m_d,
            'm_norm_g': m_norm_g, 'w_br': w_br, 'w_bm': w_bm, 'w_o': w_o, 'ln1_g': ln1_g,
            'ln1_b': ln1_b, 'w_q': w_q, 'w_kv': w_kv, 'w_co': w_co, 'ln2_g': ln2_g,
            'ln2_b': ln2_b, 'w_up': w_up, 'w_down': w_down, 'ln3_g': ln3_g, 'ln3_b': ln3_b}


def reference(x_prompt, x_sample, mem_prompt, mem_sample, w_in, rw_mu_prev, rw_mu_next, rw_w0,
              rw_w2, rw_a0, rw_a2, rw_g2, rw_k_k, rw_k_a, rw_r_k, rw_gn_g, rw_gn_b, m_conv_w,
              m_conv_b, m_dt_bias, m_a_log, m_d, m_norm_g, w_br, w_bm, w_o, ln1_g, ln1_b, w_q,
              w_kv, w_co, ln2_g, ln2_b, w_up, w_down, ln3_g, ln3_b):
    params = (w_in, rw_mu_prev, rw_mu_next, rw_w0, rw_w2, rw_a0, rw_a2, rw_g2, rw_k_k, rw_k_a,
              rw_r_k, rw_gn_g, rw_gn_b, m_conv_w, m_conv_b, m_dt_bias, m_a_log, m_d, m_norm_g,
              w_br, w_bm, w_o, ln1_g, ln1_b, w_q, w_kv, w_co, ln2_g, ln2_b, w_up, w_down,
              ln3_g, ln3_b)
    y_prompt = encoder_trunk(x_prompt, mem_prompt, params)
    y_sample = encoder_trunk(x_sample, mem_sample, params)
    return (y_prompt, y_sample)
```

```python
import math
from contextlib import ExitStack
import numpy as np
import concourse.bass as bass
import concourse.mybir as mybir
from concourse.bass_utils import run_bass_kernel_spmd

F32 = mybir.dt.float32
BF16 = mybir.dt.bfloat16
AF = mybir.ActivationFunctionType
ALU = mybir.AluOpType
AX = mybir.AxisListType

SAME_ENGINE_SYNC = True
MB_STOP = 99
MB_SUB = 99

D = 2048
NK = 16
TT = 512
TV = 508
IN_COLS = 13792
C_R, C_K, C_V, C_DW, C_DA, C_DG = 0, 1024, 2048, 3072, 3168, 3264
C_Z, C_XBC, C_DT, C_GATES = 3520, 5568, 9664, 9696


class Buf:
    __slots__ = ("w", "r", "name", "ex")

    def __init__(self, name=""):
        self.w = None
        self.r = []
        self.name = name
        self.ex = False


class V:
    __slots__ = ("ap", "buf")

    def __init__(self, ap, buf):
        self.ap = ap
        self.buf = buf

    def __getitem__(self, idx):
        return V(self.ap[idx], self.buf)

    def rr(self, pat, **kw):
        return V(self.ap.rearrange(pat, **kw), self.buf)

    def bc(self, shape):
        return V(self.ap.to_broadcast(list(shape)), self.buf)

    def us(self, axis):
        return V(self.ap.unsqueeze(axis), self.buf)


class T:
    def __init__(self, h, name="", track=True):
        self.h = h
        self.buf = Buf(name) if track else None

    def __getitem__(self, idx):
        return V(self.h[idx], self.buf)

    def v(self, ap):
        return V(ap, self.buf)


class FW:
    def __init__(self, nc):
        self.nc = nc
        self.eng = {"pe": nc.tensor, "dve": nc.vector, "act": nc.scalar, "pool": nc.gpsimd, "sp": nc.sync}
        self.sem = {}
        self.cnt = {}
        for e in self.eng:
            self.sem[e] = nc.alloc_semaphore("sem_" + e)
            self.cnt[e] = 0
        self.seen = {e: {} for e in self.eng}
        self.n_inst = 0
        self.n_wait = 0

    def dsem(self, name):
        if name in self.sem:
            return name
        self.sem[name] = self.nc.alloc_semaphore("dsem_" + name)
        self.cnt[name] = 0
        return name

    def scan(self, out, d0, d1, initial, op0, op1):
        r, w = self._bufs([d0, d1, initial]), self._bufs([out])
        self._deps("dve", r, w)
        i = self.nc.vector.tensor_tensor_scan(out.ap, d0.ap, d1.ap, self._ap(initial), op0, op1)
        return self._done(i, "dve", 1, r, w)

    def _wait(self, e, dep):
        if dep is None:
            return
        key, val = dep
        if key == e and (e == "pe" or e == "sp" or not SAME_ENGINE_SYNC):
            return
        if self.seen[e].get(key, 0) >= val:
            return
        self.seen[e][key] = val
        self.eng[e].wait_ge(self.sem[key], val)
        self.n_wait += 1

    def _deps(self, e, reads, writes):
        mx = {}
        for b in reads:
            if b.w is not None and mx.get(b.w[0], 0) < b.w[1]:
                mx[b.w[0]] = b.w[1]
            if b.ex:
                for k, v in b.r:
                    if k != e and mx.get(k, 0) < v:
                        mx[k] = v
        for b in writes:
            if b.w is not None and mx.get(b.w[0], 0) < b.w[1]:
                mx[b.w[0]] = b.w[1]
            for k, v in b.r:
                if mx.get(k, 0) < v:
                    mx[k] = v
        for k, v in mx.items():
            self._wait(e, (k, v))

    def _done(self, inst, key, inc, reads, writes, signal=True):
        if signal:
            self.cnt[key] += inc
            inst.then_inc(self.sem[key], inc)
            dep = (key, self.cnt[key])
        else:
            dep = (key, self.cnt[key] + inc)
        for b in reads:
            b.r.append(dep)
            if len(b.r) > 24:
                mx = {}
                for k, v in b.r:
                    if mx.get(k, 0) < v:
                        mx[k] = v
                b.r = list(mx.items())
        for b in writes:
            b.w = dep
            b.r = []
        self.n_inst += 1
        return dep

    @staticmethod
    def _bufs(vs):
        out = []
        for v in vs:
            if isinstance(v, V) and v.buf is not None and v.buf not in out:
                out.append(v.buf)
        return out

    @staticmethod
    def _ap(v):
        return v.ap if isinstance(v, V) else v

    def mm(self, out, lhsT, rhs, start=True, stop=True, **kw):
        r, w = self._bufs([lhsT, rhs]), self._bufs([out])
        self._deps("pe", r, w)
        i = self.nc.tensor.matmul(out.ap, lhsT.ap, rhs.ap, start=start, stop=stop, **kw)
        return self._done(i, "pe", 1, r, w, signal=True)

    def tr(self, out, in_, ident):
        r, w = self._bufs([in_, ident]), self._bufs([out])
        self._deps("pe", r, w)
        i = self.nc.tensor.transpose(out.ap, in_.ap, ident.ap)
        return self._done(i, "pe", 1, r, w)

    def act(self, out, in_, func, bias=None, scale=None, e="act", accum_out=None):
        r, w = self._bufs([in_, bias, scale]), self._bufs([out, accum_out])
        self._deps(e, r, w)
        kw = {}
        if bias is not None:
            kw["bias"] = self._ap(bias)
        if scale is not None:
            kw["scale"] = self._ap(scale)
        if accum_out is not None:
            kw["accum_out"] = self._ap(accum_out)
        i = self.eng[e].activation(out.ap, in_.ap, func, **kw)
        return self._done(i, e, 1, r, w)

    def tt(self, out, a, b, op, e="dve"):
        r, w = self._bufs([a, b]), self._bufs([out])
        self._deps(e, r, w)
        i = self.eng[e].tensor_tensor(out.ap, a.ap, b.ap, op)
        return self._done(i, e, 1, r, w)

    def ts(self, out, in0, s1, s2, op0, op1=None, e="dve"):
        r, w = self._bufs([in0, s1, s2]), self._bufs([out])
        self._deps(e, r, w)
        kw = {}
        if op1 is not None:
            kw["op1"] = op1
        i = self.eng[e].tensor_scalar(out.ap, in0.ap, self._ap(s1), self._ap(s2), op0, **kw)
        return self._done(i, e, 1, r, w)

    def stt(self, out, in0, scalar, in1, op0, op1, e="dve"):
        r, w = self._bufs([in0, scalar, in1]), self._bufs([out])
        self._deps(e, r, w)
        i = self.eng[e].scalar_tensor_tensor(out.ap, in0.ap, self._ap(scalar), in1.ap, op0, op1)
        return self._done(i, e, 1, r, w)

    def copy(self, out, in_, e="dve"):
        r, w = self._bufs([in_]), self._bufs([out])
        self._deps(e, r, w)
        if e == "act":
            i = self.nc.scalar.copy(out.ap, in_.ap)
        else:
            i = self.eng[e].tensor_copy(out.ap, in_.ap)
        return self._done(i, e, 1, r, w)

    def memset(self, out, val, e="dve"):
        w = self._bufs([out])
        self._deps(e, [], w)
        i = self.eng[e].memset(out.ap, val)
        return self._done(i, e, 1, [], w)

    def dma(self, out, in_, sem, q="sp", **kw):
        r, w = self._bufs([in_]), self._bufs([out])
        self._deps(q, r, w)
        i = self.eng[q].dma_start(out=out.ap, in_=in_.ap, **kw)
        return self._done(i, sem, 16, r, w)

    def barrier(self):
        for e in self.eng:
            for key, val in self.cnt.items():
                if val > 0:
                    self._wait(e, (key, val)) if key != e else None


class Ctx:
    pass


def _param_layout():
    off = {}
    n = 0
    for name, w in [("mup", 28), ("mun", 28), ("w0", 16), ("a0", 16), ("k_k", 8), ("k_a", 8), ("r_k", 8),
                    ("gn_g", 8), ("gn_b", 8), ("conv_w", 160), ("conv_b", 32), ("m_norm_g", 16), ("m_d", 16),
                    ("ln1_g", 16), ("ln1_b", 16), ("ln2_g", 16), ("ln2_b", 16), ("ln3_g", 16), ("ln3_b", 16)]:
        off[name] = n
        n += w
    return off, n


POFF, NPAR = _param_layout()
XOFF = {"c0": 0, "nk_k": 28, "omk_a": 36}
NX = 44


def build(T_loc, dbg=False, phases=(0, 1, 2, 3, 4, 5)):
    NT0 = (T_loc + TV - 1) // TV
    TP = NT0 * TV
    XR = TP + 4
    nc = bass.Bass("TRN2", target_bir_lowering=False)
    fw = FW(nc)
    es = ExitStack()

    def din(name, shape, dt=F32):
        return nc.dram_tensor(name, list(shape), dt, kind="ExternalInput")

    def dscr(name, shape, dt=F32):
        return nc.dram_tensor(name, list(shape), dt, kind=("ExternalOutput" if dbg else "Internal"))

    x_ext = din("x_ext", [XR, D])
    w_in = din("w_in", [D, IN_COLS])
    pvec_d = din("pvec", [128, NPAR])
    ident_d = din("ident", [128, 128])
    blk_d = din("blk64", [128, 128])
    w2_d = din("rw_w2", [96, 1024])
    a2_d = din("rw_a2", [96, 1024])
    g2_d = din("rw_g2", [256, 1024])
    dtp_d = din("dtp", [128, 2, 2, 4, 32])

    S = {}
    for nm in ["R", "V", "A", "G", "BON", "KD1", "KD2", "B1", "B2", "LW1", "LW2"]:
        S[nm] = dscr("s_" + nm, [1024, TP])
    S["XBC"] = dscr("s_XBC", [4096, TP])
    S["DTS"] = dscr("s_DTS", [TP, 4, 32])

    def sb(name, shape, dt=F32):
        return T(es.enter_context(nc.sbuf_tensor("sb_" + name, list(shape), dt)), name)

    PB = [T(nc.alloc_psum_tensor("pb%d" % i, [128, 512], F32), "pb%d" % i) for i in range(8)]
    for t_ in PB:
        t_.buf.ex = True

    pvec = sb("pvec", [128, NPAR])
    xpar = sb("xpar", [128, NX])
    ident = sb("ident", [128, 128])
    blk = sb("blk", [128, 128])
    dc = fw.dsem("const")
    dcp = fw.dsem("constp")
    fw.dma(pvec[:], V(pvec_d.ap(), None), fw.dsem("c1"))
    fw.dma(ident[:], V(ident_d.ap(), None), fw.dsem("c2"))
    fw.dma(blk[:], V(blk_d.ap(), None), fw.dsem("c3"))
    w_in_v = w_in.ap().rearrange("(k p) n -> p k n", p=128)

    def pv(name, j=0, n=128):
        c = POFF[name] + j
        return pvec[0:n, c:c + 1]

    def xp(name, j=0, n=128):
        c = XOFF[name] + j
        return xpar[0:n, c:c + 1]

    fw.tt(xpar[:, 0:28], pvec[:, POFF["mup"]:POFF["mup"] + 28], pvec[:, POFF["mun"]:POFF["mun"] + 28], ALU.add)
    fw.ts(xpar[:, 0:28], xpar[:, 0:28], -1.0, 1.0, ALU.mult, ALU.add)
    fw.ts(xpar[:, 28:36], pvec[:, POFF["k_k"]:POFF["k_k"] + 8], -1.0, None, ALU.mult)
    fw.ts(xpar[:, 36:44], pvec[:, POFF["k_a"]:POFF["k_a"] + 8], -1.0, 1.0, ALU.mult, ALU.add)

    def phase0():
        st = ExitStack()

        def sb0(name, shape, dt=F32):
            return T(st.enter_context(nc.sbuf_tensor("p0_" + name, list(shape), dt)), name)

        w2 = sb0("w2", [96, 1024], BF16)
        a2 = sb0("a2", [96, 1024], BF16)
        g2 = sb0("g2", [128, 2, 1024], BF16)
        wdt = sb0("wdt", [128, NK, 32], BF16)
        dtp = sb0("dtp", [128, 2, 2, 4, 32])
        An = sb0("An", [128, 2, 4, 32])
        fw.dma(dtp[:], V(dtp_d.ap(), None), fw.dsem("c4"))
        fw.dma(w2[:], V(w2_d.ap(), None), fw.dsem("c5"), q="pool")
        fw.dma(a2[:], V(a2_d.ap(), None), fw.dsem("c6"), q="pool")
        fw.dma(g2[:], V(g2_d.ap().rearrange("(k p) n -> p k n", p=128), None), fw.dsem("c7"), q="pool")
        fw.dma(wdt[:], V(w_in_v[:, :, C_DT:C_DT + 32], None), dcp, q="pool")
        fw.act(An[:], dtp[:, 1], AF.Exp)
        fw.ts(An[:], An[:], -1.0, None, ALU.mult)
        xs = [sb0("xs%d" % j, [128, D]) for j in range(2)]
        xs_sem = [fw.dsem("xs%d" % j) for j in range(2)]
        xT = sb0("xT", [128, NK, TT], BF16)
        WG = [sb0("wg%d" % j, [128, NK, 512], BF16) for j in range(2)]
        wg_sem = [fw.dsem("wg%d" % j) for j in range(2)]
        ag = sb0("ag", [128, 8, 2, TV])
        kp = sb0("kp", [128, 8, TV])
        rk = sb0("rk", [128, 8, TV])
        tdw = sb0("tdw", [96, TV], BF16)
        tda = sb0("tda", [96, TV], BF16)
        tdg = sb0("tdg", [128, 2, TV], BF16)
        NS = 8
        ost = [sb0("ost%d" % j, [128, TV]) for j in range(NS)]
        ost_sem = [fw.dsem("ost%d" % j) for j in range(NS)]
        tmp = [sb0("tmp%d" % j, [128, TV]) for j in range(4)]
        dts = sb0("dts", [128, 4, 4, 32])
        dtt = sb0("dtt", [128, 4, 32])
        dts_sem = fw.dsem("dts")
        cnt = {"ost": 0, "tmp": 0, "wg": 0, "pb": 0, "aux": 0}

        def new_ost():
            j = cnt["ost"] % NS
            cnt["ost"] += 1
            return ost[j], ost_sem[j]

        def new_tmp():
            j = cnt["tmp"] % 4
            cnt["tmp"] += 1
            return tmp[j]

        def new_pb():
            j = cnt["pb"] % 4
            cnt["pb"] += 1
            return PB[j]

        def new_aux():
            j = cnt["aux"] % 2
            cnt["aux"] += 1
            return PB[6 + j]

        def store(name, row0, nrow, i, o, osem):
            fw.dma(V(S[name].ap()[row0:row0 + nrow, i * TV:(i + 1) * TV], None), o[0:nrow, :], osem)

        def load_wg(col0, n):
            j = cnt["wg"] % 2
            cnt["wg"] += 1
            fw.dma(WG[j][:, :, 0:n], V(w_in_v[:, :, col0:col0 + n], None), wg_sem[j], q="pool")
            return WG[j]

        def proj(P, wt, c0, ncol):
            for k in range(NK):
                fw.mm(P[0:ncol, :], wt[:, k, c0:c0 + ncol], xT[:, k, :], start=(k == 0), stop=(k == NK - 1))

        def shift(P, pc, nrow, out):
            fw.act(out, P[0:nrow, 2:2 + TV], AF.Copy, scale=xp("c0", pc, nrow))
            fw.stt(out, P[0:nrow, 1:1 + TV], pv("mup", pc, nrow), out, ALU.mult, ALU.add)
            fw.stt(out, P[0:nrow, 3:3 + TV], pv("mun", pc, nrow), out, ALU.mult, ALU.add)

        for i in range(NT0):
            for j in range(4):
                xb = xs[j % 2]
                fw.dma(xb[:], V(x_ext.ap()[i * TV + j * 128:i * TV + (j + 1) * 128, :], None), xs_sem[j % 2])
                for kq in range(4):
                    pt = PB[4 + (kq % 2)]
                    for k4 in range(4):
                        k = kq * 4 + k4
                        fw.tr(pt[:, k4 * 128:(k4 + 1) * 128], xb[:, k * 128:(k + 1) * 128], ident[:])
                    fw.copy(xT[:, kq * 4:(kq + 1) * 4, j * 128:(j + 1) * 128],
                            pt.v(pt.h[:].rearrange("p (a b) -> p a b", a=4)), e=("act" if kq % 2 else "dve"))
            pd = new_aux()
            pdv = pd.v(pd.h[:, 0:128].rearrange("p (a b) -> p a b", a=4))
            for j in range(4):
                for k in range(NK):
                    fw.mm(pdv[:, j, :], xT[:, k, j * 128:(j + 1) * 128], wdt[:, k, :], start=(k == 0), stop=(k == NK - 1))
            for d in range(2):
                fw.tt(dtt[:], pdv, dtp[:, 0, d], ALU.add)
                fw.act(dtt[:], dtt[:], AF.Exp)
                fw.act(dts[:, :, d, :], dtt[:], AF.Ln, bias=1.0)
                fw.tt(dts[:, :, 2 + d, :], dts[:, :, d, :], An[:, d], ALU.mult)
            for j in range(4):
                lo, hi = max(2, 128 * j), min(2 + TV, 128 * j + 128)
                fw.dma(V(S["DTS"].ap()[i * TV + lo - 2:i * TV + hi - 2], None), dts[lo - 128 * j:hi - 128 * j, j], dts_sem)
            wt = load_wg(C_DW, 448)
            P = new_pb()
            proj(P, wt, 0, 96)
            t = new_tmp()
            shift(P, 24, 96, t[0:96, :])
            fw.act(tdw[:], t[0:96, :], AF.Tanh)
            P = new_pb()
            proj(P, wt, 96, 96)
            t = new_tmp()
            shift(P, 25, 96, t[0:96, :])
            fw.copy(tda[:], t[0:96, :], e="pool")
            for c in range(2):
                P = new_pb()
                proj(P, wt, 192 + 128 * c, 128)
                t = new_tmp()
                shift(P, 26 + c, 128, t[:])
                fw.act(tdg[:, c, :], t[:], AF.Sigmoid)
            for c in range(8):
                P = new_aux()
                fw.mm(P[:, 0:TV], w2[:, c * 128:(c + 1) * 128], tdw[:])
                for d in range(2):
                    t = new_tmp()
                    fw.act(t[:], P[:, 0:TV], AF.Sigmoid, bias=pv("w0", d * 8 + c))
                    o, osem = new_ost()
                    fw.ts(o[:], t[:], -math.exp(-0.5), None, ALU.mult, e="pool")
                    store("LW%d" % (d + 1), c * 128, 128, i, o, osem)
                P = new_aux()
                fw.mm(P[:, 0:TV], a2[:, c * 128:(c + 1) * 128], tda[:])
                for d in range(2):
                    fw.act(ag[:, c, d, :], P[:, 0:TV], AF.Sigmoid, bias=pv("a0", d * 8 + c))
                P = new_aux()
                for k in range(2):
                    fw.mm(P[:, 0:TV], g2[:, k, c * 128:(c + 1) * 128], tdg[:, k, :], start=(k == 0), stop=(k == 1))
                o, osem = new_ost()
                fw.copy(o[:], P[:, 0:TV], e="dve")
                store("G", c * 128, 128, i, o, osem)
            for gi in range(2):
                wt = load_wg(C_K + 512 * gi, 512)
                for cc in range(4):
                    c = gi * 4 + cc
                    P = new_pb()
                    proj(P, wt, cc * 128, 128)
                    shift(P, 8 + c, 128, kp[:, c, :])
                    sq = new_tmp()
                    fw.act(sq[:], kp[:, c, :], AF.Square, scale=pv("k_k", c))
                    Pa = new_aux()
                    fw.mm(Pa[:, 0:TV], blk[:], sq[:])
                    rn = new_tmp()
                    fw.act(rn[:], Pa[:, 0:TV], AF.Ln, bias=1e-24)
                    fw.act(rn[:], rn[:], AF.Exp, scale=-0.5)
                    oa, osem = new_ost()
                    fw.stt(oa[:], kp[:, c, :], xp("nk_k", c), rn[:], ALU.mult, ALU.mult)
                    store("A", c * 128, 128, i, oa, osem)
                    for d in range(2):
                        o, osem = new_ost()
                        fw.stt(o[:], oa[:], -1.0, ag[:, c, d, :], ALU.mult, ALU.mult)
                        store("B%d" % (d + 1), c * 128, 128, i, o, osem)
                        t = new_tmp()
                        fw.act(t[:], ag[:, c, d, :], AF.Identity, scale=pv("k_a", c), bias=xp("omk_a", c))
                        o, osem = new_ost()
                        fw.tt(o[:], t[:], kp[:, c, :], ALU.mult, e="pool")
                        store("KD%d" % (d + 1), c * 128, 128, i, o, osem)
            for gi in range(2):
                wt = load_wg(C_R + 512 * gi, 512)
                for cc in range(4):
                    c = gi * 4 + cc
                    P = new_pb()
                    proj(P, wt, cc * 128, 128)
                    o, osem = new_ost()
                    shift(P, c, 128, o[:])
                    store("R", c * 128, 128, i, o, osem)
                    t = new_tmp()
                    fw.stt(t[:], o[:], pv("r_k", c), kp[:, c, :], ALU.mult, ALU.mult)
                    Pa = new_aux()
                    fw.mm(Pa[:, 0:TV], blk[:], t[:])
                    fw.copy(rk[:, c, :], Pa[:, 0:TV], e="act")
            for gi in range(2):
                wt = load_wg(C_V + 512 * gi, 512)
                for cc in range(4):
                    c = gi * 4 + cc
                    P = new_pb()
                    proj(P, wt, cc * 128, 128)
                    o, osem = new_ost()
                    shift(P, 16 + c, 128, o[:])
                    store("V", c * 128, 128, i, o, osem)
                    o2, osem2 = new_ost()
                    fw.tt(o2[:], o[:], rk[:, c, :], ALU.mult, e="pool")
                    store("BON", c * 128, 128, i, o2, osem2)
            for gi in range(8):
                wt = load_wg(C_XBC + 512 * gi, 512)
                for cc in range(4):
                    c = gi * 4 + cc
                    P = new_pb()
                    proj(P, wt, cc * 128, 128)
                    t = new_tmp()
                    fw.act(t[:], P[:, 0:TV], AF.Identity, scale=pv("conv_w", c), bias=pv("conv_b", c))
                    for j in range(1, 5):
                        fw.stt(t[:], P[:, j:j + TV], pv("conv_w", 32 * j + c), t[:], ALU.mult, ALU.add)
                    o, osem = new_ost()
                    fw.act(o[:], t[:], AF.Silu)
                    store("XBC", c * 128, 128, i, o, osem)
        fw.barrier()
        st.close()


    NST = T_loc // 128
    S["YRW"] = dscr("s_YRW", [1024, TP])
    st_rw_out = nc.dram_tensor("st_rw_out", [128, 8, 128], F32, kind="ExternalOutput")
    st_rw_in = din("st_rw_in", [128, 8, 128])
    rwc_d = din("rwc", [128, 2, 512])
    rmask_d = din("rmask", [128, 1024])
    identb = sb("identb", [128, 128], BF16)
    fw.dma(identb[:], V(ident_d.ap(), None), fw.dsem("c8"), q="pool")

    def rwkv_phase(d):
        st = ExitStack()

        def sbp(name, shape, dt=F32):
            return T(st.enter_context(nc.sbuf_tensor("rw%d_" % d + name, list(shape), dt)), name)

        rwc = sbp("rwc", [128, 2, 512])
        rmask = sbp("rmask", [128, 1024])
        fw.dma(rwc[:], V(rwc_d.ap(), None), fw.dsem("c9"))
        fw.dma(rmask[:], V(rmask_d.ap(), None), fw.dsem("c10"))
        names = ["R", "KD%d" % (d + 1), "V", "A", "B%d" % (d + 1), "LW%d" % (d + 1)]
        LD = [[sbp("ld%d_%d" % (j, b), [128, 8, 128]) for j in range(6)] for b in range(2)]
        ld_sem = [[fw.dsem("rwld%d_%d" % (j, b)) for j in range(6)] for b in range(2)]
        Y1 = [sbp("y1_%d" % b, [128, 8, 128]) for b in range(2)]
        y1_sem = [fw.dsem("rwy1_%d" % b) for b in range(2)]
        YB = [sbp("yb_%d" % b, [128, 8, 128]) for b in range(2)]
        yb_sem = [fw.dsem("rwyb_%d" % b) for b in range(2)]
        pre = sbp("pre", [128, 1024])
        cl = sbp("cl", [128, 1024])
        ecl = sbp("ecl", [128, 8, 128])
        encl = sbp("encl", [128, 8, 128])
        ecx = sbp("ecx", [128, 8, 128])
        xt = [sbp("xt%d" % j, [128, 8, 128]) for j in range(4)]
        BT = [sbp("BTbd%d" % b, [128, 8, 128], BF16) for b in range(2)]
        KT = [sbp("KTbd%d" % b, [128, 8, 128], BF16) for b in range(2)]
        AR = [sbp("ARbd%d" % b, [128, 8, 256], BF16) for b in range(2)]
        VB = [sbp("Vbd%d" % b, [128, 8, 128], BF16) for b in range(2)]
        for b in range(2):
            for t_ in (BT[b], KT[b], AR[b], VB[b]):
                fw.memset(t_[:], 0.0, e="pool")
        S0 = sbp("S0", [128, 8, 128])
        S0b = sbp("S0b", [128, 8, 128], BF16)
        ssem = fw.dsem("rwstate")
        if d == 0:
            fw.memset(S0[:], 0.0)
        else:
            fw.dma(S0[:], V(st_rw_in.ap(), None), ssem)
        fw.copy(S0b[:], S0[:], e="act")
        ABs = [sbp("ABs%d" % b, [128, 4, 256], BF16) for b in range(2)]
        AKs = [sbp("AKs%d" % b, [128, 4, 256], BF16) for b in range(2)]
        NTs = [[sbp("NTs%d_%d" % (b, j), [128, 4, 128], BF16) for j in range(2)] for b in range(2)]
        Ns = [[sbp("Ns%d_%d" % (b, j), [128, 4, 128], BF16) for j in range(2)] for b in range(2)]
        Ps = [[sbp("Ps%d_%d" % (b, j), [128, 4, 128], BF16) for j in range(2)] for b in range(2)]
        VTs = [sbp("VTs%d" % b, [128, 4, 128], BF16) for b in range(2)]
        GTs = [sbp("GTs%d" % b, [128, 4, 128], BF16) for b in range(2)]
        UTs = [sbp("UTs%d" % b, [128, 4, 128], BF16) for b in range(2)]
        BKT = [sbp("BKT%d" % b, [128, 4, 2, 128], BF16) for b in range(2)]
        stmp = [sbp("stmp%d" % b, [128, 4, 128]) for b in range(2)]
        pbc = {"n": 0}

        def pbank():
            j = pbc["n"] % 8
            pbc["n"] += 1
            return PB[j]

        mAB = rwc[:, d, 0:256]
        mNT = rwc[:, d, 256:384]
        tiles = list(range(NST)) if d == 0 else list(range(NST - 1, -1, -1))
        chunks = (0, 1) if d == 0 else (1, 0)

        def load_tile(n):
            ti = tiles[n]
            b = n % 2
            for j in range(6):
                fw.dma(LD[b][j][:], V(S[names[j]].ap()[:, ti * 128:(ti + 1) * 128].rearrange("(c p) t -> p c t", p=128), None),
                       ld_sem[b][j])
            if d == 1:
                fw.dma(Y1[b][:], V(S["YRW"].ap()[:, ti * 128:(ti + 1) * 128].rearrange("(c p) t -> p c t", p=128), None),
                       y1_sem[b])

        load_tile(0)
        qn = 0
        for n in range(NST):
            ti = tiles[n]
            b = n % 2
            if n + 1 < NST:
                load_tile(n + 1)
            r_, k_, v_, a_, b_, lw_ = [LD[b][j] for j in range(6)]
            fw.scan(pre[:], rmask[:], lw_[:].rr("p c t -> p (c t)"), 0.0, ALU.mult, ALU.add)
            pre4 = pre[:].rr("p (c t) -> p c t", t=64)
            cl4 = cl[:].rr("p (c t) -> p c t", t=64)
            lw4 = lw_[:].rr("p c (u t) -> p (c u) t", t=64)
            if d == 0:
                clv = pre
            else:
                fw.tt(cl4, lw4, pre4, ALU.subtract)
                fw.tt(cl4, cl4, pre4[:, :, 63:64].bc([128, 16, 64]), ALU.add)
                clv = cl
            clf = clv[:]
            fw.act(ecl[:].rr("p c t -> p (c t)"), clf, AF.Exp)
            fw.act(encl[:].rr("p c t -> p (c t)"), clf, AF.Exp, scale=-1.0)
            fw.tt(ecx[:].rr("p c t -> p (c t)"), clf, lw_[:].rr("p c t -> p (c t)"), ALU.subtract, e="pool")
            fw.act(ecx[:].rr("p c t -> p (c t)"), ecx[:].rr("p c t -> p (c t)"), AF.Exp)
            fw.tt(xt[0][:], b_[:], encl[:], ALU.mult, e="pool")
            fw.tt(xt[1][:], k_[:], encl[:], ALU.mult)
            fw.tt(xt[2][:], a_[:], ecx[:], ALU.mult, e="pool")
            fw.tt(xt[3][:], r_[:], ecl[:], ALU.mult)
            for ci in chunks:
                cb = (2 * n + ci) % 2
                cs = slice(ci * 64, ci * 64 + 64)
                for hh in range(2):
                    ps = slice(64 * hh, 64 * hh + 64)
                    fs = slice(64 * hh, 64 * hh + 64)
                    fw.copy(BT[cb][ps, :, fs], xt[0][ps, :, cs], e="pool")
                    fw.copy(KT[cb][ps, :, fs], xt[1][ps, :, cs], e="dve")
                    fw.copy(AR[cb][ps, :, fs], xt[2][ps, :, cs], e="pool")
                    fw.copy(AR[cb][ps, :, slice(128 + 64 * hh, 192 + 64 * hh)], xt[3][ps, :, cs], e="act")
                    fw.copy(VB[cb][ps, :, fs], v_[ps, :, cs], e="dve")
                wl = ecl[:, :, (ci * 64 + 63) if d == 0 else (ci * 64)]
                for q in range(2):
                    qb = qn % 2
                    qn += 1
                    p0 = 4 * q
                    for half in range(2):
                        pa = pbank()
                        pk = pbank()
                        for pp in range(2):
                            p = p0 + 2 * half + pp
                            fw.mm(pa[:, pp * 256:(pp + 1) * 256], BT[cb][:, p, :], AR[cb][:, p, :])
                            fw.mm(pk[:, pp * 256:(pp + 1) * 256], KT[cb][:, p, :], AR[cb][:, p, :])
                        fw.tt(ABs[qb][:, 2 * half:2 * half + 2, :], pa[:].rr("p (a b) -> p a b", a=2),
                              mAB.us(1).bc([128, 2, 256]), ALU.mult)
                        fw.tt(AKs[qb][:, 2 * half:2 * half + 2, :], pk[:].rr("p (a b) -> p a b", a=2),
                              mAB.us(1).bc([128, 2, 256]), ALU.mult)
                    pn = pbank()
                    for pp in range(4):
                        p = p0 + pp
                        fw.mm(pn[:, pp * 128:(pp + 1) * 128], AR[cb][:, p, 0:128], BT[cb][:, p, :])
                    fw.tt(NTs[qb][0][:], pn[:].rr("p (a b) -> p a b", a=4), mNT.us(1).bc([128, 4, 128]), ALU.mult)
                    Ncur = ABs[qb][:, :, 0:128]
                    NTcur = NTs[qb][0][:]
                    fw.tt(Ps[qb][0][:], Ncur, identb[:].us(1).bc([128, 4, 128]), ALU.add, e="pool")
                    Pcur = Ps[qb][0][:]
                    for lev in range(1, 6):
                        j = lev % 2
                        pnt = pbank()
                        for pp in range(4):
                            fw.mm(pnt[:, pp * 128:(pp + 1) * 128], Ncur[:, pp, :], NTcur[:, pp, :])
                        fw.copy(NTs[qb][j][:], pnt[:].rr("p (a b) -> p a b", a=4), e="act")
                        if lev <= 4:
                            pnn = pbank()
                            for pp in range(4):
                                fw.mm(pnn[:, pp * 128:(pp + 1) * 128], NTcur[:, pp, :], Ncur[:, pp, :])
                            fw.copy(Ns[qb][j][:], pnn[:].rr("p (a b) -> p a b", a=4), e="dve")
                            Nnext = Ns[qb][j][:]
                        NTnext = NTs[qb][j][:]
                        pp_ = pbank()
                        for pp in range(4):
                            fw.mm(pp_[:, pp * 128:(pp + 1) * 128], identb[:], Pcur[:, pp, :], start=True, stop=False)
                            fw.mm(pp_[:, pp * 128:(pp + 1) * 128], NTnext[:, pp, :], Pcur[:, pp, :], start=False, stop=True)
                        fw.copy(Ps[qb][j][:], pp_[:].rr("p (a b) -> p a b", a=4), e="dve")
                        Pcur = Ps[qb][j][:]
                        NTcur = NTnext
                        if lev <= 4:
                            Ncur = Nnext
                    Minv = Pcur
                    pv_ = pbank()
                    pvb = pv_.v(pv_.h[:].bitcast(BF16)[:, 0:512].rearrange("p (a b) -> p a b", a=4))
                    for pp in range(4):
                        fw.tr(pvb[:, pp, :], VB[cb][:, p0 + pp, :], identb[:])
                    fw.copy(VTs[qb][:], pvb, e="act")
                    pg = pbank()
                    for pp in range(4):
                        p = p0 + pp
                        fw.mm(pg[:, pp * 128:(pp + 1) * 128], AR[cb][:, p, 0:128], S0b[:, p, :], start=True, stop=False)
                        fw.mm(pg[:, pp * 128:(pp + 1) * 128], AKs[qb][:, pp, 0:128], VTs[qb][:, pp, :], start=False, stop=True)
                    fw.copy(GTs[qb][:], pg[:].rr("p (a b) -> p a b", a=4), e="dve")
                    pu = pbank()
                    for pp in range(4):
                        fw.mm(pu[:, pp * 128:(pp + 1) * 128], Minv[:, pp, :], GTs[qb][:, pp, :])
                    fw.copy(UTs[qb][:], pu[:].rr("p (a b) -> p a b", a=4), e="act")
                    py = pbank()
                    for pp in range(4):
                        p = p0 + pp
                        o = py[:, pp * 128:(pp + 1) * 128]
                        fw.mm(o, S0b[:, p, :], AR[cb][:, p, 128:256], start=True, stop=False)
                        fw.mm(o, UTs[qb][:, pp, :], ABs[qb][:, pp, 128:256], start=False, stop=False)
                        fw.mm(o, VTs[qb][:, pp, :], AKs[qb][:, pp, 128:256], start=False, stop=True)
                    py4 = py[:].rr("p (a b) -> p a b", a=4)
                    if d == 0:
                        fw.copy(YB[b][0:64, p0:p0 + 4, cs], py4[0:64, :, 0:64], e="act")
                        fw.copy(YB[b][64:128, p0:p0 + 4, cs], py4[64:128, :, 64:128], e="dve")
                    else:
                        fw.tt(YB[b][0:64, p0:p0 + 4, cs], py4[0:64, :, 0:64], Y1[b][0:64, p0:p0 + 4, cs], ALU.add)
                        fw.tt(YB[b][64:128, p0:p0 + 4, cs], py4[64:128, :, 64:128], Y1[b][64:128, p0:p0 + 4, cs], ALU.add)
                    pt_ = pbank()
                    ptb = pt_.v(pt_.h[:].bitcast(BF16).rearrange("p (a c b) -> p a c b", a=4, c=2))
                    for pp in range(4):
                        p = p0 + pp
                        fw.tr(ptb[:, pp, 0, :], BT[cb][:, p, :], identb[:])
                        fw.tr(ptb[:, pp, 1, :], KT[cb][:, p, :], identb[:])
                    fw.copy(BKT[qb][:], ptb, e="act")
                    pd_ = pbank()
                    for pp in range(4):
                        o = pd_[:, pp * 128:(pp + 1) * 128]
                        fw.mm(o, BKT[qb][:, pp, 0, :], UTs[qb][:, pp, :], start=True, stop=False)
                        fw.mm(o, BKT[qb][:, pp, 1, :], VTs[qb][:, pp, :], start=False, stop=True)
                    wlb = wl[:, p0:p0 + 4].us(2).bc([128, 4, 128])
                    fw.tt(stmp[qb][:], S0[:, p0:p0 + 4, :], wlb, ALU.mult, e="pool")
                    fw.tt(S0[:, p0:p0 + 4, :], pd_[:].rr("p (a b) -> p a b", a=4), wlb, ALU.mult)
                    fw.tt(S0[:, p0:p0 + 4, :], S0[:, p0:p0 + 4, :], stmp[qb][:], ALU.add)
                    fw.copy(S0b[:, p0:p0 + 4, :], S0[:, p0:p0 + 4, :], e="act")
            fw.dma(V(S["YRW"].ap()[:, ti * 128:(ti + 1) * 128].rearrange("(c p) t -> p c t", p=128), None), YB[b][:], yb_sem[b])
        if d == 0:
            fw.dma(V(st_rw_out.ap(), None), S0[:], ssem)
        fw.barrier()
        st.close()

    S["YM"] = dscr("s_YM", [2048, TP])
    st_m_out = nc.dram_tensor("st_m_out", [128, 32, 64], F32, kind="ExternalOutput")
    st_m_in = din("st_m_in", [128, 32, 64])
    mc_d = din("mc", [128, 2, 2, 128])
    ones = sb("ones", [128, 128])
    fw.memset(ones[:], 1.0)

    def mamba_phase(d):
        st = ExitStack()

        def sbp(name, shape, dt=F32):
            return T(st.enter_context(nc.sbuf_tensor("mb%d_" % d + name, list(shape), dt)), name)

        mcst = sbp("mcst", [128, 2, 2, 128])
        fw.dma(mcst[:], V(mc_d.ap(), None), fw.dsem("c11"))
        XS = [sbp("xs%d" % b, [128, 16, 128]) for b in range(2)]
        Bb = [sbp("bb%d" % b, [128, 8, 128], BF16) for b in range(2)]
        Cb = [sbp("cb%d" % b, [128, 8, 128], BF16) for b in range(2)]
        DT = [sbp("dt%d" % b, [128, 4, 32]) for b in range(2)]
        Y1 = [sbp("y1_%d" % b, [128, 16, 128]) for b in range(2)]
        YB = [sbp("yb_%d" % b, [128, 16, 128]) for b in range(2)]
        sems = [[fw.dsem("mbld%d_%d" % (j, b)) for j in range(4)] for b in range(2)]
        psems = [[fw.dsem("mbldp%d_%d" % (j, b)) for j in range(2)] for b in range(2)]
        yb_sem = [fw.dsem("mbyb_%d" % b) for b in range(2)]
        dAexp = sbp("dAexp", [128, 32, 128])
        cs_tok = sbp("cs_tok", [128, 32])
        csl = sbp("csl", [128, 32])
        ecl_last = sbp("ecl_last", [128, 32])
        decs = sbp("decs", [128, 32])
        E = [sbp("E%d" % b, [128, 8, 128]) for b in range(2)]
        ecsR = [sbp("ecsR%d" % b, [128, 8, 128]) for b in range(2)]
        MT = sbp("MT", [128, 32, 128], BF16)
        Csc = sbp("Csc", [128, 32, 128], BF16)
        CBm = sbp("CBm", [128, 8, 128])
        xdt = sbp("xdt", [128, 32, 64], BF16)
        xdd = sbp("xdd", [128, 32, 64], BF16)
        Btok = sbp("Btok", [128, 8, 128], BF16)
        hS = sbp("hS", [128, 32, 64])
        hb = sbp("hb", [128, 32, 64], BF16)
        htmp = sbp("htmp", [128, 32, 64])
        ssem = fw.dsem("mbstate")
        if d == 0:
            fw.memset(hS[:], 0.0)
        else:
            fw.dma(hS[:], V(st_m_in.ap(), None), ssem)
        fw.copy(hb[:], hS[:], e="act")
        tri = mcst[:, d, 0, :]
        lst = mcst[:, d, 1, :]
        t_last = 127 if d == 0 else 0
        NCH = T_loc // 128
        tiles = list(range(NCH)) if d == 0 else list(range(NCH - 1, -1, -1))
        pbc = {"n": 0}

        def pbank():
            j = pbc["n"] % 8
            pbc["n"] += 1
            return PB[j]

        def load_tile(n):
            ti = tiles[n]
            b = n % 2
            cs_ = slice(ti * 128, (ti + 1) * 128)
            xb = S["XBC"].ap()
            fw.dma(XS[b][:], V(xb[0:2048, cs_].rearrange("(c p) t -> p c t", p=128), None), sems[b][0])
            fw.dma(DT[b][:], V(S["DTS"].ap()[cs_], None), sems[b][1])
            fw.dma(Bb[b][:], V(xb[2048:3072, cs_].rearrange("(c p) t -> p c t", p=128), None), psems[b][0], q="pool")
            fw.dma(Cb[b][:], V(xb[3072:4096, cs_].rearrange("(c p) t -> p c t", p=128), None), psems[b][1], q="pool")
            if d == 1:
                fw.dma(Y1[b][:], V(S["YM"].ap()[:, cs_].rearrange("(c p) t -> p c t", p=128), None), sems[b][2])

        load_tile(0)
        for n in range(NCH):
            ti = tiles[n]
            b = n % 2
            if n + 1 < NCH:
                load_tile(n + 1)
            dA = DT[b][:, 2 + d, :]
            dtv = DT[b][:, d, :]
            fw.tt(dAexp[:], dA.us(2).bc([128, 32, 128]), tri.us(1).bc([128, 32, 128]), ALU.mult)
            if MB_STOP <= 1:
                continue
            pc = pbank()
            fw.mm(pc[:, 0:32], tri, dA)
            fw.copy(cs_tok[:], pc[:, 0:32], e="act")
            if MB_STOP <= 2:
                continue
            for half in range(2):
                pcb = pbank()
                for gg in range(4):
                    g = half * 4 + gg
                    fw.mm(pcb[:, gg * 128:(gg + 1) * 128], Bb[b][:, g, :], Cb[b][:, g, :])
                fw.tt(CBm[:, half * 4:half * 4 + 4, :], pcb[:].rr("p (a b) -> p a b", a=4), tri.us(1).bc([128, 4, 128]), ALU.mult)
            if MB_STOP <= 3:
                continue
            for o in range(4):
                ob = o % 2
                pD = [pbank(), pbank()]
                pR = [pbank(), pbank()]
                for j in range(2):
                    rhs = dAexp[:, o * 8 + j * 4:o * 8 + j * 4 + 4, :].rr("p a b -> p (a b)")
                    fw.mm(pD[j][:], lst, rhs)
                    fw.mm(pR[j][:], ones[:], rhs)
                for j in range(2):
                    hs = slice(o * 8 + j * 4, o * 8 + j * 4 + 4)
                    g = o * 2 + j
                    if MB_SUB <= 1:
                        continue
                    fw.act(E[ob][:, j * 4:j * 4 + 4, :], pD[j][:].rr("p (a b) -> p a b", a=4), AF.Exp)
                    if MB_SUB <= 2:
                        continue
                    fw.tt(MT[:, hs, :], E[ob][:, j * 4:j * 4 + 4, :], CBm[:, g:g + 1, :].bc([128, 4, 128]), ALU.mult)
                    pR4 = pR[j][:].rr("p (a b) -> p a b", a=4)
                    if MB_SUB <= 3:
                        continue
                    fw.copy(csl[:, hs], pR4[:, :, t_last], e="dve")
                    if MB_SUB <= 4:
                        continue
                    fw.act(ecsR[ob][:, j * 4:j * 4 + 4, :], pR4, AF.Exp)
                    if MB_SUB <= 5:
                        continue
                    fw.tt(Csc[:, hs, :], ecsR[ob][:, j * 4:j * 4 + 4, :], Cb[b][:, g:g + 1, :].bc([128, 4, 128]), ALU.mult, e="dve")
            if MB_STOP <= 4:
                continue
            fw.act(ecl_last[:], csl[:], AF.Exp)
            fw.tt(decs[:], csl[:], cs_tok[:], ALU.subtract)
            fw.act(decs[:], decs[:], AF.Exp)
            if MB_STOP <= 5:
                continue
            for q in range(4):
                px = pbank()
                for cc in range(4):
                    c = q * 4 + cc
                    fw.tr(px[:, cc * 128:(cc + 1) * 128], XS[b][:, c, :], ident[:])
                hs = slice(q * 8, q * 8 + 8)
                fw.tt(xdt[:, hs, :], px[:].rr("p (a b) -> p a b", a=8), dtv[:, hs].us(2).bc([128, 8, 64]), ALU.mult)
                fw.tt(xdd[:, hs, :], xdt[:, hs, :], decs[:, hs].us(2).bc([128, 8, 64]), ALU.mult, e="pool")
            if MB_STOP <= 6:
                continue
            for q in range(4):
                py = pbank()
                for cc in range(4):
                    for hh in range(2):
                        h = (q * 4 + cc) * 2 + hh
                        o_ = py[64 * hh:64 * hh + 64, cc * 128:(cc + 1) * 128]
                        kw_ = {"tile_position": (0, 64)} if hh == 1 else {}
                        fw.mm(o_, xdt[:, h, :], MT[:, h, :], start=True, stop=False, **kw_)
                        fw.mm(o_, hb[:, h, :], Csc[:, h, :], start=False, stop=True, **kw_)
                py4 = py[:].rr("p (a b) -> p a b", a=4)
                if d == 0:
                    fw.copy(YB[b][:, q * 4:q * 4 + 4, :], py4, e="act")
                else:
                    fw.tt(YB[b][:, q * 4:q * 4 + 4, :], py4, Y1[b][:, q * 4:q * 4 + 4, :], ALU.add)
            fw.dma(V(S["YM"].ap()[:, ti * 128:(ti + 1) * 128].rearrange("(c p) t -> p c t", p=128), None), YB[b][:], yb_sem[b])
            if MB_STOP <= 7:
                continue
            pt_ = pbank()
            ptb = pt_.v(pt_.h[:].bitcast(BF16).rearrange("p (a b) -> p a b", a=8))
            for g in range(8):
                fw.tr(ptb[:, g, :], Bb[b][:, g, :], identb[:])
            fw.copy(Btok[:], ptb, e="act")
            fw.tt(htmp[:], hS[:], ecl_last[:].us(2).bc([128, 32, 64]), ALU.mult, e="pool")
            for q in range(4):
                pn_ = pbank()
                for gg in range(2):
                    g = q * 2 + gg
                    fw.mm(pn_[:, gg * 256:(gg + 1) * 256], Btok[:, g, :], xdd[:, 4 * g:4 * g + 4, :].rr("p a b -> p (a b)"))
                hs = slice(q * 8, q * 8 + 8)
                fw.tt(hS[:, hs, :], pn_[:].rr("p (a b) -> p a b", a=8), htmp[:, hs, :], ALU.add)
            fw.copy(hb[:], hS[:], e="act")
        if d == 0:
            fw.dma(V(st_m_out.ap(), None), hS[:], ssem)
        fw.barrier()
        st.close()

    ALPHA = 2.0 ** 0.25
    LN_EPS = 1e-5
    GN_EPS = 64e-5
    mem_d = din("mem", [256, D])
    w_br_d = din("w_br", [1024, D])
    w_bm_d = din("w_bm", [D, D])
    w_o_d = din("w_o", [D, D])
    w_q_d = din("w_q", [D, D])
    w_kv_d = din("w_kv", [D, 2 * D])
    w_co_d = din("w_co", [D, D])
    w_up_d = din("w_up", [D, 4 * D])
    w_down_d = din("w_down", [4 * D, D])
    y_out = nc.dram_tensor("y_out", [T_loc, D], F32, kind="ExternalOutput")

    def wview(w):
        return w.ap().rearrange("(k p) n -> p k n", p=128)

    def phase3():
        st = ExitStack()

        def sbp(name, shape, dt=F32):
            return T(st.enter_context(nc.sbuf_tensor("p3_" + name, list(shape), dt)), name)

        F32A = sbp("F32A", [128, NK, 512])
        BFA = sbp("BFA", [128, NK, 512], BF16)
        BFB = sbp("BFB", [128, NK, 512], BF16)
        BFC = sbp("BFC", [128, NK, 512], BF16)
        BFD = sbp("BFD", [128, 8, 512], BF16)
        HM = sbp("HM", [128, 16, 512], BF16)
        Kt = sbp("Kt", [128, NK, 256], BF16)
        Vt = sbp("Vt", [128, 2, D], BF16)
        onesb = sbp("onesb", [128, 128], BF16)
        ksc = sbp("ksc", [128, 4])
        fw.copy(onesb[:], ones[:], e="act")
        NWB = 3
        WB = [sbp("wb%d" % j, [128, 4096], BF16) for j in range(NWB)]
        wb_sem = [fw.dsem("p3wb%d" % j) for j in range(NWB)]
        NL = 6
        LB = [sbp("lb%d" % j, [128, 512]) for j in range(NL)]
        lb_sem = [fw.dsem("p3lb%d" % j) for j in range(NL)]
        NTMP = 5
        TMP = [sbp("tmp%d" % j, [128, 512]) for j in range(NTMP)]
        xs = [sbp("xs%d" % j, [128, D]) for j in range(2)]
        xs_sem = [fw.dsem("p3xs%d" % j) for j in range(2)]
        ymp = [sbp("ymp%d" % j, [128, 2, 512]) for j in range(1)]
        sqp = [sbp("sqp%d" % j, [128, 2, 512]) for j in range(1)]
        expS = [sbp("expS%d" % j, [128, 2, 512], BF16) for j in range(1)]
        ded = {nm: sbp("ded_" + nm, [128, 512]) for nm in ("mean", "rstd", "cst", "rs")}
        cnt = {"pb": 0, "lb": 0, "tmp": 0}

        def pbank():
            j = cnt["pb"] % 8
            cnt["pb"] += 1
            return PB[j]

        def tmp():
            j = cnt["tmp"] % NTMP
            cnt["tmp"] += 1
            return TMP[j]

        def ld(name, row0, t0):
            j = cnt["lb"] % NL
            cnt["lb"] += 1
            fw.dma(LB[j][:], V(S[name].ap()[row0:row0 + 128, t0:t0 + 512], None), lb_sem[j])
            return LB[j]

        class WS:
            def __init__(self):
                self.specs = []
                self.issued = 0
                self.tiles = {}

            def add(self, wv, k0, nk, col0, ncols):
                self.specs.append((wv, k0, nk, col0, ncols))
                return len(self.specs) - 1

            def _issue(self, n):
                wv, k0, nk, col0, ncols = self.specs[n]
                j = n % NWB
                tv = WB[j][:, 0:nk * ncols].rr("p (k n) -> p k n", k=nk)
                fw.dma(tv, V(wv[:, k0:k0 + nk, col0:col0 + ncols], None), wb_sem[j], q="pool")
                self.tiles[n] = tv

            def get(self, n):
                while self.issued < min(len(self.specs), n + NWB):
                    self._issue(self.issued)
                    self.issued += 1
                return self.tiles.pop(n)

        w_in_v3 = w_in_v

        def dense(ws_ids, ws, src, nk, consume):
            pass

        def layer_norm(gname, bname):
            ps1 = pbank()
            ps2 = pbank()
            for c in range(NK):
                sq = tmp()
                fw.act(sq[:], F32A[:, c, :], AF.Square)
                fw.mm(ps1[:], ones[:], F32A[:, c, :], start=(c == 0), stop=(c == NK - 1))
                fw.mm(ps2[:], ones[:], sq[:], start=(c == 0), stop=(c == NK - 1))
            mean = ded["mean"]
            fw.act(mean[:], ps1[:], AF.Copy, scale=1.0 / D)
            msq = tmp()
            fw.act(msq[:], ps1[:], AF.Square, scale=1.0 / D)
            rstd = ded["rstd"]
            fw.stt(rstd[:], ps2[:], 1.0 / D, msq[:], ALU.mult, ALU.subtract)
            fw.act(rstd[:], rstd[:], AF.Ln, bias=LN_EPS)
            fw.act(rstd[:], rstd[:], AF.Exp, scale=-0.5)
            for c in range(NK):
                t_ = tmp()
                fw.tt(t_[:], F32A[:, c, :], mean[:], ALU.subtract)
                fw.tt(t_[:], t_[:], rstd[:], ALU.mult)
                fw.act(F32A[:, c, :], t_[:], AF.Identity, scale=pv(gname, c), bias=pv(bname, c))
                fw.copy(BFA[:, c, :], F32A[:, c, :], e="dve")

        memT = BFB
        memTv = memT[:, :, 0:256]
        for mb in range(2):
            fw.dma(xs[mb][:], V(mem_d.ap()[mb * 128:(mb + 1) * 128, :], None), xs_sem[mb])
            for kq in range(4):
                pt = pbank()
                for k4 in range(4):
                    k = kq * 4 + k4
                    fw.tr(pt[:, k4 * 128:(k4 + 1) * 128], xs[mb][:, k * 128:(k + 1) * 128], ident[:])
                fw.copy(memT[:, kq * 4:(kq + 1) * 4, mb * 128:(mb + 1) * 128], pt[:].rr("p (a b) -> p a b", a=4), e="act")
        ws = WS()
        wkv = wview(w_kv_d)
        ids = [ws.add(wkv, 0, NK, c * 256, 256) for c in range(16)]
        for c in range(8):
            wt = ws.get(ids[c])
            for oo in range(2):
                oc = c * 2 + oo
                pk = pbank()
                for k in range(NK):
                    fw.mm(pk[:, 0:256], wt[:, k, oo * 128:(oo + 1) * 128], memTv[:, k, :], start=(k == 0), stop=(k == NK - 1))
                fw.copy(Kt[:, oc, :], pk[:, 0:256], e="act")
        for c in range(8):
            wt = ws.get(ids[8 + c])
            for mb in range(2):
                pvv = pbank()
                for k in range(NK):
                    fw.mm(pvv[:, 0:256], memT[:, k, mb * 128:(mb + 1) * 128], wt[:, k, :], start=(k == 0), stop=(k == NK - 1))
                fw.copy(Vt[:, mb, c * 256:(c + 1) * 256], pvv[:, 0:256], e="dve")
        for hd in range(4):
            pk2 = pbank()
            for kc in range(4):
                sqk = tmp()
                sqkb = sqk[:, 0:128].ap.bitcast(BF16)
                sqv = V(sqkb, sqk.buf)
                fw.act(sqv, Kt[:, hd * 4 + kc, :], AF.Square)
                fw.mm(pk2[:, 0:256], onesb[:], sqv, start=(kc == 0), stop=(kc == 3))
            mx = tmp()
            i_ = nc.vector
            r_, w_ = fw._bufs([pk2[:]]), fw._bufs([mx[:]])
            fw._deps("dve", r_, w_)
            ins = nc.vector.tensor_reduce(mx[:, 0:1].ap, pk2[:, 0:256].ap, AX.X, ALU.max)
            fw._done(ins, "dve", 1, r_, w_)
            fw.ts(ksc[:, hd:hd + 1], mx[:, 0:1], 1.0 / 512.0, None, ALU.mult)

        NT3 = T_loc // 512
        for i in range(NT3):
            t0 = i * 512
            ws = WS()
            wbr, wbm, wo, wq, wco, wup, wdn = [wview(w) for w in (w_br_d, w_bm_d, w_o_d, w_q_d, w_co_d, w_up_d, w_down_d)]
            id_z = [ws.add(w_in_v3, 0, NK, C_Z + c * 256, 256) for c in range(8)]
            id_d = []
            for c in range(8):
                id_d.append((ws.add(wbr, 0, 8, c * 256, 256), ws.add(w_in_v3, 0, NK, C_GATES + c * 256, 256),
                             ws.add(wbm, 0, NK, c * 256, 256), ws.add(w_in_v3, 0, NK, C_GATES + 2048 + c * 256, 256)))
            id_o = [ws.add(wo, 0, NK, c * 256, 256) for c in range(8)]
            id_q = [ws.add(wq, 0, NK, c * 256, 256) for c in range(8)]
            id_co = [ws.add(wco, 0, NK, c * 256, 256) for c in range(8)]
            id_up, id_dn = [], []
            for hf in range(4):
                id_up.append([ws.add(wup, 0, NK, hf * 2048 + c * 256, 256) for c in range(8)])
                id_dn.append([ws.add(wdn, hf * 16, 16, c * 256, 256) for c in range(8)])
            for j in range(4):
                xb = xs[j % 2]
                fw.dma(xb[:], V(x_ext.ap()[2 + t0 + j * 128:2 + t0 + (j + 1) * 128, :], None), xs_sem[j % 2])
                for kq in range(4):
                    pt = pbank()
                    for k4 in range(4):
                        k = kq * 4 + k4
                        fw.tr(pt[:, k4 * 128:(k4 + 1) * 128], xb[:, k * 128:(k + 1) * 128], ident[:])
                    pt4 = pt[:].rr("p (a b) -> p a b", a=4)
                    fw.copy(F32A[:, kq * 4:(kq + 1) * 4, j * 128:(j + 1) * 128], pt4, e="act")
                    fw.copy(BFA[:, kq * 4:(kq + 1) * 4, j * 128:(j + 1) * 128], pt4, e="dve")
            for c in range(8):
                y = ld("YRW", c * 128, t0)
                bon = ld("BON", c * 128, t0)
                gg = ld("G", c * 128, t0)
                sq = tmp()
                fw.act(sq[:], y[:], AF.Square)
                p1 = pbank()
                fw.mm(p1[:], blk[:], y[:])
                p2 = pbank()
                fw.mm(p2[:], blk[:], sq[:])
                m = tmp()
                fw.act(m[:], p1[:], AF.Copy, scale=1.0 / 64)
                msq = tmp()
                fw.act(msq[:], p1[:], AF.Square, scale=1.0 / 64)
                var = tmp()
                fw.stt(var[:], p2[:], 1.0 / 64, msq[:], ALU.mult, ALU.subtract)
                fw.act(var[:], var[:], AF.Ln, bias=GN_EPS)
                fw.act(var[:], var[:], AF.Exp, scale=-0.5)
                fw.tt(y[:], y[:], m[:], ALU.subtract)
                fw.tt(y[:], y[:], var[:], ALU.mult)
                fw.act(y[:], y[:], AF.Identity, scale=pv("gn_g", c), bias=pv("gn_b", c))
                fw.tt(y[:], y[:], bon[:], ALU.add)
                fw.tt(BFD[:, c, :], y[:], gg[:], ALU.mult)
            for c in range(NK):
                if c % 2 == 0:
                    wz = ws.get(id_z[c // 2])
                pb_ = 0
                ym = ld("YM", c * 128, t0)
                xv = ld("XBC", c * 128, t0)
                pz = pbank()
                for k in range(NK):
                    fw.mm(pz[:], wz[:, k, (c % 2) * 128:(c % 2 + 1) * 128], BFA[:, k, :], start=(k == 0), stop=(k == NK - 1))
                fw.stt(ym[:], xv[:], pv("m_d", c), ym[:], ALU.mult, ALU.add)
                sz = tmp()
                fw.act(sz[:], pz[:], AF.Silu)
                fw.tt(ymp[pb_][:, c % 2, :], ym[:], sz[:], ALU.mult)
                fw.act(sqp[pb_][:, c % 2, :], ymp[pb_][:, c % 2, :], AF.Square)
                if c % 2 == 1:
                    pss = pbank()
                    fw.mm(pss[:], ones[:], sqp[pb_][:, 0, :], start=True, stop=False)
                    fw.mm(pss[:], ones[:], sqp[pb_][:, 1, :], start=False, stop=True)
                    rms = tmp()
                    fw.act(rms[:], pss[:], AF.Ln, scale=1.0 / 256, bias=LN_EPS)
                    fw.act(rms[:], rms[:], AF.Exp, scale=-0.5)
                    for cc in range(2):
                        fw.stt(BFB[:, c - 1 + cc, :], ymp[pb_][:, cc, :], pv("m_norm_g", c - 1 + cc), rms[:], ALU.mult, ALU.mult)
            for c in range(8):
                wt = ws.get(id_d[c][0])
                pu = [pbank(), pbank()]
                for oo in range(2):
                    for k in range(8):
                        fw.mm(pu[oo][:], wt[:, k, oo * 128:(oo + 1) * 128], BFD[:, k, :], start=(k == 0), stop=(k == 7))
                wt = ws.get(id_d[c][1])
                t1 = [tmp(), tmp()]
                for oo in range(2):
                    pg = pbank()
                    for k in range(NK):
                        fw.mm(pg[:], wt[:, k, oo * 128:(oo + 1) * 128], BFA[:, k, :], start=(k == 0), stop=(k == NK - 1))
                    fw.act(t1[oo][:], pg[:], AF.Sigmoid)
                    fw.tt(t1[oo][:], t1[oo][:], pu[oo][:], ALU.mult)
                wt = ws.get(id_d[c][2])
                pm = [pbank(), pbank()]
                for oo in range(2):
                    for k in range(NK):
                        fw.mm(pm[oo][:], wt[:, k, oo * 128:(oo + 1) * 128], BFB[:, k, :], start=(k == 0), stop=(k == NK - 1))
                wt = ws.get(id_d[c][3])
                for oo in range(2):
                    oc = c * 2 + oo
                    pg2 = pbank()
                    for k in range(NK):
                        fw.mm(pg2[:], wt[:, k, oo * 128:(oo + 1) * 128], BFA[:, k, :], start=(k == 0), stop=(k == NK - 1))
                    sg2 = tmp()
                    fw.act(sg2[:], pg2[:], AF.Sigmoid)
                    fw.tt(sg2[:], sg2[:], pm[oo][:], ALU.mult)
                    fw.tt(BFC[:, oc, :], t1[oo][:], sg2[:], ALU.add)

            def proj_res(idl, src, first=True):
                for c in range(8):
                    wt = ws.get(idl[c])
                    for oo in range(2):
                        oc = c * 2 + oo
                        po = pbank()
                        for k in range(NK):
                            fw.mm(po[:], wt[:, k, oo * 128:(oo + 1) * 128], src[:, k, :], start=(k == 0), stop=(k == NK - 1))
                        fw.stt(F32A[:, oc, :], F32A[:, oc, :], ALPHA, po[:], ALU.mult, ALU.add)

            proj_res(id_o, BFC)
            layer_norm("ln1_g", "ln1_b")
            for c in range(8):
                wt = ws.get(id_q[c])
                for oo in range(2):
                    oc = c * 2 + oo
                    pq = pbank()
                    for k in range(NK):
                        fw.mm(pq[:], wt[:, k, oo * 128:(oo + 1) * 128], BFA[:, k, :], start=(k == 0), stop=(k == NK - 1))
                    fw.copy(BFB[:, oc, :], pq[:], e="act")
            inv = 1.0 / math.sqrt(512.0)
            for hd in range(4):
                eb = 0
                pq2 = pbank()
                for kc in range(4):
                    sqq = tmp()
                    sqv = V(sqq[:].ap.bitcast(BF16)[:, 0:512], sqq.buf)
                    fw.act(sqv, BFB[:, hd * 4 + kc, :], AF.Square)
                    fw.mm(pq2[:], onesb[:], sqv, start=(kc == 0), stop=(kc == 3))
                cst = ded["cst"]
                fw.act(cst[:], pq2[:], AF.Sqrt, scale=ksc[:, hd:hd + 1])
                for mc in range(2):
                    ps_ = pbank()
                    for kc in range(4):
                        fw.mm(ps_[:], Kt[:, hd * 4 + kc, mc * 128:(mc + 1) * 128], BFB[:, hd * 4 + kc, :], start=(kc == 0), stop=(kc == 3))
                    ein = tmp()
                    fw.stt(ein[:], ps_[:], inv, cst[:], ALU.mult, ALU.subtract)
                    fw.act(expS[eb][:, mc, :], ein[:], AF.Exp)
                psum_ = pbank()
                for mc in range(2):
                    fw.mm(psum_[:], onesb[:], expS[eb][:, mc, :], start=(mc == 0), stop=(mc == 1))
                rs = ded["rs"]
                r_, w_ = fw._bufs([psum_[:]]), fw._bufs([rs[:]])
                fw._deps("dve", r_, w_)
                ins = nc.vector.reciprocal(rs[:].ap, psum_[:].ap)
                fw._done(ins, "dve", 1, r_, w_)
                for dc in range(4):
                    po = pbank()
                    col = hd * 512 + dc * 128
                    for mc in range(2):
                        fw.mm(po[:], Vt[:, mc, col:col + 128], expS[eb][:, mc, :], start=(mc == 0), stop=(mc == 1))
                    fw.tt(BFC[:, hd * 4 + dc, :], po[:], rs[:], ALU.mult)
            proj_res(id_co, BFC)
            layer_norm("ln2_g", "ln2_b")
            for hf in range(4):
                for c in range(8):
                    wt = ws.get(id_up[hf][c])
                    for oo in range(2):
                        oc = c * 2 + oo
                        ph = pbank()
                        for k in range(NK):
                            fw.mm(ph[:], wt[:, k, oo * 128:(oo + 1) * 128], BFA[:, k, :], start=(k == 0), stop=(k == NK - 1))
                        rl = tmp()
                        fw.act(rl[:], ph[:], AF.Relu)
                        fw.tt(HM[:, oc, :], rl[:], rl[:], ALU.mult)
                for c in range(8):
                    wt = ws.get(id_dn[hf][c])
                    for oo in range(2):
                        oc = c * 2 + oo
                        po = pbank()
                        for k in range(16):
                            fw.mm(po[:], wt[:, k, oo * 128:(oo + 1) * 128], HM[:, k, :], start=(k == 0), stop=(k == 15))
                        if hf == 0:
                            fw.stt(F32A[:, oc, :], F32A[:, oc, :], ALPHA, po[:], ALU.mult, ALU.add)
                        else:
                            fw.tt(F32A[:, oc, :], F32A[:, oc, :], po[:], ALU.add)
            layer_norm("ln3_g", "ln3_b")
            for j in range(4):
                ob_ = xs[j % 2]
                for kq in range(4):
                    pt = pbank()
                    for k4 in range(4):
                        k = kq * 4 + k4
                        fw.tr(pt[:, k4 * 128:(k4 + 1) * 128], F32A[:, k, j * 128:(j + 1) * 128], ident[:])
                    fw.copy(ob_[:, kq * 512:(kq + 1) * 512], pt[:], e=("act" if kq % 2 else "dve"))
                fw.dma(V(y_out.ap()[t0 + j * 128:t0 + (j + 1) * 128, :], None), ob_[:], xs_sem[j % 2])
        fw.barrier()
        st.close()

    if 0 in phases:
        phase0()
    fw.barrier()
    if 1 in phases:
        rwkv_phase(0)
    if 2 in phases:
        rwkv_phase(1)
    if 3 in phases:
        mamba_phase(0)
    if 4 in phases:
        mamba_phase(1)
    if 5 in phases:
        phase3()
    es.close()
    return nc, fw


def host_params(inp, swap=False):
    g = lambda k: np.asarray(inp[k])[0]
    pvec = np.zeros((128, NPAR), np.float32)

    def put(name, vec, j0=0):
        vec = np.asarray(vec, np.float32)
        n = vec.shape[0]
        nch = (n + 127) // 128
        pad = np.zeros(nch * 128, np.float32)
        pad[:n] = vec
        pvec[:, POFF[name] + j0:POFF[name] + j0 + nch] = pad.reshape(nch, 128).T

    mup, mun = g("rw_mu_prev"), g("rw_mu_next")
    if swap:
        mup, mun = mun, mup
    for nm, mu in (("mup", mup), ("mun", mun)):
        put(nm, mu[0:3072], 0)
        put(nm, mu[3072:3168], 24)
        put(nm, mu[3168:3264], 25)
        put(nm, mu[3264:3520], 26)
    dirs = (1, 0) if swap else (0, 1)
    for d in range(2):
        put("w0", g("rw_w0")[dirs[d]], 8 * d)
        put("a0", g("rw_a0")[dirs[d]], 8 * d)
    put("k_k", g("rw_k_k"))
    put("k_a", g("rw_k_a"))
    put("r_k", g("rw_r_k").reshape(-1))
    put("gn_g", g("rw_gn_g"))
    put("gn_b", g("rw_gn_b"))
    cw = g("m_conv_w")
    if swap:
        cw = cw[::-1]
    for j in range(5):
        put("conv_w", cw[j], 32 * j)
    put("conv_b", g("m_conv_b"))
    put("m_norm_g", g("m_norm_g"))
    put("m_d", np.repeat(g("m_d"), 64))
    for nm in ("ln1_g", "ln1_b", "ln2_g", "ln2_b", "ln3_g", "ln3_b"):
        put(nm, g(nm))
    dtp = np.zeros((128, 2, 2, 4, 32), np.float32)
    for d in range(2):
        dtp[:, 0, d] = g("m_dt_bias")[dirs[d]][None, None, :]
        dtp[:, 1, d] = g("m_a_log")[dirs[d]][None, None, :]
    blk = np.zeros((128, 128), np.float32)
    blk[:64, :64] = 1.0
    blk[64:, 64:] = 1.0
    rwc = np.zeros((128, 2, 512), np.float32)
    idx = np.arange(128)
    hh, ss = idx // 64, idx % 64
    same = hh[:, None] == hh[None, :]
    lt = ss[:, None] < ss[None, :]
    le = ss[:, None] <= ss[None, :]
    rwc[:, 0, 0:128] = same & lt
    rwc[:, 0, 128:256] = same & le
    rwc[:, 0, 256:384] = same & lt.T
    rwc[:, 1, 0:128] = same & lt.T
    rwc[:, 1, 128:256] = same & le.T
    rwc[:, 1, 256:384] = same & lt
    mc = np.zeros((128, 2, 2, 128), np.float32)
    i128 = np.arange(128)
    mc[:, 0, 0] = i128[:, None] <= i128[None, :]
    mc[:, 0, 1] = i128[:, None] > i128[None, :]
    mc[:, 1, 0] = i128[:, None] >= i128[None, :]
    mc[:, 1, 1] = i128[:, None] < i128[None, :]
    rmask = np.ones((128, 1024), np.float32)
    rmask[:, ::64] = 0.0
    return dict(pvec=pvec, dtp=dtp, ident=np.eye(128, dtype=np.float32), blk64=blk, rwc=rwc, rmask=rmask,
                st_rw_in=np.zeros((128, 8, 128), np.float32), st_m_in=np.zeros((128, 32, 64), np.float32), mc=mc,
                rw_w2=g("rw_w2"), rw_a2=g("rw_a2"), rw_g2=g("rw_g2"), w_in=g("w_in"),
                w_br=g("w_br"), w_bm=g("w_bm"), w_o=g("w_o"), w_q=g("w_q"), w_kv=g("w_kv"), w_co=g("w_co"),
                w_up=g("w_up"), w_down=g("w_down"))


T_CORE = 8192
_CACHE = {}


def _x_ext(xseq, start, T_loc, rev):
    NT0 = (T_loc + TV - 1) // TV
    TP = NT0 * TV
    L = xseq.shape[0]
    out = np.zeros((TP + 4, D), np.float32)
    if not rev:
        lo, hi = start - 2, start + T_loc + 2
        slo, shi = max(lo, 0), min(hi, L)
        out[slo - lo:shi - lo] = xseq[slo:shi]
    else:
        lo, hi = start - 2, start + T_loc + 2
        slo, shi = max(lo, 0), min(hi, L)
        seg = xseq[slo:shi][::-1]
        r0 = start + T_loc + 1 - (shi - 1)
        out[r0:r0 + seg.shape[0]] = seg
    return out


def kernel(**inputs):
    inp = {k: np.asarray(v) for k, v in inputs.items()}
    xp, xs_, mp, ms = inp["x_prompt"], inp["x_sample"], inp["mem_prompt"], inp["mem_sample"]
    T_loc = T_CORE
    if "nc" not in _CACHE:
        _CACHE["nc"] = build(T_loc)[0]
    nc = _CACHE["nc"]
    hp = [host_params(inp, swap=False), host_params(inp, swap=True)]
    cores = []
    for c in range(8):
        if c < 4:
            s, half = c // 2, c % 2
            cores.append(dict(x=xp[s], start=half * T_loc, rev=(half == 1), mem=mp[s]))
        else:
            cores.append(dict(x=xs_[c - 4], start=0, rev=False, mem=ms[c - 4]))
    base_maps = []
    for c, cd in enumerate(cores):
        m = dict(hp[1 if cd["rev"] else 0])
        m["x_ext"] = _x_ext(cd["x"], cd["start"], T_loc, cd["rev"])
        m["mem"] = np.ascontiguousarray(cd["mem"], dtype=np.float32)
        base_maps.append(m)
    res1 = run_bass_kernel_spmd(nc, base_maps, core_ids=list(range(8)))
    maps2 = []
    for c in range(8):
        m = dict(base_maps[c])
        if c < 4:
            partner = c ^ 1
            m["st_rw_in"] = np.asarray(res1.results[partner]["st_rw_out"], np.float32)
            m["st_m_in"] = np.asarray(res1.results[partner]["st_m_out"], np.float32)
        maps2.append(m)
    res2 = run_bass_kernel_spmd(nc, maps2, core_ids=list(range(8)))
    ys = [np.asarray(res2.results[c]["y_out"], np.float32) for c in range(8)]
    y_prompt = np.empty_like(xp)
    for c in range(4):
        s, half = c // 2, c % 2
        y_prompt[s, half * T_loc:(half + 1) * T_loc] = ys[c][::-1] if half == 1 else ys[c]
    y_sample = np.stack(ys[4:8], axis=0)
    return (y_prompt, y_sample)
```

```python
import math
from contextlib import ExitStack
import numpy as np
import concourse.bass as bass
import concourse.mybir as mybir
from concourse.bass_utils import run_bass_kernel_spmd

F32 = mybir.dt.float32
BF16 = mybir.dt.bfloat16
AF = mybir.ActivationFunctionType
ALU = mybir.AluOpType
AX = mybir.AxisListType

SAME_ENGINE_SYNC = True
MB_STOP = 99
MB_SUB = 99

D = 2048
NK = 16
TT = 512
TV = 508
IN_COLS = 13792
C_R, C_K, C_V, C_DW, C_DA, C_DG = 0, 1024, 2048, 3072, 3168, 3264
C_Z, C_XBC, C_DT, C_GATES = 3520, 5568, 9664, 9696


class Buf:
    __slots__ = ("w", "r", "name", "ex")

    def __init__(self, name=""):
        self.w = None
        self.r = []
        self.name = name
        self.ex = False


class V:
    __slots__ = ("ap", "buf")

    def __init__(self, ap, buf):
        self.ap = ap
        self.buf = buf

    def __getitem__(self, idx):
        return V(self.ap[idx], self.buf)

    def rr(self, pat, **kw):
        return V(self.ap.rearrange(pat, **kw), self.buf)

    def bc(self, shape):
        return V(self.ap.to_broadcast(list(shape)), self.buf)

    def us(self, axis):
        return V(self.ap.unsqueeze(axis), self.buf)


class T:
    def __init__(self, h, name="", track=True):
        self.h = h
        self.buf = Buf(name) if track else None

    def __getitem__(self, idx):
        return V(self.h[idx], self.buf)

    def v(self, ap):
        return V(ap, self.buf)


class FW:
    def __init__(self, nc):
        self.nc = nc
        self.eng = {"pe": nc.tensor, "dve": nc.vector, "act": nc.scalar, "pool": nc.gpsimd, "sp": nc.sync}
        self.sem = {}
        self.cnt = {}
        for e in self.eng:
            self.sem[e] = nc.alloc_semaphore("sem_" + e)
            self.cnt[e] = 0
        self.seen = {e: {} for e in self.eng}
        self.n_inst = 0
        self.n_wait = 0

    def dsem(self, name):
        if name in self.sem:
            return name
        self.sem[name] = self.nc.alloc_semaphore("dsem_" + name)
        self.cnt[name] = 0
        return name

    def scan(self, out, d0, d1, initial, op0, op1):
        r, w = self._bufs([d0, d1, initial]), self._bufs([out])
        self._deps("dve", r, w)
        i = self.nc.vector.tensor_tensor_scan(out.ap, d0.ap, d1.ap, self._ap(initial), op0, op1)
        return self._done(i, "dve", 1, r, w)

    def _wait(self, e, dep):
        if dep is None:
            return
        key, val = dep
        if key == e and (e == "pe" or e == "sp" or not SAME_ENGINE_SYNC):
            return
        if self.seen[e].get(key, 0) >= val:
            return
        self.seen[e][key] = val
        self.eng[e].wait_ge(self.sem[key], val)
        self.n_wait += 1

    def _deps(self, e, reads, writes):
        mx = {}
        for b in reads:
            if b.w is not None and mx.get(b.w[0], 0) < b.w[1]:
                mx[b.w[0]] = b.w[1]
            if b.ex:
                for k, v in b.r:
                    if k != e and mx.get(k, 0) < v:
                        mx[k] = v
        for b in writes:
            if b.w is not None and mx.get(b.w[0], 0) < b.w[1]:
                mx[b.w[0]] = b.w[1]
            for k, v in b.r:
                if mx.get(k, 0) < v:
                    mx[k] = v
        for k, v in mx.items():
            self._wait(e, (k, v))

    def _done(self, inst, key, inc, reads, writes, signal=True):
        if signal:
            self.cnt[key] += inc
            inst.then_inc(self.sem[key], inc)
            dep = (key, self.cnt[key])
        else:
            dep = (key, self.cnt[key] + inc)
        for b in reads:
            b.r.append(dep)
            if len(b.r) > 24:
                mx = {}
                for k, v in b.r:
                    if mx.get(k, 0) < v:
                        mx[k] = v
                b.r = list(mx.items())
        for b in writes:
            b.w = dep
            b.r = []
        self.n_inst += 1
        return dep

    @staticmethod
    def _bufs(vs):
        out = []
        for v in vs:
            if isinstance(v, V) and v.buf is not None and v.buf not in out:
                out.append(v.buf)
        return out

    @staticmethod
    def _ap(v):
        return v.ap if isinstance(v, V) else v

    def mm(self, out, lhsT, rhs, start=True, stop=True, **kw):
        r, w = self._bufs([lhsT, rhs]), self._bufs([out])
        self._deps("pe", r, w)
        i = self.nc.tensor.matmul(out.ap, lhsT.ap, rhs.ap, start=start, stop=stop, **kw)
        return self._done(i, "pe", 1, r, w, signal=True)

    def tr(self, out, in_, ident):
        r, w = self._bufs([in_, ident]), self._bufs([out])
        self._deps("pe", r, w)
        i = self.nc.tensor.transpose(out.ap, in_.ap, ident.ap)
        return self._done(i, "pe", 1, r, w)

    def act(self, out, in_, func, bias=None, scale=None, e="act", accum_out=None):
        r, w = self._bufs([in_, bias, scale]), self._bufs([out, accum_out])
        self._deps(e, r, w)
        kw = {}
        if bias is not None:
            kw["bias"] = self._ap(bias)
        if scale is not None:
            kw["scale"] = self._ap(scale)
        if accum_out is not None:
            kw["accum_out"] = self._ap(accum_out)
        i = self.eng[e].activation(out.ap, in_.ap, func, **kw)
        return self._done(i, e, 1, r, w)

    def tt(self, out, a, b, op, e="dve"):
        r, w = self._bufs([a, b]), self._bufs([out])
        self._deps(e, r, w)
        i = self.eng[e].tensor_tensor(out.ap, a.ap, b.ap, op)
        return self._done(i, e, 1, r, w)

    def ts(self, out, in0, s1, s2, op0, op1=None, e="dve"):
        r, w = self._bufs([in0, s1, s2]), self._bufs([out])
        self._deps(e, r, w)
        kw = {}
        if op1 is not None:
            kw["op1"] = op1
        i = self.eng[e].tensor_scalar(out.ap, in0.ap, self._ap(s1), self._ap(s2), op0, **kw)
        return self._done(i, e, 1, r, w)

    def stt(self, out, in0, scalar, in1, op0, op1, e="dve"):
        r, w = self._bufs([in0, scalar, in1]), self._bufs([out])
        self._deps(e, r, w)
        i = self.eng[e].scalar_tensor_tensor(out.ap, in0.ap, self._ap(scalar), in1.ap, op0, op1)
        return self._done(i, e, 1, r, w)

    def copy(self, out, in_, e="dve"):
        r, w = self._bufs([in_]), self._bufs([out])
        self._deps(e, r, w)
        if e == "act":
            i = self.nc.scalar.copy(out.ap, in_.ap)
        else:
            i = self.eng[e].tensor_copy(out.ap, in_.ap)
        return self._done(i, e, 1, r, w)

    def memset(self, out, val, e="dve"):
        w = self._bufs([out])
        self._deps(e, [], w)
        i = self.eng[e].memset(out.ap, val)
        return self._done(i, e, 1, [], w)

    def dma(self, out, in_, sem, q="sp", **kw):
        r, w = self._bufs([in_]), self._bufs([out])
        self._deps(q, r, w)
        i = self.eng[q].dma_start(out=out.ap, in_=in_.ap, **kw)
        return self._done(i, sem, 16, r, w)

    def collective(self, kind, op, groups, in_v, out_v, sem):
        r, w = self._bufs([in_v]), self._bufs([out_v])
        self._deps("pool", r, w)
        i = self.nc.gpsimd.collective_compute(kind, op=op, replica_groups=groups, ins=[in_v.ap], outs=[out_v.ap])
        return self._done(i, sem, 16, r, w)

    def barrier(self):
        for e in self.eng:
            for key, val in self.cnt.items():
                if val > 0:
                    self._wait(e, (key, val)) if key != e else None


class Ctx:
    pass


def _param_layout():
    off = {}
    n = 0
    for name, w in [("mup", 28), ("mun", 28), ("w0", 16), ("a0", 16), ("k_k", 8), ("k_a", 8), ("r_k", 8),
                    ("gn_g", 8), ("gn_b", 8), ("conv_w", 160), ("conv_b", 32), ("m_norm_g", 16), ("m_d", 16),
                    ("ln1_g", 16), ("ln1_b", 16), ("ln2_g", 16), ("ln2_b", 16), ("ln3_g", 16), ("ln3_b", 16)]:
        off[name] = n
        n += w
    return off, n


POFF, NPAR = _param_layout()
XOFF = {"c0": 0, "nk_k": 28, "omk_a": 36}
NX = 44


def build(T_loc, dbg=False, phases=(6, 0, 1, 2, 3, 4, 5)):
    NT0 = (T_loc + TV - 1) // TV
    TP = NT0 * TV
    XR = TP + 4
    nc = bass.Bass("TRN2", target_bir_lowering=False)
    fw = FW(nc)
    es = ExitStack()

    def din(name, shape, dt=F32):
        return nc.dram_tensor(name, list(shape), dt, kind="ExternalInput")

    def dscr(name, shape, dt=F32):
        return nc.dram_tensor(name, list(shape), dt, kind=("ExternalOutput" if dbg else "Internal"))

    x_ext = din("x_ext", [XR, D])
    x_ext2 = din("x_ext2", [XR, D])
    pvec2_d = din("pvec2", [128, NPAR])
    dtp2_d = din("dtp2", [128, 2, 2, 4, 32])
    sel_d = din("sel", [128, 1])
    w_in = din("w_in", [D, IN_COLS])
    pvec_d = din("pvec", [128, NPAR])
    ident_d = din("ident", [128, 128])
    blk_d = din("blk64", [128, 128])
    w2_d = din("rw_w2", [96, 1024])
    a2_d = din("rw_a2", [96, 1024])
    g2_d = din("rw_g2", [256, 1024])
    dtp_d = din("dtp", [128, 2, 2, 4, 32])

    S = {}
    for nm in ["R", "V", "A", "G", "BON", "KD1", "KD2", "B1", "B2", "LW1", "LW2"]:
        S[nm] = dscr("s_" + nm, [1024, TP])
    S["XBC"] = dscr("s_XBC", [4096, TP])
    S["DTS"] = dscr("s_DTS", [TP, 4, 32])
    S2 = {}
    for nm in ["V", "A", "KD1", "B1", "LW1"]:
        S2[nm] = dscr("s2_" + nm, [1024, TP])
    S2["XBC"] = dscr("s2_XBC", [4096, TP])
    S2["DTS"] = dscr("s2_DTS", [TP, 4, 32])
    stx_rw = nc.dram_tensor("stx_rw", [128, 8, 128], F32, kind="Internal")
    stx_m = nc.dram_tensor("stx_m", [128, 32, 64], F32, kind="Internal")

    def sb(name, shape, dt=F32):
        return T(es.enter_context(nc.sbuf_tensor("sb_" + name, list(shape), dt)), name)

    PB = [T(nc.alloc_psum_tensor("pb%d" % i, [128, 512], F32), "pb%d" % i) for i in range(8)]
    for t_ in PB:
        t_.buf.ex = True

    ident = sb("ident", [128, 128])
    blk = sb("blk", [128, 128])
    sel = sb("sel", [128, 1])
    dcp = fw.dsem("constp")
    fw.dma(ident[:], V(ident_d.ap(), None), fw.dsem("c2"))
    fw.dma(blk[:], V(blk_d.ap(), None), fw.dsem("c3"))
    fw.dma(sel[:], V(sel_d.ap(), None), fw.dsem("c12"))
    w_in_v = w_in.ap().rearrange("(k p) n -> p k n", p=128)
    CUR = {}
    PSETS = []
    for si, (pd_, dd_) in enumerate(((pvec_d, dtp_d), (pvec2_d, dtp2_d))):
        pvec_t = sb("pvec%d" % si, [128, NPAR])
        xpar_t = sb("xpar%d" % si, [128, NX])
        fw.dma(pvec_t[:], V(pd_.ap(), None), fw.dsem("c1_%d" % si))
        fw.tt(xpar_t[:, 0:28], pvec_t[:, POFF["mup"]:POFF["mup"] + 28], pvec_t[:, POFF["mun"]:POFF["mun"] + 28], ALU.add)
        fw.ts(xpar_t[:, 0:28], xpar_t[:, 0:28], -1.0, 1.0, ALU.mult, ALU.add)
        fw.ts(xpar_t[:, 28:36], pvec_t[:, POFF["k_k"]:POFF["k_k"] + 8], -1.0, None, ALU.mult)
        fw.ts(xpar_t[:, 36:44], pvec_t[:, POFF["k_a"]:POFF["k_a"] + 8], -1.0, 1.0, ALU.mult, ALU.add)
        PSETS.append(dict(pvec=pvec_t, xpar=xpar_t, dtp_d=dd_))
    PSETS[0].update(x=x_ext, S=S)
    PSETS[1].update(x=x_ext2, S=S2)
    CUR.update(PSETS[0])

    def pv(name, j=0, n=128):
        c = POFF[name] + j
        return CUR["pvec"][0:n, c:c + 1]

    def xp(name, j=0, n=128):
        c = XOFF[name] + j
        return CUR["xpar"][0:n, c:c + 1]

    def phase0(lite=False):
        st = ExitStack()
        CUR.update(PSETS[1 if lite else 0])
        x_src = CUR["x"]
        Sd = CUR["S"]
        nd = 1 if lite else 2

        def sb0(name, shape, dt=F32):
            return T(st.enter_context(nc.sbuf_tensor(("p0l_" if lite else "p0_") + name, list(shape), dt)), name)

        w2 = sb0("w2", [96, 1024], BF16)
        a2 = sb0("a2", [96, 1024], BF16)
        g2 = sb0("g2", [128, 2, 1024], BF16)
        wdt = sb0("wdt", [128, NK, 32], BF16)
        dtp = sb0("dtp", [128, 2, 2, 4, 32])
        An = sb0("An", [128, 2, 4, 32])
        fw.dma(dtp[:], V(CUR["dtp_d"].ap(), None), fw.dsem("c4"))
        fw.dma(w2[:], V(w2_d.ap(), None), fw.dsem("c5"), q="pool")
        fw.dma(a2[:], V(a2_d.ap(), None), fw.dsem("c6"), q="pool")
        fw.dma(g2[:], V(g2_d.ap().rearrange("(k p) n -> p k n", p=128), None), fw.dsem("c7"), q="pool")
        fw.dma(wdt[:], V(w_in_v[:, :, C_DT:C_DT + 32], None), dcp, q="pool")
        fw.act(An[:], dtp[:, 1], AF.Exp)
        fw.ts(An[:], An[:], -1.0, None, ALU.mult)
        xs = [sb0("xs%d" % j, [128, D]) for j in range(2)]
        xs_sem = [fw.dsem("xs%d" % j) for j in range(2)]
        xT = sb0("xT", [128, NK, TT], BF16)
        WG = [sb0("wg%d" % j, [128, NK, 512], BF16) for j in range(2)]
        wg_sem = [fw.dsem("wg%d" % j) for j in range(2)]
        ag = sb0("ag", [128, 8, 2, TV])
        kp = sb0("kp", [128, 8, TV])
        rk = sb0("rk", [128, 8, TV])
        tdw = sb0("tdw", [96, TV], BF16)
        tda = sb0("tda", [96, TV], BF16)
        tdg = sb0("tdg", [128, 2, TV], BF16)
        NS = 8
        ost = [sb0("ost%d" % j, [128, TV]) for j in range(NS)]
        ost_sem = [fw.dsem("ost%d" % j) for j in range(NS)]
        tmp = [sb0("tmp%d" % j, [128, TV]) for j in range(4)]
        dts = sb0("dts", [128, 4, 4, 32])
        dtt = sb0("dtt", [128, 4, 32])
        dts_sem = fw.dsem("dts")
        cnt = {"ost": 0, "tmp": 0, "wg": 0, "pb": 0, "aux": 0}

        def new_ost():
            j = cnt["ost"] % NS
            cnt["ost"] += 1
            return ost[j], ost_sem[j]

        def new_tmp():
            j = cnt["tmp"] % 4
            cnt["tmp"] += 1
            return tmp[j]

        def new_pb():
            j = cnt["pb"] % 4
            cnt["pb"] += 1
            return PB[j]

        def new_aux():
            j = cnt["aux"] % 2
            cnt["aux"] += 1
            return PB[6 + j]

        def store(name, row0, nrow, i, o, osem):
            fw.dma(V(Sd[name].ap()[row0:row0 + nrow, i * TV:(i + 1) * TV], None), o[0:nrow, :], osem)

        def load_wg(col0, n):
            j = cnt["wg"] % 2
            cnt["wg"] += 1
            fw.dma(WG[j][:, :, 0:n], V(w_in_v[:, :, col0:col0 + n], None), wg_sem[j], q="pool")
            return WG[j]

        def proj(P, wt, c0, ncol):
            for k in range(NK):
                fw.mm(P[0:ncol, :], wt[:, k, c0:c0 + ncol], xT[:, k, :], start=(k == 0), stop=(k == NK - 1))

        def shift(P, pc, nrow, out):
            fw.act(out, P[0:nrow, 2:2 + TV], AF.Copy, scale=xp("c0", pc, nrow))
            fw.stt(out, P[0:nrow, 1:1 + TV], pv("mup", pc, nrow), out, ALU.mult, ALU.add)
            fw.stt(out, P[0:nrow, 3:3 + TV], pv("mun", pc, nrow), out, ALU.mult, ALU.add)

        for i in range(NT0):
            for j in range(4):
                xb = xs[j % 2]
                fw.dma(xb[:], V(x_src.ap()[i * TV + j * 128:i * TV + (j + 1) * 128, :], None), xs_sem[j % 2])
                for kq in range(4):
                    pt = PB[4 + (kq % 2)]
                    for k4 in range(4):
                        k = kq * 4 + k4
                        fw.tr(pt[:, k4 * 128:(k4 + 1) * 128], xb[:, k * 128:(k + 1) * 128], ident[:])
                    fw.copy(xT[:, kq * 4:(kq + 1) * 4, j * 128:(j + 1) * 128],
                            pt.v(pt.h[:].rearrange("p (a b) -> p a b", a=4)), e=("act" if kq % 2 else "dve"))
            pd = new_aux()
            pdv = pd.v(pd.h[:, 0:128].rearrange("p (a b) -> p a b", a=4))
            for j in range(4):
                for k in range(NK):
                    fw.mm(pdv[:, j, :], xT[:, k, j * 128:(j + 1) * 128], wdt[:, k, :], start=(k == 0), stop=(k == NK - 1))
            for d in range(2):
                fw.tt(dtt[:], pdv, dtp[:, 0, d], ALU.add)
                fw.act(dtt[:], dtt[:], AF.Exp)
                fw.act(dts[:, :, d, :], dtt[:], AF.Ln, bias=1.0)
                fw.tt(dts[:, :, 2 + d, :], dts[:, :, d, :], An[:, d], ALU.mult)
            for j in range(4):
                lo, hi = max(2, 128 * j), min(2 + TV, 128 * j + 128)
                fw.dma(V(Sd["DTS"].ap()[i * TV + lo - 2:i * TV + hi - 2], None), dts[lo - 128 * j:hi - 128 * j, j], dts_sem)
            wt = load_wg(C_DW, 448)
            P = new_pb()
            proj(P, wt, 0, 96)
            t = new_tmp()
            shift(P, 24, 96, t[0:96, :])
            fw.act(tdw[:], t[0:96, :], AF.Tanh)
            P = new_pb()
            proj(P, wt, 96, 96)
            t = new_tmp()
            shift(P, 25, 96, t[0:96, :])
            fw.copy(tda[:], t[0:96, :], e="pool")
            for c in range(0 if lite else 2):
                P = new_pb()
                proj(P, wt, 192 + 128 * c, 128)
                t = new_tmp()
                shift(P, 26 + c, 128, t[:])
                fw.act(tdg[:, c, :], t[:], AF.Sigmoid)
            for c in range(8):
                P = new_aux()
                fw.mm(P[:, 0:TV], w2[:, c * 128:(c + 1) * 128], tdw[:])
                for d in range(nd):
                    t = new_tmp()
                    fw.act(t[:], P[:, 0:TV], AF.Sigmoid, bias=pv("w0", d * 8 + c))
                    o, osem = new_ost()
                    fw.ts(o[:], t[:], -math.exp(-0.5), None, ALU.mult, e="pool")
                    store("LW%d" % (d + 1), c * 128, 128, i, o, osem)
                P = new_aux()
                fw.mm(P[:, 0:TV], a2[:, c * 128:(c + 1) * 128], tda[:])
                for d in range(nd):
                    fw.act(ag[:, c, d, :], P[:, 0:TV], AF.Sigmoid, bias=pv("a0", d * 8 + c))
                if not lite:
                    P = new_aux()
                    for k in range(2):
                        fw.mm(P[:, 0:TV], g2[:, k, c * 128:(c + 1) * 128], tdg[:, k, :], start=(k == 0), stop=(k == 1))
                    o, osem = new_ost()
                    fw.copy(o[:], P[:, 0:TV], e="dve")
                    store("G", c * 128, 128, i, o, osem)
            for gi in range(2):
                wt = load_wg(C_K + 512 * gi, 512)
                for cc in range(4):
                    c = gi * 4 + cc
                    P = new_pb()
                    proj(P, wt, cc * 128, 128)
                    shift(P, 8 + c, 128, kp[:, c, :])
                    sq = new_tmp()
                    fw.act(sq[:], kp[:, c, :], AF.Square, scale=pv("k_k", c))
                    Pa = new_aux()
                    fw.mm(Pa[:, 0:TV], blk[:], sq[:])
                    rn = new_tmp()
                    fw.act(rn[:], Pa[:, 0:TV], AF.Ln, bias=1e-24)
                    fw.act(rn[:], rn[:], AF.Exp, scale=-0.5)
                    oa, osem = new_ost()
                    fw.stt(oa[:], kp[:, c, :], xp("nk_k", c), rn[:], ALU.mult, ALU.mult)
                    store("A", c * 128, 128, i, oa, osem)
                    for d in range(nd):
                        o, osem = new_ost()
                        fw.stt(o[:], oa[:], -1.0, ag[:, c, d, :], ALU.mult, ALU.mult)
                        store("B%d" % (d + 1), c * 128, 128, i, o, osem)
                        t = new_tmp()
                        fw.act(t[:], ag[:, c, d, :], AF.Identity, scale=pv("k_a", c), bias=xp("omk_a", c))
                        o, osem = new_ost()
                        fw.tt(o[:], t[:], kp[:, c, :], ALU.mult, e="pool")
                        store("KD%d" % (d + 1), c * 128, 128, i, o, osem)
            for gi in range(0 if lite else 2):
                wt = load_wg(C_R + 512 * gi, 512)
                for cc in range(4):
                    c = gi * 4 + cc
                    P = new_pb()
                    proj(P, wt, cc * 128, 128)
                    o, osem = new_ost()
                    shift(P, c, 128, o[:])
                    store("R", c * 128, 128, i, o, osem)
                    t = new_tmp()
                    fw.stt(t[:], o[:], pv("r_k", c), kp[:, c, :], ALU.mult, ALU.mult)
                    Pa = new_aux()
                    fw.mm(Pa[:, 0:TV], blk[:], t[:])
                    fw.copy(rk[:, c, :], Pa[:, 0:TV], e="act")
            for gi in range(2):
                wt = load_wg(C_V + 512 * gi, 512)
                for cc in range(4):
                    c = gi * 4 + cc
                    P = new_pb()
                    proj(P, wt, cc * 128, 128)
                    o, osem = new_ost()
                    shift(P, 16 + c, 128, o[:])
                    store("V", c * 128, 128, i, o, osem)
                    if not lite:
                        o2, osem2 = new_ost()
                        fw.tt(o2[:], o[:], rk[:, c, :], ALU.mult, e="pool")
                        store("BON", c * 128, 128, i, o2, osem2)
            for gi in range(8):
                wt = load_wg(C_XBC + 512 * gi, 512)
                for cc in range(4):
                    c = gi * 4 + cc
                    P = new_pb()
                    proj(P, wt, cc * 128, 128)
                    t = new_tmp()
                    fw.act(t[:], P[:, 0:TV], AF.Identity, scale=pv("conv_w", c), bias=pv("conv_b", c))
                    for j in range(1, 5):
                        fw.stt(t[:], P[:, j:j + TV], pv("conv_w", 32 * j + c), t[:], ALU.mult, ALU.add)
                    o, osem = new_ost()
                    fw.act(o[:], t[:], AF.Silu)
                    store("XBC", c * 128, 128, i, o, osem)
        fw.barrier()
        st.close()


    NST = T_loc // 128
    S["YRW"] = dscr("s_YRW", [1024, TP])
    st_rw_out = nc.dram_tensor("st_rw_out", [128, 8, 128], F32, kind="ExternalOutput")
    rwc_d = din("rwc", [128, 2, 512])
    rmask_d = din("rmask", [128, 1024])
    identb = sb("identb", [128, 128], BF16)
    fw.dma(identb[:], V(ident_d.ap(), None), fw.dsem("c8"), q="pool")

    def rwkv_phase(d, so=False):
        st = ExitStack()
        Ss = S2 if so else S

        def sbp(name, shape, dt=F32):
            return T(st.enter_context(nc.sbuf_tensor(("rws_" if so else "rw%d_" % d) + name, list(shape), dt)), name)

        rwc = sbp("rwc", [128, 2, 512])
        rmask = sbp("rmask", [128, 1024])
        fw.dma(rwc[:], V(rwc_d.ap(), None), fw.dsem("c9"))
        fw.dma(rmask[:], V(rmask_d.ap(), None), fw.dsem("c10"))
        names = ["R", "KD%d" % (d + 1), "V", "A", "B%d" % (d + 1), "LW%d" % (d + 1)]
        LD = [[sbp("ld%d_%d" % (j, b), [128, 8, 128]) for j in range(6)] for b in range(2)]
        ld_sem = [[fw.dsem("rwld%d_%d" % (j, b)) for j in range(6)] for b in range(2)]
        Y1 = [sbp("y1_%d" % b, [128, 8, 128]) for b in range(2)]
        y1_sem = [fw.dsem("rwy1_%d" % b) for b in range(2)]
        YB = [sbp("yb_%d" % b, [128, 8, 128]) for b in range(2)]
        yb_sem = [fw.dsem("rwyb_%d" % b) for b in range(2)]
        pre = sbp("pre", [128, 1024])
        cl = sbp("cl", [128, 1024])
        ecl = sbp("ecl", [128, 8, 128])
        encl = sbp("encl", [128, 8, 128])
        ecx = sbp("ecx", [128, 8, 128])
        xt = [sbp("xt%d" % j, [128, 8, 128]) for j in range(4)]
        BT = [sbp("BTbd%d" % b, [128, 8, 128], BF16) for b in range(2)]
        KT = [sbp("KTbd%d" % b, [128, 8, 128], BF16) for b in range(2)]
        AR = [sbp("ARbd%d" % b, [128, 8, 256], BF16) for b in range(2)]
        VB = [sbp("Vbd%d" % b, [128, 8, 128], BF16) for b in range(2)]
        for b in range(2):
            for t_ in (BT[b], KT[b], AR[b], VB[b]):
                fw.memset(t_[:], 0.0, e="pool")
        S0 = sbp("S0", [128, 8, 128])
        S0b = sbp("S0b", [128, 8, 128], BF16)
        ssem = fw.dsem("rwstate")
        if d == 0:
            fw.memset(S0[:], 0.0)
        else:
            fw.dma(S0[:], V(stx_rw.ap(), None), ssem)
            fw.ts(S0[:].rr("p a b -> p (a b)"), S0[:].rr("p a b -> p (a b)"), sel[:, 0:1], None, ALU.mult)
        fw.copy(S0b[:], S0[:], e="act")
        ABs = [sbp("ABs%d" % b, [128, 4, 256], BF16) for b in range(2)]
        AKs = [sbp("AKs%d" % b, [128, 4, 256], BF16) for b in range(2)]
        NTs = [[sbp("NTs%d_%d" % (b, j), [128, 4, 128], BF16) for j in range(2)] for b in range(2)]
        Ns = [[sbp("Ns%d_%d" % (b, j), [128, 4, 128], BF16) for j in range(2)] for b in range(2)]
        Ps = [[sbp("Ps%d_%d" % (b, j), [128, 4, 128], BF16) for j in range(2)] for b in range(2)]
        VTs = [sbp("VTs%d" % b, [128, 4, 128], BF16) for b in range(2)]
        GTs = [sbp("GTs%d" % b, [128, 4, 128], BF16) for b in range(2)]
        UTs = [sbp("UTs%d" % b, [128, 4, 128], BF16) for b in range(2)]
        BKT = [sbp("BKT%d" % b, [128, 4, 2, 128], BF16) for b in range(2)]
        stmp = [sbp("stmp%d" % b, [128, 4, 128]) for b in range(2)]
        pbc = {"n": 0}

        def pbank():
            j = pbc["n"] % 8
            pbc["n"] += 1
            return PB[j]

        mAB = rwc[:, d, 0:256]
        mNT = rwc[:, d, 256:384]
        tiles = list(range(NST)) if d == 0 else list(range(NST - 1, -1, -1))
        chunks = (0, 1) if d == 0 else (1, 0)

        def load_tile(n):
            ti = tiles[n]
            b = n % 2
            for j in range(6):
                if so and j == 0:
                    continue
                fw.dma(LD[b][j][:], V(Ss[names[j]].ap()[:, ti * 128:(ti + 1) * 128].rearrange("(c p) t -> p c t", p=128), None),
                       ld_sem[b][j])
            if d == 1:
                fw.dma(Y1[b][:], V(S["YRW"].ap()[:, ti * 128:(ti + 1) * 128].rearrange("(c p) t -> p c t", p=128), None),
                       y1_sem[b])

        load_tile(0)
        qn = 0
        for n in range(NST):
            ti = tiles[n]
            b = n % 2
            if n + 1 < NST:
                load_tile(n + 1)
            r_, k_, v_, a_, b_, lw_ = [LD[b][j] for j in range(6)]
            fw.scan(pre[:], rmask[:], lw_[:].rr("p c t -> p (c t)"), 0.0, ALU.mult, ALU.add)
            pre4 = pre[:].rr("p (c t) -> p c t", t=64)
            cl4 = cl[:].rr("p (c t) -> p c t", t=64)
            lw4 = lw_[:].rr("p c (u t) -> p (c u) t", t=64)
            if d == 0:
                clv = pre
            else:
                fw.tt(cl4, lw4, pre4, ALU.subtract)
                fw.tt(cl4, cl4, pre4[:, :, 63:64].bc([128, 16, 64]), ALU.add)
                clv = cl
            clf = clv[:]
            fw.act(ecl[:].rr("p c t -> p (c t)"), clf, AF.Exp)
            fw.act(encl[:].rr("p c t -> p (c t)"), clf, AF.Exp, scale=-1.0)
            fw.tt(ecx[:].rr("p c t -> p (c t)"), clf, lw_[:].rr("p c t -> p (c t)"), ALU.subtract, e="pool")
            fw.act(ecx[:].rr("p c t -> p (c t)"), ecx[:].rr("p c t -> p (c t)"), AF.Exp)
            fw.tt(xt[0][:], b_[:], encl[:], ALU.mult, e="pool")
            fw.tt(xt[1][:], k_[:], encl[:], ALU.mult)
            fw.tt(xt[2][:], a_[:], ecx[:], ALU.mult, e="pool")
            if not so:
                fw.tt(xt[3][:], r_[:], ecl[:], ALU.mult)
            for ci in chunks:
                cb = (2 * n + ci) % 2
                cs = slice(ci * 64, ci * 64 + 64)
                for hh in range(2):
                    ps = slice(64 * hh, 64 * hh + 64)
                    fs = slice(64 * hh, 64 * hh + 64)
                    fw.copy(BT[cb][ps, :, fs], xt[0][ps, :, cs], e="pool")
                    fw.copy(KT[cb][ps, :, fs], xt[1][ps, :, cs], e="dve")
                    fw.copy(AR[cb][ps, :, fs], xt[2][ps, :, cs], e="pool")
                    if not so:
                        fw.copy(AR[cb][ps, :, slice(128 + 64 * hh, 192 + 64 * hh)], xt[3][ps, :, cs], e="act")
                    fw.copy(VB[cb][ps, :, fs], v_[ps, :, cs], e="dve")
                wl = ecl[:, :, (ci * 64 + 63) if d == 0 else (ci * 64)]
                for q in range(2):
                    qb = qn % 2
                    qn += 1
                    p0 = 4 * q
                    for half in range(2):
                        pa = pbank()
                        pk = pbank()
                        for pp in range(2):
                            p = p0 + 2 * half + pp
                            fw.mm(pa[:, pp * 256:(pp + 1) * 256], BT[cb][:, p, :], AR[cb][:, p, :])
                            fw.mm(pk[:, pp * 256:(pp + 1) * 256], KT[cb][:, p, :], AR[cb][:, p, :])
                        fw.tt(ABs[qb][:, 2 * half:2 * half + 2, :], pa[:].rr("p (a b) -> p a b", a=2),
                              mAB.us(1).bc([128, 2, 256]), ALU.mult)
                        fw.tt(AKs[qb][:, 2 * half:2 * half + 2, :], pk[:].rr("p (a b) -> p a b", a=2),
                              mAB.us(1).bc([128, 2, 256]), ALU.mult)
                    pn = pbank()
                    for pp in range(4):
                        p = p0 + pp
                        fw.mm(pn[:, pp * 128:(pp + 1) * 128], AR[cb][:, p, 0:128], BT[cb][:, p, :])
                    fw.tt(NTs[qb][0][:], pn[:].rr("p (a b) -> p a b", a=4), mNT.us(1).bc([128, 4, 128]), ALU.mult)
                    Ncur = ABs[qb][:, :, 0:128]
                    NTcur = NTs[qb][0][:]
                    fw.tt(Ps[qb][0][:], Ncur, identb[:].us(1).bc([128, 4, 128]), ALU.add, e="pool")
                    Pcur = Ps[qb][0][:]
                    for lev in range(1, 6):
                        j = lev % 2
                        pnt = pbank()
                        for pp in range(4):
                            fw.mm(pnt[:, pp * 128:(pp + 1) * 128], Ncur[:, pp, :], NTcur[:, pp, :])
                        fw.copy(NTs[qb][j][:], pnt[:].rr("p (a b) -> p a b", a=4), e="act")
                        if lev <= 4:
                            pnn = pbank()
                            for pp in range(4):
                                fw.mm(pnn[:, pp * 128:(pp + 1) * 128], NTcur[:, pp, :], Ncur[:, pp, :])
                            fw.copy(Ns[qb][j][:], pnn[:].rr("p (a b) -> p a b", a=4), e="dve")
                            Nnext = Ns[qb][j][:]
                        NTnext = NTs[qb][j][:]
                        pp_ = pbank()
                        for pp in range(4):
                            fw.mm(pp_[:, pp * 128:(pp + 1) * 128], identb[:], Pcur[:, pp, :], start=True, stop=False)
                            fw.mm(pp_[:, pp * 128:(pp + 1) * 128], NTnext[:, pp, :], Pcur[:, pp, :], start=False, stop=True)
                        fw.copy(Ps[qb][j][:], pp_[:].rr("p (a b) -> p a b", a=4), e="dve")
                        Pcur = Ps[qb][j][:]
                        NTcur = NTnext
                        if lev <= 4:
                            Ncur = Nnext
                    Minv = Pcur
                    pv_ = pbank()
                    pvb = pv_.v(pv_.h[:].bitcast(BF16)[:, 0:512].rearrange("p (a b) -> p a b", a=4))
                    for pp in range(4):
                        fw.tr(pvb[:, pp, :], VB[cb][:, p0 + pp, :], identb[:])
                    fw.copy(VTs[qb][:], pvb, e="act")
                    pg = pbank()
                    for pp in range(4):
                        p = p0 + pp
                        fw.mm(pg[:, pp * 128:(pp + 1) * 128], AR[cb][:, p, 0:128], S0b[:, p, :], start=True, stop=False)
                        fw.mm(pg[:, pp * 128:(pp + 1) * 128], AKs[qb][:, pp, 0:128], VTs[qb][:, pp, :], start=False, stop=True)
                    fw.copy(GTs[qb][:], pg[:].rr("p (a b) -> p a b", a=4), e="dve")
                    pu = pbank()
                    for pp in range(4):
                        fw.mm(pu[:, pp * 128:(pp + 1) * 128], Minv[:, pp, :], GTs[qb][:, pp, :])
                    fw.copy(UTs[qb][:], pu[:].rr("p (a b) -> p a b", a=4), e="act")
                    if not so:
                        py = pbank()
                        for pp in range(4):
                            p = p0 + pp
                            o = py[:, pp * 128:(pp + 1) * 128]
                            fw.mm(o, S0b[:, p, :], AR[cb][:, p, 128:256], start=True, stop=False)
                            fw.mm(o, UTs[qb][:, pp, :], ABs[qb][:, pp, 128:256], start=False, stop=False)
                            fw.mm(o, VTs[qb][:, pp, :], AKs[qb][:, pp, 128:256], start=False, stop=True)
                        py4 = py[:].rr("p (a b) -> p a b", a=4)
                        if d == 0:
                            fw.copy(YB[b][0:64, p0:p0 + 4, cs], py4[0:64, :, 0:64], e="act")
                            fw.copy(YB[b][64:128, p0:p0 + 4, cs], py4[64:128, :, 64:128], e="dve")
                        else:
                            fw.tt(YB[b][0:64, p0:p0 + 4, cs], py4[0:64, :, 0:64], Y1[b][0:64, p0:p0 + 4, cs], ALU.add)
                            fw.tt(YB[b][64:128, p0:p0 + 4, cs], py4[64:128, :, 64:128], Y1[b][64:128, p0:p0 + 4, cs], ALU.add)
                    pt_ = pbank()
                    ptb = pt_.v(pt_.h[:].bitcast(BF16).rearrange("p (a c b) -> p a c b", a=4, c=2))
                    for pp in range(4):
                        p = p0 + pp
                        fw.tr(ptb[:, pp, 0, :], BT[cb][:, p, :], identb[:])
                        fw.tr(ptb[:, pp, 1, :], KT[cb][:, p, :], identb[:])
                    fw.copy(BKT[qb][:], ptb, e="act")
                    pd_ = pbank()
                    for pp in range(4):
                        o = pd_[:, pp * 128:(pp + 1) * 128]
                        fw.mm(o, BKT[qb][:, pp, 0, :], UTs[qb][:, pp, :], start=True, stop=False)
                        fw.mm(o, BKT[qb][:, pp, 1, :], VTs[qb][:, pp, :], start=False, stop=True)
                    wlb = wl[:, p0:p0 + 4].us(2).bc([128, 4, 128])
                    fw.tt(stmp[qb][:], S0[:, p0:p0 + 4, :], wlb, ALU.mult, e="pool")
                    fw.tt(S0[:, p0:p0 + 4, :], pd_[:].rr("p (a b) -> p a b", a=4), wlb, ALU.mult)
                    fw.tt(S0[:, p0:p0 + 4, :], S0[:, p0:p0 + 4, :], stmp[qb][:], ALU.add)
                    fw.copy(S0b[:, p0:p0 + 4, :], S0[:, p0:p0 + 4, :], e="act")
            if not so:
                fw.dma(V(S["YRW"].ap()[:, ti * 128:(ti + 1) * 128].rearrange("(c p) t -> p c t", p=128), None), YB[b][:], yb_sem[b])
        if so:
            fw.dma(V(stx_rw.ap(), None), S0[:], ssem)
        elif d == 0:
            fw.dma(V(st_rw_out.ap(), None), S0[:], ssem)
        fw.barrier()
        st.close()

    S["YM"] = dscr("s_YM", [2048, TP])
    st_m_out = nc.dram_tensor("st_m_out", [128, 32, 64], F32, kind="ExternalOutput")
    mc_d = din("mc", [128, 2, 2, 128])
    ones = sb("ones", [128, 128])
    fw.memset(ones[:], 1.0)

    def mamba_phase(d, so=False):
        st = ExitStack()
        Ss = S2 if so else S

        def sbp(name, shape, dt=F32):
            return T(st.enter_context(nc.sbuf_tensor(("mbs_" if so else "mb%d_" % d) + name, list(shape), dt)), name)

        mcst = sbp("mcst", [128, 2, 2, 128])
        fw.dma(mcst[:], V(mc_d.ap(), None), fw.dsem("c11"))
        XS = [sbp("xs%d" % b, [128, 16, 128]) for b in range(2)]
        Bb = [sbp("bb%d" % b, [128, 8, 128], BF16) for b in range(2)]
        Cb = [sbp("cb%d" % b, [128, 8, 128], BF16) for b in range(2)]
        DT = [sbp("dt%d" % b, [128, 4, 32]) for b in range(2)]
        Y1 = [sbp("y1_%d" % b, [128, 16, 128]) for b in range(2)]
        YB = [sbp("yb_%d" % b, [128, 16, 128]) for b in range(2)]
        sems = [[fw.dsem("mbld%d_%d" % (j, b)) for j in range(4)] for b in range(2)]
        psems = [[fw.dsem("mbldp%d_%d" % (j, b)) for j in range(2)] for b in range(2)]
        yb_sem = [fw.dsem("mbyb_%d" % b) for b in range(2)]
        dAexp = sbp("dAexp", [128, 32, 128])
        cs_tok = sbp("cs_tok", [128, 32])
        csl = sbp("csl", [128, 32])
        ecl_last = sbp("ecl_last", [128, 32])
        decs = sbp("decs", [128, 32])
        E = [sbp("E%d" % b, [128, 8, 128]) for b in range(2)]
        ecsR = [sbp("ecsR%d" % b, [128, 8, 128]) for b in range(2)]
        MT = sbp("MT", [128, 32, 128], BF16)
        Csc = sbp("Csc", [128, 32, 128], BF16)
        CBm = sbp("CBm", [128, 8, 128])
        xdt = sbp("xdt", [128, 32, 64], BF16)
        xdd = sbp("xdd", [128, 32, 64], BF16)
        Btok = sbp("Btok", [128, 8, 128], BF16)
        hS = sbp("hS", [128, 32, 64])
        hb = sbp("hb", [128, 32, 64], BF16)
        htmp = sbp("htmp", [128, 32, 64])
        ssem = fw.dsem("mbstate")
        if d == 0:
            fw.memset(hS[:], 0.0)
        else:
            fw.dma(hS[:], V(stx_m.ap(), None), ssem)
            fw.ts(hS[:].rr("p a b -> p (a b)"), hS[:].rr("p a b -> p (a b)"), sel[:, 0:1], None, ALU.mult)
        fw.copy(hb[:], hS[:], e="act")
        tri = mcst[:, d, 0, :]
        lst = mcst[:, d, 1, :]
        t_last = 127 if d == 0 else 0
        NCH = T_loc // 128
        tiles = list(range(NCH)) if d == 0 else list(range(NCH - 1, -1, -1))
        pbc = {"n": 0}

        def pbank():
            j = pbc["n"] % 8
            pbc["n"] += 1
            return PB[j]

        def load_tile(n):
            ti = tiles[n]
            b = n % 2
            cs_ = slice(ti * 128, (ti + 1) * 128)
            xb = Ss["XBC"].ap()
            fw.dma(XS[b][:], V(xb[0:2048, cs_].rearrange("(c p) t -> p c t", p=128), None), sems[b][0])
            fw.dma(DT[b][:], V(Ss["DTS"].ap()[cs_], None), sems[b][1])
            fw.dma(Bb[b][:], V(xb[2048:3072, cs_].rearrange("(c p) t -> p c t", p=128), None), psems[b][0], q="pool")
            if not so:
                fw.dma(Cb[b][:], V(xb[3072:4096, cs_].rearrange("(c p) t -> p c t", p=128), None), psems[b][1], q="pool")
            if d == 1:
                fw.dma(Y1[b][:], V(S["YM"].ap()[:, cs_].rearrange("(c p) t -> p c t", p=128), None), sems[b][2])

        load_tile(0)
        for n in range(NCH):
            ti = tiles[n]
            b = n % 2
            if n + 1 < NCH:
                load_tile(n + 1)
            dA = DT[b][:, 2 + d, :]
            dtv = DT[b][:, d, :]
            if so:
                pc = pbank()
                fw.mm(pc[:, 0:32], tri, dA)
                fw.copy(cs_tok[:], pc[:, 0:32], e="act")
                pc2 = pbank()
                fw.mm(pc2[:, 0:32], ones[:], dA)
                fw.copy(csl[:], pc2[:, 0:32], e="dve")
            if not so:
                fw.tt(dAexp[:], dA.us(2).bc([128, 32, 128]), tri.us(1).bc([128, 32, 128]), ALU.mult)
                pc = pbank()
                fw.mm(pc[:, 0:32], tri, dA)
                fw.copy(cs_tok[:], pc[:, 0:32], e="act")
                for half in range(2):
                    pcb = pbank()
                    for gg in range(4):
                        g = half * 4 + gg
                        fw.mm(pcb[:, gg * 128:(gg + 1) * 128], Bb[b][:, g, :], Cb[b][:, g, :])
                    fw.tt(CBm[:, half * 4:half * 4 + 4, :], pcb[:].rr("p (a b) -> p a b", a=4), tri.us(1).bc([128, 4, 128]), ALU.mult)
                for o in range(4):
                    ob = o % 2
                    pD = [pbank(), pbank()]
                    pR = [pbank(), pbank()]
                    for j in range(2):
                        rhs = dAexp[:, o * 8 + j * 4:o * 8 + j * 4 + 4, :].rr("p a b -> p (a b)")
                        fw.mm(pD[j][:], lst, rhs)
                        fw.mm(pR[j][:], ones[:], rhs)
                    for j in range(2):
                        hs = slice(o * 8 + j * 4, o * 8 + j * 4 + 4)
                        g = o * 2 + j
                        fw.act(E[ob][:, j * 4:j * 4 + 4, :], pD[j][:].rr("p (a b) -> p a b", a=4), AF.Exp)
                        fw.tt(MT[:, hs, :], E[ob][:, j * 4:j * 4 + 4, :], CBm[:, g:g + 1, :].bc([128, 4, 128]), ALU.mult)
                        pR4 = pR[j][:].rr("p (a b) -> p a b", a=4)
                        fw.copy(csl[:, hs], pR4[:, :, t_last], e="dve")
                        fw.act(ecsR[ob][:, j * 4:j * 4 + 4, :], pR4, AF.Exp)
                        fw.tt(Csc[:, hs, :], ecsR[ob][:, j * 4:j * 4 + 4, :], Cb[b][:, g:g + 1, :].bc([128, 4, 128]), ALU.mult, e="dve")
            fw.act(ecl_last[:], csl[:], AF.Exp)
            fw.tt(decs[:], csl[:], cs_tok[:], ALU.subtract)
            fw.act(decs[:], decs[:], AF.Exp)
            for q in range(4):
                px = pbank()
                for cc in range(4):
                    c = q * 4 + cc
                    fw.tr(px[:, cc * 128:(cc + 1) * 128], XS[b][:, c, :], ident[:])
                hs = slice(q * 8, q * 8 + 8)
                fw.tt(xdt[:, hs, :], px[:].rr("p (a b) -> p a b", a=8), dtv[:, hs].us(2).bc([128, 8, 64]), ALU.mult)
                fw.tt(xdd[:, hs, :], xdt[:, hs, :], decs[:, hs].us(2).bc([128, 8, 64]), ALU.mult, e="pool")
            if not so:
                for q in range(4):
                    py = pbank()
                    for cc in range(4):
                        for hh in range(2):
                            h = (q * 4 + cc) * 2 + hh
                            o_ = py[64 * hh:64 * hh + 64, cc * 128:(cc + 1) * 128]
                            kw_ = {"tile_position": (0, 64)} if hh == 1 else {}
                            fw.mm(o_, xdt[:, h, :], MT[:, h, :], start=True, stop=False, **kw_)
                            fw.mm(o_, hb[:, h, :], Csc[:, h, :], start=False, stop=True, **kw_)
                    py4 = py[:].rr("p (a b) -> p a b", a=4)
                    if d == 0:
                        fw.copy(YB[b][:, q * 4:q * 4 + 4, :], py4, e="act")
                    else:
                        fw.tt(YB[b][:, q * 4:q * 4 + 4, :], py4, Y1[b][:, q * 4:q * 4 + 4, :], ALU.add)
                fw.dma(V(S["YM"].ap()[:, ti * 128:(ti + 1) * 128].rearrange("(c p) t -> p c t", p=128), None), YB[b][:], yb_sem[b])
            pt_ = pbank()
            ptb = pt_.v(pt_.h[:].bitcast(BF16).rearrange("p (a b) -> p a b", a=8))
            for g in range(8):
                fw.tr(ptb[:, g, :], Bb[b][:, g, :], identb[:])
            fw.copy(Btok[:], ptb, e="act")
            fw.tt(htmp[:], hS[:], ecl_last[:].us(2).bc([128, 32, 64]), ALU.mult, e="pool")
            for q in range(4):
                pn_ = pbank()
                for gg in range(2):
                    g = q * 2 + gg
                    fw.mm(pn_[:, gg * 256:(gg + 1) * 256], Btok[:, g, :], xdd[:, 4 * g:4 * g + 4, :].rr("p a b -> p (a b)"))
                hs = slice(q * 8, q * 8 + 8)
                fw.tt(hS[:, hs, :], pn_[:].rr("p (a b) -> p a b", a=8), htmp[:, hs, :], ALU.add)
            fw.copy(hb[:], hS[:], e="act")
        if so:
            fw.dma(V(stx_m.ap(), None), hS[:], ssem)
        elif d == 0:
            fw.dma(V(st_m_out.ap(), None), hS[:], ssem)
        fw.barrier()
        st.close()

    ALPHA = 2.0 ** 0.25
    LN_EPS = 1e-5
    GN_EPS = 64e-5
    mem_d = din("mem", [256, D])
    w_br_d = din("w_br", [1024, D])
    w_bm_d = din("w_bm", [D, D])
    w_o_d = din("w_o", [D, D])
    w_q_d = din("w_q", [D, D])
    w_kv_d = din("w_kv", [D, 2 * D])
    w_co_d = din("w_co", [D, D])
    w_up_d = din("w_up", [D, 4 * D])
    w_down_d = din("w_down", [4 * D, D])
    y_out = nc.dram_tensor("y_out", [T_loc, D], F32, kind="ExternalOutput")

    def wview(w):
        return w.ap().rearrange("(k p) n -> p k n", p=128)

    def phase3():
        st = ExitStack()

        def sbp(name, shape, dt=F32):
            return T(st.enter_context(nc.sbuf_tensor("p3_" + name, list(shape), dt)), name)

        F32A = sbp("F32A", [128, NK, 512])
        BFA = sbp("BFA", [128, NK, 512], BF16)
        BFB = sbp("BFB", [128, NK, 512], BF16)
        BFC = sbp("BFC", [128, NK, 512], BF16)
        BFD = sbp("BFD", [128, 8, 512], BF16)
        HM = sbp("HM", [128, 16, 512], BF16)
        Kt = sbp("Kt", [128, NK, 256], BF16)
        Vt = sbp("Vt", [128, 2, D], BF16)
        onesb = sbp("onesb", [128, 128], BF16)
        ksc = sbp("ksc", [128, 4])
        fw.copy(onesb[:], ones[:], e="act")
        NWB = 3
        WB = [sbp("wb%d" % j, [128, 4096], BF16) for j in range(NWB)]
        wb_sem = [fw.dsem("p3wb%d" % j) for j in range(NWB)]
        NL = 6
        LB = [sbp("lb%d" % j, [128, 512]) for j in range(NL)]
        lb_sem = [fw.dsem("p3lb%d" % j) for j in range(NL)]
        NTMP = 5
        TMP = [sbp("tmp%d" % j, [128, 512]) for j in range(NTMP)]
        xs = [sbp("xs%d" % j, [128, D]) for j in range(2)]
        xs_sem = [fw.dsem("p3xs%d" % j) for j in range(2)]
        ymp = [sbp("ymp%d" % j, [128, 2, 512]) for j in range(1)]
        sqp = [sbp("sqp%d" % j, [128, 2, 512]) for j in range(1)]
        expS = [sbp("expS%d" % j, [128, 2, 512], BF16) for j in range(1)]
        ded = {nm: sbp("ded_" + nm, [128, 512]) for nm in ("mean", "rstd", "cst", "rs")}
        cnt = {"pb": 0, "lb": 0, "tmp": 0}

        def pbank():
            j = cnt["pb"] % 8
            cnt["pb"] += 1
            return PB[j]

        def tmp():
            j = cnt["tmp"] % NTMP
            cnt["tmp"] += 1
            return TMP[j]

        def ld(name, row0, t0):
            j = cnt["lb"] % NL
            cnt["lb"] += 1
            fw.dma(LB[j][:], V(S[name].ap()[row0:row0 + 128, t0:t0 + 512], None), lb_sem[j])
            return LB[j]

        class WS:
            def __init__(self):
                self.specs = []
                self.issued = 0
                self.tiles = {}

            def add(self, wv, k0, nk, col0, ncols):
                self.specs.append((wv, k0, nk, col0, ncols))
                return len(self.specs) - 1

            def _issue(self, n):
                wv, k0, nk, col0, ncols = self.specs[n]
                j = n % NWB
                tv = WB[j][:, 0:nk * ncols].rr("p (k n) -> p k n", k=nk)
                fw.dma(tv, V(wv[:, k0:k0 + nk, col0:col0 + ncols], None), wb_sem[j], q="pool")
                self.tiles[n] = tv

            def get(self, n):
                while self.issued < min(len(self.specs), n + NWB):
                    self._issue(self.issued)
                    self.issued += 1
                return self.tiles.pop(n)

        w_in_v3 = w_in_v

        def dense(ws_ids, ws, src, nk, consume):
            pass

        def layer_norm(gname, bname):
            ps1 = pbank()
            ps2 = pbank()
            for c in range(NK):
                sq = tmp()
                fw.act(sq[:], F32A[:, c, :], AF.Square)
                fw.mm(ps1[:], ones[:], F32A[:, c, :], start=(c == 0), stop=(c == NK - 1))
                fw.mm(ps2[:], ones[:], sq[:], start=(c == 0), stop=(c == NK - 1))
            mean = ded["mean"]
            fw.act(mean[:], ps1[:], AF.Copy, scale=1.0 / D)
            msq = tmp()
            fw.act(msq[:], ps1[:], AF.Square, scale=1.0 / D)
            rstd = ded["rstd"]
            fw.stt(rstd[:], ps2[:], 1.0 / D, msq[:], ALU.mult, ALU.subtract)
            fw.act(rstd[:], rstd[:], AF.Ln, bias=LN_EPS)
            fw.act(rstd[:], rstd[:], AF.Exp, scale=-0.5)
            for c in range(NK):
                t_ = tmp()
                fw.tt(t_[:], F32A[:, c, :], mean[:], ALU.subtract)
                fw.tt(t_[:], t_[:], rstd[:], ALU.mult)
                fw.act(F32A[:, c, :], t_[:], AF.Identity, scale=pv(gname, c), bias=pv(bname, c))
                fw.copy(BFA[:, c, :], F32A[:, c, :], e="dve")

        memT = BFB
        memTv = memT[:, :, 0:256]
        for mb in range(2):
            fw.dma(xs[mb][:], V(mem_d.ap()[mb * 128:(mb + 1) * 128, :], None), xs_sem[mb])
            for kq in range(4):
                pt = pbank()
                for k4 in range(4):
                    k = kq * 4 + k4
                    fw.tr(pt[:, k4 * 128:(k4 + 1) * 128], xs[mb][:, k * 128:(k + 1) * 128], ident[:])
                fw.copy(memT[:, kq * 4:(kq + 1) * 4, mb * 128:(mb + 1) * 128], pt[:].rr("p (a b) -> p a b", a=4), e="act")
        ws = WS()
        wkv = wview(w_kv_d)
        ids = [ws.add(wkv, 0, NK, c * 256, 256) for c in range(16)]
        for c in range(8):
            wt = ws.get(ids[c])
            for oo in range(2):
                oc = c * 2 + oo
                pk = pbank()
                for k in range(NK):
                    fw.mm(pk[:, 0:256], wt[:, k, oo * 128:(oo + 1) * 128], memTv[:, k, :], start=(k == 0), stop=(k == NK - 1))
                fw.copy(Kt[:, oc, :], pk[:, 0:256], e="act")
        for c in range(8):
            wt = ws.get(ids[8 + c])
            for mb in range(2):
                pvv = pbank()
                for k in range(NK):
                    fw.mm(pvv[:, 0:256], memT[:, k, mb * 128:(mb + 1) * 128], wt[:, k, :], start=(k == 0), stop=(k == NK - 1))
                fw.copy(Vt[:, mb, c * 256:(c + 1) * 256], pvv[:, 0:256], e="dve")
        for hd in range(4):
            pk2 = pbank()
            for kc in range(4):
                sqk = tmp()
                sqkb = sqk[:, 0:128].ap.bitcast(BF16)
                sqv = V(sqkb, sqk.buf)
                fw.act(sqv, Kt[:, hd * 4 + kc, :], AF.Square)
                fw.mm(pk2[:, 0:256], onesb[:], sqv, start=(kc == 0), stop=(kc == 3))
            mx = tmp()
            i_ = nc.vector
            r_, w_ = fw._bufs([pk2[:]]), fw._bufs([mx[:]])
            fw._deps("dve", r_, w_)
            ins = nc.vector.tensor_reduce(mx[:, 0:1].ap, pk2[:, 0:256].ap, AX.X, ALU.max)
            fw._done(ins, "dve", 1, r_, w_)
            fw.ts(ksc[:, hd:hd + 1], mx[:, 0:1], 1.0 / 512.0, None, ALU.mult)

        NT3 = T_loc // 512
        for i in range(NT3):
            t0 = i * 512
            ws = WS()
            wbr, wbm, wo, wq, wco, wup, wdn = [wview(w) for w in (w_br_d, w_bm_d, w_o_d, w_q_d, w_co_d, w_up_d, w_down_d)]
            id_z = [ws.add(w_in_v3, 0, NK, C_Z + c * 256, 256) for c in range(8)]
            id_d = []
            for c in range(8):
                id_d.append((ws.add(wbr, 0, 8, c * 256, 256), ws.add(w_in_v3, 0, NK, C_GATES + c * 256, 256),
                             ws.add(wbm, 0, NK, c * 256, 256), ws.add(w_in_v3, 0, NK, C_GATES + 2048 + c * 256, 256)))
            id_o = [ws.add(wo, 0, NK, c * 256, 256) for c in range(8)]
            id_q = [ws.add(wq, 0, NK, c * 256, 256) for c in range(8)]
            id_co = [ws.add(wco, 0, NK, c * 256, 256) for c in range(8)]
            id_up, id_dn = [], []
            for hf in range(4):
                id_up.append([ws.add(wup, 0, NK, hf * 2048 + c * 256, 256) for c in range(8)])
                id_dn.append([ws.add(wdn, hf * 16, 16, c * 256, 256) for c in range(8)])
            for j in range(4):
                xb = xs[j % 2]
                fw.dma(xb[:], V(x_ext.ap()[2 + t0 + j * 128:2 + t0 + (j + 1) * 128, :], None), xs_sem[j % 2])
                for kq in range(4):
                    pt = pbank()
                    for k4 in range(4):
                        k = kq * 4 + k4
                        fw.tr(pt[:, k4 * 128:(k4 + 1) * 128], xb[:, k * 128:(k + 1) * 128], ident[:])
                    pt4 = pt[:].rr("p (a b) -> p a b", a=4)
                    fw.copy(F32A[:, kq * 4:(kq + 1) * 4, j * 128:(j + 1) * 128], pt4, e="act")
                    fw.copy(BFA[:, kq * 4:(kq + 1) * 4, j * 128:(j + 1) * 128], pt4, e="dve")
            for c in range(8):
                y = ld("YRW", c * 128, t0)
                bon = ld("BON", c * 128, t0)
                gg = ld("G", c * 128, t0)
                sq = tmp()
                fw.act(sq[:], y[:], AF.Square)
                p1 = pbank()
                fw.mm(p1[:], blk[:], y[:])
                p2 = pbank()
                fw.mm(p2[:], blk[:], sq[:])
                m = tmp()
                fw.act(m[:], p1[:], AF.Copy, scale=1.0 / 64)
                msq = tmp()
                fw.act(msq[:], p1[:], AF.Square, scale=1.0 / 64)
                var = tmp()
                fw.stt(var[:], p2[:], 1.0 / 64, msq[:], ALU.mult, ALU.subtract)
                fw.act(var[:], var[:], AF.Ln, bias=GN_EPS)
                fw.act(var[:], var[:], AF.Exp, scale=-0.5)
                fw.tt(y[:], y[:], m[:], ALU.subtract)
                fw.tt(y[:], y[:], var[:], ALU.mult)
                fw.act(y[:], y[:], AF.Identity, scale=pv("gn_g", c), bias=pv("gn_b", c))
                fw.tt(y[:], y[:], bon[:], ALU.add)
                fw.tt(BFD[:, c, :], y[:], gg[:], ALU.mult)
            for c in range(NK):
                if c % 2 == 0:
                    wz = ws.get(id_z[c // 2])
                pb_ = 0
                ym = ld("YM", c * 128, t0)
                xv = ld("XBC", c * 128, t0)
                pz = pbank()
                for k in range(NK):
                    fw.mm(pz[:], wz[:, k, (c % 2) * 128:(c % 2 + 1) * 128], BFA[:, k, :], start=(k == 0), stop=(k == NK - 1))
                fw.stt(ym[:], xv[:], pv("m_d", c), ym[:], ALU.mult, ALU.add)
                sz = tmp()
                fw.act(sz[:], pz[:], AF.Silu)
                fw.tt(ymp[pb_][:, c % 2, :], ym[:], sz[:], ALU.mult)
                fw.act(sqp[pb_][:, c % 2, :], ymp[pb_][:, c % 2, :], AF.Square)
                if c % 2 == 1:
                    pss = pbank()
                    fw.mm(pss[:], ones[:], sqp[pb_][:, 0, :], start=True, stop=False)
                    fw.mm(pss[:], ones[:], sqp[pb_][:, 1, :], start=False, stop=True)
                    rms = tmp()
                    fw.act(rms[:], pss[:], AF.Ln, scale=1.0 / 256, bias=LN_EPS)
                    fw.act(rms[:], rms[:], AF.Exp, scale=-0.5)
                    for cc in range(2):
                        fw.stt(BFB[:, c - 1 + cc, :], ymp[pb_][:, cc, :], pv("m_norm_g", c - 1 + cc), rms[:], ALU.mult, ALU.mult)
            for c in range(8):
                wt = ws.get(id_d[c][0])
                pu = [pbank(), pbank()]
                for oo in range(2):
                    for k in range(8):
                        fw.mm(pu[oo][:], wt[:, k, oo * 128:(oo + 1) * 128], BFD[:, k, :], start=(k == 0), stop=(k == 7))
                wt = ws.get(id_d[c][1])
                t1 = [tmp(), tmp()]
                for oo in range(2):
                    pg = pbank()
                    for k in range(NK):
                        fw.mm(pg[:], wt[:, k, oo * 128:(oo + 1) * 128], BFA[:, k, :], start=(k == 0), stop=(k == NK - 1))
                    fw.act(t1[oo][:], pg[:], AF.Sigmoid)
                    fw.tt(t1[oo][:], t1[oo][:], pu[oo][:], ALU.mult)
                wt = ws.get(id_d[c][2])
                pm = [pbank(), pbank()]
                for oo in range(2):
                    for k in range(NK):
                        fw.mm(pm[oo][:], wt[:, k, oo * 128:(oo + 1) * 128], BFB[:, k, :], start=(k == 0), stop=(k == NK - 1))
                wt = ws.get(id_d[c][3])
                for oo in range(2):
                    oc = c * 2 + oo
                    pg2 = pbank()
                    for k in range(NK):
                        fw.mm(pg2[:], wt[:, k, oo * 128:(oo + 1) * 128], BFA[:, k, :], start=(k == 0), stop=(k == NK - 1))
                    sg2 = tmp()
                    fw.act(sg2[:], pg2[:], AF.Sigmoid)
                    fw.tt(sg2[:], sg2[:], pm[oo][:], ALU.mult)
                    fw.tt(BFC[:, oc, :], t1[oo][:], sg2[:], ALU.add)

            def proj_res(idl, src, first=True):
                for c in range(8):
                    wt = ws.get(idl[c])
                    for oo in range(2):
                        oc = c * 2 + oo
                        po = pbank()
                        for k in range(NK):
                            fw.mm(po[:], wt[:, k, oo * 128:(oo + 1) * 128], src[:, k, :], start=(k == 0), stop=(k == NK - 1))
                        fw.stt(F32A[:, oc, :], F32A[:, oc, :], ALPHA, po[:], ALU.mult, ALU.add)

            proj_res(id_o, BFC)
            layer_norm("ln1_g", "ln1_b")
            for c in range(8):
                wt = ws.get(id_q[c])
                for oo in range(2):
                    oc = c * 2 + oo
                    pq = pbank()
                    for k in range(NK):
                        fw.mm(pq[:], wt[:, k, oo * 128:(oo + 1) * 128], BFA[:, k, :], start=(k == 0), stop=(k == NK - 1))
                    fw.copy(BFB[:, oc, :], pq[:], e="act")
            inv = 1.0 / math.sqrt(512.0)
            for hd in range(4):
                eb = 0
                pq2 = pbank()
                for kc in range(4):
                    sqq = tmp()
                    sqv = V(sqq[:].ap.bitcast(BF16)[:, 0:512], sqq.buf)
                    fw.act(sqv, BFB[:, hd * 4 + kc, :], AF.Square)
                    fw.mm(pq2[:], onesb[:], sqv, start=(kc == 0), stop=(kc == 3))
                cst = ded["cst"]
                fw.act(cst[:], pq2[:], AF.Sqrt, scale=ksc[:, hd:hd + 1])
                for mc in range(2):
                    ps_ = pbank()
                    for kc in range(4):
                        fw.mm(ps_[:], Kt[:, hd * 4 + kc, mc * 128:(mc + 1) * 128], BFB[:, hd * 4 + kc, :], start=(kc == 0), stop=(kc == 3))
                    ein = tmp()
                    fw.stt(ein[:], ps_[:], inv, cst[:], ALU.mult, ALU.subtract)
                    fw.act(expS[eb][:, mc, :], ein[:], AF.Exp)
                psum_ = pbank()
                for mc in range(2):
                    fw.mm(psum_[:], onesb[:], expS[eb][:, mc, :], start=(mc == 0), stop=(mc == 1))
                rs = ded["rs"]
                r_, w_ = fw._bufs([psum_[:]]), fw._bufs([rs[:]])
                fw._deps("dve", r_, w_)
                ins = nc.vector.reciprocal(rs[:].ap, psum_[:].ap)
                fw._done(ins, "dve", 1, r_, w_)
                for dc in range(4):
                    po = pbank()
                    col = hd * 512 + dc * 128
                    for mc in range(2):
                        fw.mm(po[:], Vt[:, mc, col:col + 128], expS[eb][:, mc, :], start=(mc == 0), stop=(mc == 1))
                    fw.tt(BFC[:, hd * 4 + dc, :], po[:], rs[:], ALU.mult)
            proj_res(id_co, BFC)
            layer_norm("ln2_g", "ln2_b")
            for hf in range(4):
                for c in range(8):
                    wt = ws.get(id_up[hf][c])
                    for oo in range(2):
                        oc = c * 2 + oo
                        ph = pbank()
                        for k in range(NK):
                            fw.mm(ph[:], wt[:, k, oo * 128:(oo + 1) * 128], BFA[:, k, :], start=(k == 0), stop=(k == NK - 1))
                        rl = tmp()
                        fw.act(rl[:], ph[:], AF.Relu)
                        fw.tt(HM[:, oc, :], rl[:], rl[:], ALU.mult)
                for c in range(8):
                    wt = ws.get(id_dn[hf][c])
                    for oo in range(2):
                        oc = c * 2 + oo
                        po = pbank()
                        for k in range(16):
                            fw.mm(po[:], wt[:, k, oo * 128:(oo + 1) * 128], HM[:, k, :], start=(k == 0), stop=(k == 15))
                        if hf == 0:
                            fw.stt(F32A[:, oc, :], F32A[:, oc, :], ALPHA, po[:], ALU.mult, ALU.add)
                        else:
                            fw.tt(F32A[:, oc, :], F32A[:, oc, :], po[:], ALU.add)
            layer_norm("ln3_g", "ln3_b")
            for j in range(4):
                ob_ = xs[j % 2]
                for kq in range(4):
                    pt = pbank()
                    for k4 in range(4):
                        k = kq * 4 + k4
                        fw.tr(pt[:, k4 * 128:(k4 + 1) * 128], F32A[:, k, j * 128:(j + 1) * 128], ident[:])
                    fw.copy(ob_[:, kq * 512:(kq + 1) * 512], pt[:], e=("act" if kq % 2 else "dve"))
                fw.dma(V(y_out.ap()[t0 + j * 128:t0 + (j + 1) * 128, :], None), ob_[:], xs_sem[j % 2])
        fw.barrier()
        st.close()

    if 6 in phases:
        phase0(lite=True)
        fw.barrier()
        rwkv_phase(0, so=True)
        mamba_phase(0, so=True)
    if 0 in phases:
        phase0()
    fw.barrier()
    if 1 in phases:
        rwkv_phase(0)
    if 3 in phases:
        mamba_phase(0)
    if 2 in phases:
        rwkv_phase(1)
    if 4 in phases:
        mamba_phase(1)
    if 5 in phases:
        phase3()
    fw.barrier()
    es.close()
    return nc, fw


def host_params(inp, swap=False):
    g = lambda k: np.asarray(inp[k])[0]
    pvec = np.zeros((128, NPAR), np.float32)

    def put(name, vec, j0=0):
        vec = np.asarray(vec, np.float32)
        n = vec.shape[0]
        nch = (n + 127) // 128
        pad = np.zeros(nch * 128, np.float32)
        pad[:n] = vec
        pvec[:, POFF[name] + j0:POFF[name] + j0 + nch] = pad.reshape(nch, 128).T

    mup, mun = g("rw_mu_prev"), g("rw_mu_next")
    if swap:
        mup, mun = mun, mup
    for nm, mu in (("mup", mup), ("mun", mun)):
        put(nm, mu[0:3072], 0)
        put(nm, mu[3072:3168], 24)
        put(nm, mu[3168:3264], 25)
        put(nm, mu[3264:3520], 26)
    dirs = (1, 0) if swap else (0, 1)
    for d in range(2):
        put("w0", g("rw_w0")[dirs[d]], 8 * d)
        put("a0", g("rw_a0")[dirs[d]], 8 * d)
    put("k_k", g("rw_k_k"))
    put("k_a", g("rw_k_a"))
    put("r_k", g("rw_r_k").reshape(-1))
    put("gn_g", g("rw_gn_g"))
    put("gn_b", g("rw_gn_b"))
    cw = g("m_conv_w")
    if swap:
        cw = cw[::-1]
    for j in range(5):
        put("conv_w", cw[j], 32 * j)
    put("conv_b", g("m_conv_b"))
    put("m_norm_g", g("m_norm_g"))
    put("m_d", np.repeat(g("m_d"), 64))
    for nm in ("ln1_g", "ln1_b", "ln2_g", "ln2_b", "ln3_g", "ln3_b"):
        put(nm, g(nm))
    dtp = np.zeros((128, 2, 2, 4, 32), np.float32)
    for d in range(2):
        dtp[:, 0, d] = g("m_dt_bias")[dirs[d]][None, None, :]
        dtp[:, 1, d] = g("m_a_log")[dirs[d]][None, None, :]
    blk = np.zeros((128, 128), np.float32)
    blk[:64, :64] = 1.0
    blk[64:, 64:] = 1.0
    rwc = np.zeros((128, 2, 512), np.float32)
    idx = np.arange(128)
    hh, ss = idx // 64, idx % 64
    same = hh[:, None] == hh[None, :]
    lt = ss[:, None] < ss[None, :]
    le = ss[:, None] <= ss[None, :]
    rwc[:, 0, 0:128] = same & lt
    rwc[:, 0, 128:256] = same & le
    rwc[:, 0, 256:384] = same & lt.T
    rwc[:, 1, 0:128] = same & lt.T
    rwc[:, 1, 128:256] = same & le.T
    rwc[:, 1, 256:384] = same & lt
    mc = np.zeros((128, 2, 2, 128), np.float32)
    i128 = np.arange(128)
    mc[:, 0, 0] = i128[:, None] <= i128[None, :]
    mc[:, 0, 1] = i128[:, None] > i128[None, :]
    mc[:, 1, 0] = i128[:, None] >= i128[None, :]
    mc[:, 1, 1] = i128[:, None] < i128[None, :]
    rmask = np.ones((128, 1024), np.float32)
    rmask[:, ::64] = 0.0
    return dict(pvec=pvec, dtp=dtp, ident=np.eye(128, dtype=np.float32), blk64=blk, rwc=rwc, rmask=rmask,
                mc=mc,
                rw_w2=g("rw_w2"), rw_a2=g("rw_a2"), rw_g2=g("rw_g2"), w_in=g("w_in"),
                w_br=g("w_br"), w_bm=g("w_bm"), w_o=g("w_o"), w_q=g("w_q"), w_kv=g("w_kv"), w_co=g("w_co"),
                w_up=g("w_up"), w_down=g("w_down"))


T_CORE = 8192
_CACHE = {}


def _x_ext(xseq, start, T_loc, rev):
    NT0 = (T_loc + TV - 1) // TV
    TP = NT0 * TV
    L = xseq.shape[0]
    out = np.zeros((TP + 4, D), np.float32)
    if not rev:
        lo, hi = start - 2, start + T_loc + 2
        slo, shi = max(lo, 0), min(hi, L)
        out[slo - lo:shi - lo] = xseq[slo:shi]
    else:
        lo, hi = start - 2, start + T_loc + 2
        slo, shi = max(lo, 0), min(hi, L)
        seg = xseq[slo:shi][::-1]
        r0 = start + T_loc + 1 - (shi - 1)
        out[r0:r0 + seg.shape[0]] = seg
    return out


def kernel(**inputs):
    inp = {k: np.asarray(v) for k, v in inputs.items()}
    xp, xs_, mp, ms = inp["x_prompt"], inp["x_sample"], inp["mem_prompt"], inp["mem_sample"]
    T_loc = T_CORE
    if "nc" not in _CACHE:
        _CACHE["nc"] = build(T_loc)[0]
    nc = _CACHE["nc"]
    hp = [host_params(inp, swap=False), host_params(inp, swap=True)]
    cores = []
    for c in range(8):
        if c < 4:
            s, half = c // 2, c % 2
            cores.append(dict(x=xp[s], start=half * T_loc, rev=(half == 1), mem=mp[s]))
        else:
            cores.append(dict(x=xs_[c - 4], start=0, rev=False, mem=ms[c - 4]))
    base_maps = []
    xe = [_x_ext(cd["x"], cd["start"], T_loc, cd["rev"]) for cd in cores]
    for c, cd in enumerate(cores):
        own = hp[1 if cd["rev"] else 0]
        m = dict(own)
        m["x_ext"] = xe[c]
        m["mem"] = np.ascontiguousarray(cd["mem"], dtype=np.float32)
        if c < 4:
            partner = c ^ 1
            oth = hp[1 if cores[partner]["rev"] else 0]
            m["x_ext2"] = xe[partner]
            m["pvec2"] = oth["pvec"]
            m["dtp2"] = oth["dtp"]
            m["sel"] = np.ones((128, 1), np.float32)
        else:
            m["x_ext2"] = xe[c]
            m["pvec2"] = own["pvec"]
            m["dtp2"] = own["dtp"]
            m["sel"] = np.zeros((128, 1), np.float32)
        base_maps.append(m)
    res2 = run_bass_kernel_spmd(nc, base_maps, core_ids=list(range(8)))
    ys = [np.asarray(res2.results[c]["y_out"], np.float32) for c in range(8)]
    y_prompt = np.empty_like(xp)
    for c in range(4):
        s, half = c // 2, c % 2
        y_prompt[s, half * T_loc:(half + 1) * T_loc] = ys[c][::-1] if half == 1 else ys[c]
    y_sample = np.stack(ys[4:8], axis=0)
    return (y_prompt, y_sample)
```

```python
import math
from contextlib import ExitStack
import numpy as np
import concourse.bass as bass
import concourse.mybir as mybir
from concourse.bass_utils import run_bass_kernel_spmd

F32 = mybir.dt.float32
BF16 = mybir.dt.bfloat16
AF = mybir.ActivationFunctionType
ALU = mybir.AluOpType
AX = mybir.AxisListType

SAME_ENGINE_SYNC = True
MB_STOP = 99
DBG_NOWLOAD = False
DBG_NOSTORE = False
MB_SUB = 99

D = 2048
NK = 16
TT = 512
TV = 508
IN_COLS = 13792
C_R, C_K, C_V, C_DW, C_DA, C_DG = 0, 1024, 2048, 3072, 3168, 3264
C_Z, C_XBC, C_DT, C_GATES = 3520, 5568, 9664, 9696


class Buf:
    __slots__ = ("w", "r", "name", "ex")

    def __init__(self, name=""):
        self.w = None
        self.r = []
        self.name = name
        self.ex = False


class V:
    __slots__ = ("ap", "buf")

    def __init__(self, ap, buf):
        self.ap = ap
        self.buf = buf

    def __getitem__(self, idx):
        return V(self.ap[idx], self.buf)

    def rr(self, pat, **kw):
        return V(self.ap.rearrange(pat, **kw), self.buf)

    def bc(self, shape):
        return V(self.ap.to_broadcast(list(shape)), self.buf)

    def us(self, axis):
        return V(self.ap.unsqueeze(axis), self.buf)


class T:
    def __init__(self, h, name="", track=True):
        self.h = h
        self.buf = Buf(name) if track else None

    def __getitem__(self, idx):
        return V(self.h[idx], self.buf)

    def v(self, ap):
        return V(ap, self.buf)


class FW:
    def __init__(self, nc):
        self.nc = nc
        self.eng = {"pe": nc.tensor, "dve": nc.vector, "act": nc.scalar, "pool": nc.gpsimd, "sp": nc.sync}
        self.sem = {}
        self.cnt = {}
        for e in self.eng:
            self.sem[e] = nc.alloc_semaphore("sem_" + e)
            self.cnt[e] = 0
        self.seen = {e: {} for e in self.eng}
        self.n_inst = 0
        self.n_wait = 0

    def dsem(self, name):
        if name in self.sem:
            return name
        self.sem[name] = self.nc.alloc_semaphore("dsem_" + name)
        self.cnt[name] = 0
        return name

    def scan(self, out, d0, d1, initial, op0, op1):
        r, w = self._bufs([d0, d1, initial]), self._bufs([out])
        self._deps("dve", r, w)
        i = self.nc.vector.tensor_tensor_scan(out.ap, d0.ap, d1.ap, self._ap(initial), op0, op1)
        return self._done(i, "dve", 1, r, w)

    def _wait(self, e, dep):
        if dep is None:
            return
        key, val = dep
        if key == e and (e == "pe" or e == "sp" or not SAME_ENGINE_SYNC):
            return
        if self.seen[e].get(key, 0) >= val:
            return
        assert val <= self.cnt[key], "wait on a not-yet-signalled count (%s %d > %d): potential deadlock" % (key, val, self.cnt[key])
        self.seen[e][key] = val
        self.eng[e].wait_ge(self.sem[key], val)
        self.n_wait += 1

    def _deps(self, e, reads, writes):
        mx = {}
        for b in reads:
            if b.w is not None and mx.get(b.w[0], 0) < b.w[1]:
                mx[b.w[0]] = b.w[1]
            if b.ex:
                for k, v in b.r:
                    if k != e and mx.get(k, 0) < v:
                        mx[k] = v
        for b in writes:
            if b.w is not None and mx.get(b.w[0], 0) < b.w[1]:
                mx[b.w[0]] = b.w[1]
            for k, v in b.r:
                if mx.get(k, 0) < v:
                    mx[k] = v
        for k, v in mx.items():
            self._wait(e, (k, v))

    def _done(self, inst, key, inc, reads, writes, signal=True):
        if signal:
            self.cnt[key] += inc
            inst.then_inc(self.sem[key], inc)
            dep = (key, self.cnt[key])
        else:
            dep = (key, self.cnt[key] + inc)
        for b in reads:
            b.r.append(dep)
            if len(b.r) > 24:
                mx = {}
                for k, v in b.r:
                    if mx.get(k, 0) < v:
                        mx[k] = v
                b.r = list(mx.items())
        for b in writes:
            b.w = dep
            b.r = []
        self.n_inst += 1
        return dep

    @staticmethod
    def _bufs(vs):
        out = []
        for v in vs:
            if isinstance(v, V) and v.buf is not None and v.buf not in out:
                out.append(v.buf)
        return out

    @staticmethod
    def _ap(v):
        return v.ap if isinstance(v, V) else v

    def mm(self, out, lhsT, rhs, start=True, stop=True, sig=False, **kw):
        r, w = self._bufs([lhsT, rhs]), self._bufs([out])
        self._deps("pe", r, w)
        i = self.nc.tensor.matmul(out.ap, lhsT.ap, rhs.ap, start=start, stop=stop, **kw)
        return self._done(i, "pe", 1, r, w, signal=(stop or sig))

    def tr(self, out, in_, ident):
        r, w = self._bufs([in_, ident]), self._bufs([out])
        self._deps("pe", r, w)
        i = self.nc.tensor.transpose(out.ap, in_.ap, ident.ap)
        return self._done(i, "pe", 1, r, w)

    def act(self, out, in_, func, bias=None, scale=None, e="act", accum_out=None):
        r, w = self._bufs([in_, bias, scale]), self._bufs([out, accum_out])
        self._deps(e, r, w)
        kw = {}
        if bias is not None:
            kw["bias"] = self._ap(bias)
        if scale is not None:
            kw["scale"] = self._ap(scale)
        if accum_out is not None:
            kw["accum_out"] = self._ap(accum_out)
        i = self.eng[e].activation(out.ap, in_.ap, func, **kw)
        return self._done(i, e, 1, r, w)

    def tt(self, out, a, b, op, e="dve"):
        r, w = self._bufs([a, b]), self._bufs([out])
        self._deps(e, r, w)
        i = self.eng[e].tensor_tensor(out.ap, a.ap, b.ap, op)
        return self._done(i, e, 1, r, w)

    def ts(self, out, in0, s1, s2, op0, op1=None, e="dve"):
        r, w = self._bufs([in0, s1, s2]), self._bufs([out])
        self._deps(e, r, w)
        kw = {}
        if op1 is not None:
            kw["op1"] = op1
        i = self.eng[e].tensor_scalar(out.ap, in0.ap, self._ap(s1), self._ap(s2), op0, **kw)
        return self._done(i, e, 1, r, w)

    def stt(self, out, in0, scalar, in1, op0, op1, e="dve"):
        r, w = self._bufs([in0, scalar, in1]), self._bufs([out])
        self._deps(e, r, w)
        i = self.eng[e].scalar_tensor_tensor(out.ap, in0.ap, self._ap(scalar), in1.ap, op0, op1)
        return self._done(i, e, 1, r, w)

    def copy(self, out, in_, e="dve"):
        r, w = self._bufs([in_]), self._bufs([out])
        self._deps(e, r, w)
        if e == "act":
            i = self.nc.scalar.copy(out.ap, in_.ap)
        else:
            i = self.eng[e].tensor_copy(out.ap, in_.ap)
        return self._done(i, e, 1, r, w)

    def memset(self, out, val, e="dve"):
        w = self._bufs([out])
        self._deps(e, [], w)
        i = self.eng[e].memset(out.ap, val)
        return self._done(i, e, 1, [], w)

    def dma(self, out, in_, sem, q="sp", **kw):
        r, w = self._bufs([in_]), self._bufs([out])
        self._deps(q, r, w)
        i = self.eng[q].dma_start(out=out.ap, in_=in_.ap, **kw)
        return self._done(i, sem, 16, r, w)

    def collective(self, kind, op, groups, in_v, out_v, sem):
        r, w = self._bufs([in_v]), self._bufs([out_v])
        self._deps("pool", r, w)
        i = self.nc.gpsimd.collective_compute(kind, op=op, replica_groups=groups, ins=[in_v.ap], outs=[out_v.ap])
        return self._done(i, sem, 16, r, w)

    def barrier(self):
        for e in self.eng:
            for key, val in self.cnt.items():
                if val > 0:
                    self._wait(e, (key, val)) if key != e else None


class Ctx:
    pass


def _param_layout():
    off = {}
    n = 0
    for name, w in [("mup", 28), ("mun", 28), ("w0", 16), ("a0", 16), ("k_k", 8), ("k_a", 8), ("r_k", 8),
                    ("gn_g", 8), ("gn_b", 8), ("conv_w", 160), ("conv_b", 32), ("m_norm_g", 16), ("m_d", 16),
                    ("ln1_g", 16), ("ln1_b", 16), ("ln2_g", 16), ("ln2_b", 16), ("ln3_g", 16), ("ln3_b", 16)]:
        off[name] = n
        n += w
    return off, n


POFF, NPAR = _param_layout()
XOFF = {"c0": 0, "nk_k": 28, "omk_a": 36}
NX = 44


def build(T_loc, dbg=False, phases=(6, 0, 1, 2, 3, 4, 5)):
    NT0 = (T_loc + TV - 1) // TV
    TP = NT0 * TV
    XR = TP + 4
    nc = bass.Bass("TRN2", target_bir_lowering=False)
    fw = FW(nc)
    es = ExitStack()

    def din(name, shape, dt=F32):
        return nc.dram_tensor(name, list(shape), dt, kind="ExternalInput")

    def dscr(name, shape, dt=F32):
        return nc.dram_tensor(name, list(shape), dt, kind=("ExternalOutput" if dbg else "Internal"))

    x_ext = din("x_ext", [XR, D])
    x_ext2 = din("x_ext2", [XR, D])
    pvec2_d = din("pvec2", [128, NPAR])
    dtp2_d = din("dtp2", [128, 2, 2, 4, 32])
    sel_d = din("sel", [128, 1])
    w_in = din("w_in", [D, IN_COLS])
    pvec_d = din("pvec", [128, NPAR])
    ident_d = din("ident", [128, 128])
    blk_d = din("blk64", [128, 128])
    w2_d = din("rw_w2", [96, 1024])
    a2_d = din("rw_a2", [96, 1024])
    g2_d = din("rw_g2", [256, 1024])
    dtp_d = din("dtp", [128, 2, 2, 4, 32])

    S = {}
    for nm in ["R", "V", "A", "G", "BON", "KD1", "KD2", "B1", "B2", "LW1", "LW2"]:
        S[nm] = dscr("s_" + nm, [1024, TP])
    S["XBC"] = dscr("s_XBC", [4096, TP])
    S["DTS"] = dscr("s_DTS", [TP, 4, 32])
    S2 = {}
    for nm in ["V", "A", "KD1", "B1", "LW1"]:
        S2[nm] = dscr("s2_" + nm, [1024, TP])
    S2["XBC"] = dscr("s2_XBC", [4096, TP])
    S2["DTS"] = dscr("s2_DTS", [TP, 4, 32])
    stx_rw = nc.dram_tensor("stx_rw", [128, 8, 128], F32, kind="Internal")
    stx_m = nc.dram_tensor("stx_m", [128, 32, 64], F32, kind="Internal")

    def sb(name, shape, dt=F32):
        return T(es.enter_context(nc.sbuf_tensor("sb_" + name, list(shape), dt)), name)

    PB = [T(nc.alloc_psum_tensor("pb%d" % i, [128, 512], F32), "pb%d" % i) for i in range(8)]
    for t_ in PB:
        t_.buf.ex = True

    ident = sb("ident", [128, 128])
    blk = sb("blk", [128, 128])
    sel = sb("sel", [128, 1])
    dcp = fw.dsem("constp")
    fw.dma(ident[:], V(ident_d.ap(), None), fw.dsem("c2"))
    fw.dma(blk[:], V(blk_d.ap(), None), fw.dsem("c3"))
    fw.dma(sel[:], V(sel_d.ap(), None), fw.dsem("c12"))
    w_in_v = w_in.ap().rearrange("(k p) n -> p k n", p=128)
    CUR = {}
    PSETS = []
    for si, (pd_, dd_) in enumerate(((pvec_d, dtp_d), (pvec2_d, dtp2_d))):
        pvec_t = sb("pvec%d" % si, [128, NPAR])
        xpar_t = sb("xpar%d" % si, [128, NX])
        fw.dma(pvec_t[:], V(pd_.ap(), None), fw.dsem("c1_%d" % si))
        fw.tt(xpar_t[:, 0:28], pvec_t[:, POFF["mup"]:POFF["mup"] + 28], pvec_t[:, POFF["mun"]:POFF["mun"] + 28], ALU.add)
        fw.ts(xpar_t[:, 0:28], xpar_t[:, 0:28], -1.0, 1.0, ALU.mult, ALU.add)
        fw.ts(xpar_t[:, 28:36], pvec_t[:, POFF["k_k"]:POFF["k_k"] + 8], -1.0, None, ALU.mult)
        fw.ts(xpar_t[:, 36:44], pvec_t[:, POFF["k_a"]:POFF["k_a"] + 8], -1.0, 1.0, ALU.mult, ALU.add)
        PSETS.append(dict(pvec=pvec_t, xpar=xpar_t, dtp_d=dd_))
    PSETS[0].update(x=x_ext, S=S)
    PSETS[1].update(x=x_ext2, S=S2)
    CUR.update(PSETS[0])

    def pv(name, j=0, n=128):
        c = POFF[name] + j
        return CUR["pvec"][0:n, c:c + 1]

    def xp(name, j=0, n=128):
        c = XOFF[name] + j
        return CUR["xpar"][0:n, c:c + 1]

    def phase0(lite=False):
        st = ExitStack()
        CUR.update(PSETS[1 if lite else 0])
        x_src = CUR["x"]
        Sd = CUR["S"]
        nd = 1 if lite else 2

        def sb0(name, shape, dt=F32):
            return T(st.enter_context(nc.sbuf_tensor(("p0l_" if lite else "p0_") + name, list(shape), dt)), name)

        w2 = sb0("w2", [96, 1024], BF16)
        a2 = sb0("a2", [96, 1024], BF16)
        g2 = sb0("g2", [128, 2, 1024], BF16)
        wdt = sb0("wdt", [128, NK, 32], BF16)
        dtp = sb0("dtp", [128, 2, 2, 4, 32])
        An = sb0("An", [128, 2, 4, 32])
        fw.dma(dtp[:], V(CUR["dtp_d"].ap(), None), fw.dsem("c4"))
        fw.dma(w2[:], V(w2_d.ap(), None), fw.dsem("c5"), q="pool")
        fw.dma(a2[:], V(a2_d.ap(), None), fw.dsem("c6"), q="pool")
        fw.dma(g2[:], V(g2_d.ap().rearrange("(k p) n -> p k n", p=128), None), fw.dsem("c7"), q="pool")
        fw.dma(wdt[:], V(w_in_v[:, :, C_DT:C_DT + 32], None), dcp, q="pool")
        fw.act(An[:], dtp[:, 1], AF.Exp)
        fw.ts(An[:], An[:], -1.0, None, ALU.mult)
        xs = [sb0("xs%d" % j, [128, D]) for j in range(2)]
        xs_sem = [fw.dsem("xs%d" % j) for j in range(2)]
        xT = sb0("xT", [128, NK, TT], BF16)
        WG = [sb0("wg%d" % j, [128, NK, 512], BF16) for j in range(2)]
        wg_sem = [fw.dsem("wg%d" % j) for j in range(2)]
        ag = sb0("ag", [128, 8, 2, TV])
        kp = sb0("kp", [128, 8, TV])
        rk = sb0("rk", [128, 8, TV])
        tdw = sb0("tdw", [96, TV], BF16)
        tda = sb0("tda", [96, TV], BF16)
        tdg = sb0("tdg", [128, 2, TV], BF16)
        NS = 8
        ost = [sb0("ost%d" % j, [128, TV]) for j in range(NS)]
        ost_sem = [fw.dsem("ost%d" % j) for j in range(NS)]
        tmp = [sb0("tmp%d" % j, [128, TV]) for j in range(4)]
        dts = sb0("dts", [128, 4, 4, 32])
        dtt = sb0("dtt", [128, 4, 32])
        dts_sem = fw.dsem("dts")
        cnt = {"ost": 0, "tmp": 0, "wg": 0, "pb": 0, "aux": 0}

        def new_ost():
            j = cnt["ost"] % NS
            cnt["ost"] += 1
            return ost[j], ost_sem[j]

        def new_tmp():
            j = cnt["tmp"] % 4
            cnt["tmp"] += 1
            return tmp[j]

        def new_pb():
            j = cnt["pb"] % 4
            cnt["pb"] += 1
            return PB[j]

        def new_aux():
            j = cnt["aux"] % 2
            cnt["aux"] += 1
            return PB[6 + j]

        def store(name, row0, nrow, i, o, osem):
            if DBG_NOSTORE and cnt["ost"] > 8:
                return
            fw.dma(V(Sd[name].ap()[row0:row0 + nrow, i * TV:(i + 1) * TV], None), o[0:nrow, :], osem)

        def load_wg(col0, n):
            j = cnt["wg"] % 2
            cnt["wg"] += 1
            if not (DBG_NOWLOAD and cnt["wg"] > 2):
                fw.dma(WG[j][:, :, 0:n], V(w_in_v[:, :, col0:col0 + n], None), wg_sem[j], q="pool")
            return WG[j]

        def proj(P, wt, c0, ncol):
            for k in range(NK):
                fw.mm(P[0:ncol, :], wt[:, k, c0:c0 + ncol], xT[:, k, :], start=(k == 0), stop=(k == NK - 1))

        def shift(P, pc, nrow, out):
            fw.act(out, P[0:nrow, 2:2 + TV], AF.Copy, scale=xp("c0", pc, nrow))
            fw.stt(out, P[0:nrow, 1:1 + TV], pv("mup", pc, nrow), out, ALU.mult, ALU.add)
            fw.stt(out, P[0:nrow, 3:3 + TV], pv("mun", pc, nrow), out, ALU.mult, ALU.add)

        for i in range(NT0):
            for j in range(4):
                xb = xs[j % 2]
                fw.dma(xb[:], V(x_src.ap()[i * TV + j * 128:i * TV + (j + 1) * 128, :], None), xs_sem[j % 2])
                for kq in range(4):
                    pt = PB[4 + (kq % 2)]
                    for k4 in range(4):
                        k = kq * 4 + k4
                        fw.tr(pt[:, k4 * 128:(k4 + 1) * 128], xb[:, k * 128:(k + 1) * 128], ident[:])
                    fw.copy(xT[:, kq * 4:(kq + 1) * 4, j * 128:(j + 1) * 128],
                            pt.v(pt.h[:].rearrange("p (a b) -> p a b", a=4)), e=("act" if kq % 2 else "dve"))
            pd = new_aux()
            pdv = pd.v(pd.h[:, 0:128].rearrange("p (a b) -> p a b", a=4))
            for j in range(4):
                for k in range(NK):
                    fw.mm(pdv[:, j, :], xT[:, k, j * 128:(j + 1) * 128], wdt[:, k, :], start=(k == 0), stop=(k == NK - 1))
            for d in range(2):
                fw.tt(dtt[:], pdv, dtp[:, 0, d], ALU.add)
                fw.act(dtt[:], dtt[:], AF.Exp)
                fw.act(dts[:, :, d, :], dtt[:], AF.Ln, bias=1.0)
                fw.tt(dts[:, :, 2 + d, :], dts[:, :, d, :], An[:, d], ALU.mult)
            for j in range(4):
                lo, hi = max(2, 128 * j), min(2 + TV, 128 * j + 128)
                fw.dma(V(Sd["DTS"].ap()[i * TV + lo - 2:i * TV + hi - 2], None), dts[lo - 128 * j:hi - 128 * j, j], dts_sem)
            wt = load_wg(C_DW, 448)
            P = new_pb()
            proj(P, wt, 0, 96)
            t = new_tmp()
            shift(P, 24, 96, t[0:96, :])
            fw.act(tdw[:], t[0:96, :], AF.Tanh)
            P = new_pb()
            proj(P, wt, 96, 96)
            t = new_tmp()
            shift(P, 25, 96, t[0:96, :])
            fw.copy(tda[:], t[0:96, :], e="pool")
            for c in range(0 if lite else 2):
                P = new_pb()
                proj(P, wt, 192 + 128 * c, 128)
                t = new_tmp()
                shift(P, 26 + c, 128, t[:])
                fw.act(tdg[:, c, :], t[:], AF.Sigmoid)
            for c in range(8):
                P = new_aux()
                fw.mm(P[:, 0:TV], w2[:, c * 128:(c + 1) * 128], tdw[:])
                for d in range(nd):
                    t = new_tmp()
                    fw.act(t[:], P[:, 0:TV], AF.Sigmoid, bias=pv("w0", d * 8 + c))
                    o, osem = new_ost()
                    fw.ts(o[:], t[:], -math.exp(-0.5), None, ALU.mult, e="pool")
                    store("LW%d" % (d + 1), c * 128, 128, i, o, osem)
                P = new_aux()
                fw.mm(P[:, 0:TV], a2[:, c * 128:(c + 1) * 128], tda[:])
                for d in range(nd):
                    fw.act(ag[:, c, d, :], P[:, 0:TV], AF.Sigmoid, bias=pv("a0", d * 8 + c))
                if not lite:
                    P = new_aux()
                    for k in range(2):
                        fw.mm(P[:, 0:TV], g2[:, k, c * 128:(c + 1) * 128], tdg[:, k, :], start=(k == 0), stop=(k == 1))
                    o, osem = new_ost()
                    fw.copy(o[:], P[:, 0:TV], e="dve")
                    store("G", c * 128, 128, i, o, osem)
            for gi in range(2):
                wt = load_wg(C_K + 512 * gi, 512)
                for cc in range(4):
                    c = gi * 4 + cc
                    P = new_pb()
                    proj(P, wt, cc * 128, 128)
                    shift(P, 8 + c, 128, kp[:, c, :])
                    sq = new_tmp()
                    fw.act(sq[:], kp[:, c, :], AF.Square, scale=pv("k_k", c))
                    Pa = new_aux()
                    fw.mm(Pa[:, 0:TV], blk[:], sq[:])
                    rn = new_tmp()
                    fw.act(rn[:], Pa[:, 0:TV], AF.Ln, bias=1e-24)
                    fw.act(rn[:], rn[:], AF.Exp, scale=-0.5)
                    oa, osem = new_ost()
                    fw.stt(oa[:], kp[:, c, :], xp("nk_k", c), rn[:], ALU.mult, ALU.mult)
                    store("A", c * 128, 128, i, oa, osem)
                    for d in range(nd):
                        o, osem = new_ost()
                        fw.stt(o[:], oa[:], -1.0, ag[:, c, d, :], ALU.mult, ALU.mult)
                        store("B%d" % (d + 1), c * 128, 128, i, o, osem)
                        t = new_tmp()
                        fw.act(t[:], ag[:, c, d, :], AF.Identity, scale=pv("k_a", c), bias=xp("omk_a", c))
                        o, osem = new_ost()
                        fw.tt(o[:], t[:], kp[:, c, :], ALU.mult, e="pool")
                        store("KD%d" % (d + 1), c * 128, 128, i, o, osem)
            for gi in range(0 if lite else 2):
                wt = load_wg(C_R + 512 * gi, 512)
                for cc in range(4):
                    c = gi * 4 + cc
                    P = new_pb()
                    proj(P, wt, cc * 128, 128)
                    o, osem = new_ost()
                    shift(P, c, 128, o[:])
                    store("R", c * 128, 128, i, o, osem)
                    t = new_tmp()
                    fw.stt(t[:], o[:], pv("r_k", c), kp[:, c, :], ALU.mult, ALU.mult)
                    Pa = new_aux()
                    fw.mm(Pa[:, 0:TV], blk[:], t[:])
                    fw.copy(rk[:, c, :], Pa[:, 0:TV], e="act")
            for gi in range(2):
                wt = load_wg(C_V + 512 * gi, 512)
                for cc in range(4):
                    c = gi * 4 + cc
                    P = new_pb()
                    proj(P, wt, cc * 128, 128)
                    o, osem = new_ost()
                    shift(P, 16 + c, 128, o[:])
                    store("V", c * 128, 128, i, o, osem)
                    if not lite:
                        o2, osem2 = new_ost()
                        fw.tt(o2[:], o[:], rk[:, c, :], ALU.mult, e="pool")
                        store("BON", c * 128, 128, i, o2, osem2)
            for gi in range(6 if lite else 8):
                wt = load_wg(C_XBC + 512 * gi, 512)
                for cc in range(4):
                    c = gi * 4 + cc
                    P = new_pb()
                    proj(P, wt, cc * 128, 128)
                    t = new_tmp()
                    fw.act(t[:], P[:, 0:TV], AF.Identity, scale=pv("conv_w", c), bias=pv("conv_b", c))
                    for j in range(1, 5):
                        fw.stt(t[:], P[:, j:j + TV], pv("conv_w", 32 * j + c), t[:], ALU.mult, ALU.add)
                    o, osem = new_ost()
                    fw.act(o[:], t[:], AF.Silu)
                    store("XBC", c * 128, 128, i, o, osem)
        fw.barrier()
        st.close()


    NST = T_loc // 128
    S["YRW"] = dscr("s_YRW", [1024, TP])
    st_rw_out = nc.dram_tensor("st_rw_out", [128, 8, 128], F32, kind="ExternalOutput")
    rwc_d = din("rwc", [128, 2, 512])
    rmask_d = din("rmask", [128, 1024])
    identb = sb("identb", [128, 128], BF16)
    fw.dma(identb[:], V(ident_d.ap(), None), fw.dsem("c8"), q="pool")

    def rwkv_phase(d, so=False):
        st = ExitStack()
        Ss = S2 if so else S

        def sbp(name, shape, dt=F32):
            return T(st.enter_context(nc.sbuf_tensor(("rws_" if so else "rw%d_" % d) + name, list(shape), dt)), name)

        rwc = sbp("rwc", [128, 2, 512])
        rmask = sbp("rmask", [128, 1024])
        fw.dma(rwc[:], V(rwc_d.ap(), None), fw.dsem("c9"))
        fw.dma(rmask[:], V(rmask_d.ap(), None), fw.dsem("c10"))
        names = ["R", "KD%d" % (d + 1), "V", "A", "B%d" % (d + 1), "LW%d" % (d + 1)]
        LD = [[sbp("ld%d_%d" % (j, b), [128, 8, 128]) for j in range(6)] for b in range(2)]
        ld_sem = [[fw.dsem("rwld%d_%d" % (j, b)) for j in range(6)] for b in range(2)]
        Y1 = [sbp("y1_%d" % b, [128, 8, 128]) for b in range(2)]
        y1_sem = [fw.dsem("rwy1_%d" % b) for b in range(2)]
        YB = [sbp("yb_%d" % b, [128, 8, 128]) for b in range(2)]
        yb_sem = [fw.dsem("rwyb_%d" % b) for b in range(2)]
        pre = sbp("pre", [128, 1024])
        cl = sbp("cl", [128, 1024])
        ecl = sbp("ecl", [128, 8, 128])
        encl = sbp("encl", [128, 8, 128])
        ecx = sbp("ecx", [128, 8, 128])
        xt = [sbp("xt%d" % j, [128, 8, 128]) for j in range(4)]
        BT = [sbp("BTbd%d" % b, [128, 8, 128], BF16) for b in range(2)]
        KT = [sbp("KTbd%d" % b, [128, 8, 128], BF16) for b in range(2)]
        AR = [sbp("ARbd%d" % b, [128, 8, 256], BF16) for b in range(2)]
        VB = [sbp("Vbd%d" % b, [128, 8, 128], BF16) for b in range(2)]
        for b in range(2):
            for t_ in (BT[b], KT[b], AR[b], VB[b]):
                fw.memset(t_[:], 0.0, e="pool")
        S0q = [sbp("S0_%d" % q_, [128, 4, 128]) for q_ in range(2)]
        S0bq = [sbp("S0b_%d" % q_, [128, 4, 128], BF16) for q_ in range(2)]
        ssem = fw.dsem("rwstate")
        ssem2 = fw.dsem("rwstate2")
        for q_ in range(2):
            if d == 0:
                fw.memset(S0q[q_][:], 0.0)
            else:
                fw.dma(S0q[q_][:], V(stx_rw.ap()[:, 4 * q_:4 * q_ + 4, :], None), (ssem, ssem2)[q_])
                fw.ts(S0q[q_][:].rr("p a b -> p (a b)"), S0q[q_][:].rr("p a b -> p (a b)"), sel[:, 0:1], None, ALU.mult)
            fw.copy(S0bq[q_][:], S0q[q_][:], e="act")
        ABs = [sbp("ABs%d" % b, [128, 4, 256], BF16) for b in range(2)]
        AKs = [sbp("AKs%d" % b, [128, 4, 256], BF16) for b in range(2)]
        NTs = [[sbp("NTs%d_%d" % (b, j), [128, 4, 128], BF16) for j in range(2)] for b in range(2)]
        Ns = [[sbp("Ns%d_%d" % (b, j), [128, 4, 128], BF16) for j in range(2)] for b in range(2)]
        Ps = [[sbp("Ps%d_%d" % (b, j), [128, 4, 128], BF16) for j in range(2)] for b in range(2)]
        VTs = [sbp("VTs%d" % b, [128, 4, 128], BF16) for b in range(2)]
        GTs = [sbp("GTs%d" % b, [128, 4, 128], BF16) for b in range(2)]
        UTs = [sbp("UTs%d" % b, [128, 4, 128], BF16) for b in range(2)]
        BKT = [sbp("BKT%d" % b, [128, 4, 2, 128], BF16) for b in range(2)]
        stmp = [sbp("stmp%d" % b, [128, 4, 128]) for b in range(2)]
        pbc = {"n": 0}

        def pbank():
            j = pbc["n"] % 8
            pbc["n"] += 1
            return PB[j]

        mAB = rwc[:, d, 0:256]
        mNT = rwc[:, d, 256:384]
        tiles = list(range(NST)) if d == 0 else list(range(NST - 1, -1, -1))
        chunks = (0, 1) if d == 0 else (1, 0)

        def load_tile(n):
            ti = tiles[n]
            b = n % 2
            for j in range(6):
                if so and j == 0:
                    continue
                fw.dma(LD[b][j][:], V(Ss[names[j]].ap()[:, ti * 128:(ti + 1) * 128].rearrange("(c p) t -> p c t", p=128), None),
                       ld_sem[b][j])
            if d == 1:
                fw.dma(Y1[b][:], V(S["YRW"].ap()[:, ti * 128:(ti + 1) * 128].rearrange("(c p) t -> p c t", p=128), None),
                       y1_sem[b])

        load_tile(0)
        qn = 0
        for n in range(NST):
            ti = tiles[n]
            b = n % 2
            if n + 1 < NST:
                load_tile(n + 1)
            r_, k_, v_, a_, b_, lw_ = [LD[b][j] for j in range(6)]
            fw.scan(pre[:], rmask[:], lw_[:].rr("p c t -> p (c t)"), 0.0, ALU.mult, ALU.add)
            pre4 = pre[:].rr("p (c t) -> p c t", t=64)
            cl4 = cl[:].rr("p (c t) -> p c t", t=64)
            lw4 = lw_[:].rr("p c (u t) -> p (c u) t", t=64)
            if d == 0:
                clv = pre
            else:
                fw.tt(cl4, lw4, pre4, ALU.subtract)
                fw.tt(cl4, cl4, pre4[:, :, 63:64].bc([128, 16, 64]), ALU.add)
                clv = cl
            clf = clv[:]
            fw.act(ecl[:].rr("p c t -> p (c t)"), clf, AF.Exp)
            fw.act(encl[:].rr("p c t -> p (c t)"), clf, AF.Exp, scale=-1.0)
            fw.tt(ecx[:].rr("p c t -> p (c t)"), clf, lw_[:].rr("p c t -> p (c t)"), ALU.subtract, e="pool")
            fw.act(ecx[:].rr("p c t -> p (c t)"), ecx[:].rr("p c t -> p (c t)"), AF.Exp)
            fw.tt(xt[0][:], b_[:], encl[:], ALU.mult, e="pool")
            fw.tt(xt[1][:], k_[:], encl[:], ALU.mult)
            fw.tt(xt[2][:], a_[:], ecx[:], ALU.mult, e="pool")
            if not so:
                fw.tt(xt[3][:], r_[:], ecl[:], ALU.mult)
            for ci in chunks:
                cb = (2 * n + ci) % 2
                cs = slice(ci * 64, ci * 64 + 64)
                for hh in range(2):
                    ps = slice(64 * hh, 64 * hh + 64)
                    fs = slice(64 * hh, 64 * hh + 64)
                    fw.copy(BT[cb][ps, :, fs], xt[0][ps, :, cs], e="pool")
                    fw.copy(KT[cb][ps, :, fs], xt[1][ps, :, cs], e="dve")
                    fw.copy(AR[cb][ps, :, fs], xt[2][ps, :, cs], e="pool")
                    if not so:
                        fw.copy(AR[cb][ps, :, slice(128 + 64 * hh, 192 + 64 * hh)], xt[3][ps, :, cs], e="act")
                    fw.copy(VB[cb][ps, :, fs], v_[ps, :, cs], e="dve")
                wl = ecl[:, :, (ci * 64 + 63) if d == 0 else (ci * 64)]
                def quad_body(q, qb):
                    p0 = 4 * q
                    for half in range(2):
                        pa = pbank()
                        pk = pbank()
                        for pp in range(2):
                            p = p0 + 2 * half + pp
                            fw.mm(pa[:, pp * 256:(pp + 1) * 256], BT[cb][:, p, :], AR[cb][:, p, :])
                            fw.mm(pk[:, pp * 256:(pp + 1) * 256], KT[cb][:, p, :], AR[cb][:, p, :])
                        fw.tt(ABs[qb][:, 2 * half:2 * half + 2, :], pa[:].rr("p (a b) -> p a b", a=2),
                              mAB.us(1).bc([128, 2, 256]), ALU.mult)
                        fw.tt(AKs[qb][:, 2 * half:2 * half + 2, :], pk[:].rr("p (a b) -> p a b", a=2),
                              mAB.us(1).bc([128, 2, 256]), ALU.mult)
                    pn = pbank()
                    for pp in range(4):
                        p = p0 + pp
                        fw.mm(pn[:, pp * 128:(pp + 1) * 128], AR[cb][:, p, 0:128], BT[cb][:, p, :])
                    fw.tt(NTs[qb][0][:], pn[:].rr("p (a b) -> p a b", a=4), mNT.us(1).bc([128, 4, 128]), ALU.mult)
                    Ncur = ABs[qb][:, :, 0:128]
                    NTcur = NTs[qb][0][:]
                    fw.tt(Ps[qb][0][:], Ncur, identb[:].us(1).bc([128, 4, 128]), ALU.add, e="pool")
                    Pcur = Ps[qb][0][:]
                    yield
                    for lev in range(1, 6):
                        j = lev % 2
                        pnt = pbank()
                        for pp in range(4):
                            fw.mm(pnt[:, pp * 128:(pp + 1) * 128], Ncur[:, pp, :], NTcur[:, pp, :])
                        fw.copy(NTs[qb][j][:], pnt[:].rr("p (a b) -> p a b", a=4), e="act")
                        yield
                        if lev <= 4:
                            pnn = pbank()
                            for pp in range(4):
                                fw.mm(pnn[:, pp * 128:(pp + 1) * 128], NTcur[:, pp, :], Ncur[:, pp, :])
                            fw.copy(Ns[qb][j][:], pnn[:].rr("p (a b) -> p a b", a=4), e="dve")
                            Nnext = Ns[qb][j][:]
                        NTnext = NTs[qb][j][:]
                        pp_ = pbank()
                        for pp in range(4):
                            fw.mm(pp_[:, pp * 128:(pp + 1) * 128], identb[:], Pcur[:, pp, :], start=True, stop=False)
                            fw.mm(pp_[:, pp * 128:(pp + 1) * 128], NTnext[:, pp, :], Pcur[:, pp, :], start=False, stop=True)
                        fw.copy(Ps[qb][j][:], pp_[:].rr("p (a b) -> p a b", a=4), e="dve")
                        Pcur = Ps[qb][j][:]
                        NTcur = NTnext
                        if lev <= 4:
                            Ncur = Nnext
                    Minv = Pcur
                    yield
                    pv_ = pbank()
                    pvb = pv_.v(pv_.h[:].bitcast(BF16)[:, 0:512].rearrange("p (a b) -> p a b", a=4))
                    for pp in range(4):
                        fw.tr(pvb[:, pp, :], VB[cb][:, p0 + pp, :], identb[:])
                    fw.copy(VTs[qb][:], pvb, e="act")
                    yield
                    pg = pbank()
                    for pp in range(4):
                        p = p0 + pp
                        fw.mm(pg[:, pp * 128:(pp + 1) * 128], AR[cb][:, p, 0:128], S0bq[q][:, pp, :], start=True, stop=False)
                        fw.mm(pg[:, pp * 128:(pp + 1) * 128], AKs[qb][:, pp, 0:128], VTs[qb][:, pp, :], start=False, stop=True)
                    fw.copy(GTs[qb][:], pg[:].rr("p (a b) -> p a b", a=4), e="dve")
                    yield
                    pu = pbank()
                    for pp in range(4):
                        fw.mm(pu[:, pp * 128:(pp + 1) * 128], Minv[:, pp, :], GTs[qb][:, pp, :])
                    fw.copy(UTs[qb][:], pu[:].rr("p (a b) -> p a b", a=4), e="act")
                    if not so:
                        yield
                        py = pbank()
                        for pp in range(4):
                            p = p0 + pp
                            o = py[:, pp * 128:(pp + 1) * 128]
                            fw.mm(o, S0bq[q][:, pp, :], AR[cb][:, p, 128:256], start=True, stop=False)
                            fw.mm(o, UTs[qb][:, pp, :], ABs[qb][:, pp, 128:256], start=False, stop=False)
                            fw.mm(o, VTs[qb][:, pp, :], AKs[qb][:, pp, 128:256], start=False, stop=True)
                        py4 = py[:].rr("p (a b) -> p a b", a=4)
                        if d == 0:
                            fw.copy(YB[b][0:64, p0:p0 + 4, cs], py4[0:64, :, 0:64], e="act")
                            fw.copy(YB[b][64:128, p0:p0 + 4, cs], py4[64:128, :, 64:128], e="dve")
                        else:
                            fw.tt(YB[b][0:64, p0:p0 + 4, cs], py4[0:64, :, 0:64], Y1[b][0:64, p0:p0 + 4, cs], ALU.add)
                            fw.tt(YB[b][64:128, p0:p0 + 4, cs], py4[64:128, :, 64:128], Y1[b][64:128, p0:p0 + 4, cs], ALU.add)
                    yield
                    pt_ = pbank()
                    ptb = pt_.v(pt_.h[:].bitcast(BF16).rearrange("p (a c b) -> p a c b", a=4, c=2))
                    for pp in range(4):
                        p = p0 + pp
                        fw.tr(ptb[:, pp, 0, :], BT[cb][:, p, :], identb[:])
                        fw.tr(ptb[:, pp, 1, :], KT[cb][:, p, :], identb[:])
                    fw.copy(BKT[qb][:], ptb, e="act")
                    pd_ = pbank()
                    for pp in range(4):
                        o = pd_[:, pp * 128:(pp + 1) * 128]
                        fw.mm(o, BKT[qb][:, pp, 0, :], UTs[qb][:, pp, :], start=True, stop=False)
                        fw.mm(o, BKT[qb][:, pp, 1, :], VTs[qb][:, pp, :], start=False, stop=True)
                    wlb = wl[:, p0:p0 + 4].us(2).bc([128, 4, 128])
                    fw.tt(stmp[qb][:], S0q[q][:], wlb, ALU.mult, e="pool")
                    fw.tt(S0q[q][:], pd_[:].rr("p (a b) -> p a b", a=4), wlb, ALU.mult)
                    fw.tt(S0q[q][:], S0q[q][:], stmp[qb][:], ALU.add)
                    fw.copy(S0bq[q][:], S0q[q][:], e="act")
                    yield
                gens = [quad_body(0, 0), quad_body(1, 1)]
                alive = list(gens)
                while alive:
                    for g_ in list(alive):
                        try:
                            next(g_)
                        except StopIteration:
                            alive.remove(g_)
            if not so:
                fw.dma(V(S["YRW"].ap()[:, ti * 128:(ti + 1) * 128].rearrange("(c p) t -> p c t", p=128), None), YB[b][:], yb_sem[b])
        for q_ in range(2):
            dst = stx_rw if so else st_rw_out
            if so or d == 0:
                fw.dma(V(dst.ap()[:, 4 * q_:4 * q_ + 4, :], None), S0q[q_][:], (ssem, ssem2)[q_])
        fw.barrier()
        st.close()

    S["YM"] = dscr("s_YM", [2048, TP])
    st_m_out = nc.dram_tensor("st_m_out", [128, 32, 64], F32, kind="ExternalOutput")
    mc_d = din("mc", [128, 2, 2, 128])
    ones = sb("ones", [128, 128])
    fw.memset(ones[:], 1.0)

    def mamba_phase(d, so=False):
        st = ExitStack()
        Ss = S2 if so else S

        def sbp(name, shape, dt=F32):
            return T(st.enter_context(nc.sbuf_tensor(("mbs_" if so else "mb%d_" % d) + name, list(shape), dt)), name)

        mcst = sbp("mcst", [128, 2, 2, 128])
        fw.dma(mcst[:], V(mc_d.ap(), None), fw.dsem("c11"))
        XS = [sbp("xs%d" % b, [128, 16, 128]) for b in range(2)]
        Bb = [sbp("bb%d" % b, [128, 8, 128], BF16) for b in range(2)]
        Cb = [sbp("cb%d" % b, [128, 8, 128], BF16) for b in range(2)]
        DT = [sbp("dt%d" % b, [128, 4, 32]) for b in range(2)]
        Y1 = [sbp("y1_%d" % b, [128, 16, 128]) for b in range(2)]
        YB = [sbp("yb_%d" % b, [128, 16, 128]) for b in range(2)]
        sems = [[fw.dsem("mbld%d_%d" % (j, b)) for j in range(4)] for b in range(2)]
        psems = [[fw.dsem("mbldp%d_%d" % (j, b)) for j in range(2)] for b in range(2)]
        yb_sem = [fw.dsem("mbyb_%d" % b) for b in range(2)]
        dAexp = sbp("dAexp", [128, 32, 128])
        cs_tok = sbp("cs_tok", [128, 32])
        csl = sbp("csl", [128, 32])
        ecl_last = sbp("ecl_last", [128, 32])
        decs = sbp("decs", [128, 32])
        E = [sbp("E%d" % b, [128, 8, 128]) for b in range(2)]
        ecsR = [sbp("ecsR%d" % b, [128, 8, 128]) for b in range(2)]
        MT = sbp("MT", [128, 32, 128], BF16)
        Csc = sbp("Csc", [128, 32, 128], BF16)
        CBm = sbp("CBm", [128, 8, 128])
        xdt = sbp("xdt", [128, 32, 64], BF16)
        xdd = sbp("xdd", [128, 32, 64], BF16)
        Btok = sbp("Btok", [128, 8, 128], BF16)
        hS = sbp("hS", [128, 32, 64])
        hb = sbp("hb", [128, 32, 64], BF16)
        htmp = sbp("htmp", [128, 32, 64])
        ssem = fw.dsem("mbstate")
        if d == 0:
            fw.memset(hS[:], 0.0)
        else:
            fw.dma(hS[:], V(stx_m.ap(), None), ssem)
            fw.ts(hS[:].rr("p a b -> p (a b)"), hS[:].rr("p a b -> p (a b)"), sel[:, 0:1], None, ALU.mult)
        fw.copy(hb[:], hS[:], e="act")
        tri = mcst[:, d, 0, :]
        lst = mcst[:, d, 1, :]
        t_last = 127 if d == 0 else 0
        NCH = T_loc // 128
        tiles = list(range(NCH)) if d == 0 else list(range(NCH - 1, -1, -1))
        pbc = {"n": 0}

        def pbank():
            j = pbc["n"] % 8
            pbc["n"] += 1
            return PB[j]

        def load_tile(n):
            ti = tiles[n]
            b = n % 2
            cs_ = slice(ti * 128, (ti + 1) * 128)
            xb = Ss["XBC"].ap()
            fw.dma(XS[b][:], V(xb[0:2048, cs_].rearrange("(c p) t -> p c t", p=128), None), sems[b][0])
            fw.dma(DT[b][:], V(Ss["DTS"].ap()[cs_], None), sems[b][1])
            fw.dma(Bb[b][:], V(xb[2048:3072, cs_].rearrange("(c p) t -> p c t", p=128), None), psems[b][0], q="pool")
            if not so:
                fw.dma(Cb[b][:], V(xb[3072:4096, cs_].rearrange("(c p) t -> p c t", p=128), None), psems[b][1], q="pool")
            if d == 1:
                fw.dma(Y1[b][:], V(S["YM"].ap()[:, cs_].rearrange("(c p) t -> p c t", p=128), None), sems[b][2])

        load_tile(0)
        for n in range(NCH):
            ti = tiles[n]
            b = n % 2
            if n + 1 < NCH:
                load_tile(n + 1)
            dA = DT[b][:, 2 + d, :]
            dtv = DT[b][:, d, :]
            if so:
                pc = pbank()
                fw.mm(pc[:, 0:32], tri, dA)
                fw.copy(cs_tok[:], pc[:, 0:32], e="act")
                pc2 = pbank()
                fw.mm(pc2[:, 0:32], ones[:], dA)
                fw.copy(csl[:], pc2[:, 0:32], e="dve")
            if not so:
                fw.tt(dAexp[:], dA.us(2).bc([128, 32, 128]), tri.us(1).bc([128, 32, 128]), ALU.mult)
                pc = pbank()
                fw.mm(pc[:, 0:32], tri, dA)
                fw.copy(cs_tok[:], pc[:, 0:32], e="act")
                for half in range(2):
                    pcb = pbank()
                    for gg in range(4):
                        g = half * 4 + gg
                        fw.mm(pcb[:, gg * 128:(gg + 1) * 128], Bb[b][:, g, :], Cb[b][:, g, :])
                    fw.tt(CBm[:, half * 4:half * 4 + 4, :], pcb[:].rr("p (a b) -> p a b", a=4), tri.us(1).bc([128, 4, 128]), ALU.mult)
                for o in range(4):
                    ob = o % 2
                    pD = [pbank(), pbank()]
                    pR = [pbank(), pbank()]
                    for j in range(2):
                        rhs = dAexp[:, o * 8 + j * 4:o * 8 + j * 4 + 4, :].rr("p a b -> p (a b)")
                        fw.mm(pD[j][:], lst, rhs)
                        fw.mm(pR[j][:], ones[:], rhs)
                    for j in range(2):
                        hs = slice(o * 8 + j * 4, o * 8 + j * 4 + 4)
                        g = o * 2 + j
                        fw.act(E[ob][:, j * 4:j * 4 + 4, :], pD[j][:].rr("p (a b) -> p a b", a=4), AF.Exp)
                        fw.tt(MT[:, hs, :], E[ob][:, j * 4:j * 4 + 4, :], CBm[:, g:g + 1, :].bc([128, 4, 128]), ALU.mult)
                        pR4 = pR[j][:].rr("p (a b) -> p a b", a=4)
                        fw.copy(csl[:, hs], pR4[:, :, t_last], e="dve")
                        fw.act(ecsR[ob][:, j * 4:j * 4 + 4, :], pR4, AF.Exp)
                        fw.tt(Csc[:, hs, :], ecsR[ob][:, j * 4:j * 4 + 4, :], Cb[b][:, g:g + 1, :].bc([128, 4, 128]), ALU.mult, e="dve")
            fw.act(ecl_last[:], csl[:], AF.Exp)
            fw.tt(decs[:], csl[:], cs_tok[:], ALU.subtract)
            fw.act(decs[:], decs[:], AF.Exp)
            for q in range(4):
                px = pbank()
                for cc in range(4):
                    c = q * 4 + cc
                    fw.tr(px[:, cc * 128:(cc + 1) * 128], XS[b][:, c, :], ident[:])
                hs = slice(q * 8, q * 8 + 8)
                fw.tt(xdt[:, hs, :], px[:].rr("p (a b) -> p a b", a=8), dtv[:, hs].us(2).bc([128, 8, 64]), ALU.mult)
                fw.tt(xdd[:, hs, :], xdt[:, hs, :], decs[:, hs].us(2).bc([128, 8, 64]), ALU.mult, e="pool")
            if not so:
                for q in range(4):
                    py = pbank()
                    for cc in range(4):
                        for hh in range(2):
                            h = (q * 4 + cc) * 2 + hh
                            o_ = py[64 * hh:64 * hh + 64, cc * 128:(cc + 1) * 128]
                            kw_ = {"tile_position": (0, 64)} if hh == 1 else {}
                            fw.mm(o_, xdt[:, h, :], MT[:, h, :], start=True, stop=False, **kw_)
                            fw.mm(o_, hb[:, h, :], Csc[:, h, :], start=False, stop=True, **kw_)
                    py4 = py[:].rr("p (a b) -> p a b", a=4)
                    if d == 0:
                        fw.copy(YB[b][:, q * 4:q * 4 + 4, :], py4, e="act")
                    else:
                        fw.tt(YB[b][:, q * 4:q * 4 + 4, :], py4, Y1[b][:, q * 4:q * 4 + 4, :], ALU.add)
                fw.dma(V(S["YM"].ap()[:, ti * 128:(ti + 1) * 128].rearrange("(c p) t -> p c t", p=128), None), YB[b][:], yb_sem[b])
            pt_ = pbank()
            ptb = pt_.v(pt_.h[:].bitcast(BF16).rearrange("p (a b) -> p a b", a=8))
            for g in range(8):
                fw.tr(ptb[:, g, :], Bb[b][:, g, :], identb[:])
            fw.copy(Btok[:], ptb, e="act")
            fw.tt(htmp[:], hS[:], ecl_last[:].us(2).bc([128, 32, 64]), ALU.mult, e="pool")
            for q in range(4):
                pn_ = pbank()
                for gg in range(2):
                    g = q * 2 + gg
                    fw.mm(pn_[:, gg * 256:(gg + 1) * 256], Btok[:, g, :], xdd[:, 4 * g:4 * g + 4, :].rr("p a b -> p (a b)"))
                hs = slice(q * 8, q * 8 + 8)
                fw.tt(hS[:, hs, :], pn_[:].rr("p (a b) -> p a b", a=8), htmp[:, hs, :], ALU.add)
            fw.copy(hb[:], hS[:], e="act")
        if so:
            fw.dma(V(stx_m.ap(), None), hS[:], ssem)
        elif d == 0:
            fw.dma(V(st_m_out.ap(), None), hS[:], ssem)
        fw.barrier()
        st.close()

    ALPHA = 2.0 ** 0.25
    LN_EPS = 1e-5
    GN_EPS = 64e-5
    mem_d = din("mem", [256, D])
    w_br_d = din("w_br", [1024, D])
    w_bm_d = din("w_bm", [D, D])
    w_o_d = din("w_o", [D, D])
    w_q_d = din("w_q", [D, D])
    w_kv_d = din("w_kv", [D, 2 * D])
    w_co_d = din("w_co", [D, D])
    w_up_d = din("w_up", [D, 4 * D])
    w_down_d = din("w_down", [4 * D, D])
    y_out = nc.dram_tensor("y_out", [T_loc, D], F32, kind="ExternalOutput")

    def wview(w):
        return w.ap().rearrange("(k p) n -> p k n", p=128)

    def phase3():
        st = ExitStack()

        def sbp(name, shape, dt=F32):
            return T(st.enter_context(nc.sbuf_tensor("p3_" + name, list(shape), dt)), name)

        F32A = sbp("F32A", [128, NK, 512])
        BFA = sbp("BFA", [128, NK, 512], BF16)
        BFB = sbp("BFB", [128, NK, 512], BF16)
        BFC = sbp("BFC", [128, NK, 512], BF16)
        BFD = sbp("BFD", [128, 8, 512], BF16)
        HM = sbp("HM", [128, 16, 512], BF16)
        Kt = sbp("Kt", [128, NK, 256], BF16)
        Vt = sbp("Vt", [128, 2, D], BF16)
        onesb = sbp("onesb", [128, 128], BF16)
        ksc = sbp("ksc", [128, 4])
        fw.copy(onesb[:], ones[:], e="act")
        NWB = 3
        WB = [sbp("wb%d" % j, [128, 4096], BF16) for j in range(NWB)]
        wb_sem = [fw.dsem("p3wb%d" % j) for j in range(NWB)]
        NL = 6
        LB = [sbp("lb%d" % j, [128, 512]) for j in range(NL)]
        lb_sem = [fw.dsem("p3lb%d" % j) for j in range(NL)]
        NTMP = 5
        TMP = [sbp("tmp%d" % j, [128, 512]) for j in range(NTMP)]
        xs = [sbp("xs%d" % j, [128, D]) for j in range(2)]
        xs_sem = [fw.dsem("p3xs%d" % j) for j in range(2)]
        ymp = [sbp("ymp%d" % j, [128, 2, 512]) for j in range(1)]
        sqp = [sbp("sqp%d" % j, [128, 2, 512]) for j in range(1)]
        expS = [sbp("expS%d" % j, [128, 2, 512], BF16) for j in range(1)]
        ded = {nm: sbp("ded_" + nm, [128, 512]) for nm in ("mean", "rstd", "cst", "rs")}
        cnt = {"pb": 0, "lb": 0, "tmp": 0}

        def pbank():
            j = cnt["pb"] % 8
            cnt["pb"] += 1
            return PB[j]

        def tmp():
            j = cnt["tmp"] % NTMP
            cnt["tmp"] += 1
            return TMP[j]

        def ld(name, row0, t0):
            j = cnt["lb"] % NL
            cnt["lb"] += 1
            fw.dma(LB[j][:], V(S[name].ap()[row0:row0 + 128, t0:t0 + 512], None), lb_sem[j])
            return LB[j]

        class WS:
            def __init__(self):
                self.specs = []
                self.issued = 0
                self.tiles = {}

            def add(self, wv, k0, nk, col0, ncols):
                self.specs.append((wv, k0, nk, col0, ncols))
                return len(self.specs) - 1

            def _issue(self, n):
                wv, k0, nk, col0, ncols = self.specs[n]
                j = n % NWB
                tv = WB[j][:, 0:nk * ncols].rr("p (k n) -> p k n", k=nk)
                fw.dma(tv, V(wv[:, k0:k0 + nk, col0:col0 + ncols], None), wb_sem[j], q="pool")
                self.tiles[n] = tv

            def get(self, n):
                while self.issued < min(len(self.specs), n + NWB):
                    self._issue(self.issued)
                    self.issued += 1
                return self.tiles.pop(n)

        w_in_v3 = w_in_v

        def dense(ws_ids, ws, src, nk, consume):
            pass

        def layer_norm(gname, bname):
            ps1 = pbank()
            ps2 = pbank()
            for c in range(NK):
                sq = tmp()
                fw.act(sq[:], F32A[:, c, :], AF.Square)
                fw.mm(ps1[:], ones[:], F32A[:, c, :], start=(c == 0), stop=(c == NK - 1))
                fw.mm(ps2[:], ones[:], sq[:], start=(c == 0), stop=(c == NK - 1), sig=True)
            mean = ded["mean"]
            fw.act(mean[:], ps1[:], AF.Copy, scale=1.0 / D)
            msq = tmp()
            fw.act(msq[:], ps1[:], AF.Square, scale=1.0 / D)
            rstd = ded["rstd"]
            fw.stt(rstd[:], ps2[:], 1.0 / D, msq[:], ALU.mult, ALU.subtract)
            fw.act(rstd[:], rstd[:], AF.Ln, bias=LN_EPS)
            fw.act(rstd[:], rstd[:], AF.Exp, scale=-0.5)
            for c in range(NK):
                t_ = tmp()
                fw.tt(t_[:], F32A[:, c, :], mean[:], ALU.subtract)
                fw.tt(t_[:], t_[:], rstd[:], ALU.mult)
                fw.act(F32A[:, c, :], t_[:], AF.Identity, scale=pv(gname, c), bias=pv(bname, c))
                fw.copy(BFA[:, c, :], F32A[:, c, :], e="dve")

        memT = BFB
        memTv = memT[:, :, 0:256]
        for mb in range(2):
            fw.dma(xs[mb][:], V(mem_d.ap()[mb * 128:(mb + 1) * 128, :], None), xs_sem[mb])
            for kq in range(4):
                pt = pbank()
                for k4 in range(4):
                    k = kq * 4 + k4
                    fw.tr(pt[:, k4 * 128:(k4 + 1) * 128], xs[mb][:, k * 128:(k + 1) * 128], ident[:])
                fw.copy(memT[:, kq * 4:(kq + 1) * 4, mb * 128:(mb + 1) * 128], pt[:].rr("p (a b) -> p a b", a=4), e="act")
        ws = WS()
        wkv = wview(w_kv_d)
        ids = [ws.add(wkv, 0, NK, c * 256, 256) for c in range(16)]
        for c in range(8):
            wt = ws.get(ids[c])
            for oo in range(2):
                oc = c * 2 + oo
                pk = pbank()
                for k in range(NK):
                    fw.mm(pk[:, 0:256], wt[:, k, oo * 128:(oo + 1) * 128], memTv[:, k, :], start=(k == 0), stop=(k == NK - 1))
                fw.copy(Kt[:, oc, :], pk[:, 0:256], e="act")
        for c in range(8):
            wt = ws.get(ids[8 + c])
            for mb in range(2):
                pvv = pbank()
                for k in range(NK):
                    fw.mm(pvv[:, 0:256], memT[:, k, mb * 128:(mb + 1) * 128], wt[:, k, :], start=(k == 0), stop=(k == NK - 1))
                fw.copy(Vt[:, mb, c * 256:(c + 1) * 256], pvv[:, 0:256], e="dve")
        for hd in range(4):
            pk2 = pbank()
            for kc in range(4):
                sqk = tmp()
                sqkb = sqk[:, 0:128].ap.bitcast(BF16)
                sqv = V(sqkb, sqk.buf)
                fw.act(sqv, Kt[:, hd * 4 + kc, :], AF.Square)
                fw.mm(pk2[:, 0:256], onesb[:], sqv, start=(kc == 0), stop=(kc == 3))
            mx = tmp()
            i_ = nc.vector
            r_, w_ = fw._bufs([pk2[:]]), fw._bufs([mx[:]])
            fw._deps("dve", r_, w_)
            ins = nc.vector.tensor_reduce(mx[:, 0:1].ap, pk2[:, 0:256].ap, AX.X, ALU.max)
            fw._done(ins, "dve", 1, r_, w_)
            fw.ts(ksc[:, hd:hd + 1], mx[:, 0:1], 1.0 / 512.0, None, ALU.mult)

        NT3 = T_loc // 512
        for i in range(NT3):
            t0 = i * 512
            ws = WS()
            wbr, wbm, wo, wq, wco, wup, wdn = [wview(w) for w in (w_br_d, w_bm_d, w_o_d, w_q_d, w_co_d, w_up_d, w_down_d)]
            id_z = [ws.add(w_in_v3, 0, NK, C_Z + c * 256, 256) for c in range(8)]
            id_d = []
            for c in range(8):
                id_d.append((ws.add(wbr, 0, 8, c * 256, 256), ws.add(w_in_v3, 0, NK, C_GATES + c * 256, 256),
                             ws.add(wbm, 0, NK, c * 256, 256), ws.add(w_in_v3, 0, NK, C_GATES + 2048 + c * 256, 256)))
            id_o = [ws.add(wo, 0, NK, c * 256, 256) for c in range(8)]
            id_q = [ws.add(wq, 0, NK, c * 256, 256) for c in range(8)]
            id_co = [ws.add(wco, 0, NK, c * 256, 256) for c in range(8)]
            id_up, id_dn = [], []
            for hf in range(4):
                id_up.append([ws.add(wup, 0, NK, hf * 2048 + c * 256, 256) for c in range(8)])
                id_dn.append([ws.add(wdn, hf * 16, 16, c * 256, 256) for c in range(8)])
            for j in range(4):
                xb = xs[j % 2]
                fw.dma(xb[:], V(x_ext.ap()[2 + t0 + j * 128:2 + t0 + (j + 1) * 128, :], None), xs_sem[j % 2])
                for kq in range(4):
                    pt = pbank()
                    for k4 in range(4):
                        k = kq * 4 + k4
                        fw.tr(pt[:, k4 * 128:(k4 + 1) * 128], xb[:, k * 128:(k + 1) * 128], ident[:])
                    pt4 = pt[:].rr("p (a b) -> p a b", a=4)
                    fw.copy(F32A[:, kq * 4:(kq + 1) * 4, j * 128:(j + 1) * 128], pt4, e="act")
                    fw.copy(BFA[:, kq * 4:(kq + 1) * 4, j * 128:(j + 1) * 128], pt4, e="dve")
            for c in range(8):
                y = ld("YRW", c * 128, t0)
                bon = ld("BON", c * 128, t0)
                gg = ld("G", c * 128, t0)
                sq = tmp()
                fw.act(sq[:], y[:], AF.Square)
                p1 = pbank()
                fw.mm(p1[:], blk[:], y[:])
                p2 = pbank()
                fw.mm(p2[:], blk[:], sq[:])
                m = tmp()
                fw.act(m[:], p1[:], AF.Copy, scale=1.0 / 64)
                msq = tmp()
                fw.act(msq[:], p1[:], AF.Square, scale=1.0 / 64)
                var = tmp()
                fw.stt(var[:], p2[:], 1.0 / 64, msq[:], ALU.mult, ALU.subtract)
                fw.act(var[:], var[:], AF.Ln, bias=GN_EPS)
                fw.act(var[:], var[:], AF.Exp, scale=-0.5)
                fw.tt(y[:], y[:], m[:], ALU.subtract)
                fw.tt(y[:], y[:], var[:], ALU.mult)
                fw.act(y[:], y[:], AF.Identity, scale=pv("gn_g", c), bias=pv("gn_b", c))
                fw.tt(y[:], y[:], bon[:], ALU.add)
                fw.tt(BFD[:, c, :], y[:], gg[:], ALU.mult)
            for c in range(NK):
                if c % 2 == 0:
                    wz = ws.get(id_z[c // 2])
                pb_ = 0
                ym = ld("YM", c * 128, t0)
                xv = ld("XBC", c * 128, t0)
                pz = pbank()
                for k in range(NK):
                    fw.mm(pz[:], wz[:, k, (c % 2) * 128:(c % 2 + 1) * 128], BFA[:, k, :], start=(k == 0), stop=(k == NK - 1))
                fw.stt(ym[:], xv[:], pv("m_d", c), ym[:], ALU.mult, ALU.add)
                sz = tmp()
                fw.act(sz[:], pz[:], AF.Silu)
                fw.tt(ymp[pb_][:, c % 2, :], ym[:], sz[:], ALU.mult)
                fw.act(sqp[pb_][:, c % 2, :], ymp[pb_][:, c % 2, :], AF.Square)
                if c % 2 == 1:
                    pss = pbank()
                    fw.mm(pss[:], ones[:], sqp[pb_][:, 0, :], start=True, stop=False)
                    fw.mm(pss[:], ones[:], sqp[pb_][:, 1, :], start=False, stop=True)
                    rms = tmp()
                    fw.act(rms[:], pss[:], AF.Ln, scale=1.0 / 256, bias=LN_EPS)
                    fw.act(rms[:], rms[:], AF.Exp, scale=-0.5)
                    for cc in range(2):
                        fw.stt(BFB[:, c - 1 + cc, :], ymp[pb_][:, cc, :], pv("m_norm_g", c - 1 + cc), rms[:], ALU.mult, ALU.mult)
            for c in range(8):
                wt = ws.get(id_d[c][0])
                pu = [pbank(), pbank()]
                for oo in range(2):
                    for k in range(8):
                        fw.mm(pu[oo][:], wt[:, k, oo * 128:(oo + 1) * 128], BFD[:, k, :], start=(k == 0), stop=(k == 7))
                wt = ws.get(id_d[c][1])
                t1 = [tmp(), tmp()]
                for oo in range(2):
                    pg = pbank()
                    for k in range(NK):
                        fw.mm(pg[:], wt[:, k, oo * 128:(oo + 1) * 128], BFA[:, k, :], start=(k == 0), stop=(k == NK - 1))
                    fw.act(t1[oo][:], pg[:], AF.Sigmoid)
                    fw.tt(t1[oo][:], t1[oo][:], pu[oo][:], ALU.mult)
                wt = ws.get(id_d[c][2])
                pm = [pbank(), pbank()]
                for oo in range(2):
                    for k in range(NK):
                        fw.mm(pm[oo][:], wt[:, k, oo * 128:(oo + 1) * 128], BFB[:, k, :], start=(k == 0), stop=(k == NK - 1))
                wt = ws.get(id_d[c][3])
                for oo in range(2):
                    oc = c * 2 + oo
                    pg2 = pbank()
                    for k in range(NK):
                        fw.mm(pg2[:], wt[:, k, oo * 128:(oo + 1) * 128], BFA[:, k, :], start=(k == 0), stop=(k == NK - 1))
                    sg2 = tmp()
                    fw.act(sg2[:], pg2[:], AF.Sigmoid)
                    fw.tt(sg2[:], sg2[:], pm[oo][:], ALU.mult)
                    fw.tt(BFC[:, oc, :], t1[oo][:], sg2[:], ALU.add)

            def proj_res(idl, src, first=True):
                for c in range(8):
                    wt = ws.get(idl[c])
                    for oo in range(2):
                        oc = c * 2 + oo
                        po = pbank()
                        for k in range(NK):
                            fw.mm(po[:], wt[:, k, oo * 128:(oo + 1) * 128], src[:, k, :], start=(k == 0), stop=(k == NK - 1))
                        fw.stt(F32A[:, oc, :], F32A[:, oc, :], ALPHA, po[:], ALU.mult, ALU.add)

            proj_res(id_o, BFC)
            layer_norm("ln1_g", "ln1_b")
            for c in range(8):
                wt = ws.get(id_q[c])
                for oo in range(2):
                    oc = c * 2 + oo
                    pq = pbank()
                    for k in range(NK):
                        fw.mm(pq[:], wt[:, k, oo * 128:(oo + 1) * 128], BFA[:, k, :], start=(k == 0), stop=(k == NK - 1))
                    fw.copy(BFB[:, oc, :], pq[:], e="act")
            inv = 1.0 / math.sqrt(512.0)
            for hd in range(4):
                eb = 0
                pq2 = pbank()
                for kc in range(4):
                    sqq = tmp()
                    sqv = V(sqq[:].ap.bitcast(BF16)[:, 0:512], sqq.buf)
                    fw.act(sqv, BFB[:, hd * 4 + kc, :], AF.Square)
                    fw.mm(pq2[:], onesb[:], sqv, start=(kc == 0), stop=(kc == 3))
                cst = ded["cst"]
                fw.act(cst[:], pq2[:], AF.Sqrt, scale=ksc[:, hd:hd + 1])
                for mc in range(2):
                    ps_ = pbank()
                    for kc in range(4):
                        fw.mm(ps_[:], Kt[:, hd * 4 + kc, mc * 128:(mc + 1) * 128], BFB[:, hd * 4 + kc, :], start=(kc == 0), stop=(kc == 3))
                    ein = tmp()
                    fw.stt(ein[:], ps_[:], inv, cst[:], ALU.mult, ALU.subtract)
                    fw.act(expS[eb][:, mc, :], ein[:], AF.Exp)
                psum_ = pbank()
                for mc in range(2):
                    fw.mm(psum_[:], onesb[:], expS[eb][:, mc, :], start=(mc == 0), stop=(mc == 1))
                rs = ded["rs"]
                r_, w_ = fw._bufs([psum_[:]]), fw._bufs([rs[:]])
                fw._deps("dve", r_, w_)
                ins = nc.vector.reciprocal(rs[:].ap, psum_[:].ap)
                fw._done(ins, "dve", 1, r_, w_)
                for dc in range(4):
                    po = pbank()
                    col = hd * 512 + dc * 128
                    for mc in range(2):
                        fw.mm(po[:], Vt[:, mc, col:col + 128], expS[eb][:, mc, :], start=(mc == 0), stop=(mc == 1))
                    fw.tt(BFC[:, hd * 4 + dc, :], po[:], rs[:], ALU.mult)
            proj_res(id_co, BFC)
            layer_norm("ln2_g", "ln2_b")
            for hf in range(4):
                for c in range(8):
                    wt = ws.get(id_up[hf][c])
                    for oo in range(2):
                        oc = c * 2 + oo
                        ph = pbank()
                        for k in range(NK):
                            fw.mm(ph[:], wt[:, k, oo * 128:(oo + 1) * 128], BFA[:, k, :], start=(k == 0), stop=(k == NK - 1))
                        rl = tmp()
                        fw.act(rl[:], ph[:], AF.Relu)
                        fw.tt(HM[:, oc, :], rl[:], rl[:], ALU.mult)
                for c in range(8):
                    wt = ws.get(id_dn[hf][c])
                    for oo in range(2):
                        oc = c * 2 + oo
                        po = pbank()
                        for k in range(16):
                            fw.mm(po[:], wt[:, k, oo * 128:(oo + 1) * 128], HM[:, k, :], start=(k == 0), stop=(k == 15))
                        if hf == 0:
                            fw.stt(F32A[:, oc, :], F32A[:, oc, :], ALPHA, po[:], ALU.mult, ALU.add)
                        else:
                            fw.tt(F32A[:, oc, :], F32A[:, oc, :], po[:], ALU.add)
            layer_norm("ln3_g", "ln3_b")
            for j in range(4):
                ob_ = xs[j % 2]
                for kq in range(4):
                    pt = pbank()
                    for k4 in range(4):
                        k = kq * 4 + k4
                        fw.tr(pt[:, k4 * 128:(k4 + 1) * 128], F32A[:, k, j * 128:(j + 1) * 128], ident[:])
                    fw.copy(ob_[:, kq * 512:(kq + 1) * 512], pt[:], e=("act" if kq % 2 else "dve"))
                fw.dma(V(y_out.ap()[t0 + j * 128:t0 + (j + 1) * 128, :], None), ob_[:], xs_sem[j % 2])
        fw.barrier()
        st.close()

    if 6 in phases:
        phase0(lite=True)
        fw.barrier()
        rwkv_phase(0, so=True)
        mamba_phase(0, so=True)
    if 0 in phases:
        phase0()
    fw.barrier()
    if 1 in phases:
        rwkv_phase(0)
    if 3 in phases:
        mamba_phase(0)
    if 2 in phases:
        rwkv_phase(1)
    if 4 in phases:
        mamba_phase(1)
    if 5 in phases:
        phase3()
    fw.barrier()
    es.close()
    return nc, fw


def host_params(inp, swap=False):
    g = lambda k: np.asarray(inp[k])[0]
    pvec = np.zeros((128, NPAR), np.float32)

    def put(name, vec, j0=0):
        vec = np.asarray(vec, np.float32)
        n = vec.shape[0]
        nch = (n + 127) // 128
        pad = np.zeros(nch * 128, np.float32)
        pad[:n] = vec
        pvec[:, POFF[name] + j0:POFF[name] + j0 + nch] = pad.reshape(nch, 128).T

    mup, mun = g("rw_mu_prev"), g("rw_mu_next")
    if swap:
        mup, mun = mun, mup
    for nm, mu in (("mup", mup), ("mun", mun)):
        put(nm, mu[0:3072], 0)
        put(nm, mu[3072:3168], 24)
        put(nm, mu[3168:3264], 25)
        put(nm, mu[3264:3520], 26)
    dirs = (1, 0) if swap else (0, 1)
    for d in range(2):
        put("w0", g("rw_w0")[dirs[d]], 8 * d)
        put("a0", g("rw_a0")[dirs[d]], 8 * d)
    put("k_k", g("rw_k_k"))
    put("k_a", g("rw_k_a"))
    put("r_k", g("rw_r_k").reshape(-1))
    put("gn_g", g("rw_gn_g"))
    put("gn_b", g("rw_gn_b"))
    cw = g("m_conv_w")
    if swap:
        cw = cw[::-1]
    for j in range(5):
        put("conv_w", cw[j], 32 * j)
    put("conv_b", g("m_conv_b"))
    put("m_norm_g", g("m_norm_g"))
    put("m_d", np.repeat(g("m_d"), 64))
    for nm in ("ln1_g", "ln1_b", "ln2_g", "ln2_b", "ln3_g", "ln3_b"):
        put(nm, g(nm))
    dtp = np.zeros((128, 2, 2, 4, 32), np.float32)
    for d in range(2):
        dtp[:, 0, d] = g("m_dt_bias")[dirs[d]][None, None, :]
        dtp[:, 1, d] = g("m_a_log")[dirs[d]][None, None, :]
    blk = np.zeros((128, 128), np.float32)
    blk[:64, :64] = 1.0
    blk[64:, 64:] = 1.0
    rwc = np.zeros((128, 2, 512), np.float32)
    idx = np.arange(128)
    hh, ss = idx // 64, idx % 64
    same = hh[:, None] == hh[None, :]
    lt = ss[:, None] < ss[None, :]
    le = ss[:, None] <= ss[None, :]
    rwc[:, 0, 0:128] = same & lt
    rwc[:, 0, 128:256] = same & le
    rwc[:, 0, 256:384] = same & lt.T
    rwc[:, 1, 0:128] = same & lt.T
    rwc[:, 1, 128:256] = same & le.T
    rwc[:, 1, 256:384] = same & lt
    mc = np.zeros((128, 2, 2, 128), np.float32)
    i128 = np.arange(128)
    mc[:, 0, 0] = i128[:, None] <= i128[None, :]
    mc[:, 0, 1] = i128[:, None] > i128[None, :]
    mc[:, 1, 0] = i128[:, None] >= i128[None, :]
    mc[:, 1, 1] = i128[:, None] < i128[None, :]
    rmask = np.ones((128, 1024), np.float32)
    rmask[:, ::64] = 0.0
    return dict(pvec=pvec, dtp=dtp, ident=np.eye(128, dtype=np.float32), blk64=blk, rwc=rwc, rmask=rmask,
                mc=mc,
                rw_w2=g("rw_w2"), rw_a2=g("rw_a2"), rw_g2=g("rw_g2"), w_in=g("w_in"),
                w_br=g("w_br"), w_bm=g("w_bm"), w_o=g("w_o"), w_q=g("w_q"), w_kv=g("w_kv"), w_co=g("w_co"),
                w_up=g("w_up"), w_down=g("w_down"))


T_CORE = 8192
_CACHE = {}


def _x_ext(xseq, start, T_loc, rev):
    NT0 = (T_loc + TV - 1) // TV
    TP = NT0 * TV
    L = xseq.shape[0]
    out = np.zeros((TP + 4, D), np.float32)
    if not rev:
        lo, hi = start - 2, start + T_loc + 2
        slo, shi = max(lo, 0), min(hi, L)
        out[slo - lo:shi - lo] = xseq[slo:shi]
    else:
        lo, hi = start - 2, start + T_loc + 2
        slo, shi = max(lo, 0), min(hi, L)
        seg = xseq[slo:shi][::-1]
        r0 = start + T_loc + 1 - (shi - 1)
        out[r0:r0 + seg.shape[0]] = seg
    return out


def kernel(**inputs):
    inp = {k: np.asarray(v) for k, v in inputs.items()}
    xp, xs_, mp, ms = inp["x_prompt"], inp["x_sample"], inp["mem_prompt"], inp["mem_sample"]
    T_loc = T_CORE
    if "nc" not in _CACHE:
        _CACHE["nc"] = build(T_loc)[0]
    nc = _CACHE["nc"]
    hp = [host_params(inp, swap=False), host_params(inp, swap=True)]
    cores = []
    for c in range(8):
        if c < 4:
            s, half = c // 2, c % 2
            cores.append(dict(x=xp[s], start=half * T_loc, rev=(half == 1), mem=mp[s]))
        else:
            cores.append(dict(x=xs_[c - 4], start=0, rev=False, mem=ms[c - 4]))
    base_maps = []
    xe = [_x_ext(cd["x"], cd["start"], T_loc, cd["rev"]) for cd in cores]
    for c, cd in enumerate(cores):
        own = hp[1 if cd["rev"] else 0]
        m = dict(own)
        m["x_ext"] = xe[c]
        m["mem"] = np.ascontiguousarray(cd["mem"], dtype=np.float32)
        if c < 4:
            partner = c ^ 1
            oth = hp[1 if cores[partner]["rev"] else 0]
            m["x_ext2"] = xe[partner]
            m["pvec2"] = oth["pvec"]
            m["dtp2"] = oth["dtp"]
            m["sel"] = np.ones((128, 1), np.float32)
        else:
            m["x_ext2"] = xe[c]
            m["pvec2"] = own["pvec"]
            m["dtp2"] = own["dtp"]
            m["sel"] = np.zeros((128, 1), np.float32)
        base_maps.append(m)
    res2 = run_bass_kernel_spmd(nc, base_maps, core_ids=list(range(8)))
    ys = [np.asarray(res2.results[c]["y_out"], np.float32) for c in range(8)]
    y_prompt = np.empty_like(xp)
    for c in range(4):
        s, half = c // 2, c % 2
        y_prompt[s, half * T_loc:(half + 1) * T_loc] = ys[c][::-1] if half == 1 else ys[c]
    y_sample = np.stack(ys[4:8], axis=0)
    return (y_prompt, y_sample)
```

```python
import math
from contextlib import ExitStack
import numpy as np
import concourse.bass as bass
import concourse.mybir as mybir
from concourse.bass_utils import run_bass_kernel_spmd

F32 = mybir.dt.float32
BF16 = mybir.dt.bfloat16
AF = mybir.ActivationFunctionType
ALU = mybir.AluOpType
AX = mybir.AxisListType

SAME_ENGINE_SYNC = True
MB_STOP = 99
DBG_NOWLOAD = False
DBG_NOSTORE = False
MB_SUB = 99

D = 2048
NK = 16
TT = 512
TV = 508
IN_COLS = 13792
C_R, C_K, C_V, C_DW, C_DA, C_DG = 0, 1024, 2048, 3072, 3168, 3264
C_Z, C_XBC, C_DT, C_GATES = 3520, 5568, 9664, 9696


class Buf:
    __slots__ = ("w", "r", "name", "ex")

    def __init__(self, name=""):
        self.w = None
        self.r = []
        self.name = name
        self.ex = False


class V:
    __slots__ = ("ap", "buf")

    def __init__(self, ap, buf):
        self.ap = ap
        self.buf = buf

    def __getitem__(self, idx):
        return V(self.ap[idx], self.buf)

    def rr(self, pat, **kw):
        return V(self.ap.rearrange(pat, **kw), self.buf)

    def bc(self, shape):
        return V(self.ap.to_broadcast(list(shape)), self.buf)

    def us(self, axis):
        return V(self.ap.unsqueeze(axis), self.buf)


class T:
    def __init__(self, h, name="", track=True):
        self.h = h
        self.buf = Buf(name) if track else None

    def __getitem__(self, idx):
        return V(self.h[idx], self.buf)

    def v(self, ap):
        return V(ap, self.buf)


class FW:
    def __init__(self, nc):
        self.nc = nc
        self.eng = {"pe": nc.tensor, "dve": nc.vector, "act": nc.scalar, "pool": nc.gpsimd, "sp": nc.sync}
        self.sem = {}
        self.cnt = {}
        for e in self.eng:
            self.sem[e] = nc.alloc_semaphore("sem_" + e)
            self.cnt[e] = 0
        self.seen = {e: {} for e in self.eng}
        self.n_inst = 0
        self.n_wait = 0

    def dsem(self, name):
        if name in self.sem:
            return name
        self.sem[name] = self.nc.alloc_semaphore("dsem_" + name)
        self.cnt[name] = 0
        return name

    def scan(self, out, d0, d1, initial, op0, op1):
        r, w = self._bufs([d0, d1, initial]), self._bufs([out])
        self._deps("dve", r, w)
        i = self.nc.vector.tensor_tensor_scan(out.ap, d0.ap, d1.ap, self._ap(initial), op0, op1)
        return self._done(i, "dve", 1, r, w)

    def _wait(self, e, dep):
        if dep is None:
            return
        key, val = dep
        if key == e and (e == "pe" or e == "sp" or not SAME_ENGINE_SYNC):
            return
        if self.seen[e].get(key, 0) >= val:
            return
        assert val <= self.cnt[key], "wait on a not-yet-signalled count (%s %d > %d): potential deadlock" % (key, val, self.cnt[key])
        self.seen[e][key] = val
        self.eng[e].wait_ge(self.sem[key], val)
        self.n_wait += 1

    def _deps(self, e, reads, writes):
        mx = {}
        for b in reads:
            if b.w is not None and mx.get(b.w[0], 0) < b.w[1]:
                mx[b.w[0]] = b.w[1]
            if b.ex:
                for k, v in b.r:
                    if k != e and mx.get(k, 0) < v:
                        mx[k] = v
        for b in writes:
            if b.w is not None and mx.get(b.w[0], 0) < b.w[1]:
                mx[b.w[0]] = b.w[1]
            for k, v in b.r:
                if mx.get(k, 0) < v:
                    mx[k] = v
        for k, v in mx.items():
            self._wait(e, (k, v))

    def _done(self, inst, key, inc, reads, writes, signal=True):
        if signal:
            self.cnt[key] += inc
            inst.then_inc(self.sem[key], inc)
            dep = (key, self.cnt[key])
        else:
            dep = (key, self.cnt[key] + inc)
        for b in reads:
            b.r.append(dep)
            if len(b.r) > 24:
                mx = {}
                for k, v in b.r:
                    if mx.get(k, 0) < v:
                        mx[k] = v
                b.r = list(mx.items())
        for b in writes:
            b.w = dep
            b.r = []
        self.n_inst += 1
        return dep

    @staticmethod
    def _bufs(vs):
        out = []
        for v in vs:
            if isinstance(v, V) and v.buf is not None and v.buf not in out:
                out.append(v.buf)
        return out

    @staticmethod
    def _ap(v):
        return v.ap if isinstance(v, V) else v

    def mm(self, out, lhsT, rhs, start=True, stop=True, sig=False, **kw):
        r, w = self._bufs([lhsT, rhs]), self._bufs([out])
        self._deps("pe", r, w)
        i = self.nc.tensor.matmul(out.ap, lhsT.ap, rhs.ap, start=start, stop=stop, **kw)
        return self._done(i, "pe", 1, r, w, signal=(stop or sig))

    def tr(self, out, in_, ident):
        r, w = self._bufs([in_, ident]), self._bufs([out])
        self._deps("pe", r, w)
        i = self.nc.tensor.transpose(out.ap, in_.ap, ident.ap)
        return self._done(i, "pe", 1, r, w)

    def act(self, out, in_, func, bias=None, scale=None, e="act", accum_out=None):
        r, w = self._bufs([in_, bias, scale]), self._bufs([out, accum_out])
        self._deps(e, r, w)
        kw = {}
        if bias is not None:
            kw["bias"] = self._ap(bias)
        if scale is not None:
            kw["scale"] = self._ap(scale)
        if accum_out is not None:
            kw["accum_out"] = self._ap(accum_out)
        i = self.eng[e].activation(out.ap, in_.ap, func, **kw)
        return self._done(i, e, 1, r, w)

    def tt(self, out, a, b, op, e="dve"):
        r, w = self._bufs([a, b]), self._bufs([out])
        self._deps(e, r, w)
        i = self.eng[e].tensor_tensor(out.ap, a.ap, b.ap, op)
        return self._done(i, e, 1, r, w)

    def ts(self, out, in0, s1, s2, op0, op1=None, e="dve"):
        r, w = self._bufs([in0, s1, s2]), self._bufs([out])
        self._deps(e, r, w)
        kw = {}
        if op1 is not None:
            kw["op1"] = op1
        i = self.eng[e].tensor_scalar(out.ap, in0.ap, self._ap(s1), self._ap(s2), op0, **kw)
        return self._done(i, e, 1, r, w)

    def stt(self, out, in0, scalar, in1, op0, op1, e="dve"):
        r, w = self._bufs([in0, scalar, in1]), self._bufs([out])
        self._deps(e, r, w)
        i = self.eng[e].scalar_tensor_tensor(out.ap, in0.ap, self._ap(scalar), in1.ap, op0, op1)
        return self._done(i, e, 1, r, w)

    def copy(self, out, in_, e="dve"):
        r, w = self._bufs([in_]), self._bufs([out])
        self._deps(e, r, w)
        if e == "act":
            i = self.nc.scalar.copy(out.ap, in_.ap)
        else:
            i = self.eng[e].tensor_copy(out.ap, in_.ap)
        return self._done(i, e, 1, r, w)

    def memset(self, out, val, e="dve"):
        w = self._bufs([out])
        self._deps(e, [], w)
        i = self.eng[e].memset(out.ap, val)
        return self._done(i, e, 1, [], w)

    def dma(self, out, in_, sem, q="sp", **kw):
        r, w = self._bufs([in_]), self._bufs([out])
        self._deps(q, r, w)
        i = self.eng[q].dma_start(out=out.ap, in_=in_.ap, **kw)
        return self._done(i, sem, 16, r, w)

    def collective(self, kind, op, groups, in_v, out_v, sem):
        r, w = self._bufs([in_v]), self._bufs([out_v])
        self._deps("pool", r, w)
        i = self.nc.gpsimd.collective_compute(kind, op=op, replica_groups=groups, ins=[in_v.ap], outs=[out_v.ap])
        return self._done(i, sem, 16, r, w)

    def barrier(self):
        for e in self.eng:
            for key, val in self.cnt.items():
                if val > 0:
                    self._wait(e, (key, val)) if key != e else None


class Ctx:
    pass


def _param_layout():
    off = {}
    n = 0
    for name, w in [("mup", 28), ("mun", 28), ("w0", 16), ("a0", 16), ("k_k", 8), ("k_a", 8), ("r_k", 8),
                    ("gn_g", 8), ("gn_b", 8), ("conv_w", 160), ("conv_b", 32), ("m_norm_g", 16), ("m_d", 16),
                    ("ln1_g", 16), ("ln1_b", 16), ("ln2_g", 16), ("ln2_b", 16), ("ln3_g", 16), ("ln3_b", 16)]:
        off[name] = n
        n += w
    return off, n


POFF, NPAR = _param_layout()
XOFF = {"c0": 0, "nk_k": 28, "omk_a": 36}
NX = 44


def build(T_loc, dbg=False, phases=(6, 0, 1, 2, 3, 4, 5)):
    NT0 = (T_loc + TV - 1) // TV
    TP = NT0 * TV
    XR = TP + 4
    nc = bass.Bass("TRN2", target_bir_lowering=False)
    fw = FW(nc)
    es = ExitStack()

    def din(name, shape, dt=F32):
        return nc.dram_tensor(name, list(shape), dt, kind="ExternalInput")

    def dscr(name, shape, dt=F32):
        return nc.dram_tensor(name, list(shape), dt, kind=("ExternalOutput" if dbg else "Internal"))

    x_ext = din("x_ext", [XR, D])
    x_ext2 = din("x_ext2", [XR, D])
    pvec2_d = din("pvec2", [128, NPAR])
    dtp2_d = din("dtp2", [128, 2, 2, 4, 32])
    sel_d = din("sel", [128, 1])
    w_in = din("w_in", [D, IN_COLS])
    pvec_d = din("pvec", [128, NPAR])
    ident_d = din("ident", [128, 128])
    blk_d = din("blk64", [128, 128])
    w2_d = din("rw_w2", [96, 1024])
    a2_d = din("rw_a2", [96, 1024])
    g2_d = din("rw_g2", [256, 1024])
    dtp_d = din("dtp", [128, 2, 2, 4, 32])

    S = {}
    for nm in ["R", "V", "A", "G", "BON", "KD1", "KD2", "B1", "B2", "LW1", "LW2"]:
        S[nm] = dscr("s_" + nm, [1024, TP])
    S["XBC"] = dscr("s_XBC", [4096, TP])
    S["DTS"] = dscr("s_DTS", [TP, 4, 32])
    S2 = {}
    for nm in ["V", "A", "KD1", "B1", "LW1"]:
        S2[nm] = dscr("s2_" + nm, [1024, TP])
    S2["XBC"] = dscr("s2_XBC", [4096, TP])
    S2["DTS"] = dscr("s2_DTS", [TP, 4, 32])
    stx_rw = nc.dram_tensor("stx_rw", [128, 8, 128], F32, kind="Internal")
    stx_m = nc.dram_tensor("stx_m", [128, 32, 64], F32, kind="Internal")

    def sb(name, shape, dt=F32):
        return T(es.enter_context(nc.sbuf_tensor("sb_" + name, list(shape), dt)), name)

    PB = [T(nc.alloc_psum_tensor("pb%d" % i, [128, 512], F32), "pb%d" % i) for i in range(8)]
    for t_ in PB:
        t_.buf.ex = True

    ident = sb("ident", [128, 128])
    blk = sb("blk", [128, 128])
    sel = sb("sel", [128, 1])
    dcp = fw.dsem("constp")
    fw.dma(ident[:], V(ident_d.ap(), None), fw.dsem("c2"))
    fw.dma(blk[:], V(blk_d.ap(), None), fw.dsem("c3"))
    fw.dma(sel[:], V(sel_d.ap(), None), fw.dsem("c12"))
    w_in_v = w_in.ap().rearrange("(k p) n -> p k n", p=128)
    CUR = {}
    PSETS = []
    for si, (pd_, dd_) in enumerate(((pvec_d, dtp_d), (pvec2_d, dtp2_d))):
        pvec_t = sb("pvec%d" % si, [128, NPAR])
        xpar_t = sb("xpar%d" % si, [128, NX])
        fw.dma(pvec_t[:], V(pd_.ap(), None), fw.dsem("c1_%d" % si))
        fw.tt(xpar_t[:, 0:28], pvec_t[:, POFF["mup"]:POFF["mup"] + 28], pvec_t[:, POFF["mun"]:POFF["mun"] + 28], ALU.add)
        fw.ts(xpar_t[:, 0:28], xpar_t[:, 0:28], -1.0, 1.0, ALU.mult, ALU.add)
        fw.ts(xpar_t[:, 28:36], pvec_t[:, POFF["k_k"]:POFF["k_k"] + 8], -1.0, None, ALU.mult)
        fw.ts(xpar_t[:, 36:44], pvec_t[:, POFF["k_a"]:POFF["k_a"] + 8], -1.0, 1.0, ALU.mult, ALU.add)
        PSETS.append(dict(pvec=pvec_t, xpar=xpar_t, dtp_d=dd_))
    PSETS[0].update(x=x_ext, S=S)
    PSETS[1].update(x=x_ext2, S=S2)
    CUR.update(PSETS[0])

    def pv(name, j=0, n=128):
        c = POFF[name] + j
        return CUR["pvec"][0:n, c:c + 1]

    def xp(name, j=0, n=128):
        c = XOFF[name] + j
        return CUR["xpar"][0:n, c:c + 1]

    def phase0(lite=False):
        st = ExitStack()
        CUR.update(PSETS[1 if lite else 0])
        x_src = CUR["x"]
        Sd = CUR["S"]
        nd = 1 if lite else 2

        def sb0(name, shape, dt=F32):
            return T(st.enter_context(nc.sbuf_tensor(("p0l_" if lite else "p0_") + name, list(shape), dt)), name)

        w2 = sb0("w2", [96, 1024], BF16)
        a2 = sb0("a2", [96, 1024], BF16)
        g2 = sb0("g2", [128, 2, 1024], BF16)
        wdt = sb0("wdt", [128, NK, 32], BF16)
        dtp = sb0("dtp", [128, 2, 2, 4, 32])
        An = sb0("An", [128, 2, 4, 32])
        fw.dma(dtp[:], V(CUR["dtp_d"].ap(), None), fw.dsem("c4"))
        fw.dma(w2[:], V(w2_d.ap(), None), fw.dsem("c5"), q="pool")
        fw.dma(a2[:], V(a2_d.ap(), None), fw.dsem("c6"), q="pool")
        fw.dma(g2[:], V(g2_d.ap().rearrange("(k p) n -> p k n", p=128), None), fw.dsem("c7"), q="pool")
        fw.dma(wdt[:], V(w_in_v[:, :, C_DT:C_DT + 32], None), dcp, q="pool")
        fw.act(An[:], dtp[:, 1], AF.Exp)
        fw.ts(An[:], An[:], -1.0, None, ALU.mult)
        xs = [sb0("xs%d" % j, [128, D]) for j in range(2)]
        xs_sem = [fw.dsem("xs%d" % j) for j in range(2)]
        xT = sb0("xT", [128, NK, TT], BF16)
        WG = [sb0("wg%d" % j, [128, NK, 512], BF16) for j in range(2)]
        wg_sem = [fw.dsem("wg%d" % j) for j in range(2)]
        ag = sb0("ag", [128, 8, 2, TV])
        kp = sb0("kp", [128, 8, TV])
        rk = sb0("rk", [128, 8, TV])
        tdw = sb0("tdw", [96, TV], BF16)
        tda = sb0("tda", [96, TV], BF16)
        tdg = sb0("tdg", [128, 2, TV], BF16)
        NS = 8
        ost = [sb0("ost%d" % j, [128, TV]) for j in range(NS)]
        ost_sem = [fw.dsem("ost%d" % j) for j in range(NS)]
        tmp = [sb0("tmp%d" % j, [128, TV]) for j in range(4)]
        dts = sb0("dts", [128, 4, 4, 32])
        dtt = sb0("dtt", [128, 4, 32])
        dts_sem = fw.dsem("dts")
        cnt = {"ost": 0, "tmp": 0, "wg": 0, "pb": 0, "aux": 0}

        def new_ost():
            j = cnt["ost"] % NS
            cnt["ost"] += 1
            return ost[j], ost_sem[j]

        def new_tmp():
            j = cnt["tmp"] % 4
            cnt["tmp"] += 1
            return tmp[j]

        def new_pb():
            j = cnt["pb"] % 4
            cnt["pb"] += 1
            return PB[j]

        def new_aux():
            j = cnt["aux"] % 2
            cnt["aux"] += 1
            return PB[6 + j]

        def store(name, row0, nrow, i, o, osem):
            if DBG_NOSTORE and cnt["ost"] > 8:
                return
            fw.dma(V(Sd[name].ap()[row0:row0 + nrow, i * TV:(i + 1) * TV], None), o[0:nrow, :], osem)

        def load_wg(col0, n):
            j = cnt["wg"] % 2
            cnt["wg"] += 1
            if not (DBG_NOWLOAD and cnt["wg"] > 2):
                fw.dma(WG[j][:, :, 0:n], V(w_in_v[:, :, col0:col0 + n], None), wg_sem[j], q="pool")
            return WG[j]

        def proj(P, wt, c0, ncol):
            for k in range(NK):
                fw.mm(P[0:ncol, :], wt[:, k, c0:c0 + ncol], xT[:, k, :], start=(k == 0), stop=(k == NK - 1))

        def shift(P, pc, nrow, out):
            fw.act(out, P[0:nrow, 2:2 + TV], AF.Copy, scale=xp("c0", pc, nrow))
            fw.stt(out, P[0:nrow, 1:1 + TV], pv("mup", pc, nrow), out, ALU.mult, ALU.add)
            fw.stt(out, P[0:nrow, 3:3 + TV], pv("mun", pc, nrow), out, ALU.mult, ALU.add)

        for i in range(NT0):
            for j in range(4):
                xb = xs[j % 2]
                if not (i > 0 and j < 2):
                    fw.dma(xb[:], V(x_src.ap()[i * TV + j * 128:i * TV + (j + 1) * 128, :], None), xs_sem[j % 2])
                for kq in range(4):
                    pt = PB[4 + (kq % 2)]
                    for k4 in range(4):
                        k = kq * 4 + k4
                        fw.tr(pt[:, k4 * 128:(k4 + 1) * 128], xb[:, k * 128:(k + 1) * 128], ident[:])
                    fw.copy(xT[:, kq * 4:(kq + 1) * 4, j * 128:(j + 1) * 128],
                            pt.v(pt.h[:].rearrange("p (a b) -> p a b", a=4)), e=("act" if kq % 2 else "dve"))
            if i + 1 < NT0:
                for j in range(2):
                    fw.dma(xs[j][:], V(x_src.ap()[(i + 1) * TV + j * 128:(i + 1) * TV + (j + 1) * 128, :], None), xs_sem[j])
            pd = new_aux()
            pdv = pd.v(pd.h[:, 0:128].rearrange("p (a b) -> p a b", a=4))
            for j in range(4):
                for k in range(NK):
                    fw.mm(pdv[:, j, :], xT[:, k, j * 128:(j + 1) * 128], wdt[:, k, :], start=(k == 0), stop=(k == NK - 1))
            for d in range(2):
                fw.tt(dtt[:], pdv, dtp[:, 0, d], ALU.add)
                fw.act(dtt[:], dtt[:], AF.Exp)
                fw.act(dts[:, :, d, :], dtt[:], AF.Ln, bias=1.0)
                fw.tt(dts[:, :, 2 + d, :], dts[:, :, d, :], An[:, d], ALU.mult)
            for j in range(4):
                lo, hi = max(2, 128 * j), min(2 + TV, 128 * j + 128)
                fw.dma(V(Sd["DTS"].ap()[i * TV + lo - 2:i * TV + hi - 2], None), dts[lo - 128 * j:hi - 128 * j, j], dts_sem)
            wt = load_wg(C_DW, 448)
            P = new_pb()
            proj(P, wt, 0, 96)
            t = new_tmp()
            shift(P, 24, 96, t[0:96, :])
            fw.act(tdw[:], t[0:96, :], AF.Tanh)
            P = new_pb()
            proj(P, wt, 96, 96)
            t = new_tmp()
            shift(P, 25, 96, t[0:96, :])
            fw.copy(tda[:], t[0:96, :], e="act")
            for c in range(0 if lite else 2):
                P = new_pb()
                proj(P, wt, 192 + 128 * c, 128)
                t = new_tmp()
                shift(P, 26 + c, 128, t[:])
                fw.act(tdg[:, c, :], t[:], AF.Sigmoid)
            for c in range(8):
                P = new_aux()
                fw.mm(P[:, 0:TV], w2[:, c * 128:(c + 1) * 128], tdw[:])
                for d in range(nd):
                    t = new_tmp()
                    fw.act(t[:], P[:, 0:TV], AF.Sigmoid, bias=pv("w0", d * 8 + c))
                    o, osem = new_ost()
                    fw.ts(o[:], t[:], -math.exp(-0.5), None, ALU.mult, e="dve")
                    store("LW%d" % (d + 1), c * 128, 128, i, o, osem)
                P = new_aux()
                fw.mm(P[:, 0:TV], a2[:, c * 128:(c + 1) * 128], tda[:])
                for d in range(nd):
                    fw.act(ag[:, c, d, :], P[:, 0:TV], AF.Sigmoid, bias=pv("a0", d * 8 + c))
                if not lite:
                    P = new_aux()
                    for k in range(2):
                        fw.mm(P[:, 0:TV], g2[:, k, c * 128:(c + 1) * 128], tdg[:, k, :], start=(k == 0), stop=(k == 1))
                    o, osem = new_ost()
                    fw.copy(o[:], P[:, 0:TV], e="dve")
                    store("G", c * 128, 128, i, o, osem)
            for gi in range(2):
                wt = load_wg(C_K + 512 * gi, 512)
                for cc in range(4):
                    c = gi * 4 + cc
                    P = new_pb()
                    proj(P, wt, cc * 128, 128)
                    shift(P, 8 + c, 128, kp[:, c, :])
                    sq = new_tmp()
                    fw.act(sq[:], kp[:, c, :], AF.Square, scale=pv("k_k", c))
                    Pa = new_aux()
                    fw.mm(Pa[:, 0:TV], blk[:], sq[:])
                    rn = new_tmp()
                    fw.act(rn[:], Pa[:, 0:TV], AF.Ln, bias=1e-24)
                    fw.act(rn[:], rn[:], AF.Exp, scale=-0.5)
                    oa, osem = new_ost()
                    fw.stt(oa[:], kp[:, c, :], xp("nk_k", c), rn[:], ALU.mult, ALU.mult)
                    store("A", c * 128, 128, i, oa, osem)
                    for d in range(nd):
                        o, osem = new_ost()
                        fw.stt(o[:], oa[:], -1.0, ag[:, c, d, :], ALU.mult, ALU.mult)
                        store("B%d" % (d + 1), c * 128, 128, i, o, osem)
                        t = new_tmp()
                        fw.act(t[:], ag[:, c, d, :], AF.Identity, scale=pv("k_a", c), bias=xp("omk_a", c))
                        o, osem = new_ost()
                        fw.tt(o[:], t[:], kp[:, c, :], ALU.mult, e="dve")
                        store("KD%d" % (d + 1), c * 128, 128, i, o, osem)
            for gi in range(0 if lite else 2):
                wt = load_wg(C_R + 512 * gi, 512)
                for cc in range(4):
                    c = gi * 4 + cc
                    P = new_pb()
                    proj(P, wt, cc * 128, 128)
                    o, osem = new_ost()
                    shift(P, c, 128, o[:])
                    store("R", c * 128, 128, i, o, osem)
                    t = new_tmp()
                    fw.stt(t[:], o[:], pv("r_k", c), kp[:, c, :], ALU.mult, ALU.mult)
                    Pa = new_aux()
                    fw.mm(Pa[:, 0:TV], blk[:], t[:])
                    fw.copy(rk[:, c, :], Pa[:, 0:TV], e="act")
            for gi in range(2):
                wt = load_wg(C_V + 512 * gi, 512)
                for cc in range(4):
                    c = gi * 4 + cc
                    P = new_pb()
                    proj(P, wt, cc * 128, 128)
                    o, osem = new_ost()
                    shift(P, 16 + c, 128, o[:])
                    store("V", c * 128, 128, i, o, osem)
                    if not lite:
                        o2, osem2 = new_ost()
                        fw.tt(o2[:], o[:], rk[:, c, :], ALU.mult, e="dve")
                        store("BON", c * 128, 128, i, o2, osem2)
            for gi in range(6 if lite else 8):
                wt = load_wg(C_XBC + 512 * gi, 512)
                for cc in range(4):
                    c = gi * 4 + cc
                    P = new_pb()
                    proj(P, wt, cc * 128, 128)
                    t = new_tmp()
                    fw.act(t[:], P[:, 0:TV], AF.Identity, scale=pv("conv_w", c), bias=pv("conv_b", c))
                    for j in range(1, 5):
                        fw.stt(t[:], P[:, j:j + TV], pv("conv_w", 32 * j + c), t[:], ALU.mult, ALU.add)
                    o, osem = new_ost()
                    fw.act(o[:], t[:], AF.Silu)
                    store("XBC", c * 128, 128, i, o, osem)
        fw.barrier()
        st.close()


    NST = T_loc // 128
    S["YRW"] = dscr("s_YRW", [1024, TP])
    st_rw_out = nc.dram_tensor("st_rw_out", [128, 8, 128], F32, kind="ExternalOutput")
    rwc_d = din("rwc", [128, 2, 512])
    rmask_d = din("rmask", [128, 1024])
    identb = sb("identb", [128, 128], BF16)
    fw.dma(identb[:], V(ident_d.ap(), None), fw.dsem("c8"), q="pool")

    def rwkv_phase(d, so=False):
        st = ExitStack()
        Ss = S2 if so else S

        def sbp(name, shape, dt=F32):
            return T(st.enter_context(nc.sbuf_tensor(("rws_" if so else "rw%d_" % d) + name, list(shape), dt)), name)

        rwc = sbp("rwc", [128, 2, 512])
        rmask = sbp("rmask", [128, 1024])
        fw.dma(rwc[:], V(rwc_d.ap(), None), fw.dsem("c9"))
        fw.dma(rmask[:], V(rmask_d.ap(), None), fw.dsem("c10"))
        names = ["R", "KD%d" % (d + 1), "V", "A", "B%d" % (d + 1), "LW%d" % (d + 1)]
        LD = [[sbp("ld%d_%d" % (j, b), [128, 8, 128]) for j in range(6)] for b in range(2)]
        ld_sem = [[fw.dsem("rwld%d_%d" % (j, b)) for j in range(6)] for b in range(2)]
        Y1 = [sbp("y1_%d" % b, [128, 8, 128]) for b in range(2)]
        y1_sem = [fw.dsem("rwy1_%d" % b) for b in range(2)]
        YB = [sbp("yb_%d" % b, [128, 8, 128]) for b in range(2)]
        yb_sem = [fw.dsem("rwyb_%d" % b) for b in range(2)]
        pre = sbp("pre", [128, 1024])
        cl = sbp("cl", [128, 1024])
        ecl = sbp("ecl", [128, 8, 128])
        encl = sbp("encl", [128, 8, 128])
        ecx = sbp("ecx", [128, 8, 128])
        xt = [sbp("xt%d" % j, [128, 8, 128]) for j in range(4)]
        BT = [sbp("BTbd%d" % b, [128, 8, 128], BF16) for b in range(2)]
        KT = [sbp("KTbd%d" % b, [128, 8, 128], BF16) for b in range(2)]
        AR = [sbp("ARbd%d" % b, [128, 8, 256], BF16) for b in range(2)]
        VB = [sbp("Vbd%d" % b, [128, 8, 128], BF16) for b in range(2)]
        for b in range(2):
            for t_ in (BT[b], KT[b], AR[b], VB[b]):
                fw.memset(t_[:], 0.0, e="pool")
        S0q = [sbp("S0_%d" % q_, [128, 4, 128]) for q_ in range(2)]
        S0bq = [sbp("S0b_%d" % q_, [128, 4, 128], BF16) for q_ in range(2)]
        ssem = fw.dsem("rwstate")
        ssem2 = fw.dsem("rwstate2")
        for q_ in range(2):
            if d == 0:
                fw.memset(S0q[q_][:], 0.0)
            else:
                fw.dma(S0q[q_][:], V(stx_rw.ap()[:, 4 * q_:4 * q_ + 4, :], None), (ssem, ssem2)[q_])
                fw.ts(S0q[q_][:].rr("p a b -> p (a b)"), S0q[q_][:].rr("p a b -> p (a b)"), sel[:, 0:1], None, ALU.mult)
            fw.copy(S0bq[q_][:], S0q[q_][:], e="act")
        ABs = [sbp("ABs%d" % b, [128, 4, 256], BF16) for b in range(4)]
        AKs = [sbp("AKs%d" % b, [128, 4, 256], BF16) for b in range(4)]
        NTs = [[sbp("NTs%d_%d" % (b, j), [128, 4, 128], BF16) for j in range(2)] for b in range(4)]
        Ns = [[sbp("Ns%d_%d" % (b, j), [128, 4, 128], BF16) for j in range(2)] for b in range(4)]
        Ps = [[sbp("Ps%d_%d" % (b, j), [128, 4, 128], BF16) for j in range(2)] for b in range(4)]
        VTs = [sbp("VTs%d" % b, [128, 4, 128], BF16) for b in range(4)]
        GTs = [sbp("GTs%d" % b, [128, 4, 128], BF16) for b in range(2)]
        UTs = [sbp("UTs%d" % b, [128, 4, 128], BF16) for b in range(2)]
        BKT = [sbp("BKT%d" % b, [128, 4, 2, 128], BF16) for b in range(4)]
        stmp = [sbp("stmp%d" % b, [128, 4, 128]) for b in range(2)]
        pbc = {"n": 0}

        def pbank():
            j = pbc["n"] % 8
            pbc["n"] += 1
            return PB[j]

        mAB = rwc[:, d, 0:256]
        mNT = rwc[:, d, 256:384]
        tiles = list(range(NST)) if d == 0 else list(range(NST - 1, -1, -1))
        chunks = (0, 1) if d == 0 else (1, 0)

        def load_tile(n):
            ti = tiles[n]
            b = n % 2
            for j in range(6):
                if so and j == 0:
                    continue
                fw.dma(LD[b][j][:], V(Ss[names[j]].ap()[:, ti * 128:(ti + 1) * 128].rearrange("(c p) t -> p c t", p=128), None),
                       ld_sem[b][j])
            if d == 1:
                fw.dma(Y1[b][:], V(S["YRW"].ap()[:, ti * 128:(ti + 1) * 128].rearrange("(c p) t -> p c t", p=128), None),
                       y1_sem[b])

        load_tile(0)
        qn = 0
        for n in range(NST):
            ti = tiles[n]
            b = n % 2
            if n + 1 < NST:
                load_tile(n + 1)
            r_, k_, v_, a_, b_, lw_ = [LD[b][j] for j in range(6)]
            fw.scan(pre[:], rmask[:], lw_[:].rr("p c t -> p (c t)"), 0.0, ALU.mult, ALU.add)
            pre4 = pre[:].rr("p (c t) -> p c t", t=64)
            cl4 = cl[:].rr("p (c t) -> p c t", t=64)
            lw4 = lw_[:].rr("p c (u t) -> p (c u) t", t=64)
            if d == 0:
                clv = pre
            else:
                fw.tt(cl4, lw4, pre4, ALU.subtract)
                fw.tt(cl4, cl4, pre4[:, :, 63:64].bc([128, 16, 64]), ALU.add)
                clv = cl
            clf = clv[:]
            fw.act(ecl[:].rr("p c t -> p (c t)"), clf, AF.Exp)
            fw.act(encl[:].rr("p c t -> p (c t)"), clf, AF.Exp, scale=-1.0)
            fw.tt(ecx[:].rr("p c t -> p (c t)"), clf, lw_[:].rr("p c t -> p (c t)"), ALU.subtract, e="pool")
            fw.act(ecx[:].rr("p c t -> p (c t)"), ecx[:].rr("p c t -> p (c t)"), AF.Exp)
            fw.tt(xt[0][:], b_[:], encl[:], ALU.mult, e="pool")
            fw.tt(xt[1][:], k_[:], encl[:], ALU.mult)
            fw.tt(xt[2][:], a_[:], ecx[:], ALU.mult, e="pool")
            if not so:
                fw.tt(xt[3][:], r_[:], ecl[:], ALU.mult)
            corder = list(chunks)
            cinfo = {}
            for pos, ci in enumerate(corder):
                cb = (2 * n + ci) % 2
                cs = slice(ci * 64, ci * 64 + 64)
                for hh in range(2):
                    ps = slice(64 * hh, 64 * hh + 64)
                    fs = slice(64 * hh, 64 * hh + 64)
                    fw.copy(BT[cb][ps, :, fs], xt[0][ps, :, cs], e="pool")
                    fw.copy(KT[cb][ps, :, fs], xt[1][ps, :, cs], e="dve")
                    fw.copy(AR[cb][ps, :, fs], xt[2][ps, :, cs], e="pool")
                    if not so:
                        fw.copy(AR[cb][ps, :, slice(128 + 64 * hh, 192 + 64 * hh)], xt[3][ps, :, cs], e="act")
                    fw.copy(VB[cb][ps, :, fs], v_[ps, :, cs], e="dve")
                wl = ecl[:, :, (ci * 64 + 63) if d == 0 else (ci * 64)]
                cinfo[pos] = (cb, cs, wl)
            minv_of = {}

            def pre_body(pos, q, cb, cs, wl):
                qb = pos * 2 + q
                p0 = 4 * q
                for half in range(2):
                    pa = pbank()
                    pk = pbank()
                    for pp in range(2):
                        p = p0 + 2 * half + pp
                        fw.mm(pa[:, pp * 256:(pp + 1) * 256], BT[cb][:, p, :], AR[cb][:, p, :])
                        fw.mm(pk[:, pp * 256:(pp + 1) * 256], KT[cb][:, p, :], AR[cb][:, p, :])
                    fw.tt(ABs[qb][:, 2 * half:2 * half + 2, :], pa[:].rr("p (a b) -> p a b", a=2),
                          mAB.us(1).bc([128, 2, 256]), ALU.mult)
                    fw.tt(AKs[qb][:, 2 * half:2 * half + 2, :], pk[:].rr("p (a b) -> p a b", a=2),
                          mAB.us(1).bc([128, 2, 256]), ALU.mult)
                pn = pbank()
                for pp in range(4):
                    p = p0 + pp
                    fw.mm(pn[:, pp * 128:(pp + 1) * 128], AR[cb][:, p, 0:128], BT[cb][:, p, :])
                fw.tt(NTs[qb][0][:], pn[:].rr("p (a b) -> p a b", a=4), mNT.us(1).bc([128, 4, 128]), ALU.mult)
                Ncur = ABs[qb][:, :, 0:128]
                NTcur = NTs[qb][0][:]
                fw.tt(Ps[qb][0][:], Ncur, identb[:].us(1).bc([128, 4, 128]), ALU.add, e="pool")
                Pcur = Ps[qb][0][:]
                yield
                for lev in range(1, 6):
                    j = lev % 2
                    pnt = pbank()
                    for pp in range(4):
                        fw.mm(pnt[:, pp * 128:(pp + 1) * 128], Ncur[:, pp, :], NTcur[:, pp, :])
                    fw.copy(NTs[qb][j][:], pnt[:].rr("p (a b) -> p a b", a=4), e="act")
                    yield
                    if lev <= 4:
                        pnn = pbank()
                        for pp in range(4):
                            fw.mm(pnn[:, pp * 128:(pp + 1) * 128], NTcur[:, pp, :], Ncur[:, pp, :])
                        fw.copy(Ns[qb][j][:], pnn[:].rr("p (a b) -> p a b", a=4), e="dve")
                        Nnext = Ns[qb][j][:]
                    NTnext = NTs[qb][j][:]
                    pp_ = pbank()
                    for pp in range(4):
                        fw.mm(pp_[:, pp * 128:(pp + 1) * 128], identb[:], Pcur[:, pp, :], start=True, stop=False)
                        fw.mm(pp_[:, pp * 128:(pp + 1) * 128], NTnext[:, pp, :], Pcur[:, pp, :], start=False, stop=True)
                    fw.copy(Ps[qb][j][:], pp_[:].rr("p (a b) -> p a b", a=4), e="dve")
                    Pcur = Ps[qb][j][:]
                    NTcur = NTnext
                    if lev <= 4:
                        Ncur = Nnext
                Minv = Pcur
                yield
                pv_ = pbank()
                pvb = pv_.v(pv_.h[:].bitcast(BF16)[:, 0:512].rearrange("p (a b) -> p a b", a=4))
                for pp in range(4):
                    fw.tr(pvb[:, pp, :], VB[cb][:, p0 + pp, :], identb[:])
                fw.copy(VTs[qb][:], pvb, e="act")
                minv_of[(pos, q)] = Minv
                yield
                pt_ = pbank()
                ptb = pt_.v(pt_.h[:].bitcast(BF16).rearrange("p (a c b) -> p a c b", a=4, c=2))
                for pp in range(4):
                    p = p0 + pp
                    fw.tr(ptb[:, pp, 0, :], BT[cb][:, p, :], identb[:])
                    fw.tr(ptb[:, pp, 1, :], KT[cb][:, p, :], identb[:])
                fw.copy(BKT[qb][:], ptb, e="act")
                yield

            def state_body(pos, q, cb, cs, wl):
                qb = pos * 2 + q
                sq_ = q
                p0 = 4 * q
                Minv = minv_of[(pos, q)]
                yield
                pg = pbank()
                for pp in range(4):
                    p = p0 + pp
                    fw.mm(pg[:, pp * 128:(pp + 1) * 128], AR[cb][:, p, 0:128], S0bq[q][:, pp, :], start=True, stop=False)
                    fw.mm(pg[:, pp * 128:(pp + 1) * 128], AKs[qb][:, pp, 0:128], VTs[qb][:, pp, :], start=False, stop=True)
                fw.copy(GTs[sq_][:], pg[:].rr("p (a b) -> p a b", a=4), e="dve")
                yield
                pu = pbank()
                for pp in range(4):
                    fw.mm(pu[:, pp * 128:(pp + 1) * 128], Minv[:, pp, :], GTs[sq_][:, pp, :])
                fw.copy(UTs[sq_][:], pu[:].rr("p (a b) -> p a b", a=4), e="act")
                if not so:
                    yield
                    py = pbank()
                    for pp in range(4):
                        p = p0 + pp
                        o = py[:, pp * 128:(pp + 1) * 128]
                        fw.mm(o, S0bq[q][:, pp, :], AR[cb][:, p, 128:256], start=True, stop=False)
                        fw.mm(o, UTs[sq_][:, pp, :], ABs[qb][:, pp, 128:256], start=False, stop=False)
                        fw.mm(o, VTs[qb][:, pp, :], AKs[qb][:, pp, 128:256], start=False, stop=True)
                    py4 = py[:].rr("p (a b) -> p a b", a=4)
                    if d == 0:
                        fw.copy(YB[b][0:64, p0:p0 + 4, cs], py4[0:64, :, 0:64], e="act")
                        fw.copy(YB[b][64:128, p0:p0 + 4, cs], py4[64:128, :, 64:128], e="dve")
                    else:
                        fw.tt(YB[b][0:64, p0:p0 + 4, cs], py4[0:64, :, 0:64], Y1[b][0:64, p0:p0 + 4, cs], ALU.add)
                        fw.tt(YB[b][64:128, p0:p0 + 4, cs], py4[64:128, :, 64:128], Y1[b][64:128, p0:p0 + 4, cs], ALU.add)
                    yield
                pd_ = pbank()
                for pp in range(4):
                    o = pd_[:, pp * 128:(pp + 1) * 128]
                    fw.mm(o, BKT[qb][:, pp, 0, :], UTs[sq_][:, pp, :], start=True, stop=False)
                    fw.mm(o, BKT[qb][:, pp, 1, :], VTs[qb][:, pp, :], start=False, stop=True)
                wlb = wl[:, p0:p0 + 4].us(2).bc([128, 4, 128])
                fw.tt(stmp[sq_][:], S0q[q][:], wlb, ALU.mult, e="pool")
                fw.tt(S0q[q][:], pd_[:].rr("p (a b) -> p a b", a=4), wlb, ALU.mult)
                fw.tt(S0q[q][:], S0q[q][:], stmp[sq_][:], ALU.add)
                fw.copy(S0bq[q][:], S0q[q][:], e="act")
                yield

            def run_gens(gens):
                alive = list(gens)
                while alive:
                    for g_ in list(alive):
                        try:
                            next(g_)
                        except StopIteration:
                            alive.remove(g_)

            run_gens([pre_body(pos, q, *cinfo[pos]) for pos in range(2) for q in range(2)])
            for pos in range(2):
                run_gens([state_body(pos, q, *cinfo[pos]) for q in range(2)])
            if not so:
                fw.dma(V(S["YRW"].ap()[:, ti * 128:(ti + 1) * 128].rearrange("(c p) t -> p c t", p=128), None), YB[b][:], yb_sem[b])
        for q_ in range(2):
            dst = stx_rw if so else st_rw_out
            if so or d == 0:
                fw.dma(V(dst.ap()[:, 4 * q_:4 * q_ + 4, :], None), S0q[q_][:], (ssem, ssem2)[q_])
        fw.barrier()
        st.close()

    S["YM"] = dscr("s_YM", [2048, TP])
    st_m_out = nc.dram_tensor("st_m_out", [128, 32, 64], F32, kind="ExternalOutput")
    mc_d = din("mc", [128, 2, 2, 128])
    ones = sb("ones", [128, 128])
    fw.memset(ones[:], 1.0)

    def mamba_phase(d, so=False):
        st = ExitStack()
        Ss = S2 if so else S

        def sbp(name, shape, dt=F32):
            return T(st.enter_context(nc.sbuf_tensor(("mbs_" if so else "mb%d_" % d) + name, list(shape), dt)), name)

        mcst = sbp("mcst", [128, 2, 2, 128])
        fw.dma(mcst[:], V(mc_d.ap(), None), fw.dsem("c11"))
        XS = [sbp("xs%d" % b, [128, 16, 128]) for b in range(2)]
        Bb = [sbp("bb%d" % b, [128, 8, 128], BF16) for b in range(2)]
        Cb = [sbp("cb%d" % b, [128, 8, 128], BF16) for b in range(2)]
        DT = [sbp("dt%d" % b, [128, 4, 32]) for b in range(2)]
        Y1 = [sbp("y1_%d" % b, [128, 16, 128]) for b in range(2)]
        YB = [sbp("yb_%d" % b, [128, 16, 128]) for b in range(2)]
        sems = [[fw.dsem("mbld%d_%d" % (j, b)) for j in range(4)] for b in range(2)]
        psems = [[fw.dsem("mbldp%d_%d" % (j, b)) for j in range(2)] for b in range(2)]
        yb_sem = [fw.dsem("mbyb_%d" % b) for b in range(2)]
        dAexp = sbp("dAexp", [128, 32, 128])
        cs_tok = sbp("cs_tok", [128, 32])
        csl = sbp("csl", [128, 32])
        ecl_last = sbp("ecl_last", [128, 32])
        decs = sbp("decs", [128, 32])
        E = [sbp("E%d" % b, [128, 8, 128]) for b in range(2)]
        ecsR = [sbp("ecsR%d" % b, [128, 8, 128]) for b in range(2)]
        MT = sbp("MT", [128, 32, 128], BF16)
        Csc = sbp("Csc", [128, 32, 128], BF16)
        CBm = sbp("CBm", [128, 8, 128])
        xdt = sbp("xdt", [128, 32, 64], BF16)
        xdd = sbp("xdd", [128, 32, 64], BF16)
        Btok = sbp("Btok", [128, 8, 128], BF16)
        hS = sbp("hS", [128, 32, 64])
        hb = sbp("hb", [128, 32, 64], BF16)
        htmp = sbp("htmp", [128, 32, 64])
        ssem = fw.dsem("mbstate")
        if d == 0:
            fw.memset(hS[:], 0.0)
        else:
            fw.dma(hS[:], V(stx_m.ap(), None), ssem)
            fw.ts(hS[:].rr("p a b -> p (a b)"), hS[:].rr("p a b -> p (a b)"), sel[:, 0:1], None, ALU.mult)
        fw.copy(hb[:], hS[:], e="act")
        tri = mcst[:, d, 0, :]
        lst = mcst[:, d, 1, :]
        t_last = 127 if d == 0 else 0
        NCH = T_loc // 128
        tiles = list(range(NCH)) if d == 0 else list(range(NCH - 1, -1, -1))
        pbc = {"n": 0}

        def pbank():
            j = pbc["n"] % 8
            pbc["n"] += 1
            return PB[j]

        def load_tile(n):
            ti = tiles[n]
            b = n % 2
            cs_ = slice(ti * 128, (ti + 1) * 128)
            xb = Ss["XBC"].ap()
            fw.dma(XS[b][:], V(xb[0:2048, cs_].rearrange("(c p) t -> p c t", p=128), None), sems[b][0])
            fw.dma(DT[b][:], V(Ss["DTS"].ap()[cs_], None), sems[b][1])
            fw.dma(Bb[b][:], V(xb[2048:3072, cs_].rearrange("(c p) t -> p c t", p=128), None), psems[b][0], q="pool")
            if not so:
                fw.dma(Cb[b][:], V(xb[3072:4096, cs_].rearrange("(c p) t -> p c t", p=128), None), psems[b][1], q="pool")
            if d == 1:
                fw.dma(Y1[b][:], V(S["YM"].ap()[:, cs_].rearrange("(c p) t -> p c t", p=128), None), sems[b][2])

        load_tile(0)
        for n in range(NCH):
            ti = tiles[n]
            b = n % 2
            if n + 1 < NCH:
                load_tile(n + 1)
            dA = DT[b][:, 2 + d, :]
            dtv = DT[b][:, d, :]
            if so:
                pc = pbank()
                fw.mm(pc[:, 0:32], tri, dA)
                fw.copy(cs_tok[:], pc[:, 0:32], e="act")
                pc2 = pbank()
                fw.mm(pc2[:, 0:32], ones[:], dA)
                fw.copy(csl[:], pc2[:, 0:32], e="dve")
            if not so:
                fw.tt(dAexp[:], dA.us(2).bc([128, 32, 128]), tri.us(1).bc([128, 32, 128]), ALU.mult)
                pc = pbank()
                fw.mm(pc[:, 0:32], tri, dA)
                fw.copy(cs_tok[:], pc[:, 0:32], e="act")
                for half in range(2):
                    pcb = pbank()
                    for gg in range(4):
                        g = half * 4 + gg
                        fw.mm(pcb[:, gg * 128:(gg + 1) * 128], Bb[b][:, g, :], Cb[b][:, g, :])
                    fw.tt(CBm[:, half * 4:half * 4 + 4, :], pcb[:].rr("p (a b) -> p a b", a=4), tri.us(1).bc([128, 4, 128]), ALU.mult)
                for o in range(4):
                    ob = o % 2
                    pD = [pbank(), pbank()]
                    pR = [pbank(), pbank()]
                    for j in range(2):
                        rhs = dAexp[:, o * 8 + j * 4:o * 8 + j * 4 + 4, :].rr("p a b -> p (a b)")
                        fw.mm(pD[j][:], lst, rhs)
                        fw.mm(pR[j][:], ones[:], rhs)
                    for j in range(2):
                        hs = slice(o * 8 + j * 4, o * 8 + j * 4 + 4)
                        g = o * 2 + j
                        fw.act(E[ob][:, j * 4:j * 4 + 4, :], pD[j][:].rr("p (a b) -> p a b", a=4), AF.Exp)
                        fw.tt(MT[:, hs, :], E[ob][:, j * 4:j * 4 + 4, :], CBm[:, g:g + 1, :].bc([128, 4, 128]), ALU.mult)
                        pR4 = pR[j][:].rr("p (a b) -> p a b", a=4)
                        fw.copy(csl[:, hs], pR4[:, :, t_last], e="dve")
                        fw.act(ecsR[ob][:, j * 4:j * 4 + 4, :], pR4, AF.Exp)
                        fw.tt(Csc[:, hs, :], ecsR[ob][:, j * 4:j * 4 + 4, :], Cb[b][:, g:g + 1, :].bc([128, 4, 128]), ALU.mult, e="dve")
            fw.act(ecl_last[:], csl[:], AF.Exp)
            fw.tt(decs[:], csl[:], cs_tok[:], ALU.subtract)
            fw.act(decs[:], decs[:], AF.Exp)
            for q in range(4):
                px = pbank()
                for cc in range(4):
                    c = q * 4 + cc
                    fw.tr(px[:, cc * 128:(cc + 1) * 128], XS[b][:, c, :], ident[:])
                hs = slice(q * 8, q * 8 + 8)
                fw.tt(xdt[:, hs, :], px[:].rr("p (a b) -> p a b", a=8), dtv[:, hs].us(2).bc([128, 8, 64]), ALU.mult)
                fw.tt(xdd[:, hs, :], xdt[:, hs, :], decs[:, hs].us(2).bc([128, 8, 64]), ALU.mult, e="pool")
            if not so:
                for q in range(4):
                    py = pbank()
                    for cc in range(4):
                        for hh in range(2):
                            h = (q * 4 + cc) * 2 + hh
                            o_ = py[64 * hh:64 * hh + 64, cc * 128:(cc + 1) * 128]
                            kw_ = {"tile_position": (0, 64)} if hh == 1 else {}
                            fw.mm(o_, xdt[:, h, :], MT[:, h, :], start=True, stop=False, **kw_)
                            fw.mm(o_, hb[:, h, :], Csc[:, h, :], start=False, stop=True, **kw_)
                    py4 = py[:].rr("p (a b) -> p a b", a=4)
                    if d == 0:
                        fw.copy(YB[b][:, q * 4:q * 4 + 4, :], py4, e="act")
                    else:
                        fw.tt(YB[b][:, q * 4:q * 4 + 4, :], py4, Y1[b][:, q * 4:q * 4 + 4, :], ALU.add)
                fw.dma(V(S["YM"].ap()[:, ti * 128:(ti + 1) * 128].rearrange("(c p) t -> p c t", p=128), None), YB[b][:], yb_sem[b])
            pt_ = pbank()
            ptb = pt_.v(pt_.h[:].bitcast(BF16).rearrange("p (a b) -> p a b", a=8))
            for g in range(8):
                fw.tr(ptb[:, g, :], Bb[b][:, g, :], identb[:])
            fw.copy(Btok[:], ptb, e="act")
            fw.tt(htmp[:], hS[:], ecl_last[:].us(2).bc([128, 32, 64]), ALU.mult, e="pool")
            for q in range(4):
                pn_ = pbank()
                for gg in range(2):
                    g = q * 2 + gg
                    fw.mm(pn_[:, gg * 256:(gg + 1) * 256], Btok[:, g, :], xdd[:, 4 * g:4 * g + 4, :].rr("p a b -> p (a b)"))
                hs = slice(q * 8, q * 8 + 8)
                fw.tt(hS[:, hs, :], pn_[:].rr("p (a b) -> p a b", a=8), htmp[:, hs, :], ALU.add)
            fw.copy(hb[:], hS[:], e="act")
        if so:
            fw.dma(V(stx_m.ap(), None), hS[:], ssem)
        elif d == 0:
            fw.dma(V(st_m_out.ap(), None), hS[:], ssem)
        fw.barrier()
        st.close()

    ALPHA = 2.0 ** 0.25
    LN_EPS = 1e-5
    GN_EPS = 64e-5
    mem_d = din("mem", [256, D])
    w_br_d = din("w_br", [1024, D])
    w_bm_d = din("w_bm", [D, D])
    w_o_d = din("w_o", [D, D])
    w_q_d = din("w_q", [D, D])
    w_kv_d = din("w_kv", [D, 2 * D])
    w_co_d = din("w_co", [D, D])
    w_up_d = din("w_up", [D, 4 * D])
    w_down_d = din("w_down", [4 * D, D])
    y_out = nc.dram_tensor("y_out", [T_loc, D], F32, kind="ExternalOutput")

    def wview(w):
        return w.ap().rearrange("(k p) n -> p k n", p=128)

    def phase3():
        st = ExitStack()

        def sbp(name, shape, dt=F32):
            return T(st.enter_context(nc.sbuf_tensor("p3_" + name, list(shape), dt)), name)

        F32A = sbp("F32A", [128, NK, 512])
        BFA = sbp("BFA", [128, NK, 512], BF16)
        BFB = sbp("BFB", [128, NK, 512], BF16)
        BFC = sbp("BFC", [128, NK, 512], BF16)
        BFD = sbp("BFD", [128, 8, 512], BF16)
        HM = sbp("HM", [128, 16, 512], BF16)
        Kt = sbp("Kt", [128, NK, 256], BF16)
        Vt = sbp("Vt", [128, 2, D], BF16)
        onesb = sbp("onesb", [128, 128], BF16)
        ksc = sbp("ksc", [128, 4])
        fw.copy(onesb[:], ones[:], e="act")
        NWB = 3
        WB = [sbp("wb%d" % j, [128, 4096], BF16) for j in range(NWB)]
        wb_sem = [fw.dsem("p3wb%d" % j) for j in range(NWB)]
        NL = 6
        LB = [sbp("lb%d" % j, [128, 512]) for j in range(NL)]
        lb_sem = [fw.dsem("p3lb%d" % j) for j in range(NL)]
        NTMP = 5
        TMP = [sbp("tmp%d" % j, [128, 512]) for j in range(NTMP)]
        xs = [sbp("xs%d" % j, [128, D]) for j in range(2)]
        xs_sem = [fw.dsem("p3xs%d" % j) for j in range(2)]
        ymp = [sbp("ymp%d" % j, [128, 2, 512]) for j in range(1)]
        sqp = [sbp("sqp%d" % j, [128, 2, 512]) for j in range(1)]
        expS = [sbp("expS%d" % j, [128, 2, 512], BF16) for j in range(1)]
        ded = {nm: sbp("ded_" + nm, [128, 512]) for nm in ("mean", "rstd", "cst", "rs")}
        cnt = {"pb": 0, "lb": 0, "tmp": 0}

        def pbank():
            j = cnt["pb"] % 8
            cnt["pb"] += 1
            return PB[j]

        def tmp():
            j = cnt["tmp"] % NTMP
            cnt["tmp"] += 1
            return TMP[j]

        def ld(name, row0, t0):
            j = cnt["lb"] % NL
            cnt["lb"] += 1
            fw.dma(LB[j][:], V(S[name].ap()[row0:row0 + 128, t0:t0 + 512], None), lb_sem[j])
            return LB[j]

        class WS:
            def __init__(self):
                self.specs = []
                self.issued = 0
                self.tiles = {}

            def add(self, wv, k0, nk, col0, ncols):
                self.specs.append((wv, k0, nk, col0, ncols))
                return len(self.specs) - 1

            def _issue(self, n):
                wv, k0, nk, col0, ncols = self.specs[n]
                j = n % NWB
                tv = WB[j][:, 0:nk * ncols].rr("p (k n) -> p k n", k=nk)
                fw.dma(tv, V(wv[:, k0:k0 + nk, col0:col0 + ncols], None), wb_sem[j], q="pool")
                self.tiles[n] = tv

            def get(self, n):
                while self.issued < min(len(self.specs), n + NWB):
                    self._issue(self.issued)
                    self.issued += 1
                return self.tiles.pop(n)

        w_in_v3 = w_in_v

        def dense(ws_ids, ws, src, nk, consume):
            pass

        def layer_norm(gname, bname):
            ps1 = pbank()
            ps2 = pbank()
            for c in range(NK):
                sq = tmp()
                fw.act(sq[:], F32A[:, c, :], AF.Square)
                fw.mm(ps1[:], ones[:], F32A[:, c, :], start=(c == 0), stop=(c == NK - 1))
                fw.mm(ps2[:], ones[:], sq[:], start=(c == 0), stop=(c == NK - 1), sig=True)
            mean = ded["mean"]
            fw.act(mean[:], ps1[:], AF.Copy, scale=1.0 / D)
            msq = tmp()
            fw.act(msq[:], ps1[:], AF.Square, scale=1.0 / D)
            rstd = ded["rstd"]
            fw.stt(rstd[:], ps2[:], 1.0 / D, msq[:], ALU.mult, ALU.subtract)
            fw.act(rstd[:], rstd[:], AF.Ln, bias=LN_EPS)
            fw.act(rstd[:], rstd[:], AF.Exp, scale=-0.5)
            for c in range(NK):
                t_ = tmp()
                fw.tt(t_[:], F32A[:, c, :], mean[:], ALU.subtract)
                fw.tt(t_[:], t_[:], rstd[:], ALU.mult)
                fw.act(F32A[:, c, :], t_[:], AF.Identity, scale=pv(gname, c), bias=pv(bname, c))
                fw.copy(BFA[:, c, :], F32A[:, c, :], e="dve")

        memT = BFB
        memTv = memT[:, :, 0:256]
        for mb in range(2):
            fw.dma(xs[mb][:], V(mem_d.ap()[mb * 128:(mb + 1) * 128, :], None), xs_sem[mb])
            for kq in range(4):
                pt = pbank()
                for k4 in range(4):
                    k = kq * 4 + k4
                    fw.tr(pt[:, k4 * 128:(k4 + 1) * 128], xs[mb][:, k * 128:(k + 1) * 128], ident[:])
                fw.copy(memT[:, kq * 4:(kq + 1) * 4, mb * 128:(mb + 1) * 128], pt[:].rr("p (a b) -> p a b", a=4), e="act")
        ws = WS()
        wkv = wview(w_kv_d)
        ids = [ws.add(wkv, 0, NK, c * 256, 256) for c in range(16)]
        for c in range(8):
            wt = ws.get(ids[c])
            for oo in range(2):
                oc = c * 2 + oo
                pk = pbank()
                for k in range(NK):
                    fw.mm(pk[:, 0:256], wt[:, k, oo * 128:(oo + 1) * 128], memTv[:, k, :], start=(k == 0), stop=(k == NK - 1))
                fw.copy(Kt[:, oc, :], pk[:, 0:256], e="act")
        for c in range(8):
            wt = ws.get(ids[8 + c])
            for mb in range(2):
                pvv = pbank()
                for k in range(NK):
                    fw.mm(pvv[:, 0:256], memT[:, k, mb * 128:(mb + 1) * 128], wt[:, k, :], start=(k == 0), stop=(k == NK - 1))
                fw.copy(Vt[:, mb, c * 256:(c + 1) * 256], pvv[:, 0:256], e="dve")
        for hd in range(4):
            pk2 = pbank()
            for kc in range(4):
                sqk = tmp()
                sqkb = sqk[:, 0:128].ap.bitcast(BF16)
                sqv = V(sqkb, sqk.buf)
                fw.act(sqv, Kt[:, hd * 4 + kc, :], AF.Square)
                fw.mm(pk2[:, 0:256], onesb[:], sqv, start=(kc == 0), stop=(kc == 3))
            mx = tmp()
            i_ = nc.vector
            r_, w_ = fw._bufs([pk2[:]]), fw._bufs([mx[:]])
            fw._deps("dve", r_, w_)
            ins = nc.vector.tensor_reduce(mx[:, 0:1].ap, pk2[:, 0:256].ap, AX.X, ALU.max)
            fw._done(ins, "dve", 1, r_, w_)
            fw.ts(ksc[:, hd:hd + 1], mx[:, 0:1], 1.0 / 512.0, None, ALU.mult)

        NT3 = T_loc // 512
        for i in range(NT3):
            t0 = i * 512
            ws = WS()
            wbr, wbm, wo, wq, wco, wup, wdn = [wview(w) for w in (w_br_d, w_bm_d, w_o_d, w_q_d, w_co_d, w_up_d, w_down_d)]
            id_z = [ws.add(w_in_v3, 0, NK, C_Z + c * 256, 256) for c in range(8)]
            id_d = []
            for c in range(8):
                id_d.append((ws.add(wbr, 0, 8, c * 256, 256), ws.add(w_in_v3, 0, NK, C_GATES + c * 256, 256),
                             ws.add(wbm, 0, NK, c * 256, 256), ws.add(w_in_v3, 0, NK, C_GATES + 2048 + c * 256, 256)))
            id_o = [ws.add(wo, 0, NK, c * 256, 256) for c in range(8)]
            id_q = [ws.add(wq, 0, NK, c * 256, 256) for c in range(8)]
            id_co = [ws.add(wco, 0, NK, c * 256, 256) for c in range(8)]
            id_up, id_dn = [], []
            for hf in range(4):
                id_up.append([ws.add(wup, 0, NK, hf * 2048 + c * 256, 256) for c in range(8)])
                id_dn.append([ws.add(wdn, hf * 16, 16, c * 256, 256) for c in range(8)])
            for j in range(4):
                xb = xs[j % 2]
                fw.dma(xb[:], V(x_ext.ap()[2 + t0 + j * 128:2 + t0 + (j + 1) * 128, :], None), xs_sem[j % 2])
                for kq in range(4):
                    pt = pbank()
                    for k4 in range(4):
                        k = kq * 4 + k4
                        fw.tr(pt[:, k4 * 128:(k4 + 1) * 128], xb[:, k * 128:(k + 1) * 128], ident[:])
                    pt4 = pt[:].rr("p (a b) -> p a b", a=4)
                    fw.copy(F32A[:, kq * 4:(kq + 1) * 4, j * 128:(j + 1) * 128], pt4, e="act")
                    fw.copy(BFA[:, kq * 4:(kq + 1) * 4, j * 128:(j + 1) * 128], pt4, e="dve")
            for c in range(8):
                y = ld("YRW", c * 128, t0)
                bon = ld("BON", c * 128, t0)
                gg = ld("G", c * 128, t0)
                sq = tmp()
                fw.act(sq[:], y[:], AF.Square)
                p1 = pbank()
                fw.mm(p1[:], blk[:], y[:])
                p2 = pbank()
                fw.mm(p2[:], blk[:], sq[:])
                m = tmp()
                fw.act(m[:], p1[:], AF.Copy, scale=1.0 / 64)
                msq = tmp()
                fw.act(msq[:], p1[:], AF.Square, scale=1.0 / 64)
                var = tmp()
                fw.stt(var[:], p2[:], 1.0 / 64, msq[:], ALU.mult, ALU.subtract)
                fw.act(var[:], var[:], AF.Ln, bias=GN_EPS)
                fw.act(var[:], var[:], AF.Exp, scale=-0.5)
                fw.tt(y[:], y[:], m[:], ALU.subtract)
                fw.tt(y[:], y[:], var[:], ALU.mult)
                fw.act(y[:], y[:], AF.Identity, scale=pv("gn_g", c), bias=pv("gn_b", c))
                fw.tt(y[:], y[:], bon[:], ALU.add)
                fw.tt(BFD[:, c, :], y[:], gg[:], ALU.mult)
            for c in range(NK):
                if c % 2 == 0:
                    wz = ws.get(id_z[c // 2])
                pb_ = 0
                ym = ld("YM", c * 128, t0)
                xv = ld("XBC", c * 128, t0)
                pz = pbank()
                for k in range(NK):
                    fw.mm(pz[:], wz[:, k, (c % 2) * 128:(c % 2 + 1) * 128], BFA[:, k, :], start=(k == 0), stop=(k == NK - 1))
                fw.stt(ym[:], xv[:], pv("m_d", c), ym[:], ALU.mult, ALU.add)
                sz = tmp()
                fw.act(sz[:], pz[:], AF.Silu)
                fw.tt(ymp[pb_][:, c % 2, :], ym[:], sz[:], ALU.mult)
                fw.act(sqp[pb_][:, c % 2, :], ymp[pb_][:, c % 2, :], AF.Square)
                if c % 2 == 1:
                    pss = pbank()
                    fw.mm(pss[:], ones[:], sqp[pb_][:, 0, :], start=True, stop=False)
                    fw.mm(pss[:], ones[:], sqp[pb_][:, 1, :], start=False, stop=True)
                    rms = tmp()
                    fw.act(rms[:], pss[:], AF.Ln, scale=1.0 / 256, bias=LN_EPS)
                    fw.act(rms[:], rms[:], AF.Exp, scale=-0.5)
                    for cc in range(2):
                        fw.stt(BFB[:, c - 1 + cc, :], ymp[pb_][:, cc, :], pv("m_norm_g", c - 1 + cc), rms[:], ALU.mult, ALU.mult)
            for c in range(8):
                wt = ws.get(id_d[c][0])
                pu = [pbank(), pbank()]
                for oo in range(2):
                    for k in range(8):
                        fw.mm(pu[oo][:], wt[:, k, oo * 128:(oo + 1) * 128], BFD[:, k, :], start=(k == 0), stop=(k == 7))
                wt = ws.get(id_d[c][1])
                t1 = [tmp(), tmp()]
                for oo in range(2):
                    pg = pbank()
                    for k in range(NK):
                        fw.mm(pg[:], wt[:, k, oo * 128:(oo + 1) * 128], BFA[:, k, :], start=(k == 0), stop=(k == NK - 1))
                    fw.act(t1[oo][:], pg[:], AF.Sigmoid)
                    fw.tt(t1[oo][:], t1[oo][:], pu[oo][:], ALU.mult)
                wt = ws.get(id_d[c][2])
                pm = [pbank(), pbank()]
                for oo in range(2):
                    for k in range(NK):
                        fw.mm(pm[oo][:], wt[:, k, oo * 128:(oo + 1) * 128], BFB[:, k, :], start=(k == 0), stop=(k == NK - 1))
                wt = ws.get(id_d[c][3])
                for oo in range(2):
                    oc = c * 2 + oo
                    pg2 = pbank()
                    for k in range(NK):
                        fw.mm(pg2[:], wt[:, k, oo * 128:(oo + 1) * 128], BFA[:, k, :], start=(k == 0), stop=(k == NK - 1))
                    sg2 = tmp()
                    fw.act(sg2[:], pg2[:], AF.Sigmoid)
                    fw.tt(sg2[:], sg2[:], pm[oo][:], ALU.mult)
                    fw.tt(BFC[:, oc, :], t1[oo][:], sg2[:], ALU.add)

            def proj_res(idl, src, first=True):
                for c in range(8):
                    wt = ws.get(idl[c])
                    for oo in range(2):
                        oc = c * 2 + oo
                        po = pbank()
                        for k in range(NK):
                            fw.mm(po[:], wt[:, k, oo * 128:(oo + 1) * 128], src[:, k, :], start=(k == 0), stop=(k == NK - 1))
                        fw.stt(F32A[:, oc, :], F32A[:, oc, :], ALPHA, po[:], ALU.mult, ALU.add)

            proj_res(id_o, BFC)
            layer_norm("ln1_g", "ln1_b")
            for c in range(8):
                wt = ws.get(id_q[c])
                for oo in range(2):
                    oc = c * 2 + oo
                    pq = pbank()
                    for k in range(NK):
                        fw.mm(pq[:], wt[:, k, oo * 128:(oo + 1) * 128], BFA[:, k, :], start=(k == 0), stop=(k == NK - 1))
                    fw.copy(BFB[:, oc, :], pq[:], e="act")
            inv = 1.0 / math.sqrt(512.0)
            for hd in range(4):
                eb = 0
                pq2 = pbank()
                for kc in range(4):
                    sqq = tmp()
                    sqv = V(sqq[:].ap.bitcast(BF16)[:, 0:512], sqq.buf)
                    fw.act(sqv, BFB[:, hd * 4 + kc, :], AF.Square)
                    fw.mm(pq2[:], onesb[:], sqv, start=(kc == 0), stop=(kc == 3))
                cst = ded["cst"]
                fw.act(cst[:], pq2[:], AF.Sqrt, scale=ksc[:, hd:hd + 1])
                for mc in range(2):
                    ps_ = pbank()
                    for kc in range(4):
                        fw.mm(ps_[:], Kt[:, hd * 4 + kc, mc * 128:(mc + 1) * 128], BFB[:, hd * 4 + kc, :], start=(kc == 0), stop=(kc == 3))
                    ein = tmp()
                    fw.stt(ein[:], ps_[:], inv, cst[:], ALU.mult, ALU.subtract)
                    fw.act(expS[eb][:, mc, :], ein[:], AF.Exp)
                psum_ = pbank()
                for mc in range(2):
                    fw.mm(psum_[:], onesb[:], expS[eb][:, mc, :], start=(mc == 0), stop=(mc == 1))
                rs = ded["rs"]
                r_, w_ = fw._bufs([psum_[:]]), fw._bufs([rs[:]])
                fw._deps("dve", r_, w_)
                ins = nc.vector.reciprocal(rs[:].ap, psum_[:].ap)
                fw._done(ins, "dve", 1, r_, w_)
                for dc in range(4):
                    po = pbank()
                    col = hd * 512 + dc * 128
                    for mc in range(2):
                        fw.mm(po[:], Vt[:, mc, col:col + 128], expS[eb][:, mc, :], start=(mc == 0), stop=(mc == 1))
                    fw.tt(BFC[:, hd * 4 + dc, :], po[:], rs[:], ALU.mult)
            proj_res(id_co, BFC)
            layer_norm("ln2_g", "ln2_b")
            for hf in range(4):
                for c in range(8):
                    wt = ws.get(id_up[hf][c])
                    for oo in range(2):
                        oc = c * 2 + oo
                        ph = pbank()
                        for k in range(NK):
                            fw.mm(ph[:], wt[:, k, oo * 128:(oo + 1) * 128], BFA[:, k, :], start=(k == 0), stop=(k == NK - 1))
                        rl = tmp()
                        fw.act(rl[:], ph[:], AF.Relu)
                        fw.tt(HM[:, oc, :], rl[:], rl[:], ALU.mult)
                for c in range(8):
                    wt = ws.get(id_dn[hf][c])
                    for oo in range(2):
                        oc = c * 2 + oo
                        po = pbank()
                        for k in range(16):
                            fw.mm(po[:], wt[:, k, oo * 128:(oo + 1) * 128], HM[:, k, :], start=(k == 0), stop=(k == 15))
                        if hf == 0:
                            fw.stt(F32A[:, oc, :], F32A[:, oc, :], ALPHA, po[:], ALU.mult, ALU.add)
                        else:
                            fw.tt(F32A[:, oc, :], F32A[:, oc, :], po[:], ALU.add)
            layer_norm("ln3_g", "ln3_b")
            for j in range(4):
                ob_ = xs[j % 2]
                for kq in range(4):
                    pt = pbank()
                    for k4 in range(4):
                        k = kq * 4 + k4
                        fw.tr(pt[:, k4 * 128:(k4 + 1) * 128], F32A[:, k, j * 128:(j + 1) * 128], ident[:])
                    fw.copy(ob_[:, kq * 512:(kq + 1) * 512], pt[:], e=("act" if kq % 2 else "dve"))
                fw.dma(V(y_out.ap()[t0 + j * 128:t0 + (j + 1) * 128, :], None), ob_[:], xs_sem[j % 2])
        fw.barrier()
        st.close()

    if 6 in phases:
        phase0(lite=True)
        fw.barrier()
        rwkv_phase(0, so=True)
        mamba_phase(0, so=True)
    if 0 in phases:
        phase0()
    fw.barrier()
    if 1 in phases:
        rwkv_phase(0)
    if 3 in phases:
        mamba_phase(0)
    if 2 in phases:
        rwkv_phase(1)
    if 4 in phases:
        mamba_phase(1)
    if 5 in phases:
        phase3()
    fw.barrier()
    es.close()
    return nc, fw


def host_params(inp, swap=False):
    g = lambda k: np.asarray(inp[k])[0]
    pvec = np.zeros((128, NPAR), np.float32)

    def put(name, vec, j0=0):
        vec = np.asarray(vec, np.float32)
        n = vec.shape[0]
        nch = (n + 127) // 128
        pad = np.zeros(nch * 128, np.float32)
        pad[:n] = vec
        pvec[:, POFF[name] + j0:POFF[name] + j0 + nch] = pad.reshape(nch, 128).T

    mup, mun = g("rw_mu_prev"), g("rw_mu_next")
    if swap:
        mup, mun = mun, mup
    for nm, mu in (("mup", mup), ("mun", mun)):
        put(nm, mu[0:3072], 0)
        put(nm, mu[3072:3168], 24)
        put(nm, mu[3168:3264], 25)
        put(nm, mu[3264:3520], 26)
    dirs = (1, 0) if swap else (0, 1)
    for d in range(2):
        put("w0", g("rw_w0")[dirs[d]], 8 * d)
        put("a0", g("rw_a0")[dirs[d]], 8 * d)
    put("k_k", g("rw_k_k"))
    put("k_a", g("rw_k_a"))
    put("r_k", g("rw_r_k").reshape(-1))
    put("gn_g", g("rw_gn_g"))
    put("gn_b", g("rw_gn_b"))
    cw = g("m_conv_w")
    if swap:
        cw = cw[::-1]
    for j in range(5):
        put("conv_w", cw[j], 32 * j)
    put("conv_b", g("m_conv_b"))
    put("m_norm_g", g("m_norm_g"))
    put("m_d", np.repeat(g("m_d"), 64))
    for nm in ("ln1_g", "ln1_b", "ln2_g", "ln2_b", "ln3_g", "ln3_b"):
        put(nm, g(nm))
    dtp = np.zeros((128, 2, 2, 4, 32), np.float32)
    for d in range(2):
        dtp[:, 0, d] = g("m_dt_bias")[dirs[d]][None, None, :]
        dtp[:, 1, d] = g("m_a_log")[dirs[d]][None, None, :]
    blk = np.zeros((128, 128), np.float32)
    blk[:64, :64] = 1.0
    blk[64:, 64:] = 1.0
    rwc = np.zeros((128, 2, 512), np.float32)
    idx = np.arange(128)
    hh, ss = idx // 64, idx % 64
    same = hh[:, None] == hh[None, :]
    lt = ss[:, None] < ss[None, :]
    le = ss[:, None] <= ss[None, :]
    rwc[:, 0, 0:128] = same & lt
    rwc[:, 0, 128:256] = same & le
    rwc[:, 0, 256:384] = same & lt.T
    rwc[:, 1, 0:128] = same & lt.T
    rwc[:, 1, 128:256] = same & le.T
    rwc[:, 1, 256:384] = same & lt
    mc = np.zeros((128, 2, 2, 128), np.float32)
    i128 = np.arange(128)
    mc[:, 0, 0] = i128[:, None] <= i128[None, :]
    mc[:, 0, 1] = i128[:, None] > i128[None, :]
    mc[:, 1, 0] = i128[:, None] >= i128[None, :]
    mc[:, 1, 1] = i128[:, None] < i128[None, :]
    rmask = np.ones((128, 1024), np.float32)
    rmask[:, ::64] = 0.0
    return dict(pvec=pvec, dtp=dtp, ident=np.eye(128, dtype=np.float32), blk64=blk, rwc=rwc, rmask=rmask,
                mc=mc,
                rw_w2=g("rw_w2"), rw_a2=g("rw_a2"), rw_g2=g("rw_g2"), w_in=g("w_in"),
                w_br=g("w_br"), w_bm=g("w_bm"), w_o=g("w_o"), w_q=g("w_q"), w_kv=g("w_kv"), w_co=g("w_co"),
                w_up=g("w_up"), w_down=g("w_down"))


T_CORE = 8192
_CACHE = {}


def _x_ext(xseq, start, T_loc, rev):
    NT0 = (T_loc + TV - 1) // TV
    TP = NT0 * TV
    L = xseq.shape[0]
    out = np.zeros((TP + 4, D), np.float32)
    if not rev:
        lo, hi = start - 2, start + T_loc + 2
        slo, shi = max(lo, 0), min(hi, L)
        out[slo - lo:shi - lo] = xseq[slo:shi]
    else:
        lo, hi = start - 2, start + T_loc + 2
        slo, shi = max(lo, 0), min(hi, L)
        seg = xseq[slo:shi][::-1]
        r0 = start + T_loc + 1 - (shi - 1)
        out[r0:r0 + seg.shape[0]] = seg
    return out


def kernel(**inputs):
    inp = {k: np.asarray(v) for k, v in inputs.items()}
    xp, xs_, mp, ms = inp["x_prompt"], inp["x_sample"], inp["mem_prompt"], inp["mem_sample"]
    T_loc = T_CORE
    if "nc" not in _CACHE:
        _CACHE["nc"] = build(T_loc)[0]
    nc = _CACHE["nc"]
    hp = [host_params(inp, swap=False), host_params(inp, swap=True)]
    cores = []
    for c in range(8):
        if c < 4:
            s, half = c // 2, c % 2
            cores.append(dict(x=xp[s], start=half * T_loc, rev=(half == 1), mem=mp[s]))
        else:
            cores.append(dict(x=xs_[c - 4], start=0, rev=False, mem=ms[c - 4]))
    base_maps = []
    xe = [_x_ext(cd["x"], cd["start"], T_loc, cd["rev"]) for cd in cores]
    for c, cd in enumerate(cores):
        own = hp[1 if cd["rev"] else 0]
        m = dict(own)
        m["x_ext"] = xe[c]
        m["mem"] = np.ascontiguousarray(cd["mem"], dtype=np.float32)
        if c < 4:
            partner = c ^ 1
            oth = hp[1 if cores[partner]["rev"] else 0]
            m["x_ext2"] = xe[partner]
            m["pvec2"] = oth["pvec"]
            m["dtp2"] = oth["dtp"]
            m["sel"] = np.ones((128, 1), np.float32)
        else:
            m["x_ext2"] = xe[c]
            m["pvec2"] = own["pvec"]
            m["dtp2"] = own["dtp"]
            m["sel"] = np.zeros((128, 1), np.float32)
        base_maps.append(m)
    res2 = run_bass_kernel_spmd(nc, base_maps, core_ids=list(range(8)))
    ys = [np.asarray(res2.results[c]["y_out"], np.float32) for c in range(8)]
    y_prompt = np.empty_like(xp)
    for c in range(4):
        s, half = c // 2, c % 2
        y_prompt[s, half * T_loc:(half + 1) * T_loc] = ys[c][::-1] if half == 1 else ys[c]
    y_sample = np.stack(ys[4:8], axis=0)
    return (y_prompt, y_sample)
```

```python
import math
from contextlib import ExitStack
import numpy as np
import concourse.bass as bass
import concourse.mybir as mybir
from concourse.bass_utils import run_bass_kernel_spmd

F32 = mybir.dt.float32
BF16 = mybir.dt.bfloat16
AF = mybir.ActivationFunctionType
ALU = mybir.AluOpType
AX = mybir.AxisListType

SAME_ENGINE_SYNC = True
MB_STOP = 99
DBG_NOWLOAD = False
DBG_NOSTORE = False
MB_SUB = 99

D = 2048
NK = 16
TT = 512
TV = 508
IN_COLS = 13792
C_R, C_K, C_V, C_DW, C_DA, C_DG = 0, 1024, 2048, 3072, 3168, 3264
C_Z, C_XBC, C_DT, C_GATES = 3520, 5568, 9664, 9696


class Buf:
    __slots__ = ("w", "r", "name", "ex")

    def __init__(self, name=""):
        self.w = None
        self.r = []
        self.name = name
        self.ex = False


class V:
    __slots__ = ("ap", "buf")

    def __init__(self, ap, buf):
        self.ap = ap
        self.buf = buf

    def __getitem__(self, idx):
        return V(self.ap[idx], self.buf)

    def rr(self, pat, **kw):
        return V(self.ap.rearrange(pat, **kw), self.buf)

    def bc(self, shape):
        return V(self.ap.to_broadcast(list(shape)), self.buf)

    def us(self, axis):
        return V(self.ap.unsqueeze(axis), self.buf)


class T:
    def __init__(self, h, name="", track=True):
        self.h = h
        self.buf = Buf(name) if track else None

    def __getitem__(self, idx):
        return V(self.h[idx], self.buf)

    def v(self, ap):
        return V(ap, self.buf)


class FW:
    def __init__(self, nc):
        self.nc = nc
        self.eng = {"pe": nc.tensor, "dve": nc.vector, "act": nc.scalar, "pool": nc.gpsimd, "sp": nc.sync}
        self.sem = {}
        self.cnt = {}
        for e in self.eng:
            self.sem[e] = nc.alloc_semaphore("sem_" + e)
            self.cnt[e] = 0
        self.seen = {e: {} for e in self.eng}
        self.n_inst = 0
        self.n_wait = 0

    def dsem(self, name):
        if name in self.sem:
            return name
        self.sem[name] = self.nc.alloc_semaphore("dsem_" + name)
        self.cnt[name] = 0
        return name

    def scan(self, out, d0, d1, initial, op0, op1):
        r, w = self._bufs([d0, d1, initial]), self._bufs([out])
        self._deps("dve", r, w)
        i = self.nc.vector.tensor_tensor_scan(out.ap, d0.ap, d1.ap, self._ap(initial), op0, op1)
        return self._done(i, "dve", 1, r, w)

    def _wait(self, e, dep):
        if dep is None:
            return
        key, val = dep
        if key == e and (e == "pe" or e == "sp" or not SAME_ENGINE_SYNC):
            return
        if self.seen[e].get(key, 0) >= val:
            return
        assert val <= self.cnt[key], "wait on a not-yet-signalled count (%s %d > %d): potential deadlock" % (key, val, self.cnt[key])
        self.seen[e][key] = val
        self.eng[e].wait_ge(self.sem[key], val)
        self.n_wait += 1

    def _deps(self, e, reads, writes):
        mx = {}
        for b in reads:
            if b.w is not None and mx.get(b.w[0], 0) < b.w[1]:
                mx[b.w[0]] = b.w[1]
            if b.ex:
                for k, v in b.r:
                    if k != e and mx.get(k, 0) < v:
                        mx[k] = v
        for b in writes:
            if b.w is not None and mx.get(b.w[0], 0) < b.w[1]:
                mx[b.w[0]] = b.w[1]
            for k, v in b.r:
                if mx.get(k, 0) < v:
                    mx[k] = v
        for k, v in mx.items():
            self._wait(e, (k, v))

    def _done(self, inst, key, inc, reads, writes, signal=True):
        if signal:
            self.cnt[key] += inc
            inst.then_inc(self.sem[key], inc)
            dep = (key, self.cnt[key])
        else:
            dep = (key, self.cnt[key] + inc)
        for b in reads:
            b.r.append(dep)
            if len(b.r) > 24:
                mx = {}
                for k, v in b.r:
                    if mx.get(k, 0) < v:
                        mx[k] = v
                b.r = list(mx.items())
        for b in writes:
            b.w = dep
            b.r = []
        self.n_inst += 1
        return dep

    @staticmethod
    def _bufs(vs):
        out = []
        for v in vs:
            if isinstance(v, V) and v.buf is not None and v.buf not in out:
                out.append(v.buf)
        return out

    @staticmethod
    def _ap(v):
        return v.ap if isinstance(v, V) else v

    def mm(self, out, lhsT, rhs, start=True, stop=True, sig=False, **kw):
        r, w = self._bufs([lhsT, rhs]), self._bufs([out])
        self._deps("pe", r, w)
        i = self.nc.tensor.matmul(out.ap, lhsT.ap, rhs.ap, start=start, stop=stop, **kw)
        return self._done(i, "pe", 1, r, w, signal=(stop or sig))

    def tr(self, out, in_, ident):
        r, w = self._bufs([in_, ident]), self._bufs([out])
        self._deps("pe", r, w)
        i = self.nc.tensor.transpose(out.ap, in_.ap, ident.ap)
        return self._done(i, "pe", 1, r, w)

    def act(self, out, in_, func, bias=None, scale=None, e="act", accum_out=None):
        r, w = self._bufs([in_, bias, scale]), self._bufs([out, accum_out])
        self._deps(e, r, w)
        kw = {}
        if bias is not None:
            kw["bias"] = self._ap(bias)
        if scale is not None:
            kw["scale"] = self._ap(scale)
        if accum_out is not None:
            kw["accum_out"] = self._ap(accum_out)
        i = self.eng[e].activation(out.ap, in_.ap, func, **kw)
        return self._done(i, e, 1, r, w)

    def tt(self, out, a, b, op, e="dve"):
        r, w = self._bufs([a, b]), self._bufs([out])
        self._deps(e, r, w)
        i = self.eng[e].tensor_tensor(out.ap, a.ap, b.ap, op)
        return self._done(i, e, 1, r, w)

    def ts(self, out, in0, s1, s2, op0, op1=None, e="dve"):
        r, w = self._bufs([in0, s1, s2]), self._bufs([out])
        self._deps(e, r, w)
        kw = {}
        if op1 is not None:
            kw["op1"] = op1
        i = self.eng[e].tensor_scalar(out.ap, in0.ap, self._ap(s1), self._ap(s2), op0, **kw)
        return self._done(i, e, 1, r, w)

    def stt(self, out, in0, scalar, in1, op0, op1, e="dve"):
        r, w = self._bufs([in0, scalar, in1]), self._bufs([out])
        self._deps(e, r, w)
        i = self.eng[e].scalar_tensor_tensor(out.ap, in0.ap, self._ap(scalar), in1.ap, op0, op1)
        return self._done(i, e, 1, r, w)

    def copy(self, out, in_, e="dve"):
        r, w = self._bufs([in_]), self._bufs([out])
        self._deps(e, r, w)
        if e == "act":
            i = self.nc.scalar.copy(out.ap, in_.ap)
        else:
            i = self.eng[e].tensor_copy(out.ap, in_.ap)
        return self._done(i, e, 1, r, w)

    def memset(self, out, val, e="dve"):
        w = self._bufs([out])
        self._deps(e, [], w)
        i = self.eng[e].memset(out.ap, val)
        return self._done(i, e, 1, [], w)

    def dma(self, out, in_, sem, q="sp", **kw):
        r, w = self._bufs([in_]), self._bufs([out])
        self._deps(q, r, w)
        i = self.eng[q].dma_start(out=out.ap, in_=in_.ap, **kw)
        return self._done(i, sem, 16, r, w)

    def collective(self, kind, op, groups, in_v, out_v, sem):
        r, w = self._bufs([in_v]), self._bufs([out_v])
        self._deps("pool", r, w)
        i = self.nc.gpsimd.collective_compute(kind, op=op, replica_groups=groups, ins=[in_v.ap], outs=[out_v.ap])
        return self._done(i, sem, 16, r, w)

    def barrier(self):
        for e in self.eng:
            for key, val in self.cnt.items():
                if val > 0:
                    self._wait(e, (key, val)) if key != e else None


class Ctx:
    pass


def _param_layout():
    off = {}
    n = 0
    for name, w in [("mup", 28), ("mun", 28), ("w0", 16), ("a0", 16), ("k_k", 8), ("k_a", 8), ("r_k", 8),
                    ("gn_g", 8), ("gn_b", 8), ("conv_w", 160), ("conv_b", 32), ("m_norm_g", 16), ("m_d", 16),
                    ("ln1_g", 16), ("ln1_b", 16), ("ln2_g", 16), ("ln2_b", 16), ("ln3_g", 16), ("ln3_b", 16)]:
        off[name] = n
        n += w
    return off, n


POFF, NPAR = _param_layout()
XOFF = {"c0": 0, "nk_k": 28, "omk_a": 36}
NX = 44


def build(T_loc, dbg=False, phases=(6, 0, 1, 2, 3, 4, 5)):
    NT0 = (T_loc + TV - 1) // TV
    TP = NT0 * TV
    XR = TP + 4
    nc = bass.Bass("TRN2", target_bir_lowering=False)
    fw = FW(nc)
    es = ExitStack()

    def din(name, shape, dt=F32):
        return nc.dram_tensor(name, list(shape), dt, kind="ExternalInput")

    def dscr(name, shape, dt=F32):
        return nc.dram_tensor(name, list(shape), dt, kind=("ExternalOutput" if dbg else "Internal"))

    x_ext = din("x_ext", [XR, D])
    x_ext2 = din("x_ext2", [XR, D])
    pvec2_d = din("pvec2", [128, NPAR])
    dtp2_d = din("dtp2", [128, 2, 2, 4, 32])
    sel_d = din("sel", [128, 1])
    w_in = din("w_in", [D, IN_COLS])
    pvec_d = din("pvec", [128, NPAR])
    ident_d = din("ident", [128, 128])
    blk_d = din("blk64", [128, 128])
    w2_d = din("rw_w2", [96, 1024])
    a2_d = din("rw_a2", [96, 1024])
    g2_d = din("rw_g2", [256, 1024])
    dtp_d = din("dtp", [128, 2, 2, 4, 32])

    S = {}
    for nm in ["R", "V", "A", "G", "BON", "KD1", "KD2", "B1", "B2", "LW1", "LW2"]:
        S[nm] = dscr("s_" + nm, [1024, TP])
    S["XBC"] = dscr("s_XBC", [4096, TP])
    S["DTS"] = dscr("s_DTS", [TP, 4, 32])
    S2 = {}
    for nm in ["V", "A", "KD1", "B1", "LW1"]:
        S2[nm] = dscr("s2_" + nm, [1024, TP])
    S2["XBC"] = dscr("s2_XBC", [4096, TP])
    S2["DTS"] = dscr("s2_DTS", [TP, 4, 32])
    stx_rw = nc.dram_tensor("stx_rw", [128, 8, 128], F32, kind="Internal")
    stx_m = nc.dram_tensor("stx_m", [128, 32, 64], F32, kind="Internal")

    def sb(name, shape, dt=F32):
        return T(es.enter_context(nc.sbuf_tensor("sb_" + name, list(shape), dt)), name)

    PB = [T(nc.alloc_psum_tensor("pb%d" % i, [128, 512], F32), "pb%d" % i) for i in range(8)]
    for t_ in PB:
        t_.buf.ex = True

    ident = sb("ident", [128, 128])
    blk = sb("blk", [128, 128])
    sel = sb("sel", [128, 1])
    dcp = fw.dsem("constp")
    fw.dma(ident[:], V(ident_d.ap(), None), fw.dsem("c2"))
    fw.dma(blk[:], V(blk_d.ap(), None), fw.dsem("c3"))
    fw.dma(sel[:], V(sel_d.ap(), None), fw.dsem("c12"))
    w_in_v = w_in.ap().rearrange("(k p) n -> p k n", p=128)
    CUR = {}
    PSETS = []
    for si, (pd_, dd_) in enumerate(((pvec_d, dtp_d), (pvec2_d, dtp2_d))):
        pvec_t = sb("pvec%d" % si, [128, NPAR])
        xpar_t = sb("xpar%d" % si, [128, NX])
        fw.dma(pvec_t[:], V(pd_.ap(), None), fw.dsem("c1_%d" % si))
        fw.tt(xpar_t[:, 0:28], pvec_t[:, POFF["mup"]:POFF["mup"] + 28], pvec_t[:, POFF["mun"]:POFF["mun"] + 28], ALU.add)
        fw.ts(xpar_t[:, 0:28], xpar_t[:, 0:28], -1.0, 1.0, ALU.mult, ALU.add)
        fw.ts(xpar_t[:, 28:36], pvec_t[:, POFF["k_k"]:POFF["k_k"] + 8], -1.0, None, ALU.mult)
        fw.ts(xpar_t[:, 36:44], pvec_t[:, POFF["k_a"]:POFF["k_a"] + 8], -1.0, 1.0, ALU.mult, ALU.add)
        PSETS.append(dict(pvec=pvec_t, xpar=xpar_t, dtp_d=dd_))
    PSETS[0].update(x=x_ext, S=S)
    PSETS[1].update(x=x_ext2, S=S2)
    CUR.update(PSETS[0])

    def pv(name, j=0, n=128):
        c = POFF[name] + j
        return CUR["pvec"][0:n, c:c + 1]

    def xp(name, j=0, n=128):
        c = XOFF[name] + j
        return CUR["xpar"][0:n, c:c + 1]

    def phase0(lite=False):
        st = ExitStack()
        CUR.update(PSETS[1 if lite else 0])
        x_src = CUR["x"]
        Sd = CUR["S"]
        nd = 1 if lite else 2

        def sb0(name, shape, dt=F32):
            return T(st.enter_context(nc.sbuf_tensor(("p0l_" if lite else "p0_") + name, list(shape), dt)), name)

        w2 = sb0("w2", [96, 1024], BF16)
        a2 = sb0("a2", [96, 1024], BF16)
        g2 = sb0("g2", [128, 2, 1024], BF16)
        wdt = sb0("wdt", [128, NK, 32], BF16)
        dtp = sb0("dtp", [128, 2, 2, 4, 32])
        An = sb0("An", [128, 2, 4, 32])
        fw.dma(dtp[:], V(CUR["dtp_d"].ap(), None), fw.dsem("c4"))
        fw.dma(w2[:], V(w2_d.ap(), None), fw.dsem("c5"), q="pool")
        fw.dma(a2[:], V(a2_d.ap(), None), fw.dsem("c6"), q="pool")
        fw.dma(g2[:], V(g2_d.ap().rearrange("(k p) n -> p k n", p=128), None), fw.dsem("c7"), q="pool")
        fw.dma(wdt[:], V(w_in_v[:, :, C_DT:C_DT + 32], None), dcp, q="pool")
        fw.act(An[:], dtp[:, 1], AF.Exp)
        fw.ts(An[:], An[:], -1.0, None, ALU.mult)
        xs = [sb0("xs%d" % j, [128, D]) for j in range(2)]
        xs_sem = [fw.dsem("xs%d" % j) for j in range(2)]
        xT = sb0("xT", [128, NK, TT], BF16)
        WG = [sb0("wg%d" % j, [128, NK, 512], BF16) for j in range(2)]
        wg_sem = [fw.dsem("wg%d" % j) for j in range(2)]
        ag = sb0("ag", [128, 8, 2, TV])
        kp = sb0("kp", [128, 8, TV])
        rk = sb0("rk", [128, 8, TV])
        tdw = sb0("tdw", [96, TV], BF16)
        tda = sb0("tda", [96, TV], BF16)
        tdg = sb0("tdg", [128, 2, TV], BF16)
        NS = 8
        ost = [sb0("ost%d" % j, [128, TV]) for j in range(NS)]
        ost_sem = [fw.dsem("ost%d" % j) for j in range(NS)]
        tmp = [sb0("tmp%d" % j, [128, TV]) for j in range(4)]
        dts = sb0("dts", [128, 4, 4, 32])
        dtt = sb0("dtt", [128, 4, 32])
        dts_sem = fw.dsem("dts")
        cnt = {"ost": 0, "tmp": 0, "wg": 0, "pb": 0, "aux": 0}

        def new_ost():
            j = cnt["ost"] % NS
            cnt["ost"] += 1
            return ost[j], ost_sem[j]

        def new_tmp():
            j = cnt["tmp"] % 4
            cnt["tmp"] += 1
            return tmp[j]

        def new_pb():
            j = cnt["pb"] % 4
            cnt["pb"] += 1
            return PB[j]

        def new_aux():
            j = cnt["aux"] % 4
            cnt["aux"] += 1
            return PB[4 + j]

        def store(name, row0, nrow, i, o, osem):
            if DBG_NOSTORE and cnt["ost"] > 8:
                return
            fw.dma(V(Sd[name].ap()[row0:row0 + nrow, i * TV:(i + 1) * TV], None), o[0:nrow, :], osem)

        def load_wg(col0, n):
            j = cnt["wg"] % 2
            cnt["wg"] += 1
            if not (DBG_NOWLOAD and cnt["wg"] > 2):
                fw.dma(WG[j][:, :, 0:n], V(w_in_v[:, :, col0:col0 + n], None), wg_sem[j], q="pool")
            return WG[j]

        def proj(P, wt, c0, ncol):
            for k in range(NK):
                fw.mm(P[0:ncol, :], wt[:, k, c0:c0 + ncol], xT[:, k, :], start=(k == 0), stop=(k == NK - 1))

        def shift(P, pc, nrow, out):
            fw.act(out, P[0:nrow, 2:2 + TV], AF.Copy, scale=xp("c0", pc, nrow))
            fw.stt(out, P[0:nrow, 1:1 + TV], pv("mup", pc, nrow), out, ALU.mult, ALU.add)
            fw.stt(out, P[0:nrow, 3:3 + TV], pv("mun", pc, nrow), out, ALU.mult, ALU.add)

        for i in range(NT0):
            for j in range(4):
                xb = xs[j % 2]
                if not (i > 0 and j < 2):
                    fw.dma(xb[:], V(x_src.ap()[i * TV + j * 128:i * TV + (j + 1) * 128, :], None), xs_sem[j % 2])
                for kq in range(4):
                    pt = PB[4 + (kq % 2)]
                    for k4 in range(4):
                        k = kq * 4 + k4
                        fw.tr(pt[:, k4 * 128:(k4 + 1) * 128], xb[:, k * 128:(k + 1) * 128], ident[:])
                    fw.copy(xT[:, kq * 4:(kq + 1) * 4, j * 128:(j + 1) * 128],
                            pt.v(pt.h[:].rearrange("p (a b) -> p a b", a=4)), e=("act" if kq % 2 else "dve"))
            if i + 1 < NT0:
                for j in range(2):
                    fw.dma(xs[j][:], V(x_src.ap()[(i + 1) * TV + j * 128:(i + 1) * TV + (j + 1) * 128, :], None), xs_sem[j])
            wt = load_wg(C_DW, 448)
            P = new_pb()
            proj(P, wt, 0, 96)
            t = new_tmp()
            shift(P, 24, 96, t[0:96, :])
            fw.act(tdw[:], t[0:96, :], AF.Tanh)
            P = new_pb()
            proj(P, wt, 96, 96)
            t = new_tmp()
            shift(P, 25, 96, t[0:96, :])
            fw.copy(tda[:], t[0:96, :], e="act")
            for c in range(0 if lite else 2):
                P = new_pb()
                proj(P, wt, 192 + 128 * c, 128)
                t = new_tmp()
                shift(P, 26 + c, 128, t[:])
                fw.act(tdg[:, c, :], t[:], AF.Sigmoid)
            pd = new_aux()
            pdv = pd.v(pd.h[:, 0:128].rearrange("p (a b) -> p a b", a=4))
            for j in range(4):
                for k in range(NK):
                    fw.mm(pdv[:, j, :], xT[:, k, j * 128:(j + 1) * 128], wdt[:, k, :], start=(k == 0), stop=(k == NK - 1))
            for d in range(2):
                fw.tt(dtt[:], pdv, dtp[:, 0, d], ALU.add)
                fw.act(dtt[:], dtt[:], AF.Exp)
                fw.act(dts[:, :, d, :], dtt[:], AF.Ln, bias=1.0)
                fw.tt(dts[:, :, 2 + d, :], dts[:, :, d, :], An[:, d], ALU.mult)
            for j in range(4):
                lo, hi = max(2, 128 * j), min(2 + TV, 128 * j + 128)
                fw.dma(V(Sd["DTS"].ap()[i * TV + lo - 2:i * TV + hi - 2], None), dts[lo - 128 * j:hi - 128 * j, j], dts_sem)
            for c in range(8):
                P = new_aux()
                fw.mm(P[:, 0:TV], w2[:, c * 128:(c + 1) * 128], tdw[:])
                for d in range(nd):
                    t = new_tmp()
                    fw.act(t[:], P[:, 0:TV], AF.Sigmoid, bias=pv("w0", d * 8 + c))
                    o, osem = new_ost()
                    fw.ts(o[:], t[:], -math.exp(-0.5), None, ALU.mult, e="dve")
                    store("LW%d" % (d + 1), c * 128, 128, i, o, osem)
                P = new_aux()
                fw.mm(P[:, 0:TV], a2[:, c * 128:(c + 1) * 128], tda[:])
                for d in range(nd):
                    fw.act(ag[:, c, d, :], P[:, 0:TV], AF.Sigmoid, bias=pv("a0", d * 8 + c))
                if not lite:
                    P = new_aux()
                    for k in range(2):
                        fw.mm(P[:, 0:TV], g2[:, k, c * 128:(c + 1) * 128], tdg[:, k, :], start=(k == 0), stop=(k == 1))
                    o, osem = new_ost()
                    fw.copy(o[:], P[:, 0:TV], e="dve")
                    store("G", c * 128, 128, i, o, osem)
            for gi in range(2):
                wt = load_wg(C_K + 512 * gi, 512)
                Pn = None
                for cc in range(4):
                    c = gi * 4 + cc
                    if Pn is None:
                        P = new_pb()
                        proj(P, wt, cc * 128, 128)
                    else:
                        P = Pn
                    shift(P, 8 + c, 128, kp[:, c, :])
                    sq = new_tmp()
                    fw.act(sq[:], kp[:, c, :], AF.Square, scale=pv("k_k", c))
                    Pn = None
                    if cc < 3:
                        Pn = new_pb()
                        proj(Pn, wt, (cc + 1) * 128, 128)
                    Pa = new_aux()
                    fw.mm(Pa[:, 0:TV], blk[:], sq[:])
                    rn = new_tmp()
                    fw.act(rn[:], Pa[:, 0:TV], AF.Ln, bias=1e-24)
                    fw.act(rn[:], rn[:], AF.Exp, scale=-0.5)
                    oa, osem = new_ost()
                    fw.stt(oa[:], kp[:, c, :], xp("nk_k", c), rn[:], ALU.mult, ALU.mult)
                    store("A", c * 128, 128, i, oa, osem)
                    for d in range(nd):
                        o, osem = new_ost()
                        fw.stt(o[:], oa[:], -1.0, ag[:, c, d, :], ALU.mult, ALU.mult)
                        store("B%d" % (d + 1), c * 128, 128, i, o, osem)
                        t = new_tmp()
                        fw.act(t[:], ag[:, c, d, :], AF.Identity, scale=pv("k_a", c), bias=xp("omk_a", c))
                        o, osem = new_ost()
                        fw.tt(o[:], t[:], kp[:, c, :], ALU.mult, e="dve")
                        store("KD%d" % (d + 1), c * 128, 128, i, o, osem)
            for gi in range(0 if lite else 2):
                wt = load_wg(C_R + 512 * gi, 512)
                Pn = None
                for cc in range(4):
                    c = gi * 4 + cc
                    if Pn is None:
                        P = new_pb()
                        proj(P, wt, cc * 128, 128)
                    else:
                        P = Pn
                    o, osem = new_ost()
                    shift(P, c, 128, o[:])
                    store("R", c * 128, 128, i, o, osem)
                    t = new_tmp()
                    fw.stt(t[:], o[:], pv("r_k", c), kp[:, c, :], ALU.mult, ALU.mult)
                    Pn = None
                    if cc < 3:
                        Pn = new_pb()
                        proj(Pn, wt, (cc + 1) * 128, 128)
                    Pa = new_aux()
                    fw.mm(Pa[:, 0:TV], blk[:], t[:])
                    fw.copy(rk[:, c, :], Pa[:, 0:TV], e="act")
            for gi in range(2):
                wt = load_wg(C_V + 512 * gi, 512)
                for cc in range(4):
                    c = gi * 4 + cc
                    P = new_pb()
                    proj(P, wt, cc * 128, 128)
                    o, osem = new_ost()
                    shift(P, 16 + c, 128, o[:])
                    store("V", c * 128, 128, i, o, osem)
                    if not lite:
                        o2, osem2 = new_ost()
                        fw.tt(o2[:], o[:], rk[:, c, :], ALU.mult, e="dve")
                        store("BON", c * 128, 128, i, o2, osem2)
            for gi in range(6 if lite else 8):
                wt = load_wg(C_XBC + 512 * gi, 512)
                for cc in range(4):
                    c = gi * 4 + cc
                    P = new_pb()
                    proj(P, wt, cc * 128, 128)
                    t = new_tmp()
                    fw.act(t[:], P[:, 0:TV], AF.Identity, scale=pv("conv_w", c), bias=pv("conv_b", c))
                    for j in range(1, 5):
                        fw.stt(t[:], P[:, j:j + TV], pv("conv_w", 32 * j + c), t[:], ALU.mult, ALU.add)
                    o, osem = new_ost()
                    fw.act(o[:], t[:], AF.Silu)
                    store("XBC", c * 128, 128, i, o, osem)
        fw.barrier()
        st.close()


    NST = T_loc // 128
    S["YRW"] = dscr("s_YRW", [1024, TP])
    st_rw_out = nc.dram_tensor("st_rw_out", [128, 8, 128], F32, kind="ExternalOutput")
    rwc_d = din("rwc", [128, 2, 512])
    rmask_d = din("rmask", [128, 1024])
    identb = sb("identb", [128, 128], BF16)
    fw.dma(identb[:], V(ident_d.ap(), None), fw.dsem("c8"), q="pool")

    def rwkv_phase(d, so=False):
        st = ExitStack()
        Ss = S2 if so else S

        def sbp(name, shape, dt=F32):
            return T(st.enter_context(nc.sbuf_tensor(("rws_" if so else "rw%d_" % d) + name, list(shape), dt)), name)

        rwc = sbp("rwc", [128, 2, 512])
        rmask = sbp("rmask", [128, 1024])
        fw.dma(rwc[:], V(rwc_d.ap(), None), fw.dsem("c9"))
        fw.dma(rmask[:], V(rmask_d.ap(), None), fw.dsem("c10"))
        names = ["R", "KD%d" % (d + 1), "V", "A", "B%d" % (d + 1), "LW%d" % (d + 1)]
        LD = [[sbp("ld%d_%d" % (j, b), [128, 8, 128]) for j in range(6)] for b in range(2)]
        ld_sem = [[fw.dsem("rwld%d_%d" % (j, b)) for j in range(6)] for b in range(2)]
        Y1 = [sbp("y1_%d" % b, [128, 8, 128]) for b in range(2)]
        y1_sem = [fw.dsem("rwy1_%d" % b) for b in range(2)]
        YB = [sbp("yb_%d" % b, [128, 8, 128]) for b in range(2)]
        yb_sem = [fw.dsem("rwyb_%d" % b) for b in range(2)]
        pre = sbp("pre", [128, 1024])
        cl = sbp("cl", [128, 1024])
        ecl = sbp("ecl", [128, 8, 128])
        encl = sbp("encl", [128, 8, 128])
        ecx = sbp("ecx", [128, 8, 128])
        xt = [sbp("xt%d" % j, [128, 8, 128]) for j in range(4)]
        BT = [sbp("BTbd%d" % b, [128, 8, 128], BF16) for b in range(2)]
        KT = [sbp("KTbd%d" % b, [128, 8, 128], BF16) for b in range(2)]
        AR = [sbp("ARbd%d" % b, [128, 8, 256], BF16) for b in range(2)]
        VB = [sbp("Vbd%d" % b, [128, 8, 128], BF16) for b in range(2)]
        for b in range(2):
            for t_ in (BT[b], KT[b], AR[b], VB[b]):
                fw.memset(t_[:], 0.0, e="pool")
        S0q = [sbp("S0_%d" % q_, [128, 4, 128]) for q_ in range(2)]
        S0bq = [sbp("S0b_%d" % q_, [128, 4, 128], BF16) for q_ in range(2)]
        ssem = fw.dsem("rwstate")
        ssem2 = fw.dsem("rwstate2")
        for q_ in range(2):
            if d == 0:
                fw.memset(S0q[q_][:], 0.0)
            else:
                fw.dma(S0q[q_][:], V(stx_rw.ap()[:, 4 * q_:4 * q_ + 4, :], None), (ssem, ssem2)[q_])
                fw.ts(S0q[q_][:].rr("p a b -> p (a b)"), S0q[q_][:].rr("p a b -> p (a b)"), sel[:, 0:1], None, ALU.mult)
            fw.copy(S0bq[q_][:], S0q[q_][:], e="act")
        ABs = [sbp("ABs%d" % b, [128, 4, 256], BF16) for b in range(4)]
        AKs = [sbp("AKs%d" % b, [128, 4, 256], BF16) for b in range(4)]
        NTs = [[sbp("NTs%d_%d" % (b, j), [128, 4, 128], BF16) for j in range(2)] for b in range(4)]
        Ns = [[sbp("Ns%d_%d" % (b, j), [128, 4, 128], BF16) for j in range(2)] for b in range(4)]
        Ps = [[sbp("Ps%d_%d" % (b, j), [128, 4, 128], BF16) for j in range(2)] for b in range(4)]
        VTs = [sbp("VTs%d" % b, [128, 4, 128], BF16) for b in range(4)]
        GTs = [sbp("GTs%d" % b, [128, 4, 128], BF16) for b in range(2)]
        UTs = [sbp("UTs%d" % b, [128, 4, 128], BF16) for b in range(2)]
        BKT = [sbp("BKT%d" % b, [128, 4, 2, 128], BF16) for b in range(4)]
        stmp = [sbp("stmp%d" % b, [128, 4, 128]) for b in range(2)]
        pbc = {"n": 0}

        def pbank():
            j = pbc["n"] % 8
            pbc["n"] += 1
            return PB[j]

        mAB = rwc[:, d, 0:256]
        mNT = rwc[:, d, 256:384]
        tiles = list(range(NST)) if d == 0 else list(range(NST - 1, -1, -1))
        chunks = (0, 1) if d == 0 else (1, 0)

        def load_tile(n):
            ti = tiles[n]
            b = n % 2
            for j in range(6):
                if so and j == 0:
                    continue
                fw.dma(LD[b][j][:], V(Ss[names[j]].ap()[:, ti * 128:(ti + 1) * 128].rearrange("(c p) t -> p c t", p=128), None),
                       ld_sem[b][j])
            if d == 1:
                fw.dma(Y1[b][:], V(S["YRW"].ap()[:, ti * 128:(ti + 1) * 128].rearrange("(c p) t -> p c t", p=128), None),
                       y1_sem[b])

        load_tile(0)
        qn = 0
        for n in range(NST):
            ti = tiles[n]
            b = n % 2
            if n + 1 < NST:
                load_tile(n + 1)
            r_, k_, v_, a_, b_, lw_ = [LD[b][j] for j in range(6)]
            fw.scan(pre[:], rmask[:], lw_[:].rr("p c t -> p (c t)"), 0.0, ALU.mult, ALU.add)
            pre4 = pre[:].rr("p (c t) -> p c t", t=64)
            cl4 = cl[:].rr("p (c t) -> p c t", t=64)
            lw4 = lw_[:].rr("p c (u t) -> p (c u) t", t=64)
            if d == 0:
                clv = pre
            else:
                fw.tt(cl4, lw4, pre4, ALU.subtract)
                fw.tt(cl4, cl4, pre4[:, :, 63:64].bc([128, 16, 64]), ALU.add)
                clv = cl
            clf = clv[:]
            fw.act(ecl[:].rr("p c t -> p (c t)"), clf, AF.Exp)
            fw.act(encl[:].rr("p c t -> p (c t)"), clf, AF.Exp, scale=-1.0)
            fw.tt(ecx[:].rr("p c t -> p (c t)"), clf, lw_[:].rr("p c t -> p (c t)"), ALU.subtract, e="pool")
            fw.act(ecx[:].rr("p c t -> p (c t)"), ecx[:].rr("p c t -> p (c t)"), AF.Exp)
            fw.tt(xt[0][:], b_[:], encl[:], ALU.mult, e="pool")
            fw.tt(xt[1][:], k_[:], encl[:], ALU.mult)
            fw.tt(xt[2][:], a_[:], ecx[:], ALU.mult, e="pool")
            if not so:
                fw.tt(xt[3][:], r_[:], ecl[:], ALU.mult)
            corder = list(chunks)
            cinfo = {}
            for pos, ci in enumerate(corder):
                cb = (2 * n + ci) % 2
                cs = slice(ci * 64, ci * 64 + 64)
                for hh in range(2):
                    ps = slice(64 * hh, 64 * hh + 64)
                    fs = slice(64 * hh, 64 * hh + 64)
                    fw.copy(BT[cb][ps, :, fs], xt[0][ps, :, cs], e="pool")
                    fw.copy(KT[cb][ps, :, fs], xt[1][ps, :, cs], e="dve")
                    fw.copy(AR[cb][ps, :, fs], xt[2][ps, :, cs], e="pool")
                    if not so:
                        fw.copy(AR[cb][ps, :, slice(128 + 64 * hh, 192 + 64 * hh)], xt[3][ps, :, cs], e="act")
                    fw.copy(VB[cb][ps, :, fs], v_[ps, :, cs], e="dve")
                wl = ecl[:, :, (ci * 64 + 63) if d == 0 else (ci * 64)]
                cinfo[pos] = (cb, cs, wl)
            minv_of = {}

            def pre_body(pos, q, cb, cs, wl):
                qb = pos * 2 + q
                p0 = 4 * q
                for half in range(2):
                    pa = pbank()
                    pk = pbank()
                    for pp in range(2):
                        p = p0 + 2 * half + pp
                        fw.mm(pa[:, pp * 256:(pp + 1) * 256], BT[cb][:, p, :], AR[cb][:, p, :])
                        fw.mm(pk[:, pp * 256:(pp + 1) * 256], KT[cb][:, p, :], AR[cb][:, p, :])
                    fw.tt(ABs[qb][:, 2 * half:2 * half + 2, :], pa[:].rr("p (a b) -> p a b", a=2),
                          mAB.us(1).bc([128, 2, 256]), ALU.mult)
                    fw.tt(AKs[qb][:, 2 * half:2 * half + 2, :], pk[:].rr("p (a b) -> p a b", a=2),
                          mAB.us(1).bc([128, 2, 256]), ALU.mult)
                pn = pbank()
                for pp in range(4):
                    p = p0 + pp
                    fw.mm(pn[:, pp * 128:(pp + 1) * 128], AR[cb][:, p, 0:128], BT[cb][:, p, :])
                fw.tt(NTs[qb][0][:], pn[:].rr("p (a b) -> p a b", a=4), mNT.us(1).bc([128, 4, 128]), ALU.mult)
                Ncur = ABs[qb][:, :, 0:128]
                NTcur = NTs[qb][0][:]
                fw.tt(Ps[qb][0][:], Ncur, identb[:].us(1).bc([128, 4, 128]), ALU.add, e="pool")
                Pcur = Ps[qb][0][:]
                yield
                for lev in range(1, 6):
                    j = lev % 2
                    pnt = pbank()
                    for pp in range(4):
                        fw.mm(pnt[:, pp * 128:(pp + 1) * 128], Ncur[:, pp, :], NTcur[:, pp, :])
                    fw.copy(NTs[qb][j][:], pnt[:].rr("p (a b) -> p a b", a=4), e="act")
                    yield
                    if lev <= 4:
                        pnn = pbank()
                        for pp in range(4):
                            fw.mm(pnn[:, pp * 128:(pp + 1) * 128], NTcur[:, pp, :], Ncur[:, pp, :])
                        fw.copy(Ns[qb][j][:], pnn[:].rr("p (a b) -> p a b", a=4), e="dve")
                        Nnext = Ns[qb][j][:]
                    NTnext = NTs[qb][j][:]
                    pp_ = pbank()
                    for pp in range(4):
                        fw.mm(pp_[:, pp * 128:(pp + 1) * 128], identb[:], Pcur[:, pp, :], start=True, stop=False)
                        fw.mm(pp_[:, pp * 128:(pp + 1) * 128], NTnext[:, pp, :], Pcur[:, pp, :], start=False, stop=True)
                    fw.copy(Ps[qb][j][:], pp_[:].rr("p (a b) -> p a b", a=4), e="dve")
                    Pcur = Ps[qb][j][:]
                    NTcur = NTnext
                    if lev <= 4:
                        Ncur = Nnext
                Minv = Pcur
                yield
                pv_ = pbank()
                pvb = pv_.v(pv_.h[:].bitcast(BF16)[:, 0:512].rearrange("p (a b) -> p a b", a=4))
                for pp in range(4):
                    fw.tr(pvb[:, pp, :], VB[cb][:, p0 + pp, :], identb[:])
                fw.copy(VTs[qb][:], pvb, e="act")
                minv_of[(pos, q)] = Minv
                yield
                pt_ = pbank()
                ptb = pt_.v(pt_.h[:].bitcast(BF16).rearrange("p (a c b) -> p a c b", a=4, c=2))
                for pp in range(4):
                    p = p0 + pp
                    fw.tr(ptb[:, pp, 0, :], BT[cb][:, p, :], identb[:])
                    fw.tr(ptb[:, pp, 1, :], KT[cb][:, p, :], identb[:])
                fw.copy(BKT[qb][:], ptb, e="act")
                yield

            def state_body(pos, q, cb, cs, wl):
                qb = pos * 2 + q
                sq_ = q
                p0 = 4 * q
                Minv = minv_of[(pos, q)]
                yield
                pg = pbank()
                for pp in range(4):
                    p = p0 + pp
                    fw.mm(pg[:, pp * 128:(pp + 1) * 128], AR[cb][:, p, 0:128], S0bq[q][:, pp, :], start=True, stop=False)
                    fw.mm(pg[:, pp * 128:(pp + 1) * 128], AKs[qb][:, pp, 0:128], VTs[qb][:, pp, :], start=False, stop=True)
                fw.copy(GTs[sq_][:], pg[:].rr("p (a b) -> p a b", a=4), e="dve")
                yield
                pu = pbank()
                for pp in range(4):
                    fw.mm(pu[:, pp * 128:(pp + 1) * 128], Minv[:, pp, :], GTs[sq_][:, pp, :])
                fw.copy(UTs[sq_][:], pu[:].rr("p (a b) -> p a b", a=4), e="act")
                if not so:
                    yield
                    py = pbank()
                    for pp in range(4):
                        p = p0 + pp
                        o = py[:, pp * 128:(pp + 1) * 128]
                        fw.mm(o, S0bq[q][:, pp, :], AR[cb][:, p, 128:256], start=True, stop=False)
                        fw.mm(o, UTs[sq_][:, pp, :], ABs[qb][:, pp, 128:256], start=False, stop=False)
                        fw.mm(o, VTs[qb][:, pp, :], AKs[qb][:, pp, 128:256], start=False, stop=True)
                    py4 = py[:].rr("p (a b) -> p a b", a=4)
                    if d == 0:
                        fw.copy(YB[b][0:64, p0:p0 + 4, cs], py4[0:64, :, 0:64], e="act")
                        fw.copy(YB[b][64:128, p0:p0 + 4, cs], py4[64:128, :, 64:128], e="dve")
                    else:
                        fw.tt(YB[b][0:64, p0:p0 + 4, cs], py4[0:64, :, 0:64], Y1[b][0:64, p0:p0 + 4, cs], ALU.add)
                        fw.tt(YB[b][64:128, p0:p0 + 4, cs], py4[64:128, :, 64:128], Y1[b][64:128, p0:p0 + 4, cs], ALU.add)
                    yield
                pd_ = pbank()
                for pp in range(4):
                    o = pd_[:, pp * 128:(pp + 1) * 128]
                    fw.mm(o, BKT[qb][:, pp, 0, :], UTs[sq_][:, pp, :], start=True, stop=False)
                    fw.mm(o, BKT[qb][:, pp, 1, :], VTs[qb][:, pp, :], start=False, stop=True)
                wlb = wl[:, p0:p0 + 4].us(2).bc([128, 4, 128])
                fw.tt(stmp[sq_][:], S0q[q][:], wlb, ALU.mult, e="pool")
                fw.tt(S0q[q][:], pd_[:].rr("p (a b) -> p a b", a=4), wlb, ALU.mult)
                fw.tt(S0q[q][:], S0q[q][:], stmp[sq_][:], ALU.add)
                fw.copy(S0bq[q][:], S0q[q][:], e="act")
                yield

            def run_gens(gens):
                alive = list(gens)
                while alive:
                    for g_ in list(alive):
                        try:
                            next(g_)
                        except StopIteration:
                            alive.remove(g_)

            run_gens([pre_body(pos, q, *cinfo[pos]) for pos in range(2) for q in range(2)])
            for pos in range(2):
                run_gens([state_body(pos, q, *cinfo[pos]) for q in range(2)])
            if not so:
                fw.dma(V(S["YRW"].ap()[:, ti * 128:(ti + 1) * 128].rearrange("(c p) t -> p c t", p=128), None), YB[b][:], yb_sem[b])
        for q_ in range(2):
            dst = stx_rw if so else st_rw_out
            if so or d == 0:
                fw.dma(V(dst.ap()[:, 4 * q_:4 * q_ + 4, :], None), S0q[q_][:], (ssem, ssem2)[q_])
        fw.barrier()
        st.close()

    S["YM"] = dscr("s_YM", [2048, TP])
    st_m_out = nc.dram_tensor("st_m_out", [128, 32, 64], F32, kind="ExternalOutput")
    mc_d = din("mc", [128, 2, 2, 128])
    ones = sb("ones", [128, 128])
    fw.memset(ones[:], 1.0)

    def mamba_phase(d, so=False):
        st = ExitStack()
        Ss = S2 if so else S

        def sbp(name, shape, dt=F32):
            return T(st.enter_context(nc.sbuf_tensor(("mbs_" if so else "mb%d_" % d) + name, list(shape), dt)), name)

        mcst = sbp("mcst", [128, 2, 2, 128])
        fw.dma(mcst[:], V(mc_d.ap(), None), fw.dsem("c11"))
        XS = [sbp("xs%d" % b, [128, 16, 128]) for b in range(2)]
        Bb = [sbp("bb%d" % b, [128, 8, 128], BF16) for b in range(2)]
        Cb = [sbp("cb%d" % b, [128, 8, 128], BF16) for b in range(2)]
        DT = [sbp("dt%d" % b, [128, 4, 32]) for b in range(2)]
        Y1 = [sbp("y1_%d" % b, [128, 16, 128]) for b in range(2)]
        YB = [sbp("yb_%d" % b, [128, 16, 128]) for b in range(2)]
        sems = [[fw.dsem("mbld%d_%d" % (j, b)) for j in range(4)] for b in range(2)]
        psems = [[fw.dsem("mbldp%d_%d" % (j, b)) for j in range(2)] for b in range(2)]
        yb_sem = [fw.dsem("mbyb_%d" % b) for b in range(2)]
        dAexp = sbp("dAexp", [128, 32, 128])
        cs_tok = sbp("cs_tok", [128, 32])
        csl = sbp("csl", [128, 32])
        ecl_last = sbp("ecl_last", [128, 32])
        decs = sbp("decs", [128, 32])
        E = [sbp("E%d" % b, [128, 8, 128]) for b in range(2)]
        ecsR = [sbp("ecsR%d" % b, [128, 8, 128]) for b in range(2)]
        MT = sbp("MT", [128, 32, 128], BF16)
        Csc = sbp("Csc", [128, 32, 128], BF16)
        CBm = sbp("CBm", [128, 8, 128])
        xdt = sbp("xdt", [128, 32, 64], BF16)
        xdd = sbp("xdd", [128, 32, 64], BF16)
        Btok = sbp("Btok", [128, 8, 128], BF16)
        hS = sbp("hS", [128, 32, 64])
        hb = sbp("hb", [128, 32, 64], BF16)
        htmp = sbp("htmp", [128, 32, 64])
        ssem = fw.dsem("mbstate")
        if d == 0:
            fw.memset(hS[:], 0.0)
        else:
            fw.dma(hS[:], V(stx_m.ap(), None), ssem)
            fw.ts(hS[:].rr("p a b -> p (a b)"), hS[:].rr("p a b -> p (a b)"), sel[:, 0:1], None, ALU.mult)
        fw.copy(hb[:], hS[:], e="act")
        tri = mcst[:, d, 0, :]
        lst = mcst[:, d, 1, :]
        t_last = 127 if d == 0 else 0
        NCH = T_loc // 128
        tiles = list(range(NCH)) if d == 0 else list(range(NCH - 1, -1, -1))
        pbc = {"n": 0}

        def pbank():
            j = pbc["n"] % 8
            pbc["n"] += 1
            return PB[j]

        def load_tile(n):
            ti = tiles[n]
            b = n % 2
            cs_ = slice(ti * 128, (ti + 1) * 128)
            xb = Ss["XBC"].ap()
            fw.dma(XS[b][:], V(xb[0:2048, cs_].rearrange("(c p) t -> p c t", p=128), None), sems[b][0])
            fw.dma(DT[b][:], V(Ss["DTS"].ap()[cs_], None), sems[b][1])
            fw.dma(Bb[b][:], V(xb[2048:3072, cs_].rearrange("(c p) t -> p c t", p=128), None), psems[b][0], q="pool")
            if not so:
                fw.dma(Cb[b][:], V(xb[3072:4096, cs_].rearrange("(c p) t -> p c t", p=128), None), psems[b][1], q="pool")
            if d == 1:
                fw.dma(Y1[b][:], V(S["YM"].ap()[:, cs_].rearrange("(c p) t -> p c t", p=128), None), sems[b][2])

        load_tile(0)
        for n in range(NCH):
            ti = tiles[n]
            b = n % 2
            if n + 1 < NCH:
                load_tile(n + 1)
            dA = DT[b][:, 2 + d, :]
            dtv = DT[b][:, d, :]
            if so:
                pc = pbank()
                fw.mm(pc[:, 0:32], tri, dA)
                fw.copy(cs_tok[:], pc[:, 0:32], e="act")
                pc2 = pbank()
                fw.mm(pc2[:, 0:32], ones[:], dA)
                fw.copy(csl[:], pc2[:, 0:32], e="dve")
            if not so:
                fw.tt(dAexp[:], dA.us(2).bc([128, 32, 128]), tri.us(1).bc([128, 32, 128]), ALU.mult)
                pc = pbank()
                fw.mm(pc[:, 0:32], tri, dA)
                fw.copy(cs_tok[:], pc[:, 0:32], e="act")
                for half in range(2):
                    pcb = pbank()
                    for gg in range(4):
                        g = half * 4 + gg
                        fw.mm(pcb[:, gg * 128:(gg + 1) * 128], Bb[b][:, g, :], Cb[b][:, g, :])
                    fw.tt(CBm[:, half * 4:half * 4 + 4, :], pcb[:].rr("p (a b) -> p a b", a=4), tri.us(1).bc([128, 4, 128]), ALU.mult)
                for o in range(4):
                    ob = o % 2
                    pD = [pbank(), pbank()]
                    pR = [pbank(), pbank()]
                    for j in range(2):
                        rhs = dAexp[:, o * 8 + j * 4:o * 8 + j * 4 + 4, :].rr("p a b -> p (a b)")
                        fw.mm(pD[j][:], lst, rhs)
                        fw.mm(pR[j][:], ones[:], rhs)
                    for j in range(2):
                        hs = slice(o * 8 + j * 4, o * 8 + j * 4 + 4)
                        g = o * 2 + j
                        fw.act(E[ob][:, j * 4:j * 4 + 4, :], pD[j][:].rr("p (a b) -> p a b", a=4), AF.Exp)
                        fw.tt(MT[:, hs, :], E[ob][:, j * 4:j * 4 + 4, :], CBm[:, g:g + 1, :].bc([128, 4, 128]), ALU.mult)
                        pR4 = pR[j][:].rr("p (a b) -> p a b", a=4)
                        fw.copy(csl[:, hs], pR4[:, :, t_last], e="dve")
                        fw.act(ecsR[ob][:, j * 4:j * 4 + 4, :], pR4, AF.Exp)
                        fw.tt(Csc[:, hs, :], ecsR[ob][:, j * 4:j * 4 + 4, :], Cb[b][:, g:g + 1, :].bc([128, 4, 128]), ALU.mult, e="dve")
            fw.act(ecl_last[:], csl[:], AF.Exp)
            fw.tt(decs[:], csl[:], cs_tok[:], ALU.subtract)
            fw.act(decs[:], decs[:], AF.Exp)
            for q in range(4):
                px = pbank()
                for cc in range(4):
                    c = q * 4 + cc
                    fw.tr(px[:, cc * 128:(cc + 1) * 128], XS[b][:, c, :], ident[:])
                hs = slice(q * 8, q * 8 + 8)
                fw.tt(xdt[:, hs, :], px[:].rr("p (a b) -> p a b", a=8), dtv[:, hs].us(2).bc([128, 8, 64]), ALU.mult)
                fw.tt(xdd[:, hs, :], xdt[:, hs, :], decs[:, hs].us(2).bc([128, 8, 64]), ALU.mult, e="pool")
            if not so:
                for q in range(4):
                    py = pbank()
                    for cc in range(4):
                        for hh in range(2):
                            h = (q * 4 + cc) * 2 + hh
                            o_ = py[64 * hh:64 * hh + 64, cc * 128:(cc + 1) * 128]
                            kw_ = {"tile_position": (0, 64)} if hh == 1 else {}
                            fw.mm(o_, xdt[:, h, :], MT[:, h, :], start=True, stop=False, **kw_)
                            fw.mm(o_, hb[:, h, :], Csc[:, h, :], start=False, stop=True, **kw_)
                    py4 = py[:].rr("p (a b) -> p a b", a=4)
                    if d == 0:
                        fw.copy(YB[b][:, q * 4:q * 4 + 4, :], py4, e="act")
                    else:
                        fw.tt(YB[b][:, q * 4:q * 4 + 4, :], py4, Y1[b][:, q * 4:q * 4 + 4, :], ALU.add)
                fw.dma(V(S["YM"].ap()[:, ti * 128:(ti + 1) * 128].rearrange("(c p) t -> p c t", p=128), None), YB[b][:], yb_sem[b])
            pt_ = pbank()
            ptb = pt_.v(pt_.h[:].bitcast(BF16).rearrange("p (a b) -> p a b", a=8))
            for g in range(8):
                fw.tr(ptb[:, g, :], Bb[b][:, g, :], identb[:])
            fw.copy(Btok[:], ptb, e="act")
            fw.tt(htmp[:], hS[:], ecl_last[:].us(2).bc([128, 32, 64]), ALU.mult, e="pool")
            for q in range(4):
                pn_ = pbank()
                for gg in range(2):
                    g = q * 2 + gg
                    fw.mm(pn_[:, gg * 256:(gg + 1) * 256], Btok[:, g, :], xdd[:, 4 * g:4 * g + 4, :].rr("p a b -> p (a b)"))
                hs = slice(q * 8, q * 8 + 8)
                fw.tt(hS[:, hs, :], pn_[:].rr("p (a b) -> p a b", a=8), htmp[:, hs, :], ALU.add)
            fw.copy(hb[:], hS[:], e="act")
        if so:
            fw.dma(V(stx_m.ap(), None), hS[:], ssem)
        elif d == 0:
            fw.dma(V(st_m_out.ap(), None), hS[:], ssem)
        fw.barrier()
        st.close()

    ALPHA = 2.0 ** 0.25
    LN_EPS = 1e-5
    GN_EPS = 64e-5
    mem_d = din("mem", [256, D])
    w_br_d = din("w_br", [1024, D])
    w_bm_d = din("w_bm", [D, D])
    w_o_d = din("w_o", [D, D])
    w_q_d = din("w_q", [D, D])
    w_kv_d = din("w_kv", [D, 2 * D])
    w_co_d = din("w_co", [D, D])
    w_up_d = din("w_up", [D, 4 * D])
    w_down_d = din("w_down", [4 * D, D])
    y_out = nc.dram_tensor("y_out", [T_loc, D], F32, kind="ExternalOutput")

    def wview(w):
        return w.ap().rearrange("(k p) n -> p k n", p=128)

    def phase3():
        st = ExitStack()

        def sbp(name, shape, dt=F32):
            return T(st.enter_context(nc.sbuf_tensor("p3_" + name, list(shape), dt)), name)

        F32A = sbp("F32A", [128, NK, 512])
        BFA = sbp("BFA", [128, NK, 512], BF16)
        BFB = sbp("BFB", [128, NK, 512], BF16)
        BFC = sbp("BFC", [128, NK, 512], BF16)
        BFD = sbp("BFD", [128, 8, 512], BF16)
        HM = sbp("HM", [128, 16, 512], BF16)
        Kt = sbp("Kt", [128, NK, 256], BF16)
        Vt = sbp("Vt", [128, 2, D], BF16)
        onesb = sbp("onesb", [128, 128], BF16)
        ksc = sbp("ksc", [128, 4])
        fw.copy(onesb[:], ones[:], e="act")
        NWB = 3
        WB = [sbp("wb%d" % j, [128, 4096], BF16) for j in range(NWB)]
        wb_sem = [fw.dsem("p3wb%d" % j) for j in range(NWB)]
        NL = 6
        LB = [sbp("lb%d" % j, [128, 512]) for j in range(NL)]
        lb_sem = [fw.dsem("p3lb%d" % j) for j in range(NL)]
        NTMP = 5
        TMP = [sbp("tmp%d" % j, [128, 512]) for j in range(NTMP)]
        xs = [sbp("xs%d" % j, [128, D]) for j in range(2)]
        xs_sem = [fw.dsem("p3xs%d" % j) for j in range(2)]
        ymp = [sbp("ymp%d" % j, [128, 2, 512]) for j in range(1)]
        sqp = [sbp("sqp%d" % j, [128, 2, 512]) for j in range(1)]
        expS = [sbp("expS%d" % j, [128, 2, 512], BF16) for j in range(1)]
        ded = {nm: sbp("ded_" + nm, [128, 512]) for nm in ("mean", "rstd", "cst", "rs")}
        cnt = {"pb": 0, "lb": 0, "tmp": 0}

        def pbank():
            j = cnt["pb"] % 8
            cnt["pb"] += 1
            return PB[j]

        def tmp():
            j = cnt["tmp"] % NTMP
            cnt["tmp"] += 1
            return TMP[j]

        def ld(name, row0, t0):
            j = cnt["lb"] % NL
            cnt["lb"] += 1
            fw.dma(LB[j][:], V(S[name].ap()[row0:row0 + 128, t0:t0 + 512], None), lb_sem[j])
            return LB[j]

        class WS:
            def __init__(self):
                self.specs = []
                self.issued = 0
                self.tiles = {}

            def add(self, wv, k0, nk, col0, ncols):
                self.specs.append((wv, k0, nk, col0, ncols))
                return len(self.specs) - 1

            def _issue(self, n):
                wv, k0, nk, col0, ncols = self.specs[n]
                j = n % NWB
                tv = WB[j][:, 0:nk * ncols].rr("p (k n) -> p k n", k=nk)
                fw.dma(tv, V(wv[:, k0:k0 + nk, col0:col0 + ncols], None), wb_sem[j], q="pool")
                self.tiles[n] = tv

            def get(self, n):
                while self.issued < min(len(self.specs), n + NWB):
                    self._issue(self.issued)
                    self.issued += 1
                return self.tiles.pop(n)

        w_in_v3 = w_in_v

        def dense(ws_ids, ws, src, nk, consume):
            pass

        def layer_norm(gname, bname):
            ps1 = pbank()
            ps2 = pbank()
            for c in range(NK):
                sq = tmp()
                fw.act(sq[:], F32A[:, c, :], AF.Square)
                fw.mm(ps1[:], ones[:], F32A[:, c, :], start=(c == 0), stop=(c == NK - 1))
                fw.mm(ps2[:], ones[:], sq[:], start=(c == 0), stop=(c == NK - 1), sig=True)
            mean = ded["mean"]
            fw.act(mean[:], ps1[:], AF.Copy, scale=1.0 / D)
            msq = tmp()
            fw.act(msq[:], ps1[:], AF.Square, scale=1.0 / D)
            rstd = ded["rstd"]
            fw.stt(rstd[:], ps2[:], 1.0 / D, msq[:], ALU.mult, ALU.subtract)
            fw.act(rstd[:], rstd[:], AF.Ln, bias=LN_EPS)
            fw.act(rstd[:], rstd[:], AF.Exp, scale=-0.5)
            for c in range(NK):
                t_ = tmp()
                fw.tt(t_[:], F32A[:, c, :], mean[:], ALU.subtract)
                fw.tt(t_[:], t_[:], rstd[:], ALU.mult)
                fw.act(F32A[:, c, :], t_[:], AF.Identity, scale=pv(gname, c), bias=pv(bname, c))
                fw.copy(BFA[:, c, :], F32A[:, c, :], e="dve")

        memT = BFB
        memTv = memT[:, :, 0:256]
        for mb in range(2):
            fw.dma(xs[mb][:], V(mem_d.ap()[mb * 128:(mb + 1) * 128, :], None), xs_sem[mb])
            for kq in range(4):
                pt = pbank()
                for k4 in range(4):
                    k = kq * 4 + k4
                    fw.tr(pt[:, k4 * 128:(k4 + 1) * 128], xs[mb][:, k * 128:(k + 1) * 128], ident[:])
                fw.copy(memT[:, kq * 4:(kq + 1) * 4, mb * 128:(mb + 1) * 128], pt[:].rr("p (a b) -> p a b", a=4), e="act")
        ws = WS()
        wkv = wview(w_kv_d)
        ids = [ws.add(wkv, 0, NK, c * 256, 256) for c in range(16)]
        for c in range(8):
            wt = ws.get(ids[c])
            for oo in range(2):
                oc = c * 2 + oo
                pk = pbank()
                for k in range(NK):
                    fw.mm(pk[:, 0:256], wt[:, k, oo * 128:(oo + 1) * 128], memTv[:, k, :], start=(k == 0), stop=(k == NK - 1))
                fw.copy(Kt[:, oc, :], pk[:, 0:256], e="act")
        for c in range(8):
            wt = ws.get(ids[8 + c])
            for mb in range(2):
                pvv = pbank()
                for k in range(NK):
                    fw.mm(pvv[:, 0:256], memT[:, k, mb * 128:(mb + 1) * 128], wt[:, k, :], start=(k == 0), stop=(k == NK - 1))
                fw.copy(Vt[:, mb, c * 256:(c + 1) * 256], pvv[:, 0:256], e="dve")
        for hd in range(4):
            pk2 = pbank()
            for kc in range(4):
                sqk = tmp()
                sqkb = sqk[:, 0:128].ap.bitcast(BF16)
                sqv = V(sqkb, sqk.buf)
                fw.act(sqv, Kt[:, hd * 4 + kc, :], AF.Square)
                fw.mm(pk2[:, 0:256], onesb[:], sqv, start=(kc == 0), stop=(kc == 3))
            mx = tmp()
            i_ = nc.vector
            r_, w_ = fw._bufs([pk2[:]]), fw._bufs([mx[:]])
            fw._deps("dve", r_, w_)
            ins = nc.vector.tensor_reduce(mx[:, 0:1].ap, pk2[:, 0:256].ap, AX.X, ALU.max)
            fw._done(ins, "dve", 1, r_, w_)
            fw.ts(ksc[:, hd:hd + 1], mx[:, 0:1], 1.0 / 512.0, None, ALU.mult)

        NT3 = T_loc // 512
        for i in range(NT3):
            t0 = i * 512
            ws = WS()
            wbr, wbm, wo, wq, wco, wup, wdn = [wview(w) for w in (w_br_d, w_bm_d, w_o_d, w_q_d, w_co_d, w_up_d, w_down_d)]
            id_z = [ws.add(w_in_v3, 0, NK, C_Z + c * 256, 256) for c in range(8)]
            id_d = []
            for c in range(8):
                id_d.append((ws.add(wbr, 0, 8, c * 256, 256), ws.add(w_in_v3, 0, NK, C_GATES + c * 256, 256),
                             ws.add(wbm, 0, NK, c * 256, 256), ws.add(w_in_v3, 0, NK, C_GATES + 2048 + c * 256, 256)))
            id_o = [ws.add(wo, 0, NK, c * 256, 256) for c in range(8)]
            id_q = [ws.add(wq, 0, NK, c * 256, 256) for c in range(8)]
            id_co = [ws.add(wco, 0, NK, c * 256, 256) for c in range(8)]
            id_up, id_dn = [], []
            for hf in range(4):
                id_up.append([ws.add(wup, 0, NK, hf * 2048 + c * 256, 256) for c in range(8)])
                id_dn.append([ws.add(wdn, hf * 16, 16, c * 256, 256) for c in range(8)])
            for j in range(4):
                xb = xs[j % 2]
                fw.dma(xb[:], V(x_ext.ap()[2 + t0 + j * 128:2 + t0 + (j + 1) * 128, :], None), xs_sem[j % 2])
                for kq in range(4):
                    pt = pbank()
                    for k4 in range(4):
                        k = kq * 4 + k4
                        fw.tr(pt[:, k4 * 128:(k4 + 1) * 128], xb[:, k * 128:(k + 1) * 128], ident[:])
                    pt4 = pt[:].rr("p (a b) -> p a b", a=4)
                    fw.copy(F32A[:, kq * 4:(kq + 1) * 4, j * 128:(j + 1) * 128], pt4, e="act")
                    fw.copy(BFA[:, kq * 4:(kq + 1) * 4, j * 128:(j + 1) * 128], pt4, e="dve")
            for c in range(8):
                y = ld("YRW", c * 128, t0)
                bon = ld("BON", c * 128, t0)
                gg = ld("G", c * 128, t0)
                sq = tmp()
                fw.act(sq[:], y[:], AF.Square)
                p1 = pbank()
                fw.mm(p1[:], blk[:], y[:])
                p2 = pbank()
                fw.mm(p2[:], blk[:], sq[:])
                m = tmp()
                fw.act(m[:], p1[:], AF.Copy, scale=1.0 / 64)
                msq = tmp()
                fw.act(msq[:], p1[:], AF.Square, scale=1.0 / 64)
                var = tmp()
                fw.stt(var[:], p2[:], 1.0 / 64, msq[:], ALU.mult, ALU.subtract)
                fw.act(var[:], var[:], AF.Ln, bias=GN_EPS)
                fw.act(var[:], var[:], AF.Exp, scale=-0.5)
                fw.tt(y[:], y[:], m[:], ALU.subtract)
                fw.tt(y[:], y[:], var[:], ALU.mult)
                fw.act(y[:], y[:], AF.Identity, scale=pv("gn_g", c), bias=pv("gn_b", c))
                fw.tt(y[:], y[:], bon[:], ALU.add)
                fw.tt(BFD[:, c, :], y[:], gg[:], ALU.mult)
            for c in range(NK):
                if c % 2 == 0:
                    wz = ws.get(id_z[c // 2])
                pb_ = 0
                ym = ld("YM", c * 128, t0)
                xv = ld("XBC", c * 128, t0)
                pz = pbank()
                for k in range(NK):
                    fw.mm(pz[:], wz[:, k, (c % 2) * 128:(c % 2 + 1) * 128], BFA[:, k, :], start=(k == 0), stop=(k == NK - 1))
                fw.stt(ym[:], xv[:], pv("m_d", c), ym[:], ALU.mult, ALU.add)
                sz = tmp()
                fw.act(sz[:], pz[:], AF.Silu)
                fw.tt(ymp[pb_][:, c % 2, :], ym[:], sz[:], ALU.mult)
                fw.act(sqp[pb_][:, c % 2, :], ymp[pb_][:, c % 2, :], AF.Square)
                if c % 2 == 1:
                    pss = pbank()
                    fw.mm(pss[:], ones[:], sqp[pb_][:, 0, :], start=True, stop=False)
                    fw.mm(pss[:], ones[:], sqp[pb_][:, 1, :], start=False, stop=True)
                    rms = tmp()
                    fw.act(rms[:], pss[:], AF.Ln, scale=1.0 / 256, bias=LN_EPS)
                    fw.act(rms[:], rms[:], AF.Exp, scale=-0.5)
                    for cc in range(2):
                        fw.stt(BFB[:, c - 1 + cc, :], ymp[pb_][:, cc, :], pv("m_norm_g", c - 1 + cc), rms[:], ALU.mult, ALU.mult)
            for c in range(8):
                wt = ws.get(id_d[c][0])
                pu = [pbank(), pbank()]
                for oo in range(2):
                    for k in range(8):
                        fw.mm(pu[oo][:], wt[:, k, oo * 128:(oo + 1) * 128], BFD[:, k, :], start=(k == 0), stop=(k == 7))
                wt = ws.get(id_d[c][1])
                t1 = [tmp(), tmp()]
                for oo in range(2):
                    pg = pbank()
                    for k in range(NK):
                        fw.mm(pg[:], wt[:, k, oo * 128:(oo + 1) * 128], BFA[:, k, :], start=(k == 0), stop=(k == NK - 1))
                    fw.act(t1[oo][:], pg[:], AF.Sigmoid)
                    fw.tt(t1[oo][:], t1[oo][:], pu[oo][:], ALU.mult)
                wt = ws.get(id_d[c][2])
                pm = [pbank(), pbank()]
                for oo in range(2):
                    for k in range(NK):
                        fw.mm(pm[oo][:], wt[:, k, oo * 128:(oo + 1) * 128], BFB[:, k, :], start=(k == 0), stop=(k == NK - 1))
                wt = ws.get(id_d[c][3])
                for oo in range(2):
                    oc = c * 2 + oo
                    pg2 = pbank()
                    for k in range(NK):
                        fw.mm(pg2[:], wt[:, k, oo * 128:(oo + 1) * 128], BFA[:, k, :], start=(k == 0), stop=(k == NK - 1))
                    sg2 = tmp()
                    fw.act(sg2[:], pg2[:], AF.Sigmoid)
                    fw.tt(sg2[:], sg2[:], pm[oo][:], ALU.mult)
                    fw.tt(BFC[:, oc, :], t1[oo][:], sg2[:], ALU.add)

            def proj_res(idl, src, first=True):
                for c in range(8):
                    wt = ws.get(idl[c])
                    for oo in range(2):
                        oc = c * 2 + oo
                        po = pbank()
                        for k in range(NK):
                            fw.mm(po[:], wt[:, k, oo * 128:(oo + 1) * 128], src[:, k, :], start=(k == 0), stop=(k == NK - 1))
                        fw.stt(F32A[:, oc, :], F32A[:, oc, :], ALPHA, po[:], ALU.mult, ALU.add)

            proj_res(id_o, BFC)
            layer_norm("ln1_g", "ln1_b")
            for c in range(8):
                wt = ws.get(id_q[c])
                for oo in range(2):
                    oc = c * 2 + oo
                    pq = pbank()
                    for k in range(NK):
                        fw.mm(pq[:], wt[:, k, oo * 128:(oo + 1) * 128], BFA[:, k, :], start=(k == 0), stop=(k == NK - 1))
                    fw.copy(BFB[:, oc, :], pq[:], e="act")
            inv = 1.0 / math.sqrt(512.0)
            for hd in range(4):
                eb = 0
                pq2 = pbank()
                for kc in range(4):
                    sqq = tmp()
                    sqv = V(sqq[:].ap.bitcast(BF16)[:, 0:512], sqq.buf)
                    fw.act(sqv, BFB[:, hd * 4 + kc, :], AF.Square)
                    fw.mm(pq2[:], onesb[:], sqv, start=(kc == 0), stop=(kc == 3))
                cst = ded["cst"]
                fw.act(cst[:], pq2[:], AF.Sqrt, scale=ksc[:, hd:hd + 1])
                for mc in range(2):
                    ps_ = pbank()
                    for kc in range(4):
                        fw.mm(ps_[:], Kt[:, hd * 4 + kc, mc * 128:(mc + 1) * 128], BFB[:, hd * 4 + kc, :], start=(kc == 0), stop=(kc == 3))
                    ein = tmp()
                    fw.stt(ein[:], ps_[:], inv, cst[:], ALU.mult, ALU.subtract)
                    fw.act(expS[eb][:, mc, :], ein[:], AF.Exp)
                psum_ = pbank()
                for mc in range(2):
                    fw.mm(psum_[:], onesb[:], expS[eb][:, mc, :], start=(mc == 0), stop=(mc == 1))
                rs = ded["rs"]
                r_, w_ = fw._bufs([psum_[:]]), fw._bufs([rs[:]])
                fw._deps("dve", r_, w_)
                ins = nc.vector.reciprocal(rs[:].ap, psum_[:].ap)
                fw._done(ins, "dve", 1, r_, w_)
                for dc in range(4):
                    po = pbank()
                    col = hd * 512 + dc * 128
                    for mc in range(2):
                        fw.mm(po[:], Vt[:, mc, col:col + 128], expS[eb][:, mc, :], start=(mc == 0), stop=(mc == 1))
                    fw.tt(BFC[:, hd * 4 + dc, :], po[:], rs[:], ALU.mult)
            proj_res(id_co, BFC)
            layer_norm("ln2_g", "ln2_b")
            for hf in range(4):
                for c in range(8):
                    wt = ws.get(id_up[hf][c])
                    for oo in range(2):
                        oc = c * 2 + oo
                        ph = pbank()
                        for k in range(NK):
                            fw.mm(ph[:], wt[:, k, oo * 128:(oo + 1) * 128], BFA[:, k, :], start=(k == 0), stop=(k == NK - 1))
                        rl = tmp()
                        fw.act(rl[:], ph[:], AF.Relu)
                        fw.tt(HM[:, oc, :], rl[:], rl[:], ALU.mult)
                for c in range(8):
                    wt = ws.get(id_dn[hf][c])
                    for oo in range(2):
                        oc = c * 2 + oo
                        po = pbank()
                        for k in range(16):
                            fw.mm(po[:], wt[:, k, oo * 128:(oo + 1) * 128], HM[:, k, :], start=(k == 0), stop=(k == 15))
                        if hf == 0:
                            fw.stt(F32A[:, oc, :], F32A[:, oc, :], ALPHA, po[:], ALU.mult, ALU.add)
                        else:
                            fw.tt(F32A[:, oc, :], F32A[:, oc, :], po[:], ALU.add)
            layer_norm("ln3_g", "ln3_b")
            for j in range(4):
                ob_ = xs[j % 2]
                for kq in range(4):
                    pt = pbank()
                    for k4 in range(4):
                        k = kq * 4 + k4
                        fw.tr(pt[:, k4 * 128:(k4 + 1) * 128], F32A[:, k, j * 128:(j + 1) * 128], ident[:])
                    fw.copy(ob_[:, kq * 512:(kq + 1) * 512], pt[:], e=("act" if kq % 2 else "dve"))
                fw.dma(V(y_out.ap()[t0 + j * 128:t0 + (j + 1) * 128, :], None), ob_[:], xs_sem[j % 2])
        fw.barrier()
        st.close()

    if 6 in phases:
        phase0(lite=True)
        fw.barrier()
        rwkv_phase(0, so=True)
        mamba_phase(0, so=True)
    if 0 in phases:
        phase0()
    fw.barrier()
    if 1 in phases:
        rwkv_phase(0)
    if 3 in phases:
        mamba_phase(0)
    if 2 in phases:
        rwkv_phase(1)
    if 4 in phases:
        mamba_phase(1)
    if 5 in phases:
        phase3()
    fw.barrier()
    es.close()
    return nc, fw


def host_params(inp, swap=False):
    g = lambda k: np.asarray(inp[k])[0]
    pvec = np.zeros((128, NPAR), np.float32)

    def put(name, vec, j0=0):
        vec = np.asarray(vec, np.float32)
        n = vec.shape[0]
        nch = (n + 127) // 128
        pad = np.zeros(nch * 128, np.float32)
        pad[:n] = vec
        pvec[:, POFF[name] + j0:POFF[name] + j0 + nch] = pad.reshape(nch, 128).T

    mup, mun = g("rw_mu_prev"), g("rw_mu_next")
    if swap:
        mup, mun = mun, mup
    for nm, mu in (("mup", mup), ("mun", mun)):
        put(nm, mu[0:3072], 0)
        put(nm, mu[3072:3168], 24)
        put(nm, mu[3168:3264], 25)
        put(nm, mu[3264:3520], 26)
    dirs = (1, 0) if swap else (0, 1)
    for d in range(2):
        put("w0", g("rw_w0")[dirs[d]], 8 * d)
        put("a0", g("rw_a0")[dirs[d]], 8 * d)
    put("k_k", g("rw_k_k"))
    put("k_a", g("rw_k_a"))
    put("r_k", g("rw_r_k").reshape(-1))
    put("gn_g", g("rw_gn_g"))
    put("gn_b", g("rw_gn_b"))
    cw = g("m_conv_w")
    if swap:
        cw = cw[::-1]
    for j in range(5):
        put("conv_w", cw[j], 32 * j)
    put("conv_b", g("m_conv_b"))
    put("m_norm_g", g("m_norm_g"))
    put("m_d", np.repeat(g("m_d"), 64))
    for nm in ("ln1_g", "ln1_b", "ln2_g", "ln2_b", "ln3_g", "ln3_b"):
        put(nm, g(nm))
    dtp = np.zeros((128, 2, 2, 4, 32), np.float32)
    for d in range(2):
        dtp[:, 0, d] = g("m_dt_bias")[dirs[d]][None, None, :]
        dtp[:, 1, d] = g("m_a_log")[dirs[d]][None, None, :]
    blk = np.zeros((128, 128), np.float32)
    blk[:64, :64] = 1.0
    blk[64:, 64:] = 1.0
    rwc = np.zeros((128, 2, 512), np.float32)
    idx = np.arange(128)
    hh, ss = idx // 64, idx % 64
    same = hh[:, None] == hh[None, :]
    lt = ss[:, None] < ss[None, :]
    le = ss[:, None] <= ss[None, :]
    rwc[:, 0, 0:128] = same & lt
    rwc[:, 0, 128:256] = same & le
    rwc[:, 0, 256:384] = same & lt.T
    rwc[:, 1, 0:128] = same & lt.T
    rwc[:, 1, 128:256] = same & le.T
    rwc[:, 1, 256:384] = same & lt
    mc = np.zeros((128, 2, 2, 128), np.float32)
    i128 = np.arange(128)
    mc[:, 0, 0] = i128[:, None] <= i128[None, :]
    mc[:, 0, 1] = i128[:, None] > i128[None, :]
    mc[:, 1, 0] = i128[:, None] >= i128[None, :]
    mc[:, 1, 1] = i128[:, None] < i128[None, :]
    rmask = np.ones((128, 1024), np.float32)
    rmask[:, ::64] = 0.0
    return dict(pvec=pvec, dtp=dtp, ident=np.eye(128, dtype=np.float32), blk64=blk, rwc=rwc, rmask=rmask,
                mc=mc,
                rw_w2=g("rw_w2"), rw_a2=g("rw_a2"), rw_g2=g("rw_g2"), w_in=g("w_in"),
                w_br=g("w_br"), w_bm=g("w_bm"), w_o=g("w_o"), w_q=g("w_q"), w_kv=g("w_kv"), w_co=g("w_co"),
                w_up=g("w_up"), w_down=g("w_down"))


T_CORE = 8192
_CACHE = {}


def _x_ext(xseq, start, T_loc, rev):
    NT0 = (T_loc + TV - 1) // TV
    TP = NT0 * TV
    L = xseq.shape[0]
    out = np.zeros((TP + 4, D), np.float32)
    if not rev:
        lo, hi = start - 2, start + T_loc + 2
        slo, shi = max(lo, 0), min(hi, L)
        out[slo - lo:shi - lo] = xseq[slo:shi]
    else:
        lo, hi = start - 2, start + T_loc + 2
        slo, shi = max(lo, 0), min(hi, L)
        seg = xseq[slo:shi][::-1]
        r0 = start + T_loc + 1 - (shi - 1)
        out[r0:r0 + seg.shape[0]] = seg
    return out


def kernel(**inputs):
    inp = {k: np.asarray(v) for k, v in inputs.items()}
    xp, xs_, mp, ms = inp["x_prompt"], inp["x_sample"], inp["mem_prompt"], inp["mem_sample"]
    T_loc = T_CORE
    if "nc" not in _CACHE:
        _CACHE["nc"] = build(T_loc)[0]
    nc = _CACHE["nc"]
    hp = [host_params(inp, swap=False), host_params(inp, swap=True)]
    cores = []
    for c in range(8):
        if c < 4:
            s, half = c // 2, c % 2
            cores.append(dict(x=xp[s], start=half * T_loc, rev=(half == 1), mem=mp[s]))
        else:
            cores.append(dict(x=xs_[c - 4], start=0, rev=False, mem=ms[c - 4]))
    base_maps = []
    xe = [_x_ext(cd["x"], cd["start"], T_loc, cd["rev"]) for cd in cores]
    for c, cd in enumerate(cores):
        own = hp[1 if cd["rev"] else 0]
        m = dict(own)
        m["x_ext"] = xe[c]
        m["mem"] = np.ascontiguousarray(cd["mem"], dtype=np.float32)
        if c < 4:
            partner = c ^ 1
            oth = hp[1 if cores[partner]["rev"] else 0]
            m["x_ext2"] = xe[partner]
            m["pvec2"] = oth["pvec"]
            m["dtp2"] = oth["dtp"]
            m["sel"] = np.ones((128, 1), np.float32)
        else:
            m["x_ext2"] = xe[c]
            m["pvec2"] = own["pvec"]
            m["dtp2"] = own["dtp"]
            m["sel"] = np.zeros((128, 1), np.float32)
        base_maps.append(m)
    res2 = run_bass_kernel_spmd(nc, base_maps, core_ids=list(range(8)))
    ys = [np.asarray(res2.results[c]["y_out"], np.float32) for c in range(8)]
    y_prompt = np.empty_like(xp)
    for c in range(4):
        s, half = c // 2, c % 2
        y_prompt[s, half * T_loc:(half + 1) * T_loc] = ys[c][::-1] if half == 1 else ys[c]
    y_sample = np.stack(ys[4:8], axis=0)
    return (y_prompt, y_sample)
```

```python
import math
from contextlib import ExitStack
import numpy as np
import concourse.bass as bass
import concourse.mybir as mybir
from concourse.bass_utils import run_bass_kernel_spmd

F32 = mybir.dt.float32
BF16 = mybir.dt.bfloat16
AF = mybir.ActivationFunctionType
ALU = mybir.AluOpType
AX = mybir.AxisListType

SAME_ENGINE_SYNC = True
MB_STOP = 99
DBG_NOWLOAD = False
DBG_NOSTORE = False
MB_SUB = 99

D = 2048
NK = 16
TT = 512
TV = 508
IN_COLS = 13792
C_R, C_K, C_V, C_DW, C_DA, C_DG = 0, 1024, 2048, 3072, 3168, 3264
C_Z, C_XBC, C_DT, C_GATES = 3520, 5568, 9664, 9696


class Buf:
    __slots__ = ("w", "r", "name", "ex")

    def __init__(self, name=""):
        self.w = None
        self.r = []
        self.name = name
        self.ex = False


class V:
    __slots__ = ("ap", "buf")

    def __init__(self, ap, buf):
        self.ap = ap
        self.buf = buf

    def __getitem__(self, idx):
        return V(self.ap[idx], self.buf)

    def rr(self, pat, **kw):
        return V(self.ap.rearrange(pat, **kw), self.buf)

    def bc(self, shape):
        return V(self.ap.to_broadcast(list(shape)), self.buf)

    def us(self, axis):
        return V(self.ap.unsqueeze(axis), self.buf)


class T:
    def __init__(self, h, name="", track=True):
        self.h = h
        self.buf = Buf(name) if track else None

    def __getitem__(self, idx):
        return V(self.h[idx], self.buf)

    def v(self, ap):
        return V(ap, self.buf)


class FW:
    def __init__(self, nc):
        self.nc = nc
        self.eng = {"pe": nc.tensor, "dve": nc.vector, "act": nc.scalar, "pool": nc.gpsimd, "sp": nc.sync}
        self.sem = {}
        self.cnt = {}
        for e in self.eng:
            self.sem[e] = nc.alloc_semaphore("sem_" + e)
            self.cnt[e] = 0
        self.seen = {e: {} for e in self.eng}
        self.n_inst = 0
        self.n_wait = 0

    def dsem(self, name):
        if name in self.sem:
            return name
        self.sem[name] = self.nc.alloc_semaphore("dsem_" + name)
        self.cnt[name] = 0
        return name

    def scan(self, out, d0, d1, initial, op0, op1):
        r, w = self._bufs([d0, d1, initial]), self._bufs([out])
        self._deps("dve", r, w)
        i = self.nc.vector.tensor_tensor_scan(out.ap, d0.ap, d1.ap, self._ap(initial), op0, op1)
        return self._done(i, "dve", 1, r, w)

    def _wait(self, e, dep):
        if dep is None:
            return
        key, val = dep
        if key == e and (e == "pe" or e == "sp" or not SAME_ENGINE_SYNC):
            return
        if self.seen[e].get(key, 0) >= val:
            return
        assert val <= self.cnt[key], "wait on a not-yet-signalled count (%s %d > %d): potential deadlock" % (key, val, self.cnt[key])
        self.seen[e][key] = val
        self.eng[e].wait_ge(self.sem[key], val)
        self.n_wait += 1

    def _deps(self, e, reads, writes):
        mx = {}
        for b in reads:
            if b.w is not None and mx.get(b.w[0], 0) < b.w[1]:
                mx[b.w[0]] = b.w[1]
            if b.ex:
                for k, v in b.r:
                    if k != e and mx.get(k, 0) < v:
                        mx[k] = v
        for b in writes:
            if b.w is not None and mx.get(b.w[0], 0) < b.w[1]:
                mx[b.w[0]] = b.w[1]
            for k, v in b.r:
                if mx.get(k, 0) < v:
                    mx[k] = v
        for k, v in mx.items():
            self._wait(e, (k, v))

    def _done(self, inst, key, inc, reads, writes, signal=True):
        if signal:
            self.cnt[key] += inc
            inst.then_inc(self.sem[key], inc)
            dep = (key, self.cnt[key])
        else:
            dep = (key, self.cnt[key] + inc)
        for b in reads:
            b.r.append(dep)
            if len(b.r) > 24:
                mx = {}
                for k, v in b.r:
                    if mx.get(k, 0) < v:
                        mx[k] = v
                b.r = list(mx.items())
        for b in writes:
            b.w = dep
            b.r = []
        self.n_inst += 1
        return dep

    @staticmethod
    def _bufs(vs):
        out = []
        for v in vs:
            if isinstance(v, V) and v.buf is not None and v.buf not in out:
                out.append(v.buf)
        return out

    @staticmethod
    def _ap(v):
        return v.ap if isinstance(v, V) else v

    def mm(self, out, lhsT, rhs, start=True, stop=True, sig=False, **kw):
        r, w = self._bufs([lhsT, rhs]), self._bufs([out])
        self._deps("pe", r, w)
        i = self.nc.tensor.matmul(out.ap, lhsT.ap, rhs.ap, start=start, stop=stop, **kw)
        return self._done(i, "pe", 1, r, w, signal=(stop or sig))

    def tr(self, out, in_, ident):
        r, w = self._bufs([in_, ident]), self._bufs([out])
        self._deps("pe", r, w)
        i = self.nc.tensor.transpose(out.ap, in_.ap, ident.ap)
        return self._done(i, "pe", 1, r, w)

    def act(self, out, in_, func, bias=None, scale=None, e="act", accum_out=None):
        r, w = self._bufs([in_, bias, scale]), self._bufs([out, accum_out])
        self._deps(e, r, w)
        kw = {}
        if bias is not None:
            kw["bias"] = self._ap(bias)
        if scale is not None:
            kw["scale"] = self._ap(scale)
        if accum_out is not None:
            kw["accum_out"] = self._ap(accum_out)
        i = self.eng[e].activation(out.ap, in_.ap, func, **kw)
        return self._done(i, e, 1, r, w)

    def tt(self, out, a, b, op, e="dve"):
        r, w = self._bufs([a, b]), self._bufs([out])
        self._deps(e, r, w)
        i = self.eng[e].tensor_tensor(out.ap, a.ap, b.ap, op)
        return self._done(i, e, 1, r, w)

    def ts(self, out, in0, s1, s2, op0, op1=None, e="dve"):
        r, w = self._bufs([in0, s1, s2]), self._bufs([out])
        self._deps(e, r, w)
        kw = {}
        if op1 is not None:
            kw["op1"] = op1
        i = self.eng[e].tensor_scalar(out.ap, in0.ap, self._ap(s1), self._ap(s2), op0, **kw)
        return self._done(i, e, 1, r, w)

    def stt(self, out, in0, scalar, in1, op0, op1, e="dve"):
        r, w = self._bufs([in0, scalar, in1]), self._bufs([out])
        self._deps(e, r, w)
        i = self.eng[e].scalar_tensor_tensor(out.ap, in0.ap, self._ap(scalar), in1.ap, op0, op1)
        return self._done(i, e, 1, r, w)

    def copy(self, out, in_, e="dve"):
        r, w = self._bufs([in_]), self._bufs([out])
        self._deps(e, r, w)
        if e == "act":
            i = self.nc.scalar.copy(out.ap, in_.ap)
        else:
            i = self.eng[e].tensor_copy(out.ap, in_.ap)
        return self._done(i, e, 1, r, w)

    def memset(self, out, val, e="dve"):
        w = self._bufs([out])
        self._deps(e, [], w)
        i = self.eng[e].memset(out.ap, val)
        return self._done(i, e, 1, [], w)

    def dma(self, out, in_, sem, q="sp", **kw):
        r, w = self._bufs([in_]), self._bufs([out])
        self._deps(q, r, w)
        i = self.eng[q].dma_start(out=out.ap, in_=in_.ap, **kw)
        return self._done(i, sem, 16, r, w)

    def collective(self, kind, op, groups, in_v, out_v, sem):
        r, w = self._bufs([in_v]), self._bufs([out_v])
        self._deps("pool", r, w)
        i = self.nc.gpsimd.collective_compute(kind, op=op, replica_groups=groups, ins=[in_v.ap], outs=[out_v.ap])
        return self._done(i, sem, 16, r, w)

    def barrier(self):
        for e in self.eng:
            for key, val in self.cnt.items():
                if val > 0:
                    self._wait(e, (key, val)) if key != e else None


class Ctx:
    pass


def _param_layout():
    off = {}
    n = 0
    for name, w in [("mup", 28), ("mun", 28), ("w0", 16), ("a0", 16), ("k_k", 8), ("k_a", 8), ("r_k", 8),
                    ("gn_g", 8), ("gn_b", 8), ("conv_w", 160), ("conv_b", 32), ("m_norm_g", 16), ("m_d", 16),
                    ("ln1_g", 16), ("ln1_b", 16), ("ln2_g", 16), ("ln2_b", 16), ("ln3_g", 16), ("ln3_b", 16)]:
        off[name] = n
        n += w
    return off, n


POFF, NPAR = _param_layout()
XOFF = {"c0": 0, "nk_k": 28, "omk_a": 36}
NX = 44


def build(T_loc, dbg=False, phases=(6, 0, 1, 2, 3, 4, 5)):
    NT0 = (T_loc + TV - 1) // TV
    TP = NT0 * TV
    XR = TP + 4
    nc = bass.Bass("TRN2", target_bir_lowering=False)
    fw = FW(nc)
    es = ExitStack()

    def din(name, shape, dt=F32):
        return nc.dram_tensor(name, list(shape), dt, kind="ExternalInput")

    def dscr(name, shape, dt=F32):
        return nc.dram_tensor(name, list(shape), dt, kind=("ExternalOutput" if dbg else "Internal"))

    x_ext = din("x_ext", [XR, D])
    x_ext2 = din("x_ext2", [XR, D])
    pvec2_d = din("pvec2", [128, NPAR])
    dtp2_d = din("dtp2", [128, 2, 2, 4, 32])
    sel_d = din("sel", [128, 1])
    w_in = din("w_in", [D, IN_COLS])
    pvec_d = din("pvec", [128, NPAR])
    ident_d = din("ident", [128, 128])
    blk_d = din("blk64", [128, 128])
    w2_d = din("rw_w2", [96, 1024])
    a2_d = din("rw_a2", [96, 1024])
    g2_d = din("rw_g2", [256, 1024])
    dtp_d = din("dtp", [128, 2, 2, 4, 32])

    S = {}
    for nm in ["R", "V", "A", "G", "BON", "KD1", "KD2", "B1", "B2", "LW1", "LW2"]:
        S[nm] = dscr("s_" + nm, [1024, TP])
    S["XBC"] = dscr("s_XBC", [4096, TP])
    S["DTS"] = dscr("s_DTS", [TP, 4, 32])
    S2 = {}
    for nm in ["V", "A", "KD1", "B1", "LW1"]:
        S2[nm] = dscr("s2_" + nm, [1024, TP])
    S2["XBC"] = dscr("s2_XBC", [4096, TP])
    S2["DTS"] = dscr("s2_DTS", [TP, 4, 32])
    stx_rw = nc.dram_tensor("stx_rw", [128, 8, 128], F32, kind="Internal")
    stx_m = nc.dram_tensor("stx_m", [128, 32, 64], F32, kind="Internal")

    def sb(name, shape, dt=F32):
        return T(es.enter_context(nc.sbuf_tensor("sb_" + name, list(shape), dt)), name)

    PB = [T(nc.alloc_psum_tensor("pb%d" % i, [128, 512], F32), "pb%d" % i) for i in range(8)]
    for t_ in PB:
        t_.buf.ex = True

    ident = sb("ident", [128, 128])
    blk = sb("blk", [128, 128])
    sel = sb("sel", [128, 1])
    dcp = fw.dsem("constp")
    fw.dma(ident[:], V(ident_d.ap(), None), fw.dsem("c2"))
    fw.dma(blk[:], V(blk_d.ap(), None), fw.dsem("c3"))
    fw.dma(sel[:], V(sel_d.ap(), None), fw.dsem("c12"))
    w_in_v = w_in.ap().rearrange("(k p) n -> p k n", p=128)
    CUR = {}
    PSETS = []
    for si, (pd_, dd_) in enumerate(((pvec_d, dtp_d), (pvec2_d, dtp2_d))):
        pvec_t = sb("pvec%d" % si, [128, NPAR])
        xpar_t = sb("xpar%d" % si, [128, NX])
        fw.dma(pvec_t[:], V(pd_.ap(), None), fw.dsem("c1_%d" % si))
        fw.tt(xpar_t[:, 0:28], pvec_t[:, POFF["mup"]:POFF["mup"] + 28], pvec_t[:, POFF["mun"]:POFF["mun"] + 28], ALU.add)
        fw.ts(xpar_t[:, 0:28], xpar_t[:, 0:28], -1.0, 1.0, ALU.mult, ALU.add)
        fw.ts(xpar_t[:, 28:36], pvec_t[:, POFF["k_k"]:POFF["k_k"] + 8], -1.0, None, ALU.mult)
        fw.ts(xpar_t[:, 36:44], pvec_t[:, POFF["k_a"]:POFF["k_a"] + 8], -1.0, 1.0, ALU.mult, ALU.add)
        PSETS.append(dict(pvec=pvec_t, xpar=xpar_t, dtp_d=dd_))
    PSETS[0].update(x=x_ext, S=S)
    PSETS[1].update(x=x_ext2, S=S2)
    CUR.update(PSETS[0])

    def pv(name, j=0, n=128):
        c = POFF[name] + j
        return CUR["pvec"][0:n, c:c + 1]

    def xp(name, j=0, n=128):
        c = XOFF[name] + j
        return CUR["xpar"][0:n, c:c + 1]

    def phase0(lite=False):
        st = ExitStack()
        CUR.update(PSETS[1 if lite else 0])
        x_src = CUR["x"]
        Sd = CUR["S"]
        nd = 1 if lite else 2

        def sb0(name, shape, dt=F32):
            return T(st.enter_context(nc.sbuf_tensor(("p0l_" if lite else "p0_") + name, list(shape), dt)), name)

        w2 = sb0("w2", [96, 1024], BF16)
        a2 = sb0("a2", [96, 1024], BF16)
        g2 = sb0("g2", [128, 2, 1024], BF16)
        wdt = sb0("wdt", [128, NK, 32], BF16)
        dtp = sb0("dtp", [128, 2, 2, 4, 32])
        An = sb0("An", [128, 2, 4, 32])
        fw.dma(dtp[:], V(CUR["dtp_d"].ap(), None), fw.dsem("c4"))
        fw.dma(w2[:], V(w2_d.ap(), None), fw.dsem("c5"), q="pool")
        fw.dma(a2[:], V(a2_d.ap(), None), fw.dsem("c6"), q="pool")
        fw.dma(g2[:], V(g2_d.ap().rearrange("(k p) n -> p k n", p=128), None), fw.dsem("c7"), q="pool")
        fw.dma(wdt[:], V(w_in_v[:, :, C_DT:C_DT + 32], None), dcp, q="pool")
        fw.act(An[:], dtp[:, 1], AF.Exp)
        fw.ts(An[:], An[:], -1.0, None, ALU.mult)
        xs = [sb0("xs%d" % j, [128, D]) for j in range(2)]
        xs_sem = [fw.dsem("xs%d" % j) for j in range(2)]
        xTs = [sb0("xT%d" % j, [128, NK, TT], BF16) for j in range(2)]
        xT = xTs[0]

        def load_blk(ti_, j):
            fw.dma(xs[j % 2][:], V(x_src.ap()[ti_ * TV + j * 128:ti_ * TV + (j + 1) * 128, :], None), xs_sem[j % 2])

        def tr_blk(ti_, j):
            xb = xs[j % 2]
            dst = xTs[ti_ % 2]
            for kq in range(4):
                pt = PB[4 + (kq % 2)]
                for k4 in range(4):
                    k = kq * 4 + k4
                    fw.tr(pt[:, k4 * 128:(k4 + 1) * 128], xb[:, k * 128:(k + 1) * 128], ident[:])
                fw.copy(dst[:, kq * 4:(kq + 1) * 4, j * 128:(j + 1) * 128],
                        pt.v(pt.h[:].rearrange("p (a b) -> p a b", a=4)), e=("act" if kq % 2 else "dve"))
        WG = [sb0("wg%d" % j, [128, NK, 512], BF16) for j in range(2)]
        wg_sem = [fw.dsem("wg%d" % j) for j in range(2)]
        ag = sb0("ag", [128, 8, 2, TV])
        kp = sb0("kp", [128, 8, TV])
        rk = sb0("rk", [128, 8, TV])
        tdw = sb0("tdw", [96, TV], BF16)
        tda = sb0("tda", [96, TV], BF16)
        tdg = sb0("tdg", [128, 2, TV], BF16)
        NS = 8
        ost = [sb0("ost%d" % j, [128, TV]) for j in range(NS)]
        ost_sem = [fw.dsem("ost%d" % j) for j in range(NS)]
        tmp = [sb0("tmp%d" % j, [128, TV]) for j in range(4)]
        dts = sb0("dts", [128, 4, 4, 32])
        dtt = sb0("dtt", [128, 4, 32])
        dts_sem = fw.dsem("dts")
        cnt = {"ost": 0, "tmp": 0, "wg": 0, "pb": 0, "aux": 0}

        def new_ost():
            j = cnt["ost"] % NS
            cnt["ost"] += 1
            return ost[j], ost_sem[j]

        def new_tmp():
            j = cnt["tmp"] % 4
            cnt["tmp"] += 1
            return tmp[j]

        def new_pb():
            j = cnt["pb"] % 4
            cnt["pb"] += 1
            return PB[j]

        def new_aux():
            j = cnt["aux"] % 4
            cnt["aux"] += 1
            return PB[4 + j]

        def store(name, row0, nrow, i, o, osem):
            if DBG_NOSTORE and cnt["ost"] > 8:
                return
            fw.dma(V(Sd[name].ap()[row0:row0 + nrow, i * TV:(i + 1) * TV], None), o[0:nrow, :], osem)

        def load_wg(col0, n):
            j = cnt["wg"] % 2
            cnt["wg"] += 1
            if not (DBG_NOWLOAD and cnt["wg"] > 2):
                fw.dma(WG[j][:, :, 0:n], V(w_in_v[:, :, col0:col0 + n], None), wg_sem[j], q="pool")
            return WG[j]

        def proj(P, wt, c0, ncol):
            for k in range(NK):
                fw.mm(P[0:ncol, :], wt[:, k, c0:c0 + ncol], xT[:, k, :], start=(k == 0), stop=(k == NK - 1))

        def shift(P, pc, nrow, out):
            fw.act(out, P[0:nrow, 2:2 + TV], AF.Copy, scale=xp("c0", pc, nrow))
            fw.stt(out, P[0:nrow, 1:1 + TV], pv("mup", pc, nrow), out, ALU.mult, ALU.add)
            fw.stt(out, P[0:nrow, 3:3 + TV], pv("mun", pc, nrow), out, ALU.mult, ALU.add)

        for i in range(NT0):
            if i == 0:
                load_blk(0, 0)
                load_blk(0, 1)
                tr_blk(0, 0)
                load_blk(0, 2)
                tr_blk(0, 1)
                load_blk(0, 3)
                tr_blk(0, 2)
                tr_blk(0, 3)
            xT = xTs[i % 2]
            nxt = i + 1 < NT0
            if nxt:
                load_blk(i + 1, 0)
                load_blk(i + 1, 1)
            wt = load_wg(C_DW, 448)
            P = new_pb()
            proj(P, wt, 0, 96)
            t = new_tmp()
            shift(P, 24, 96, t[0:96, :])
            fw.act(tdw[:], t[0:96, :], AF.Tanh)
            P = new_pb()
            proj(P, wt, 96, 96)
            t = new_tmp()
            shift(P, 25, 96, t[0:96, :])
            fw.copy(tda[:], t[0:96, :], e="act")
            for c in range(0 if lite else 2):
                P = new_pb()
                proj(P, wt, 192 + 128 * c, 128)
                t = new_tmp()
                shift(P, 26 + c, 128, t[:])
                fw.act(tdg[:, c, :], t[:], AF.Sigmoid)
            pd = new_aux()
            pdv = pd.v(pd.h[:, 0:128].rearrange("p (a b) -> p a b", a=4))
            for j in range(4):
                for k in range(NK):
                    fw.mm(pdv[:, j, :], xT[:, k, j * 128:(j + 1) * 128], wdt[:, k, :], start=(k == 0), stop=(k == NK - 1))
            for d in range(2):
                fw.tt(dtt[:], pdv, dtp[:, 0, d], ALU.add)
                fw.act(dtt[:], dtt[:], AF.Exp)
                fw.act(dts[:, :, d, :], dtt[:], AF.Ln, bias=1.0)
                fw.tt(dts[:, :, 2 + d, :], dts[:, :, d, :], An[:, d], ALU.mult)
            for j in range(4):
                lo, hi = max(2, 128 * j), min(2 + TV, 128 * j + 128)
                fw.dma(V(Sd["DTS"].ap()[i * TV + lo - 2:i * TV + hi - 2], None), dts[lo - 128 * j:hi - 128 * j, j], dts_sem)
            for c in range(8):
                P = new_aux()
                fw.mm(P[:, 0:TV], w2[:, c * 128:(c + 1) * 128], tdw[:])
                for d in range(nd):
                    t = new_tmp()
                    fw.act(t[:], P[:, 0:TV], AF.Sigmoid, bias=pv("w0", d * 8 + c))
                    o, osem = new_ost()
                    fw.ts(o[:], t[:], -math.exp(-0.5), None, ALU.mult, e="dve")
                    store("LW%d" % (d + 1), c * 128, 128, i, o, osem)
                P = new_aux()
                fw.mm(P[:, 0:TV], a2[:, c * 128:(c + 1) * 128], tda[:])
                for d in range(nd):
                    fw.act(ag[:, c, d, :], P[:, 0:TV], AF.Sigmoid, bias=pv("a0", d * 8 + c))
                if not lite:
                    P = new_aux()
                    for k in range(2):
                        fw.mm(P[:, 0:TV], g2[:, k, c * 128:(c + 1) * 128], tdg[:, k, :], start=(k == 0), stop=(k == 1))
                    o, osem = new_ost()
                    fw.copy(o[:], P[:, 0:TV], e="dve")
                    store("G", c * 128, 128, i, o, osem)
            for gi in range(2):
                wt = load_wg(C_K + 512 * gi, 512)
                Pn = None
                for cc in range(4):
                    c = gi * 4 + cc
                    if Pn is None:
                        P = new_pb()
                        proj(P, wt, cc * 128, 128)
                    else:
                        P = Pn
                    shift(P, 8 + c, 128, kp[:, c, :])
                    sq = new_tmp()
                    fw.act(sq[:], kp[:, c, :], AF.Square, scale=pv("k_k", c))
                    Pn = None
                    if cc < 3:
                        Pn = new_pb()
                        proj(Pn, wt, (cc + 1) * 128, 128)
                    Pa = new_aux()
                    fw.mm(Pa[:, 0:TV], blk[:], sq[:])
                    rn = new_tmp()
                    fw.act(rn[:], Pa[:, 0:TV], AF.Ln, bias=1e-24)
                    fw.act(rn[:], rn[:], AF.Exp, scale=-0.5)
                    oa, osem = new_ost()
                    fw.stt(oa[:], kp[:, c, :], xp("nk_k", c), rn[:], ALU.mult, ALU.mult)
                    store("A", c * 128, 128, i, oa, osem)
                    for d in range(nd):
                        o, osem = new_ost()
                        fw.stt(o[:], oa[:], -1.0, ag[:, c, d, :], ALU.mult, ALU.mult)
                        store("B%d" % (d + 1), c * 128, 128, i, o, osem)
                        t = new_tmp()
                        fw.act(t[:], ag[:, c, d, :], AF.Identity, scale=pv("k_a", c), bias=xp("omk_a", c))
                        o, osem = new_ost()
                        fw.tt(o[:], t[:], kp[:, c, :], ALU.mult, e="dve")
                        store("KD%d" % (d + 1), c * 128, 128, i, o, osem)
            if nxt:
                tr_blk(i + 1, 0)
                tr_blk(i + 1, 1)
                load_blk(i + 1, 2)
                load_blk(i + 1, 3)
            for gi in range(0 if lite else 2):
                wt = load_wg(C_R + 512 * gi, 512)
                Pn = None
                for cc in range(4):
                    c = gi * 4 + cc
                    if Pn is None:
                        P = new_pb()
                        proj(P, wt, cc * 128, 128)
                    else:
                        P = Pn
                    o, osem = new_ost()
                    shift(P, c, 128, o[:])
                    store("R", c * 128, 128, i, o, osem)
                    t = new_tmp()
                    fw.stt(t[:], o[:], pv("r_k", c), kp[:, c, :], ALU.mult, ALU.mult)
                    Pn = None
                    if cc < 3:
                        Pn = new_pb()
                        proj(Pn, wt, (cc + 1) * 128, 128)
                    Pa = new_aux()
                    fw.mm(Pa[:, 0:TV], blk[:], t[:])
                    fw.copy(rk[:, c, :], Pa[:, 0:TV], e="act")
            for gi in range(2):
                wt = load_wg(C_V + 512 * gi, 512)
                for cc in range(4):
                    c = gi * 4 + cc
                    P = new_pb()
                    proj(P, wt, cc * 128, 128)
                    o, osem = new_ost()
                    shift(P, 16 + c, 128, o[:])
                    store("V", c * 128, 128, i, o, osem)
                    if not lite:
                        o2, osem2 = new_ost()
                        fw.tt(o2[:], o[:], rk[:, c, :], ALU.mult, e="dve")
                        store("BON", c * 128, 128, i, o2, osem2)
            if nxt:
                tr_blk(i + 1, 2)
                tr_blk(i + 1, 3)
            for gi in range(6 if lite else 8):
                wt = load_wg(C_XBC + 512 * gi, 512)
                for cc in range(4):
                    c = gi * 4 + cc
                    P = new_pb()
                    proj(P, wt, cc * 128, 128)
                    t = new_tmp()
                    fw.act(t[:], P[:, 0:TV], AF.Identity, scale=pv("conv_w", c), bias=pv("conv_b", c))
                    for j in range(1, 5):
                        fw.stt(t[:], P[:, j:j + TV], pv("conv_w", 32 * j + c), t[:], ALU.mult, ALU.add)
                    o, osem = new_ost()
                    fw.act(o[:], t[:], AF.Silu)
                    store("XBC", c * 128, 128, i, o, osem)
        fw.barrier()
        st.close()


    NST = T_loc // 128
    S["YRW"] = dscr("s_YRW", [1024, TP])
    st_rw_out = nc.dram_tensor("st_rw_out", [128, 8, 128], F32, kind="ExternalOutput")
    rwc_d = din("rwc", [128, 2, 512])
    rmask_d = din("rmask", [128, 1024])
    identb = sb("identb", [128, 128], BF16)
    fw.dma(identb[:], V(ident_d.ap(), None), fw.dsem("c8"), q="pool")

    def rwkv_phase(d, so=False):
        st = ExitStack()
        Ss = S2 if so else S

        def sbp(name, shape, dt=F32):
            return T(st.enter_context(nc.sbuf_tensor(("rws_" if so else "rw%d_" % d) + name, list(shape), dt)), name)

        rwc = sbp("rwc", [128, 2, 512])
        rmask = sbp("rmask", [128, 1024])
        fw.dma(rwc[:], V(rwc_d.ap(), None), fw.dsem("c9"))
        fw.dma(rmask[:], V(rmask_d.ap(), None), fw.dsem("c10"))
        names = ["R", "KD%d" % (d + 1), "V", "A", "B%d" % (d + 1), "LW%d" % (d + 1)]
        LD = [[sbp("ld%d_%d" % (j, b), [128, 8, 128]) for j in range(6)] for b in range(2)]
        ld_sem = [[fw.dsem("rwld%d_%d" % (j, b)) for j in range(6)] for b in range(2)]
        Y1 = [sbp("y1_%d" % b, [128, 8, 128]) for b in range(2)]
        y1_sem = [fw.dsem("rwy1_%d" % b) for b in range(2)]
        YB = [sbp("yb_%d" % b, [128, 8, 128]) for b in range(2)]
        yb_sem = [fw.dsem("rwyb_%d" % b) for b in range(2)]
        pre = sbp("pre", [128, 1024])
        cl = sbp("cl", [128, 1024])
        ecl = sbp("ecl", [128, 8, 128])
        encl = sbp("encl", [128, 8, 128])
        ecx = sbp("ecx", [128, 8, 128])
        xt = [sbp("xt%d" % j, [128, 8, 128]) for j in range(4)]
        BT = [sbp("BTbd%d" % b, [128, 8, 128], BF16) for b in range(2)]
        KT = [sbp("KTbd%d" % b, [128, 8, 128], BF16) for b in range(2)]
        AR = [sbp("ARbd%d" % b, [128, 8, 256], BF16) for b in range(2)]
        VB = [sbp("Vbd%d" % b, [128, 8, 128], BF16) for b in range(2)]
        for b in range(2):
            for t_ in (BT[b], KT[b], AR[b], VB[b]):
                fw.memset(t_[:], 0.0, e="pool")
        S0q = [sbp("S0_%d" % q_, [128, 4, 128]) for q_ in range(2)]
        S0bq = [sbp("S0b_%d" % q_, [128, 4, 128], BF16) for q_ in range(2)]
        ssem = fw.dsem("rwstate")
        ssem2 = fw.dsem("rwstate2")
        for q_ in range(2):
            if d == 0:
                fw.memset(S0q[q_][:], 0.0)
            else:
                fw.dma(S0q[q_][:], V(stx_rw.ap()[:, 4 * q_:4 * q_ + 4, :], None), (ssem, ssem2)[q_])
                fw.ts(S0q[q_][:].rr("p a b -> p (a b)"), S0q[q_][:].rr("p a b -> p (a b)"), sel[:, 0:1], None, ALU.mult)
            fw.copy(S0bq[q_][:], S0q[q_][:], e="act")
        ABs = [sbp("ABs%d" % b, [128, 4, 256], BF16) for b in range(4)]
        AKs = [sbp("AKs%d" % b, [128, 4, 256], BF16) for b in range(4)]
        NTs = [[sbp("NTs%d_%d" % (b, j), [128, 4, 128], BF16) for j in range(2)] for b in range(4)]
        Ns = [[sbp("Ns%d_%d" % (b, j), [128, 4, 128], BF16) for j in range(2)] for b in range(4)]
        Ps = [[sbp("Ps%d_%d" % (b, j), [128, 4, 128], BF16) for j in range(2)] for b in range(4)]
        VTs = [sbp("VTs%d" % b, [128, 4, 128], BF16) for b in range(4)]
        GTs = [sbp("GTs%d" % b, [128, 4, 128], BF16) for b in range(2)]
        UTs = [sbp("UTs%d" % b, [128, 4, 128], BF16) for b in range(2)]
        BKT = [sbp("BKT%d" % b, [128, 4, 2, 128], BF16) for b in range(4)]
        stmp = [sbp("stmp%d" % b, [128, 4, 128]) for b in range(2)]
        pbc = {"n": 0}

        def pbank():
            j = pbc["n"] % 8
            pbc["n"] += 1
            return PB[j]

        mAB = rwc[:, d, 0:256]
        mNT = rwc[:, d, 256:384]
        tiles = list(range(NST)) if d == 0 else list(range(NST - 1, -1, -1))
        chunks = (0, 1) if d == 0 else (1, 0)

        def load_tile(n):
            ti = tiles[n]
            b = n % 2
            for j in range(6):
                if so and j == 0:
                    continue
                fw.dma(LD[b][j][:], V(Ss[names[j]].ap()[:, ti * 128:(ti + 1) * 128].rearrange("(c p) t -> p c t", p=128), None),
                       ld_sem[b][j])
            if d == 1:
                fw.dma(Y1[b][:], V(S["YRW"].ap()[:, ti * 128:(ti + 1) * 128].rearrange("(c p) t -> p c t", p=128), None),
                       y1_sem[b])

        load_tile(0)
        qn = 0
        for n in range(NST):
            ti = tiles[n]
            b = n % 2
            if n + 1 < NST:
                load_tile(n + 1)
            r_, k_, v_, a_, b_, lw_ = [LD[b][j] for j in range(6)]
            fw.scan(pre[:], rmask[:], lw_[:].rr("p c t -> p (c t)"), 0.0, ALU.mult, ALU.add)
            pre4 = pre[:].rr("p (c t) -> p c t", t=64)
            cl4 = cl[:].rr("p (c t) -> p c t", t=64)
            lw4 = lw_[:].rr("p c (u t) -> p (c u) t", t=64)
            if d == 0:
                clv = pre
            else:
                fw.tt(cl4, lw4, pre4, ALU.subtract)
                fw.tt(cl4, cl4, pre4[:, :, 63:64].bc([128, 16, 64]), ALU.add)
                clv = cl
            clf = clv[:]
            fw.act(ecl[:].rr("p c t -> p (c t)"), clf, AF.Exp)
            fw.act(encl[:].rr("p c t -> p (c t)"), clf, AF.Exp, scale=-1.0)
            fw.tt(ecx[:].rr("p c t -> p (c t)"), clf, lw_[:].rr("p c t -> p (c t)"), ALU.subtract, e="pool")
            fw.act(ecx[:].rr("p c t -> p (c t)"), ecx[:].rr("p c t -> p (c t)"), AF.Exp)
            fw.tt(xt[0][:], b_[:], encl[:], ALU.mult, e="pool")
            fw.tt(xt[1][:], k_[:], encl[:], ALU.mult)
            fw.tt(xt[2][:], a_[:], ecx[:], ALU.mult, e="pool")
            if not so:
                fw.tt(xt[3][:], r_[:], ecl[:], ALU.mult)
            corder = list(chunks)
            cinfo = {}
            for pos, ci in enumerate(corder):
                cb = (2 * n + ci) % 2
                cs = slice(ci * 64, ci * 64 + 64)
                for hh in range(2):
                    ps = slice(64 * hh, 64 * hh + 64)
                    fs = slice(64 * hh, 64 * hh + 64)
                    fw.copy(BT[cb][ps, :, fs], xt[0][ps, :, cs], e="pool")
                    fw.copy(KT[cb][ps, :, fs], xt[1][ps, :, cs], e="dve")
                    fw.copy(AR[cb][ps, :, fs], xt[2][ps, :, cs], e="pool")
                    if not so:
                        fw.copy(AR[cb][ps, :, slice(128 + 64 * hh, 192 + 64 * hh)], xt[3][ps, :, cs], e="act")
                    fw.copy(VB[cb][ps, :, fs], v_[ps, :, cs], e="dve")
                wl = ecl[:, :, (ci * 64 + 63) if d == 0 else (ci * 64)]
                cinfo[pos] = (cb, cs, wl)
            minv_of = {}

            def pre_body(pos, q, cb, cs, wl):
                qb = pos * 2 + q
                p0 = 4 * q
                for half in range(2):
                    pa = pbank()
                    pk = pbank()
                    for pp in range(2):
                        p = p0 + 2 * half + pp
                        fw.mm(pa[:, pp * 256:(pp + 1) * 256], BT[cb][:, p, :], AR[cb][:, p, :])
                        fw.mm(pk[:, pp * 256:(pp + 1) * 256], KT[cb][:, p, :], AR[cb][:, p, :])
                    fw.tt(ABs[qb][:, 2 * half:2 * half + 2, :], pa[:].rr("p (a b) -> p a b", a=2),
                          mAB.us(1).bc([128, 2, 256]), ALU.mult)
                    fw.tt(AKs[qb][:, 2 * half:2 * half + 2, :], pk[:].rr("p (a b) -> p a b", a=2),
                          mAB.us(1).bc([128, 2, 256]), ALU.mult)
                pn = pbank()
                for pp in range(4):
                    p = p0 + pp
                    fw.mm(pn[:, pp * 128:(pp + 1) * 128], AR[cb][:, p, 0:128], BT[cb][:, p, :])
                fw.tt(NTs[qb][0][:], pn[:].rr("p (a b) -> p a b", a=4), mNT.us(1).bc([128, 4, 128]), ALU.mult)
                Ncur = ABs[qb][:, :, 0:128]
                NTcur = NTs[qb][0][:]
                fw.tt(Ps[qb][0][:], Ncur, identb[:].us(1).bc([128, 4, 128]), ALU.add, e="pool")
                Pcur = Ps[qb][0][:]
                yield
                for lev in range(1, 6):
                    j = lev % 2
                    pnt = pbank()
                    for pp in range(4):
                        fw.mm(pnt[:, pp * 128:(pp + 1) * 128], Ncur[:, pp, :], NTcur[:, pp, :])
                    fw.copy(NTs[qb][j][:], pnt[:].rr("p (a b) -> p a b", a=4), e="act")
                    yield
                    if lev <= 4:
                        pnn = pbank()
                        for pp in range(4):
                            fw.mm(pnn[:, pp * 128:(pp + 1) * 128], NTcur[:, pp, :], Ncur[:, pp, :])
                        fw.copy(Ns[qb][j][:], pnn[:].rr("p (a b) -> p a b", a=4), e="dve")
                        Nnext = Ns[qb][j][:]
                    NTnext = NTs[qb][j][:]
                    pp_ = pbank()
                    for pp in range(4):
                        fw.mm(pp_[:, pp * 128:(pp + 1) * 128], identb[:], Pcur[:, pp, :], start=True, stop=False)
                        fw.mm(pp_[:, pp * 128:(pp + 1) * 128], NTnext[:, pp, :], Pcur[:, pp, :], start=False, stop=True)
                    fw.copy(Ps[qb][j][:], pp_[:].rr("p (a b) -> p a b", a=4), e="dve")
                    Pcur = Ps[qb][j][:]
                    NTcur = NTnext
                    if lev <= 4:
                        Ncur = Nnext
                Minv = Pcur
                yield
                pv_ = pbank()
                pvb = pv_.v(pv_.h[:].bitcast(BF16)[:, 0:512].rearrange("p (a b) -> p a b", a=4))
                for pp in range(4):
                    fw.tr(pvb[:, pp, :], VB[cb][:, p0 + pp, :], identb[:])
                fw.copy(VTs[qb][:], pvb, e="act")
                minv_of[(pos, q)] = Minv
                yield
                pt_ = pbank()
                ptb = pt_.v(pt_.h[:].bitcast(BF16).rearrange("p (a c b) -> p a c b", a=4, c=2))
                for pp in range(4):
                    p = p0 + pp
                    fw.tr(ptb[:, pp, 0, :], BT[cb][:, p, :], identb[:])
                    fw.tr(ptb[:, pp, 1, :], KT[cb][:, p, :], identb[:])
                fw.copy(BKT[qb][:], ptb, e="act")
                yield

            def state_body(pos, q, cb, cs, wl):
                qb = pos * 2 + q
                sq_ = q
                p0 = 4 * q
                Minv = minv_of[(pos, q)]
                yield
                pg = pbank()
                for pp in range(4):
                    p = p0 + pp
                    fw.mm(pg[:, pp * 128:(pp + 1) * 128], AR[cb][:, p, 0:128], S0bq[q][:, pp, :], start=True, stop=False)
                    fw.mm(pg[:, pp * 128:(pp + 1) * 128], AKs[qb][:, pp, 0:128], VTs[qb][:, pp, :], start=False, stop=True)
                fw.copy(GTs[sq_][:], pg[:].rr("p (a b) -> p a b", a=4), e="dve")
                yield
                pu = pbank()
                for pp in range(4):
                    fw.mm(pu[:, pp * 128:(pp + 1) * 128], Minv[:, pp, :], GTs[sq_][:, pp, :])
                fw.copy(UTs[sq_][:], pu[:].rr("p (a b) -> p a b", a=4), e="act")
                if not so:
                    yield
                    py = pbank()
                    for pp in range(4):
                        p = p0 + pp
                        o = py[:, pp * 128:(pp + 1) * 128]
                        fw.mm(o, S0bq[q][:, pp, :], AR[cb][:, p, 128:256], start=True, stop=False)
                        fw.mm(o, UTs[sq_][:, pp, :], ABs[qb][:, pp, 128:256], start=False, stop=False)
                        fw.mm(o, VTs[qb][:, pp, :], AKs[qb][:, pp, 128:256], start=False, stop=True)
                    py4 = py[:].rr("p (a b) -> p a b", a=4)
                    if d == 0:
                        fw.copy(YB[b][0:64, p0:p0 + 4, cs], py4[0:64, :, 0:64], e="act")
                        fw.copy(YB[b][64:128, p0:p0 + 4, cs], py4[64:128, :, 64:128], e="dve")
                    else:
                        fw.tt(YB[b][0:64, p0:p0 + 4, cs], py4[0:64, :, 0:64], Y1[b][0:64, p0:p0 + 4, cs], ALU.add)
                        fw.tt(YB[b][64:128, p0:p0 + 4, cs], py4[64:128, :, 64:128], Y1[b][64:128, p0:p0 + 4, cs], ALU.add)
                    yield
                pd_ = pbank()
                for pp in range(4):
                    o = pd_[:, pp * 128:(pp + 1) * 128]
                    fw.mm(o, BKT[qb][:, pp, 0, :], UTs[sq_][:, pp, :], start=True, stop=False)
                    fw.mm(o, BKT[qb][:, pp, 1, :], VTs[qb][:, pp, :], start=False, stop=True)
                wlb = wl[:, p0:p0 + 4].us(2).bc([128, 4, 128])
                fw.tt(stmp[sq_][:], S0q[q][:], wlb, ALU.mult, e="pool")
                fw.tt(S0q[q][:], pd_[:].rr("p (a b) -> p a b", a=4), wlb, ALU.mult)
                fw.tt(S0q[q][:], S0q[q][:], stmp[sq_][:], ALU.add)
                fw.copy(S0bq[q][:], S0q[q][:], e="act")
                yield

            def run_gens(gens):
                alive = list(gens)
                while alive:
                    for g_ in list(alive):
                        try:
                            next(g_)
                        except StopIteration:
                            alive.remove(g_)

            run_gens([pre_body(pos, q, *cinfo[pos]) for pos in range(2) for q in range(2)])
            for pos in range(2):
                run_gens([state_body(pos, q, *cinfo[pos]) for q in range(2)])
            if not so:
                fw.dma(V(S["YRW"].ap()[:, ti * 128:(ti + 1) * 128].rearrange("(c p) t -> p c t", p=128), None), YB[b][:], yb_sem[b])
        for q_ in range(2):
            dst = stx_rw if so else st_rw_out
            if so or d == 0:
                fw.dma(V(dst.ap()[:, 4 * q_:4 * q_ + 4, :], None), S0q[q_][:], (ssem, ssem2)[q_])
        fw.barrier()
        st.close()

    S["YM"] = dscr("s_YM", [2048, TP])
    st_m_out = nc.dram_tensor("st_m_out", [128, 32, 64], F32, kind="ExternalOutput")
    mc_d = din("mc", [128, 2, 2, 128])
    ones = sb("ones", [128, 128])
    fw.memset(ones[:], 1.0)

    def mamba_phase(d, so=False):
        st = ExitStack()
        Ss = S2 if so else S

        def sbp(name, shape, dt=F32):
            return T(st.enter_context(nc.sbuf_tensor(("mbs_" if so else "mb%d_" % d) + name, list(shape), dt)), name)

        mcst = sbp("mcst", [128, 2, 2, 128])
        fw.dma(mcst[:], V(mc_d.ap(), None), fw.dsem("c11"))
        XS = [sbp("xs%d" % b, [128, 16, 128]) for b in range(2)]
        Bb = [sbp("bb%d" % b, [128, 8, 128], BF16) for b in range(2)]
        Cb = [sbp("cb%d" % b, [128, 8, 128], BF16) for b in range(2)]
        DT = [sbp("dt%d" % b, [128, 4, 32]) for b in range(2)]
        Y1 = [sbp("y1_%d" % b, [128, 16, 128]) for b in range(2)]
        YB = [sbp("yb_%d" % b, [128, 16, 128]) for b in range(2)]
        sems = [[fw.dsem("mbld%d_%d" % (j, b)) for j in range(4)] for b in range(2)]
        psems = [[fw.dsem("mbldp%d_%d" % (j, b)) for j in range(2)] for b in range(2)]
        yb_sem = [fw.dsem("mbyb_%d" % b) for b in range(2)]
        dAexp = sbp("dAexp", [128, 32, 128])
        cs_tok = sbp("cs_tok", [128, 32])
        csl = sbp("csl", [128, 32])
        ecl_last = sbp("ecl_last", [128, 32])
        decs = sbp("decs", [128, 32])
        E = [sbp("E%d" % b, [128, 8, 128]) for b in range(2)]
        ecsR = [sbp("ecsR%d" % b, [128, 8, 128]) for b in range(2)]
        MT = sbp("MT", [128, 32, 128], BF16)
        Csc = sbp("Csc", [128, 32, 128], BF16)
        CBm = sbp("CBm", [128, 8, 128])
        xdt = sbp("xdt", [128, 32, 64], BF16)
        xdd = sbp("xdd", [128, 32, 64], BF16)
        Btok = sbp("Btok", [128, 8, 128], BF16)
        hS = sbp("hS", [128, 32, 64])
        hb = sbp("hb", [128, 32, 64], BF16)
        htmp = sbp("htmp", [128, 32, 64])
        ssem = fw.dsem("mbstate")
        if d == 0:
            fw.memset(hS[:], 0.0)
        else:
            fw.dma(hS[:], V(stx_m.ap(), None), ssem)
            fw.ts(hS[:].rr("p a b -> p (a b)"), hS[:].rr("p a b -> p (a b)"), sel[:, 0:1], None, ALU.mult)
        fw.copy(hb[:], hS[:], e="act")
        tri = mcst[:, d, 0, :]
        lst = mcst[:, d, 1, :]
        t_last = 127 if d == 0 else 0
        NCH = T_loc // 128
        tiles = list(range(NCH)) if d == 0 else list(range(NCH - 1, -1, -1))
        pbc = {"n": 0}

        def pbank():
            j = pbc["n"] % 8
            pbc["n"] += 1
            return PB[j]

        def load_tile(n):
            ti = tiles[n]
            b = n % 2
            cs_ = slice(ti * 128, (ti + 1) * 128)
            xb = Ss["XBC"].ap()
            fw.dma(XS[b][:], V(xb[0:2048, cs_].rearrange("(c p) t -> p c t", p=128), None), sems[b][0])
            fw.dma(DT[b][:], V(Ss["DTS"].ap()[cs_], None), sems[b][1])
            fw.dma(Bb[b][:], V(xb[2048:3072, cs_].rearrange("(c p) t -> p c t", p=128), None), psems[b][0], q="pool")
            if not so:
                fw.dma(Cb[b][:], V(xb[3072:4096, cs_].rearrange("(c p) t -> p c t", p=128), None), psems[b][1], q="pool")
            if d == 1:
                fw.dma(Y1[b][:], V(S["YM"].ap()[:, cs_].rearrange("(c p) t -> p c t", p=128), None), sems[b][2])

        load_tile(0)
        for n in range(NCH):
            ti = tiles[n]
            b = n % 2
            if n + 1 < NCH:
                load_tile(n + 1)
            dA = DT[b][:, 2 + d, :]
            dtv = DT[b][:, d, :]
            if so:
                pc = pbank()
                fw.mm(pc[:, 0:32], tri, dA)
                fw.copy(cs_tok[:], pc[:, 0:32], e="act")
                pc2 = pbank()
                fw.mm(pc2[:, 0:32], ones[:], dA)
                fw.copy(csl[:], pc2[:, 0:32], e="dve")
            if not so:
                fw.tt(dAexp[:], dA.us(2).bc([128, 32, 128]), tri.us(1).bc([128, 32, 128]), ALU.mult)
                pc = pbank()
                fw.mm(pc[:, 0:32], tri, dA)
                fw.copy(cs_tok[:], pc[:, 0:32], e="act")
                for half in range(2):
                    pcb = pbank()
                    for gg in range(4):
                        g = half * 4 + gg
                        fw.mm(pcb[:, gg * 128:(gg + 1) * 128], Bb[b][:, g, :], Cb[b][:, g, :])
                    fw.tt(CBm[:, half * 4:half * 4 + 4, :], pcb[:].rr("p (a b) -> p a b", a=4), tri.us(1).bc([128, 4, 128]), ALU.mult)
                for o in range(4):
                    ob = o % 2
                    pD = [pbank(), pbank()]
                    pR = [pbank(), pbank()]
                    for j in range(2):
                        rhs = dAexp[:, o * 8 + j * 4:o * 8 + j * 4 + 4, :].rr("p a b -> p (a b)")
                        fw.mm(pD[j][:], lst, rhs)
                        fw.mm(pR[j][:], ones[:], rhs)
                    for j in range(2):
                        hs = slice(o * 8 + j * 4, o * 8 + j * 4 + 4)
                        g = o * 2 + j
                        fw.act(E[ob][:, j * 4:j * 4 + 4, :], pD[j][:].rr("p (a b) -> p a b", a=4), AF.Exp)
                        fw.tt(MT[:, hs, :], E[ob][:, j * 4:j * 4 + 4, :], CBm[:, g:g + 1, :].bc([128, 4, 128]), ALU.mult)
                        pR4 = pR[j][:].rr("p (a b) -> p a b", a=4)
                        fw.copy(csl[:, hs], pR4[:, :, t_last], e="dve")
                        fw.act(ecsR[ob][:, j * 4:j * 4 + 4, :], pR4, AF.Exp)
                        fw.tt(Csc[:, hs, :], ecsR[ob][:, j * 4:j * 4 + 4, :], Cb[b][:, g:g + 1, :].bc([128, 4, 128]), ALU.mult, e="dve")
            fw.act(ecl_last[:], csl[:], AF.Exp)
            fw.tt(decs[:], csl[:], cs_tok[:], ALU.subtract)
            fw.act(decs[:], decs[:], AF.Exp)
            for q in range(4):
                px = pbank()
                for cc in range(4):
                    c = q * 4 + cc
                    fw.tr(px[:, cc * 128:(cc + 1) * 128], XS[b][:, c, :], ident[:])
                hs = slice(q * 8, q * 8 + 8)
                fw.tt(xdt[:, hs, :], px[:].rr("p (a b) -> p a b", a=8), dtv[:, hs].us(2).bc([128, 8, 64]), ALU.mult)
                fw.tt(xdd[:, hs, :], xdt[:, hs, :], decs[:, hs].us(2).bc([128, 8, 64]), ALU.mult, e="pool")
            if not so:
                for q in range(4):
                    py = pbank()
                    for cc in range(4):
                        for hh in range(2):
                            h = (q * 4 + cc) * 2 + hh
                            o_ = py[64 * hh:64 * hh + 64, cc * 128:(cc + 1) * 128]
                            kw_ = {"tile_position": (0, 64)} if hh == 1 else {}
                            fw.mm(o_, xdt[:, h, :], MT[:, h, :], start=True, stop=False, **kw_)
                            fw.mm(o_, hb[:, h, :], Csc[:, h, :], start=False, stop=True, **kw_)
                    py4 = py[:].rr("p (a b) -> p a b", a=4)
                    if d == 0:
                        fw.copy(YB[b][:, q * 4:q * 4 + 4, :], py4, e="act")
                    else:
                        fw.tt(YB[b][:, q * 4:q * 4 + 4, :], py4, Y1[b][:, q * 4:q * 4 + 4, :], ALU.add)
                fw.dma(V(S["YM"].ap()[:, ti * 128:(ti + 1) * 128].rearrange("(c p) t -> p c t", p=128), None), YB[b][:], yb_sem[b])
            pt_ = pbank()
            ptb = pt_.v(pt_.h[:].bitcast(BF16).rearrange("p (a b) -> p a b", a=8))
            for g in range(8):
                fw.tr(ptb[:, g, :], Bb[b][:, g, :], identb[:])
            fw.copy(Btok[:], ptb, e="act")
            fw.tt(htmp[:], hS[:], ecl_last[:].us(2).bc([128, 32, 64]), ALU.mult, e="pool")
            for q in range(4):
                pn_ = pbank()
                for gg in range(2):
                    g = q * 2 + gg
                    fw.mm(pn_[:, gg * 256:(gg + 1) * 256], Btok[:, g, :], xdd[:, 4 * g:4 * g + 4, :].rr("p a b -> p (a b)"))
                hs = slice(q * 8, q * 8 + 8)
                fw.tt(hS[:, hs, :], pn_[:].rr("p (a b) -> p a b", a=8), htmp[:, hs, :], ALU.add)
            fw.copy(hb[:], hS[:], e="act")
        if so:
            fw.dma(V(stx_m.ap(), None), hS[:], ssem)
        elif d == 0:
            fw.dma(V(st_m_out.ap(), None), hS[:], ssem)
        fw.barrier()
        st.close()

    ALPHA = 2.0 ** 0.25
    LN_EPS = 1e-5
    GN_EPS = 64e-5
    mem_d = din("mem", [256, D])
    w_br_d = din("w_br", [1024, D])
    w_bm_d = din("w_bm", [D, D])
    w_o_d = din("w_o", [D, D])
    w_q_d = din("w_q", [D, D])
    w_kv_d = din("w_kv", [D, 2 * D])
    w_co_d = din("w_co", [D, D])
    w_up_d = din("w_up", [D, 4 * D])
    w_down_d = din("w_down", [4 * D, D])
    y_out = nc.dram_tensor("y_out", [T_loc, D], F32, kind="ExternalOutput")

    def wview(w):
        return w.ap().rearrange("(k p) n -> p k n", p=128)

    def phase3():
        st = ExitStack()

        def sbp(name, shape, dt=F32):
            return T(st.enter_context(nc.sbuf_tensor("p3_" + name, list(shape), dt)), name)

        F32A = sbp("F32A", [128, NK, 512])
        BFA = sbp("BFA", [128, NK, 512], BF16)
        BFB = sbp("BFB", [128, NK, 512], BF16)
        BFC = sbp("BFC", [128, NK, 512], BF16)
        BFD = sbp("BFD", [128, 8, 512], BF16)
        HM = sbp("HM", [128, 16, 512], BF16)
        Kt = sbp("Kt", [128, NK, 256], BF16)
        Vt = sbp("Vt", [128, 2, D], BF16)
        onesb = sbp("onesb", [128, 128], BF16)
        ksc = sbp("ksc", [128, 4])
        fw.copy(onesb[:], ones[:], e="act")
        NWB = 3
        WB = [sbp("wb%d" % j, [128, 4096], BF16) for j in range(NWB)]
        wb_sem = [fw.dsem("p3wb%d" % j) for j in range(NWB)]
        NL = 6
        LB = [sbp("lb%d" % j, [128, 512]) for j in range(NL)]
        lb_sem = [fw.dsem("p3lb%d" % j) for j in range(NL)]
        NTMP = 5
        TMP = [sbp("tmp%d" % j, [128, 512]) for j in range(NTMP)]
        xs = [sbp("xs%d" % j, [128, D]) for j in range(2)]
        xs_sem = [fw.dsem("p3xs%d" % j) for j in range(2)]
        ymp = [sbp("ymp%d" % j, [128, 2, 512]) for j in range(1)]
        sqp = [sbp("sqp%d" % j, [128, 2, 512]) for j in range(1)]
        expS = [sbp("expS%d" % j, [128, 2, 512], BF16) for j in range(1)]
        ded = {nm: sbp("ded_" + nm, [128, 512]) for nm in ("mean", "rstd", "cst", "rs")}
        cnt = {"pb": 0, "lb": 0, "tmp": 0}

        def pbank():
            j = cnt["pb"] % 8
            cnt["pb"] += 1
            return PB[j]

        def tmp():
            j = cnt["tmp"] % NTMP
            cnt["tmp"] += 1
            return TMP[j]

        def ld(name, row0, t0):
            j = cnt["lb"] % NL
            cnt["lb"] += 1
            fw.dma(LB[j][:], V(S[name].ap()[row0:row0 + 128, t0:t0 + 512], None), lb_sem[j])
            return LB[j]

        class WS:
            def __init__(self):
                self.specs = []
                self.issued = 0
                self.tiles = {}

            def add(self, wv, k0, nk, col0, ncols):
                self.specs.append((wv, k0, nk, col0, ncols))
                return len(self.specs) - 1

            def _issue(self, n):
                wv, k0, nk, col0, ncols = self.specs[n]
                j = n % NWB
                tv = WB[j][:, 0:nk * ncols].rr("p (k n) -> p k n", k=nk)
                fw.dma(tv, V(wv[:, k0:k0 + nk, col0:col0 + ncols], None), wb_sem[j], q="pool")
                self.tiles[n] = tv

            def get(self, n):
                while self.issued < min(len(self.specs), n + NWB):
                    self._issue(self.issued)
                    self.issued += 1
                return self.tiles.pop(n)

        w_in_v3 = w_in_v

        def dense(ws_ids, ws, src, nk, consume):
            pass

        def layer_norm(gname, bname):
            ps1 = pbank()
            ps2 = pbank()
            for c in range(NK):
                sq = tmp()
                fw.act(sq[:], F32A[:, c, :], AF.Square)
                fw.mm(ps1[:], ones[:], F32A[:, c, :], start=(c == 0), stop=(c == NK - 1))
                fw.mm(ps2[:], ones[:], sq[:], start=(c == 0), stop=(c == NK - 1), sig=True)
            mean = ded["mean"]
            fw.act(mean[:], ps1[:], AF.Copy, scale=1.0 / D)
            msq = tmp()
            fw.act(msq[:], ps1[:], AF.Square, scale=1.0 / D)
            rstd = ded["rstd"]
            fw.stt(rstd[:], ps2[:], 1.0 / D, msq[:], ALU.mult, ALU.subtract)
            fw.act(rstd[:], rstd[:], AF.Ln, bias=LN_EPS)
            fw.act(rstd[:], rstd[:], AF.Exp, scale=-0.5)
            for c in range(NK):
                t_ = tmp()
                fw.tt(t_[:], F32A[:, c, :], mean[:], ALU.subtract)
                fw.tt(t_[:], t_[:], rstd[:], ALU.mult)
                fw.act(F32A[:, c, :], t_[:], AF.Identity, scale=pv(gname, c), bias=pv(bname, c))
                fw.copy(BFA[:, c, :], F32A[:, c, :], e="dve")

        memT = BFB
        memTv = memT[:, :, 0:256]
        for mb in range(2):
            fw.dma(xs[mb][:], V(mem_d.ap()[mb * 128:(mb + 1) * 128, :], None), xs_sem[mb])
            for kq in range(4):
                pt = pbank()
                for k4 in range(4):
                    k = kq * 4 + k4
                    fw.tr(pt[:, k4 * 128:(k4 + 1) * 128], xs[mb][:, k * 128:(k + 1) * 128], ident[:])
                fw.copy(memT[:, kq * 4:(kq + 1) * 4, mb * 128:(mb + 1) * 128], pt[:].rr("p (a b) -> p a b", a=4), e="act")
        ws = WS()
        wkv = wview(w_kv_d)
        ids = [ws.add(wkv, 0, NK, c * 256, 256) for c in range(16)]
        for c in range(8):
            wt = ws.get(ids[c])
            for oo in range(2):
                oc = c * 2 + oo
                pk = pbank()
                for k in range(NK):
                    fw.mm(pk[:, 0:256], wt[:, k, oo * 128:(oo + 1) * 128], memTv[:, k, :], start=(k == 0), stop=(k == NK - 1))
                fw.copy(Kt[:, oc, :], pk[:, 0:256], e="act")
        for c in range(8):
            wt = ws.get(ids[8 + c])
            for mb in range(2):
                pvv = pbank()
                for k in range(NK):
                    fw.mm(pvv[:, 0:256], memT[:, k, mb * 128:(mb + 1) * 128], wt[:, k, :], start=(k == 0), stop=(k == NK - 1))
                fw.copy(Vt[:, mb, c * 256:(c + 1) * 256], pvv[:, 0:256], e="dve")
        for hd in range(4):
            pk2 = pbank()
            for kc in range(4):
                sqk = tmp()
                sqkb = sqk[:, 0:128].ap.bitcast(BF16)
                sqv = V(sqkb, sqk.buf)
                fw.act(sqv, Kt[:, hd * 4 + kc, :], AF.Square)
                fw.mm(pk2[:, 0:256], onesb[:], sqv, start=(kc == 0), stop=(kc == 3))
            mx = tmp()
            i_ = nc.vector
            r_, w_ = fw._bufs([pk2[:]]), fw._bufs([mx[:]])
            fw._deps("dve", r_, w_)
            ins = nc.vector.tensor_reduce(mx[:, 0:1].ap, pk2[:, 0:256].ap, AX.X, ALU.max)
            fw._done(ins, "dve", 1, r_, w_)
            fw.ts(ksc[:, hd:hd + 1], mx[:, 0:1], 1.0 / 512.0, None, ALU.mult)

        NT3 = T_loc // 512
        for i in range(NT3):
            t0 = i * 512
            ws = WS()
            wbr, wbm, wo, wq, wco, wup, wdn = [wview(w) for w in (w_br_d, w_bm_d, w_o_d, w_q_d, w_co_d, w_up_d, w_down_d)]
            id_z = [ws.add(w_in_v3, 0, NK, C_Z + c * 256, 256) for c in range(8)]
            id_d = []
            for c in range(8):
                id_d.append((ws.add(wbr, 0, 8, c * 256, 256), ws.add(w_in_v3, 0, NK, C_GATES + c * 256, 256),
                             ws.add(wbm, 0, NK, c * 256, 256), ws.add(w_in_v3, 0, NK, C_GATES + 2048 + c * 256, 256)))
            id_o = [ws.add(wo, 0, NK, c * 256, 256) for c in range(8)]
            id_q = [ws.add(wq, 0, NK, c * 256, 256) for c in range(8)]
            id_co = [ws.add(wco, 0, NK, c * 256, 256) for c in range(8)]
            id_up, id_dn = [], []
            for hf in range(4):
                id_up.append([ws.add(wup, 0, NK, hf * 2048 + c * 256, 256) for c in range(8)])
                id_dn.append([ws.add(wdn, hf * 16, 16, c * 256, 256) for c in range(8)])
            for j in range(4):
                xb = xs[j % 2]
                fw.dma(xb[:], V(x_ext.ap()[2 + t0 + j * 128:2 + t0 + (j + 1) * 128, :], None), xs_sem[j % 2])
                for kq in range(4):
                    pt = pbank()
                    for k4 in range(4):
                        k = kq * 4 + k4
                        fw.tr(pt[:, k4 * 128:(k4 + 1) * 128], xb[:, k * 128:(k + 1) * 128], ident[:])
                    pt4 = pt[:].rr("p (a b) -> p a b", a=4)
                    fw.copy(F32A[:, kq * 4:(kq + 1) * 4, j * 128:(j + 1) * 128], pt4, e="act")
                    fw.copy(BFA[:, kq * 4:(kq + 1) * 4, j * 128:(j + 1) * 128], pt4, e="dve")
            for c in range(8):
                y = ld("YRW", c * 128, t0)
                bon = ld("BON", c * 128, t0)
                gg = ld("G", c * 128, t0)
                sq = tmp()
                fw.act(sq[:], y[:], AF.Square)
                p1 = pbank()
                fw.mm(p1[:], blk[:], y[:])
                p2 = pbank()
                fw.mm(p2[:], blk[:], sq[:])
                m = tmp()
                fw.act(m[:], p1[:], AF.Copy, scale=1.0 / 64)
                msq = tmp()
                fw.act(msq[:], p1[:], AF.Square, scale=1.0 / 64)
                var = tmp()
                fw.stt(var[:], p2[:], 1.0 / 64, msq[:], ALU.mult, ALU.subtract)
                fw.act(var[:], var[:], AF.Ln, bias=GN_EPS)
                fw.act(var[:], var[:], AF.Exp, scale=-0.5)
                fw.tt(y[:], y[:], m[:], ALU.subtract)
                fw.tt(y[:], y[:], var[:], ALU.mult)
                fw.act(y[:], y[:], AF.Identity, scale=pv("gn_g", c), bias=pv("gn_b", c))
                fw.tt(y[:], y[:], bon[:], ALU.add)
                fw.tt(BFD[:, c, :], y[:], gg[:], ALU.mult)
            for c in range(NK):
                if c % 2 == 0:
                    wz = ws.get(id_z[c // 2])
                pb_ = 0
                ym = ld("YM", c * 128, t0)
                xv = ld("XBC", c * 128, t0)
                pz = pbank()
                for k in range(NK):
                    fw.mm(pz[:], wz[:, k, (c % 2) * 128:(c % 2 + 1) * 128], BFA[:, k, :], start=(k == 0), stop=(k == NK - 1))
                fw.stt(ym[:], xv[:], pv("m_d", c), ym[:], ALU.mult, ALU.add)
                sz = tmp()
                fw.act(sz[:], pz[:], AF.Silu)
                fw.tt(ymp[pb_][:, c % 2, :], ym[:], sz[:], ALU.mult)
                fw.act(sqp[pb_][:, c % 2, :], ymp[pb_][:, c % 2, :], AF.Square)
                if c % 2 == 1:
                    pss = pbank()
                    fw.mm(pss[:], ones[:], sqp[pb_][:, 0, :], start=True, stop=False)
                    fw.mm(pss[:], ones[:], sqp[pb_][:, 1, :], start=False, stop=True)
                    rms = tmp()
                    fw.act(rms[:], pss[:], AF.Ln, scale=1.0 / 256, bias=LN_EPS)
                    fw.act(rms[:], rms[:], AF.Exp, scale=-0.5)
                    for cc in range(2):
                        fw.stt(BFB[:, c - 1 + cc, :], ymp[pb_][:, cc, :], pv("m_norm_g", c - 1 + cc), rms[:], ALU.mult, ALU.mult)
            for c in range(8):
                wt = ws.get(id_d[c][0])
                pu = [pbank(), pbank()]
                for oo in range(2):
                    for k in range(8):
                        fw.mm(pu[oo][:], wt[:, k, oo * 128:(oo + 1) * 128], BFD[:, k, :], start=(k == 0), stop=(k == 7))
                wt = ws.get(id_d[c][1])
                t1 = [tmp(), tmp()]
                for oo in range(2):
                    pg = pbank()
                    for k in range(NK):
                        fw.mm(pg[:], wt[:, k, oo * 128:(oo + 1) * 128], BFA[:, k, :], start=(k == 0), stop=(k == NK - 1))
                    fw.act(t1[oo][:], pg[:], AF.Sigmoid)
                    fw.tt(t1[oo][:], t1[oo][:], pu[oo][:], ALU.mult)
                wt = ws.get(id_d[c][2])
                pm = [pbank(), pbank()]
                for oo in range(2):
                    for k in range(NK):
                        fw.mm(pm[oo][:], wt[:, k, oo * 128:(oo + 1) * 128], BFB[:, k, :], start=(k == 0), stop=(k == NK - 1))
                wt = ws.get(id_d[c][3])
                for oo in range(2):
                    oc = c * 2 + oo
                    pg2 = pbank()
                    for k in range(NK):
                        fw.mm(pg2[:], wt[:, k, oo * 128:(oo + 1) * 128], BFA[:, k, :], start=(k == 0), stop=(k == NK - 1))
                    sg2 = tmp()
                    fw.act(sg2[:], pg2[:], AF.Sigmoid)
                    fw.tt(sg2[:], sg2[:], pm[oo][:], ALU.mult)
                    fw.tt(BFC[:, oc, :], t1[oo][:], sg2[:], ALU.add)

            def proj_res(idl, src, first=True):
                for c in range(8):
                    wt = ws.get(idl[c])
                    for oo in range(2):
                        oc = c * 2 + oo
                        po = pbank()
                        for k in range(NK):
                            fw.mm(po[:], wt[:, k, oo * 128:(oo + 1) * 128], src[:, k, :], start=(k == 0), stop=(k == NK - 1))
                        fw.stt(F32A[:, oc, :], F32A[:, oc, :], ALPHA, po[:], ALU.mult, ALU.add)

            proj_res(id_o, BFC)
            layer_norm("ln1_g", "ln1_b")
            for c in range(8):
                wt = ws.get(id_q[c])
                for oo in range(2):
                    oc = c * 2 + oo
                    pq = pbank()
                    for k in range(NK):
                        fw.mm(pq[:], wt[:, k, oo * 128:(oo + 1) * 128], BFA[:, k, :], start=(k == 0), stop=(k == NK - 1))
                    fw.copy(BFB[:, oc, :], pq[:], e="act")
            inv = 1.0 / math.sqrt(512.0)
            for hd in range(4):
                eb = 0
                pq2 = pbank()
                for kc in range(4):
                    sqq = tmp()
                    sqv = V(sqq[:].ap.bitcast(BF16)[:, 0:512], sqq.buf)
                    fw.act(sqv, BFB[:, hd * 4 + kc, :], AF.Square)
                    fw.mm(pq2[:], onesb[:], sqv, start=(kc == 0), stop=(kc == 3))
                cst = ded["cst"]
                fw.act(cst[:], pq2[:], AF.Sqrt, scale=ksc[:, hd:hd + 1])
                for mc in range(2):
                    ps_ = pbank()
                    for kc in range(4):
                        fw.mm(ps_[:], Kt[:, hd * 4 + kc, mc * 128:(mc + 1) * 128], BFB[:, hd * 4 + kc, :], start=(kc == 0), stop=(kc == 3))
                    ein = tmp()
                    fw.stt(ein[:], ps_[:], inv, cst[:], ALU.mult, ALU.subtract)
                    fw.act(expS[eb][:, mc, :], ein[:], AF.Exp)
                psum_ = pbank()
                for mc in range(2):
                    fw.mm(psum_[:], onesb[:], expS[eb][:, mc, :], start=(mc == 0), stop=(mc == 1))
                rs = ded["rs"]
                r_, w_ = fw._bufs([psum_[:]]), fw._bufs([rs[:]])
                fw._deps("dve", r_, w_)
                ins = nc.vector.reciprocal(rs[:].ap, psum_[:].ap)
                fw._done(ins, "dve", 1, r_, w_)
                for dc in range(4):
                    po = pbank()
                    col = hd * 512 + dc * 128
                    for mc in range(2):
                        fw.mm(po[:], Vt[:, mc, col:col + 128], expS[eb][:, mc, :], start=(mc == 0), stop=(mc == 1))
                    fw.tt(BFC[:, hd * 4 + dc, :], po[:], rs[:], ALU.mult)
            proj_res(id_co, BFC)
            layer_norm("ln2_g", "ln2_b")
            for hf in range(4):
                for c in range(8):
                    wt = ws.get(id_up[hf][c])
                    for oo in range(2):
                        oc = c * 2 + oo
                        ph = pbank()
                        for k in range(NK):
                            fw.mm(ph[:], wt[:, k, oo * 128:(oo + 1) * 128], BFA[:, k, :], start=(k == 0), stop=(k == NK - 1))
                        rl = tmp()
                        fw.act(rl[:], ph[:], AF.Relu)
                        fw.tt(HM[:, oc, :], rl[:], rl[:], ALU.mult)
                for c in range(8):
                    wt = ws.get(id_dn[hf][c])
                    for oo in range(2):
                        oc = c * 2 + oo
                        po = pbank()
                        for k in range(16):
                            fw.mm(po[:], wt[:, k, oo * 128:(oo + 1) * 128], HM[:, k, :], start=(k == 0), stop=(k == 15))
                        if hf == 0:
                            fw.stt(F32A[:, oc, :], F32A[:, oc, :], ALPHA, po[:], ALU.mult, ALU.add)
                        else:
                            fw.tt(F32A[:, oc, :], F32A[:, oc, :], po[:], ALU.add)
            layer_norm("ln3_g", "ln3_b")
            for j in range(4):
                ob_ = xs[j % 2]
                for kq in range(4):
                    pt = pbank()
                    for k4 in range(4):
                        k = kq * 4 + k4
                        fw.tr(pt[:, k4 * 128:(k4 + 1) * 128], F32A[:, k, j * 128:(j + 1) * 128], ident[:])
                    fw.copy(ob_[:, kq * 512:(kq + 1) * 512], pt[:], e=("act" if kq % 2 else "dve"))
                fw.dma(V(y_out.ap()[t0 + j * 128:t0 + (j + 1) * 128, :], None), ob_[:], xs_sem[j % 2])
        fw.barrier()
        st.close()

    if 6 in phases:
        phase0(lite=True)
        fw.barrier()
        rwkv_phase(0, so=True)
        mamba_phase(0, so=True)
    if 0 in phases:
        phase0()
    fw.barrier()
    if 1 in phases:
        rwkv_phase(0)
    if 3 in phases:
        mamba_phase(0)
    if 2 in phases:
        rwkv_phase(1)
    if 4 in phases:
        mamba_phase(1)
    if 5 in phases:
        phase3()
    fw.barrier()
    es.close()
    return nc, fw


def host_params(inp, swap=False):
    g = lambda k: np.asarray(inp[k])[0]
    pvec = np.zeros((128, NPAR), np.float32)

    def put(name, vec, j0=0):
        vec = np.asarray(vec, np.float32)
        n = vec.shape[0]
        nch = (n + 127) // 128
        pad = np.zeros(nch * 128, np.float32)
        pad[:n] = vec
        pvec[:, POFF[name] + j0:POFF[name] + j0 + nch] = pad.reshape(nch, 128).T

    mup, mun = g("rw_mu_prev"), g("rw_mu_next")
    if swap:
        mup, mun = mun, mup
    for nm, mu in (("mup", mup), ("mun", mun)):
        put(nm, mu[0:3072], 0)
        put(nm, mu[3072:3168], 24)
        put(nm, mu[3168:3264], 25)
        put(nm, mu[3264:3520], 26)
    dirs = (1, 0) if swap else (0, 1)
    for d in range(2):
        put("w0", g("rw_w0")[dirs[d]], 8 * d)
        put("a0", g("rw_a0")[dirs[d]], 8 * d)
    put("k_k", g("rw_k_k"))
    put("k_a", g("rw_k_a"))
    put("r_k", g("rw_r_k").reshape(-1))
    put("gn_g", g("rw_gn_g"))
    put("gn_b", g("rw_gn_b"))
    cw = g("m_conv_w")
    if swap:
        cw = cw[::-1]
    for j in range(5):
        put("conv_w", cw[j], 32 * j)
    put("conv_b", g("m_conv_b"))
    put("m_norm_g", g("m_norm_g"))
    put("m_d", np.repeat(g("m_d"), 64))
    for nm in ("ln1_g", "ln1_b", "ln2_g", "ln2_b", "ln3_g", "ln3_b"):
        put(nm, g(nm))
    dtp = np.zeros((128, 2, 2, 4, 32), np.float32)
    for d in range(2):
        dtp[:, 0, d] = g("m_dt_bias")[dirs[d]][None, None, :]
        dtp[:, 1, d] = g("m_a_log")[dirs[d]][None, None, :]
    blk = np.zeros((128, 128), np.float32)
    blk[:64, :64] = 1.0
    blk[64:, 64:] = 1.0
    rwc = np.zeros((128, 2, 512), np.float32)
    idx = np.arange(128)
    hh, ss = idx // 64, idx % 64
    same = hh[:, None] == hh[None, :]
    lt = ss[:, None] < ss[None, :]
    le = ss[:, None] <= ss[None, :]
    rwc[:, 0, 0:128] = same & lt
    rwc[:, 0, 128:256] = same & le
    rwc[:, 0, 256:384] = same & lt.T
    rwc[:, 1, 0:128] = same & lt.T
    rwc[:, 1, 128:256] = same & le.T
    rwc[:, 1, 256:384] = same & lt
    mc = np.zeros((128, 2, 2, 128), np.float32)
    i128 = np.arange(128)
    mc[:, 0, 0] = i128[:, None] <= i128[None, :]
    mc[:, 0, 1] = i128[:, None] > i128[None, :]
    mc[:, 1, 0] = i128[:, None] >= i128[None, :]
    mc[:, 1, 1] = i128[:, None] < i128[None, :]
    rmask = np.ones((128, 1024), np.float32)
    rmask[:, ::64] = 0.0
    return dict(pvec=pvec, dtp=dtp, ident=np.eye(128, dtype=np.float32), blk64=blk, rwc=rwc, rmask=rmask,
                mc=mc,
                rw_w2=g("rw_w2"), rw_a2=g("rw_a2"), rw_g2=g("rw_g2"), w_in=g("w_in"),
                w_br=g("w_br"), w_bm=g("w_bm"), w_o=g("w_o"), w_q=g("w_q"), w_kv=g("w_kv"), w_co=g("w_co"),
                w_up=g("w_up"), w_down=g("w_down"))


T_CORE = 8192
_CACHE = {}


def _x_ext(xseq, start, T_loc, rev):
    NT0 = (T_loc + TV - 1) // TV
    TP = NT0 * TV
    L = xseq.shape[0]
    out = np.zeros((TP + 4, D), np.float32)
    if not rev:
        lo, hi = start - 2, start + T_loc + 2
        slo, shi = max(lo, 0), min(hi, L)
        out[slo - lo:shi - lo] = xseq[slo:shi]
    else:
        lo, hi = start - 2, start + T_loc + 2
        slo, shi = max(lo, 0), min(hi, L)
        seg = xseq[slo:shi][::-1]
        r0 = start + T_loc + 1 - (shi - 1)
        out[r0:r0 + seg.shape[0]] = seg
    return out


def kernel(**inputs):
    inp = {k: np.asarray(v) for k, v in inputs.items()}
    xp, xs_, mp, ms = inp["x_prompt"], inp["x_sample"], inp["mem_prompt"], inp["mem_sample"]
    T_loc = T_CORE
    if "nc" not in _CACHE:
        _CACHE["nc"] = build(T_loc)[0]
    nc = _CACHE["nc"]
    hp = [host_params(inp, swap=False), host_params(inp, swap=True)]
    cores = []
    for c in range(8):
        if c < 4:
            s, half = c // 2, c % 2
            cores.append(dict(x=xp[s], start=half * T_loc, rev=(half == 1), mem=mp[s]))
        else:
            cores.append(dict(x=xs_[c - 4], start=0, rev=False, mem=ms[c - 4]))
    base_maps = []
    xe = [_x_ext(cd["x"], cd["start"], T_loc, cd["rev"]) for cd in cores]
    for c, cd in enumerate(cores):
        own = hp[1 if cd["rev"] else 0]
        m = dict(own)
        m["x_ext"] = xe[c]
        m["mem"] = np.ascontiguousarray(cd["mem"], dtype=np.float32)
        if c < 4:
            partner = c ^ 1
            oth = hp[1 if cores[partner]["rev"] else 0]
            m["x_ext2"] = xe[partner]
            m["pvec2"] = oth["pvec"]
            m["dtp2"] = oth["dtp"]
            m["sel"] = np.ones((128, 1), np.float32)
        else:
            m["x_ext2"] = xe[c]
            m["pvec2"] = own["pvec"]
            m["dtp2"] = own["dtp"]
            m["sel"] = np.zeros((128, 1), np.float32)
        base_maps.append(m)
    res2 = run_bass_kernel_spmd(nc, base_maps, core_ids=list(range(8)))
    ys = [np.asarray(res2.results[c]["y_out"], np.float32) for c in range(8)]
    y_prompt = np.empty_like(xp)
    for c in range(4):
        s, half = c // 2, c % 2
        y_prompt[s, half * T_loc:(half + 1) * T_loc] = ys[c][::-1] if half == 1 else ys[c]
    y_sample = np.stack(ys[4:8], axis=0)
    return (y_prompt, y_sample)
```
